# Optimizing a Trainium2 kernel written in Bass

```python
import math
import jax
import jax.numpy as jnp
from jax import lax
import numpy as np


D_MODEL = 2048
BATCH = 2
SEQ = 16384
DEPTH = 2

GRID_W = 64
CTX_LEN = 256
D_MIX = D_MODEL
S5_WIDTH = D_MIX // 4
POOL_WIDTH = D_MIX // 4
RWKV_WIDTH = D_MIX // 2
S5_GROUP = 16
S5_GROUPS = S5_WIDTH // S5_GROUP
S5_STATE = 64
POOL_WINDOWS = (2, 4, 8, 16)
POOL_GROUP = POOL_WIDTH // len(POOL_WINDOWS)
RWKV_HEAD = 64
RWKV_HEADS = RWKV_WIDTH // RWKV_HEAD
RWKV_LORA = 64
RWKV_CONV = 3
N_DIR = 2
IN_SIZES = (S5_WIDTH, S5_WIDTH, POOL_WIDTH, POOL_WIDTH, 3 * RWKV_WIDTH, RWKV_WIDTH, N_DIR * RWKV_LORA, N_DIR * RWKV_LORA)
IN_OFFSETS = tuple(int(s) for s in np.cumsum(IN_SIZES)[:-1])
D_IN = sum(IN_SIZES)
DEEPNORM_ALPHA = (2 * DEPTH) ** 0.25
DEEPNORM_BETA = (8 * DEPTH) ** -0.25
RWKV_DECAY_SCALE = 0.606531
S5_MAX_RE = -1e-4
ADALN_EPS = 1e-6
LN_EPS = 1e-5
GN_EPS = 64e-5
L2_EPS = 1e-12
F32 = jnp.float32

kernel_name = "hybrid_s5_pool_rwkv7_prefix_dit"


def _layernorm(x, eps):
    x = x.astype(F32)
    mu = jnp.mean(x, axis=-1, keepdims=True)
    var = jnp.mean(jnp.square(x - mu), axis=-1, keepdims=True)
    return (x - mu) * lax.rsqrt(var + eps)


def _modulation(cond, w_ada, b_ada):
    m = jax.nn.silu(cond.astype(F32)) @ w_ada + b_ada
    return jnp.split(m, 3, axis=-1)


def _split_in(z):
    return jnp.split(z, IN_OFFSETS, axis=-1)


def _short_conv(z, w):
    n = z.shape[1]
    pad = w.shape[0] // 2
    zp = jnp.pad(z.astype(F32), ((0, 0), (pad, pad), (0, 0)))
    out = zp[:, 0:n] * w[0]
    for i in range(1, w.shape[0]):
        out = out + zp[:, i:i + n] * w[i]
    return out


def _s5_discretize(lam_re, lam_im, log_step):
    lam = lax.complex(jnp.minimum(lam_re.astype(F32), S5_MAX_RE), lam_im.astype(F32))
    step = jnp.exp(log_step.astype(F32))[..., None]
    log_lam_bar = lam * step
    b_scale = (jnp.exp(log_lam_bar) - 1.0) / lam
    return log_lam_bar, b_scale


def _s5_scan(bu, log_lam_bar, x0, reverse):
    n = bu.shape[0]
    first = n - 1 if reverse else 0
    bu = bu.at[first].add(jnp.exp(log_lam_bar) * x0)
    counts = jnp.ones((n, 1, 1, 1), F32)

    def combine(earlier, later):
        n_e, b_e = earlier
        n_l, b_l = later
        return n_e + n_l, jnp.exp(log_lam_bar * n_l) * b_e + b_l

    _, states = lax.associative_scan(combine, (counts, bu), reverse=reverse)
    return states


def _s5_states(u, init_states, log_lam_bar, b_scale, b_re, b_im):
    bsz, n, _ = u.shape
    ug = u.astype(F32).reshape(bsz, n, S5_GROUPS, S5_GROUP)
    bu = lax.complex(jnp.einsum('bngi,gpi->nbgp', ug, b_re.astype(F32)),
                     jnp.einsum('bngi,gpi->nbgp', ug, b_im.astype(F32)))
    return [_s5_scan(b_scale[d] * bu, log_lam_bar[d], init_states[d], reverse=(d == 1)) for d in range(N_DIR)]


def _s5_readout(u, states, c_re, c_im, d_skip, w_glu, b_glu):
    u = u.astype(F32)
    bsz, n, _ = u.shape
    y = d_skip * u
    for d in range(N_DIR):
        y_d = (jnp.einsum('nbgp,gip->bngi', states[d].real, c_re[d])
               - jnp.einsum('nbgp,gip->bngi', states[d].imag, c_im[d]))
        y = y + y_d.reshape(bsz, n, S5_WIDTH)
    y = jax.nn.gelu(y)
    return y * jax.nn.sigmoid(y @ w_glu + b_glu)


def _box_mean(z, w, axis):
    n = z.shape[axis]
    lo_off = w // 2
    idx = jnp.arange(n)
    lo = jnp.clip(idx - lo_off, 0, n - 1)
    hi = jnp.clip(idx - lo_off + w - 1, 0, n - 1)
    cs = jnp.cumsum(z, axis=axis)
    cs = jnp.concatenate([jnp.zeros_like(lax.slice_in_dim(cs, 0, 1, axis=axis)), cs], axis=axis)
    total = jnp.take(cs, hi + 1, axis=axis) - jnp.take(cs, lo, axis=axis)
    shape = [1] * z.ndim
    shape[axis] = n
    cnt = (hi - lo + 1).astype(z.dtype).reshape(shape)
    return total / cnt


def _pool_branch(u, w_pool, pool_scale, on_grid):
    u = u.astype(F32)
    bsz, n, _ = u.shape
    outs = []
    for g, (ug, win) in enumerate(zip(jnp.split(u, len(POOL_WINDOWS), axis=-1), POOL_WINDOWS)):
        if on_grid:
            rows = n // GRID_W
            grid = ug.reshape(bsz, rows, GRID_W, POOL_GROUP)
            m = _box_mean(_box_mean(grid, win, 1), win, 2).reshape(bsz, n, POOL_GROUP)
        else:
            m = _box_mean(ug, win, 1)
        outs.append((m - ug) @ w_pool[g])
    return jnp.concatenate(outs, axis=-1) * pool_scale


def _rwkv_scan(s0, r, w, k, v, kk, kka, reverse, want_out):
    def step(s, inp):
        w_t, k_t, v_t, kk_t, kka_t = inp[:5]
        s_kk = jnp.einsum('bhvk,bhk->bhv', s, kk_t)
        s = (s * w_t[:, :, None, :] - s_kk[..., None] * kka_t[:, :, None, :]
             + v_t[..., None] * k_t[:, :, None, :])
        if want_out:
            return s, jnp.einsum('bhvk,bhk->bhv', s, inp[5])
        return s, None

    xs = (w, k, v, kk, kka) + ((r,) if want_out else ())
    s_fin, o = lax.scan(step, s0, tuple(jnp.moveaxis(t, 1, 0) for t in xs), reverse=reverse)
    return s_fin, (jnp.moveaxis(o, 0, 1) if want_out else None)


def _rwkv_sequence(rkv, w_codes, a_codes, init_states, w0, w2, a0, a2, k_k, k_a, r_k, gn_w, gn_b, want_out):
    bsz, n = rkv.shape[:2]

    def heads(t):
        return t.astype(F32).reshape(bsz, n, RWKV_HEADS, RWKV_HEAD)

    r, k, v = (heads(t) for t in jnp.split(rkv, 3, axis=-1))
    kk = k * k_k.reshape(RWKV_HEADS, RWKV_HEAD)
    kk = kk / jnp.maximum(jnp.sqrt(jnp.sum(jnp.square(kk), axis=-1, keepdims=True)), L2_EPS)
    k_a_h = k_a.reshape(RWKV_HEADS, RWKV_HEAD)
    w_codes = w_codes.astype(F32).reshape(bsz, n, N_DIR, RWKV_LORA)
    a_codes = a_codes.astype(F32).reshape(bsz, n, N_DIR, RWKV_LORA)
    finals = []
    o_sum = 0.0
    k_sum = 0.0
    for d in range(N_DIR):
        w = heads(jnp.exp(-RWKV_DECAY_SCALE * jax.nn.sigmoid(w0[d] + jnp.tanh(w_codes[:, :, d]) @ w2[d])))
        a = heads(jax.nn.sigmoid(a0[d] + a_codes[:, :, d] @ a2[d]))
        k_d = k * (1.0 + (a - 1.0) * k_a_h)
        s_fin, o = _rwkv_scan(init_states[d], r, w, k_d, v, kk, kk * a, reverse=(d == 1), want_out=want_out)
        finals.append(s_fin)
        if want_out:
            o_sum = o_sum + o
            k_sum = k_sum + k_d
    if not want_out:
        return None, finals
    mu = jnp.mean(o_sum, axis=-1, keepdims=True)
    var = jnp.mean(jnp.square(o_sum - mu), axis=-1, keepdims=True)
    on = ((o_sum - mu) * lax.rsqrt(var + GN_EPS)).reshape(bsz, n, RWKV_WIDTH) * gn_w + gn_b
    bonus = jnp.sum(r * k_sum * r_k, axis=-1, keepdims=True) * v
    return on + bonus.reshape(bsz, n, RWKV_WIDTH), finals


def _merge(ys, gates, w_out):
    return jnp.concatenate([y * jax.nn.silu(g.astype(F32)) for y, g in zip(ys, gates)], axis=-1) @ w_out


def _layer(x, xc, c, c_ctx, w_ada, b_ada, w_in, conv_rkv, s5_lam_re, s5_lam_im, s5_log_step,
           s5_b_re, s5_b_im, s5_c_re, s5_c_im, s5_d, w_glu, b_glu, w_pool, pool_scale,
           rwkv_w0, rwkv_w2, rwkv_a0, rwkv_a2, rwkv_k_k, rwkv_k_a, rwkv_r_k, gn_w, gn_b,
           w_out, ln_g, ln_b, ctx_out):
    bsz = x.shape[0]
    shift, scale, gate = _modulation(c, w_ada, b_ada)
    shift_c, scale_c, gate_c = _modulation(c_ctx, w_ada, b_ada)
    h = _layernorm(x, ADALN_EPS) * (1.0 + scale[:, None]) + shift[:, None]
    hc = _layernorm(xc, ADALN_EPS) * (1.0 + scale_c) + shift_c
    s5_u, s5_g, pool_u, pool_g, rkv, rwkv_g, w_codes, a_codes = _split_in(h @ w_in)
    s5_uc, s5_gc, pool_uc, pool_gc, rkv_c, rwkv_gc, w_codes_c, a_codes_c = _split_in(hc @ w_in)

    log_lam_bar, b_scale = _s5_discretize(s5_lam_re, s5_lam_im, s5_log_step)
    zero_s5 = jnp.zeros((bsz, S5_GROUPS, S5_STATE), jnp.complex64)
    st_c = _s5_states(s5_uc, (zero_s5, zero_s5), log_lam_bar, b_scale, s5_b_re, s5_b_im)
    st = _s5_states(s5_u, (st_c[0][-1], st_c[1][0]), log_lam_bar, b_scale, s5_b_re, s5_b_im)
    y_s5 = _s5_readout(s5_u, st, s5_c_re, s5_c_im, s5_d, w_glu, b_glu)

    y_pool = _pool_branch(pool_u, w_pool, pool_scale, on_grid=True)

    rwkv_p = (rwkv_w0, rwkv_w2, rwkv_a0, rwkv_a2, rwkv_k_k, rwkv_k_a, rwkv_r_k, gn_w, gn_b)
    zero_rw = jnp.zeros((bsz, RWKV_HEADS, RWKV_HEAD, RWKV_HEAD), F32)
    y_rwkv_c, fin_c = _rwkv_sequence(_short_conv(rkv_c, conv_rkv), w_codes_c, a_codes_c,
                                     (zero_rw, zero_rw), *rwkv_p, want_out=ctx_out)
    y_rwkv, _ = _rwkv_sequence(_short_conv(rkv, conv_rkv), w_codes, a_codes, fin_c, *rwkv_p, want_out=True)

    out = _merge((y_s5, y_pool, y_rwkv), (s5_g, pool_g, rwkv_g), w_out)
    x_new = _layernorm(DEEPNORM_ALPHA * x + gate[:, None] * out, LN_EPS) * ln_g + ln_b

    xc_new = None
    if ctx_out:
        y_s5_c = _s5_readout(s5_uc, st_c, s5_c_re, s5_c_im, s5_d, w_glu, b_glu)
        y_pool_c = _pool_branch(pool_uc, w_pool, pool_scale, on_grid=False)
        out_c = _merge((y_s5_c, y_pool_c, y_rwkv_c), (s5_gc, pool_gc, rwkv_gc), w_out)
        xc_new = _layernorm(DEEPNORM_ALPHA * xc + gate_c * out_c, LN_EPS) * ln_g + ln_b
    return x_new, xc_new


def setup_inputs(seed: int = 0) -> dict:
    key = jax.random.key(seed)
    ks = iter(jax.random.split(key, 40))
    L = DEPTH

    def nrm(shape, s):
        return jax.random.normal(next(ks), shape, F32) * s

    x = nrm((BATCH, SEQ, D_MODEL), 1.0)
    c = nrm((BATCH, D_MODEL), 1.0)
    ctx = nrm((BATCH, CTX_LEN, D_MODEL), 1.0)
    c_ctx = nrm((D_MODEL,), 1.0)
    w_ada = nrm((L, D_MODEL, 3 * D_MODEL), 0.5 * D_MODEL ** -0.5)
    b_ada = nrm((L, 3 * D_MODEL), 0.02)
    w_in = nrm((L, D_MODEL, D_IN), D_MODEL ** -0.5)
    conv_rkv = jnp.array([0.25, 0.5, 0.25], F32)[:, None] + nrm((L, RWKV_CONV, 3 * RWKV_WIDTH), 0.1)
    s5_lam_re = -0.5 + nrm((L, N_DIR, S5_GROUPS, S5_STATE), 0.01)
    s5_lam_im = jnp.pi * jnp.arange(S5_STATE, dtype=F32) + nrm((L, N_DIR, S5_GROUPS, S5_STATE), 0.01)
    s5_log_step = jax.random.uniform(next(ks), (L, N_DIR, S5_GROUPS), F32, math.log(1e-3), math.log(1e-1))
    s5_b_re = nrm((L, S5_GROUPS, S5_STATE, S5_GROUP), (2 * S5_GROUP) ** -0.5)
    s5_b_im = nrm((L, S5_GROUPS, S5_STATE, S5_GROUP), (2 * S5_GROUP) ** -0.5)
    s5_c_re = nrm((L, N_DIR, S5_GROUPS, S5_GROUP, S5_STATE), S5_STATE ** -0.5)
    s5_c_im = nrm((L, N_DIR, S5_GROUPS, S5_GROUP, S5_STATE), S5_STATE ** -0.5)
    s5_d = nrm((L, S5_WIDTH), 1.0)
    w_glu = nrm((L, S5_WIDTH, S5_WIDTH), S5_WIDTH ** -0.5)
    b_glu = nrm((L, S5_WIDTH), 0.02)
    w_pool = nrm((L, len(POOL_WINDOWS), POOL_GROUP, POOL_GROUP), POOL_GROUP ** -0.5)
    pool_scale = 1.0 + nrm((L, POOL_WIDTH), 0.1)
    rwkv_w0 = -0.5 + nrm((L, N_DIR, RWKV_WIDTH), 0.5)
    rwkv_w2 = nrm((L, N_DIR, RWKV_LORA, RWKV_WIDTH), 0.1)
    rwkv_a0 = nrm((L, N_DIR, RWKV_WIDTH), 0.1)
    rwkv_a2 = nrm((L, N_DIR, RWKV_LORA, RWKV_WIDTH), 0.1)
    rwkv_k_k = 0.85 + nrm((L, RWKV_WIDTH), 0.02)
    rwkv_k_a = 1.0 + nrm((L, RWKV_WIDTH), 0.02)
    rwkv_r_k = nrm((L, RWKV_HEADS, RWKV_HEAD), 0.1)
    gn_w = 1.0 + nrm((L, RWKV_WIDTH), 0.02)
    gn_b = nrm((L, RWKV_WIDTH), 0.02)
    w_out = nrm((L, D_MIX, D_MODEL), DEEPNORM_BETA * D_MIX ** -0.5)
    ln_g = 1.0 + nrm((L, D_MODEL), 0.02)
    ln_b = nrm((L, D_MODEL), 0.02)
    return {"x": x, "c": c, "ctx": ctx, "c_ctx": c_ctx, "w_ada": w_ada, "b_ada": b_ada,
            "w_in": w_in, "conv_rkv": conv_rkv, "s5_lam_re": s5_lam_re, "s5_lam_im": s5_lam_im,
            "s5_log_step": s5_log_step, "s5_b_re": s5_b_re, "s5_b_im": s5_b_im,
            "s5_c_re": s5_c_re, "s5_c_im": s5_c_im, "s5_d": s5_d, "w_glu": w_glu, "b_glu": b_glu,
            "w_pool": w_pool, "pool_scale": pool_scale, "rwkv_w0": rwkv_w0, "rwkv_w2": rwkv_w2,
            "rwkv_a0": rwkv_a0, "rwkv_a2": rwkv_a2, "rwkv_k_k": rwkv_k_k, "rwkv_k_a": rwkv_k_a,
            "rwkv_r_k": rwkv_r_k, "gn_w": gn_w, "gn_b": gn_b, "w_out": w_out,
            "ln_g": ln_g, "ln_b": ln_b}


def reference(x, c, ctx, c_ctx, w_ada, b_ada, w_in, conv_rkv, s5_lam_re, s5_lam_im, s5_log_step,
              s5_b_re, s5_b_im, s5_c_re, s5_c_im, s5_d, w_glu, b_glu, w_pool, pool_scale,
              rwkv_w0, rwkv_w2, rwkv_a0, rwkv_a2, rwkv_k_k, rwkv_k_a, rwkv_r_k, gn_w, gn_b,
              w_out, ln_g, ln_b):
    xc = ctx
    for l in range(DEPTH):
        x, xc = _layer(x, xc, c, c_ctx, w_ada[l], b_ada[l], w_in[l], conv_rkv[l],
                       s5_lam_re[l], s5_lam_im[l], s5_log_step[l], s5_b_re[l], s5_b_im[l],
                       s5_c_re[l], s5_c_im[l], s5_d[l], w_glu[l], b_glu[l], w_pool[l], pool_scale[l],
                       rwkv_w0[l], rwkv_w2[l], rwkv_a0[l], rwkv_a2[l], rwkv_k_k[l], rwkv_k_a[l],
                       rwkv_r_k[l], gn_w[l], gn_b[l], w_out[l], ln_g[l], ln_b[l],
                       ctx_out=(l < DEPTH - 1))
    return x
```

```python
import numpy as np
from contextlib import ExitStack
import concourse.bass as bass
import concourse.mybir as mybir
from concourse.bass_utils import run_bass_kernel_spmd

F32 = mybir.dt.float32
BF16 = mybir.dt.bfloat16
AF = mybir.ActivationFunctionType
ALU = mybir.AluOpType
AX = mybir.AxisListType

SEM_CHUNK = 20000
DMA_K = 8
DMA_CHUNK = 1000


class Prog:
    ENG = ('pe', 'dve', 'act', 'pool', 'sp')

    def __init__(self, nc):
        self.nc = nc
        self.st = ExitStack()
        self.sem_st = ExitStack()
        self.ops = {e: [] for e in self.ENG}
        self.n = {e: 0 for e in self.ENG}
        self.sems = {e: [] for e in self.ENG}
        self.waited_c = {e: {x: 0 for x in self.ENG} for e in self.ENG}
        self.waited_d = {e: {} for e in self.ENG}
        self.lastw = {}
        self.readers = {}
        self.dma_n = {e: 0 for e in self.ENG}
        self.dma_sems = {e: {} for e in self.ENG}
        self.nsem = 0
        self.ninst = 0

    def sbuf(self, name, shape, dt=F32):
        self.ninst += 0
        self._uid = getattr(self, '_uid', 0) + 1
        return self.st.enter_context(self.nc.sbuf_tensor(f"{name}_u{self._uid}", list(shape), dt))

    def psum(self, name, shape, dt=F32):
        self._uid = getattr(self, '_uid', 0) + 1
        return self.st.enter_context(self.nc.psum_tensor(f"{name}_u{self._uid}", list(shape), dt))

    def _newsem(self, name):
        self.nsem += 1
        return self.sem_st.enter_context(self.nc.semaphore(name))

    def _csem(self, eng, n):
        idx = (n - 1) // SEM_CHUNK
        while len(self.sems[eng]) <= idx:
            self.sems[eng].append(self._newsem(f"s_{eng}_{len(self.sems[eng])}"))
        return self.sems[eng][idx], (n - 1) % SEM_CHUNK + 1

    def _dsem(self, q, j):
        slot = j % DMA_K
        cnt = j // DMA_K
        key = (slot, cnt // DMA_CHUNK)
        if key not in self.dma_sems[q]:
            self.dma_sems[q][key] = self._newsem(f"d_{q}_{slot}_{cnt // DMA_CHUNK}")
        return self.dma_sems[q][key], 16 * (cnt % DMA_CHUNK + 1), key

    def _wait(self, eng, tok):
        if tok is None:
            return
        if tok[0] == 'c':
            _, e2, n = tok
            if e2 == eng and eng == 'pe':
                return
            if self.waited_c[eng][e2] >= n:
                return
            self.waited_c[eng][e2] = n
            sem, val = self._csem(e2, n)
        else:
            _, q, j = tok
            sem, val, key = self._dsem(q, j)
            k2 = (q, key)
            if self.waited_d[eng].get(k2, 0) >= val:
                return
            self.waited_d[eng][k2] = val
        self.ops[eng].append(lambda e, sem=sem, val=val: e.wait_ge(sem, val))

    def _deps(self, eng, reads, writes):
        toks = []
        for r in reads:
            if r in self.lastw:
                toks.append(self.lastw[r])
        for w in writes:
            if w in self.lastw:
                toks.append(self.lastw[w])
            for t in self.readers.get(w, {}).values():
                toks.append(t)
        for t in toks:
            self._wait(eng, t)

    def _commit(self, tok, reads, writes):
        for w in writes:
            self.lastw[w] = tok
            self.readers[w] = {}
        for r in reads:
            if r in writes:
                continue
            d = self.readers.setdefault(r, {})
            if tok[0] == 'c':
                d[('c', tok[1])] = tok
            else:
                q, j = tok[1], tok[2]
                d[('d', q, j % DMA_K)] = tok

    def op(self, eng, name, reads=(), writes=(), **kw):
        fn = (lambda e, name=name, kw=kw: getattr(e, name)(**kw))
        self._deps(eng, reads, writes)
        self.n[eng] += 1
        n = self.n[eng]
        sem, _ = self._csem(eng, n)
        self.ops[eng].append(lambda e, fn=fn, sem=sem: fn(e).then_inc(sem, 1))
        self._commit(('c', eng, n), reads, writes)
        self.ninst += 1

    def dma(self, q, out, in_, reads=(), writes=(), **kw):
        self.dmaop(q, lambda e, out=out, in_=in_, kw=kw: e.dma_start(out=out, in_=in_, **kw), reads, writes)

    def dmaop(self, q, fn, reads=(), writes=()):
        self._deps(q, reads, writes)
        j = self.dma_n[q]
        self.dma_n[q] += 1
        if j >= DMA_K:
            self._wait(q, ('d', q, j - DMA_K))
        sem, _, _ = self._dsem(q, j)
        self.ops[q].append(lambda e, fn=fn, sem=sem: fn(e).then_inc(sem, 16))
        self._commit(('d', q, j), reads, writes)
        self.ninst += 1

    def barrier(self):
        toks = [('c', e, self.n[e]) for e in self.ENG if self.n[e] > 0]
        for q in self.ENG:
            for j in range(max(0, self.dma_n[q] - DMA_K), self.dma_n[q]):
                toks.append(('d', q, j))
        for e in self.ENG:
            for t in toks:
                self._wait(e, t)

    def phase_begin(self):
        if not hasattr(self, '_stk'):
            self._stk = []
        self._stk.append(self.st)
        self.st = ExitStack()

    def phase_end(self):
        self.barrier()
        self.st.close()
        self.st = self._stk.pop()

    def finish(self, final_res):
        for r in final_res:
            self._wait('sp', self.lastw[r])
        nc = self.nc
        with nc.Block() as block:
            @block.sync
            def _(e):
                for f in self.ops['sp']:
                    f(e)

            @block.tensor
            def _(e):
                for f in self.ops['pe']:
                    f(e)

            @block.vector
            def _(e):
                for f in self.ops['dve']:
                    f(e)

            @block.scalar
            def _(e):
                for f in self.ops['act']:
                    f(e)

            @block.gpsimd
            def _(e):
                for f in self.ops['pool']:
                    f(e)
        self.st.close()


L = 128


class Rot:
    def __init__(self, P, name, shape, n, psum=False, dt=F32):
        self.bufs = []
        for i in range(n):
            t = P.psum(f"{name}{i}", shape, dt) if psum else P.sbuf(f"{name}{i}", shape, dt)
            self.bufs.append((t, (name, i)))
        self.i = 0

    def get(self):
        b = self.bufs[self.i % len(self.bufs)]
        self.i += 1
        return b


def rwkv_consts(P, cst):
    c = P.sbuf("rwc", [128, 6 * 128])
    P.dma('sp', c[:], cst, writes=['rwc'])
    return c


def rwkv_core(P, c, H, segs, d_r, d_kk, d_v, d_lw, d_ka, d_kd, d_o, HB=2):
    ident = c[:, 0:128]
    ones = c[:, 5 * 128:6 * 128]
    combT = [c[:, 128:384], c[:, 384:640]]
    Mst = [c[:, 384:512], c[:, 128:256]]
    nb = H // HB
    Tst = {}
    for d in range(2):
        for b in range(nb):
            Tst[d, b] = Rot(P, f"T{d}_{b}_", [64, HB, 64], 2)
    inp = {k: Rot(P, "in_" + k, [64, HB, L], 2) for k in ("r", "kk", "v", "lw", "ka", "kd")}
    cs_r = Rot(P, "cs", [64, HB, L], 2)
    g1_r = Rot(P, "g1", [64, HB, L], 2)
    gi_r = Rot(P, "gi", [64, HB, L], 2)
    g0_r = Rot(P, "g0", [64, HB, L], 2)
    KR_r = Rot(P, "KR", [64, HB, 2 * L], 2)
    nKa_r = Rot(P, "nKa", [64, HB, L], 2)
    Kd_r = Rot(P, "Kd", [64, HB, L], 2)
    A1_r = Rot(P, "A1", [128, HB, 2 * L], 2)
    A2_r = Rot(P, "A2", [128, HB, 2 * L], 2)
    PT_r = [Rot(P, f"PT{j}_", [128, HB, L], 2) for j in range(1, 7)]
    Pn_r = Rot(P, "Pn", [128, HB, L], 3)
    TOK_r = Rot(P, "TOK", [128, HB, 256], 2)
    X_r = Rot(P, "X", [128, HB, 128], 3)
    GT_r = Rot(P, "GT", [64, HB, 64], 2)
    Hg_r = Rot(P, "Hg", [64, HB, 64], 2)
    QT_r = Rot(P, "QT", [64, HB, L], 2)
    OL_r = Rot(P, "OL", [64, HB, L], 2)
    O_r = Rot(P, "O", [64, HB, L], 2)
    ps_r = Rot(P, "ps", [128, 512], 7, psum=True)

    def mm(out, lhsT, rhs, reads, wres, start=True, stop=True):
        P.op('pe', 'matmul', reads=reads, writes=[wres], out=out, lhsT=lhsT, rhs=rhs, start=start, stop=stop)

    order = []
    nsteps = sum(n for _, n in segs)
    fwd = [(s0 + i * L) for (s0, n) in segs for i in range(n)]
    bwd = [(s0 + i * L) for (s0, n) in segs for i in reversed(range(n))]
    for i in range(nsteps):
        order.append((0, fwd[i], i == 0))
        order.append((1, bwd[i], i == 0))

    evac_flip = [0]

    def evac(out, in_, reads, wres):
        evac_flip[0] ^= 1
        if evac_flip[0]:
            P.op('act', 'activation', reads=reads, writes=[wres], out=out, in_=in_, func=AF.Copy)
        else:
            P.op('dve', 'tensor_copy', reads=reads, writes=[wres], out=out, in_=in_)

    for (d, t0, first) in order:
        for b in range(nb):
            rows = slice(b * HB * 64, (b + 1) * HB * 64)

            def src(ap):
                return ap[rows, t0:t0 + L].rearrange("(h p) t -> p h t", p=64)
            tl = {}
            for k, ap in (("r", d_r), ("kk", d_kk), ("v", d_v), ("lw", d_lw[d]), ("ka", d_ka[d]), ("kd", d_kd[d])):
                t, res = inp[k].get()
                P.dma('sp', t[:], src(ap), writes=[res])
                tl[k] = (t, res)
            (r_t, r_s), (kk_t, kk_s), (v_t, v_s) = tl["r"], tl["kk"], tl["v"]
            (lw_t, lw_s), (ka_t, ka_s), (kd_t, kd_s) = tl["lw"], tl["ka"], tl["kd"]
            cs, cs_s = cs_r.get()
            g1, g1_s = g1_r.get()
            gi, gi_s = gi_r.get()
            g0, g0_s = g0_r.get()
            KR, KR_s = KR_r.get()
            nKa, nKa_s = nKa_r.get()
            Kd, Kd_s = Kd_r.get()
            for h in range(HB):
                if d == 0:
                    P.op('dve', 'tensor_tensor_scan', reads=[lw_s, 'rwc'], writes=[cs_s], out=cs[:, h, :], data0=ones[0:64, :], data1=lw_t[:, h, :], initial=0.0, op0=ALU.mult, op1=ALU.add)
                else:
                    P.op('dve', 'tensor_tensor_scan', reads=[lw_s, 'rwc'], writes=[cs_s], out=cs[:, h, ::-1], data0=ones[0:64, :], data1=lw_t[:, h, ::-1], initial=0.0, op0=ALU.mult, op1=ALU.add)
            P.op('act', 'activation', reads=[cs_s], writes=[g1_s], out=g1[:], in_=cs[:], func=AF.Exp)
            P.op('act', 'activation', reads=[cs_s], writes=[gi_s], out=gi[:], in_=cs[:], func=AF.Exp, scale=-1.0)
            P.op('pool', 'tensor_tensor', reads=[cs_s, lw_s], writes=[g0_s], out=g0[:], in0=cs[:], in1=lw_t[:], op=ALU.subtract)
            P.op('act', 'activation', reads=[g0_s], writes=[g0_s], out=g0[:], in_=g0[:], func=AF.Exp)
            P.op('dve', 'tensor_tensor', reads=[kk_s, g0_s], writes=[KR_s], out=KR[:, :, 0:L], in0=kk_t[:], in1=g0[:], op=ALU.mult)
            P.op('pool', 'tensor_tensor', reads=[r_s, g1_s, KR_s], writes=[KR_s], out=KR[:, :, L:2 * L], in0=r_t[:], in1=g1[:], op=ALU.mult)
            P.op('dve', 'scalar_tensor_tensor', reads=[ka_s, gi_s], writes=[nKa_s], out=nKa[:], in0=ka_t[:], scalar=-1.0, in1=gi[:], op0=ALU.mult, op1=ALU.mult)
            P.op('pool', 'tensor_tensor', reads=[kd_s, gi_s], writes=[Kd_s], out=Kd[:], in0=kd_t[:], in1=gi[:], op=ALU.mult)
            A1, A1_s = A1_r.get()
            A2, A2_s = A2_r.get()
            for (Adst, Ares, lh, lres) in ((A1, A1_s, nKa, nKa_s), (A2, A2_s, Kd, Kd_s)):
                for h in range(HB):
                    ps, ps_s = ps_r.get()
                    mm(ps[:, 0:2 * L], lh[:, h, :], KR[:, h, :], [lres, KR_s], ps_s)
                    P.op('dve', 'tensor_tensor', reads=[ps_s, 'rwc', Ares], writes=[Ares], out=Adst[:, h, :], in0=ps[:, 0:2 * L], in1=combT[d], op=ALU.mult)
            Pc, Pc_s = Pn_r.get()
            ps, ps_s = ps_r.get()
            for h in range(HB):
                mm(ps[:, h * L:(h + 1) * L], KR[:, h, 0:L], nKa[:, h, :], [KR_s, nKa_s, ps_s], ps_s)
            P.op('dve', 'tensor_tensor', reads=[ps_s, 'rwc'], writes=[Pc_s], out=Pc[:], in0=ps[:, 0:HB * L].rearrange("p (h t) -> p h t", h=HB), in1=Mst[d].unsqueeze(1).to_broadcast([128, HB, L]), op=ALU.mult)
            PTs = [(A1, A1_s, lambda h: A1[:, h, 0:L])]
            for j in range(1, 7):
                PTp, PTp_s, PTp_f = PTs[-1]
                PTn, PTn_s = PT_r[j - 1].get()
                ps, ps_s = ps_r.get()
                for h in range(HB):
                    mm(ps[:, h * L:(h + 1) * L], Pc[:, h, :], PTp_f(h), [Pc_s, PTp_s, ps_s], ps_s)
                evac(PTn[:], ps[:, 0:HB * L].rearrange("p (h t) -> p h t", h=HB), [ps_s], PTn_s)
                if j < 6:
                    Pn, Pn_s = Pn_r.get()
                    ps2, ps2_s = ps_r.get()
                    for h in range(HB):
                        mm(ps2[:, h * L:(h + 1) * L], PTp_f(h), Pc[:, h, :], [Pc_s, PTp_s, ps2_s], ps2_s)
                    evac(Pn[:], ps2[:, 0:HB * L].rearrange("p (h t) -> p h t", h=HB), [ps2_s], Pn_s)
                    Pc, Pc_s = Pn, Pn_s
                PTs.append((PTn, PTn_s, lambda h, PTn=PTn: PTn[:, h, :]))
            TOK, TOK_s = TOK_r.get()
            ps, ps_s = ps_r.get()
            for h in range(HB):
                for qi, (srcT, sres, sl) in enumerate(((KR, KR_s, slice(0, L)), (v_t, v_s, slice(0, L)), (Kd, Kd_s, slice(0, L)), (nKa, nKa_s, slice(0, L)))):
                    P.op('pe', 'transpose', reads=[sres, 'rwc', ps_s], writes=[ps_s], out=ps[:, h * 256 + qi * 64:h * 256 + (qi + 1) * 64], in_=srcT[:, h, sl], identity=ident[0:64, 0:64])
            evac(TOK[:], ps[:, 0:HB * 256].rearrange("p (h t) -> p h t", h=HB), [ps_s], TOK_s)
            X, X_s = X_r.get()
            ps, ps_s = ps_r.get()
            for h in range(HB):
                mm(ps[:, h * 64:(h + 1) * 64], A2[:, h, 0:L], TOK[:, h, 64:128], [A2_s, TOK_s, ps_s], ps_s)
            P.op('act', 'activation', reads=[ps_s], writes=[X_s], out=X[:, :, 64:128], in_=ps[:, 0:HB * 64].rearrange("p (h t) -> p h t", h=HB), func=AF.Copy)
            P.op('pool', 'tensor_copy', reads=[TOK_s, X_s], writes=[X_s], out=X[:, :, 0:64], in_=TOK[:, :, 0:64])
            for j in range(7):
                PTj, PTj_s, PTj_f = PTs[j]
                Xn, Xn_s = X_r.get()
                ps, ps_s = ps_r.get()
                for h in range(HB):
                    mm(ps[:, h * 128:(h + 1) * 128], PTj_f(h), X[:, h, :], [PTj_s, X_s, ps_s], ps_s)
                P.op('dve', 'tensor_tensor', reads=[ps_s, X_s], writes=[Xn_s], out=Xn[:], in0=ps[:, 0:HB * 128].rearrange("p (h t) -> p h t", h=HB), in1=X[:], op=ALU.add)
                X, X_s = Xn, Xn_s
            GT, GT_s = GT_r.get()
            Hg, Hg_s = Hg_r.get()
            ps, ps_s = ps_r.get()
            for h in range(HB):
                mm(ps[0:64, h * 64:(h + 1) * 64], X[:, h, 0:64], TOK[:, h, 192:256], [X_s, TOK_s, ps_s], ps_s)
            P.op('dve', 'tensor_tensor', reads=[ps_s, 'rwc'], writes=[GT_s], out=GT[:], in0=ps[0:64, 0:HB * 64].rearrange("p (h t) -> p h t", h=HB), in1=ident[0:64, 0:64].unsqueeze(1).to_broadcast([64, HB, 64]), op=ALU.add)
            ps, ps_s = ps_r.get()
            for h in range(HB):
                mm(ps[0:64, h * 64:(h + 1) * 64], TOK[:, h, 128:192], TOK[:, h, 64:128], [TOK_s, ps_s], ps_s, start=True, stop=False)
                mm(ps[0:64, h * 64:(h + 1) * 64], TOK[:, h, 192:256], X[:, h, 64:128], [TOK_s, X_s, ps_s], ps_s, start=False, stop=True)
            gl = (L - 1) if d == 0 else 0
            for h in range(HB):
                P.op('dve', 'tensor_scalar', reads=[ps_s, g1_s, Hg_s], writes=[Hg_s], out=Hg[:, h, :], in0=ps[0:64, h * 64:(h + 1) * 64], scalar1=g1[:, h, gl:gl + 1], scalar2=None, op0=ALU.mult)
            QT, QT_s = QT_r.get()
            OL, OL_s = OL_r.get()
            ps, ps_s = ps_r.get()
            for h in range(HB):
                mm(ps[0:64, h * L:(h + 1) * L], X[:, h, 0:64], A1[:, h, L:2 * L], [X_s, A1_s, ps_s], ps_s)
            P.op('dve', 'tensor_tensor', reads=[ps_s, KR_s], writes=[QT_s], out=QT[:], in0=ps[0:64, 0:HB * L].rearrange("p (h t) -> p h t", h=HB), in1=KR[:, :, L:2 * L], op=ALU.add)
            ps, ps_s = ps_r.get()
            for h in range(HB):
                mm(ps[0:64, h * L:(h + 1) * L], TOK[:, h, 64:128], A2[:, h, L:2 * L], [TOK_s, A2_s, ps_s], ps_s, start=True, stop=False)
                mm(ps[0:64, h * L:(h + 1) * L], X[:, h, 64:128], A1[:, h, L:2 * L], [X_s, A1_s, ps_s], ps_s, start=False, stop=True)
            evac(OL[:], ps[0:64, 0:HB * L].rearrange("p (h t) -> p h t", h=HB), [ps_s], OL_s)
            O, O_s = O_r.get()
            if first:
                Tn, Tn_s = Tst[d, b].get()
                P.op('pool', 'tensor_copy', reads=[Hg_s], writes=[Tn_s], out=Tn[:], in_=Hg[:])
                P.dma('sp', d_o[d][rows, t0:t0 + L].rearrange("(h p) t -> p h t", p=64), OL[:], reads=[OL_s], writes=[('d_o', d)])
                Tst[d, b].cur = (Tn, Tn_s)
            else:
                T0, T0_s = Tst[d, b].cur
                ps, ps_s = ps_r.get()
                for h in range(HB):
                    mm(ps[0:64, h * L:(h + 1) * L], T0[:, h, :], QT[:, h, :], [T0_s, QT_s, ps_s], ps_s)
                P.op('dve', 'tensor_tensor', reads=[ps_s, OL_s], writes=[O_s], out=O[:], in0=ps[0:64, 0:HB * L].rearrange("p (h t) -> p h t", h=HB), in1=OL[:], op=ALU.add)
                P.dma('sp', d_o[d][rows, t0:t0 + L].rearrange("(h p) t -> p h t", p=64), O[:], reads=[O_s], writes=[('d_o', d)])
                ps, ps_s = ps_r.get()
                for h in range(HB):
                    mm(ps[0:64, h * 64:(h + 1) * 64], GT[:, h, :], T0[:, h, :], [GT_s, T0_s, ps_s], ps_s)
                Tn, Tn_s = Tst[d, b].get()
                for h in range(HB):
                    P.op('dve', 'scalar_tensor_tensor', reads=[ps_s, g1_s, Hg_s, Tn_s], writes=[Tn_s], out=Tn[:, h, :], in0=ps[0:64, h * 64:(h + 1) * 64], scalar=g1[:, h, gl:gl + 1], in1=Hg[:, h, :], op0=ALU.mult, op1=ALU.add)
                Tst[d, b].cur = (Tn, Tn_s)

D = 2048
NCTX = 256
KC = 16
DIN = 6400
ALPHA = 4.0 ** 0.25
PI = 3.14159265358979
VO = {}
_o = 0
for _n, _w in (("b_ada", 48), ("d_skip", 4), ("b_glu", 4), ("pool_scale", 4), ("k_k", 8), ("k_a", 8), ("r_k", 8),
               ("gn_w", 8), ("gn_b", 8), ("ln_g", 16), ("ln_b", 16), ("w0", 16), ("a0", 16), ("conv", 72)):
    VO[_n] = _o
    _o += _w
VW = _o


def token_tiles(T):
    tiles = [(0, NCTX, 1)]
    t = NCTX
    while t < T:
        tiles.append((t, 512, 0))
        t += 512
    return tiles


def ln_stats(P, R, Rs, n, eps, onesD, sq, sq_s, ps_r, mean, mean_s, rstd, rstd_s, tmp, tmp_s):
    P.op('act', 'activation', reads=[Rs], writes=[sq_s], out=sq[:, :, 0:n], in_=R[:, :, 0:n], func=AF.Square)
    pm, pm_s = ps_r.get()
    for kc in range(KC):
        P.op('pe', 'matmul', reads=[Rs, 'onesD', pm_s], writes=[pm_s], out=pm[:, 0:n], lhsT=onesD[:], rhs=R[:, kc, 0:n], start=(kc == 0), stop=(kc == KC - 1))
    pq, pq_s = ps_r.get()
    for kc in range(KC):
        P.op('pe', 'matmul', reads=[sq_s, 'onesD', pq_s], writes=[pq_s], out=pq[:, 0:n], lhsT=onesD[:], rhs=sq[:, kc, 0:n], start=(kc == 0), stop=(kc == KC - 1))
    P.op('act', 'activation', reads=[pm_s], writes=[mean_s], out=mean[:, 0:n], in_=pm[:, 0:n], func=AF.Copy)
    P.op('dve', 'tensor_tensor', reads=[mean_s], writes=[tmp_s], out=tmp[:, 0:n], in0=mean[:, 0:n], in1=mean[:, 0:n], op=ALU.mult)
    P.op('dve', 'tensor_tensor', reads=[pq_s, tmp_s], writes=[tmp_s], out=tmp[:, 0:n], in0=pq[:, 0:n], in1=tmp[:, 0:n], op=ALU.subtract)
    P.op('dve', 'tensor_scalar', reads=[tmp_s], writes=[tmp_s], out=tmp[:, 0:n], in0=tmp[:, 0:n], scalar1=float(eps), scalar2=None, op0=ALU.add)
    P.op('act', 'activation', reads=[tmp_s], writes=[tmp_s], out=tmp[:, 0:n], in_=tmp[:, 0:n], func=AF.Sqrt)
    P.op('dve', 'reciprocal', reads=[tmp_s], writes=[rstd_s], out=rstd[:, 0:n], in_=tmp[:, 0:n])


def proj(P, Wd, M, H, Hs, n, Wt_r, ps_r, consume):
    m0 = 0
    while m0 < M:
        w = min(512, M - m0)
        Wt, Wt_s = Wt_r.get()
        P.dma('pool', Wt[:, :, 0:w], Wd[:, m0:m0 + w].rearrange("(kc p) c -> p kc c", p=128), writes=[Wt_s])
        for mi in range(w // 128):
            ps, ps_s = ps_r.get()
            for kc in range(KC):
                P.op('pe', 'matmul', reads=[Wt_s, Hs, ps_s], writes=[ps_s], out=ps[:, 0:n], lhsT=Wt[:, kc, mi * 128:(mi + 1) * 128], rhs=H[:, kc, 0:n], start=(kc == 0), stop=(kc == KC - 1))
            consume(m0 // 128 + mi, ps, ps_s)
        m0 += w


def phase_mod(P, l, io, mod, vec):
    P.phase_begin()
    sc = P.sbuf("m_sc", [128, KC, 2])
    ps_r = Rot(P, "m_ps", [128, 512], 2, psum=True)
    Wt_r = Rot(P, "m_W", [128, KC, 512], 2)
    P.dma('sp', sc[:], io["ccT"], writes=['m_sc'])
    P.op('act', 'activation', reads=['m_sc'], writes=['m_sc'], out=sc[:], in_=sc[:], func=AF.Silu)
    ms = ('mod', l)
    for mg in range(12):
        Wt, Wt_s = Wt_r.get()
        P.dma('pool', Wt[:], io[f"w_ada{l}"][:, mg * 512:(mg + 1) * 512].rearrange("(kc p) c -> p kc c", p=128), writes=[Wt_s])
        ps, ps_s = ps_r.get()
        for mi in range(4):
            for kc in range(KC):
                P.op('pe', 'matmul', reads=[Wt_s, 'm_sc', ps_s], writes=[ps_s], out=ps[:, mi * 2:mi * 2 + 2], lhsT=Wt[:, kc, mi * 128:(mi + 1) * 128], rhs=sc[:, kc, :], start=(kc == 0), stop=(kc == KC - 1))
        P.op('dve', 'tensor_tensor', reads=[ps_s, 'vec', ms], writes=[ms], out=mod[:, mg * 4:mg * 4 + 4, :], in0=ps[:, 0:8].rearrange("p (m j) -> p m j", j=2),
             in1=vec[:, VO["b_ada"] + mg * 4:VO["b_ada"] + mg * 4 + 4].unsqueeze(2).to_broadcast([128, 4, 2]), op=ALU.add)
    P.op('dve', 'tensor_scalar', reads=[ms], writes=[ms], out=mod[:, 16:32, :], in0=mod[:, 16:32, :], scalar1=1.0, scalar2=None, op0=ALU.add)
    P.phase_end()


def phase_inproj(P, l, io, mod, xT, zT, T, onesD):
    P.phase_begin()
    ms = ('mod', l)
    X_r = Rot(P, "p1_X", [128, KC, 512], 2)
    H_r = Rot(P, "p1_H", [128, KC, 512], 1)
    mean = P.sbuf("p1_mean", [128, 512]); rstd = P.sbuf("p1_rstd", [128, 512]); tmp = P.sbuf("p1_tmp", [128, 512])
    st_r = Rot(P, "p1_st", [128, 512], 4)
    Wt_r = Rot(P, "p1_W", [128, KC, 512], 2)
    ps_r = Rot(P, "p1_ps", [128, 512], 6, psum=True)
    for (t0, n, j) in token_tiles(T):
        X, Xs = X_r.get()
        P.dma('sp', X[:, :, 0:n], xT[:, t0:t0 + n].rearrange("(kc p) t -> p kc t", p=128), reads=[('xT',)], writes=[Xs])
        H, Hs = H_r.get()
        ln_stats(P, X, Xs, n, 1e-6, onesD, H, Hs, ps_r, mean, 'p1_mean', rstd, 'p1_rstd', tmp, 'p1_tmp')
        for kc in range(KC):
            eng = 'dve' if kc % 2 == 0 else 'pool'
            P.op(eng, 'tensor_tensor', reads=[Xs, 'p1_mean', Hs], writes=[Hs], out=H[:, kc, 0:n], in0=X[:, kc, 0:n], in1=mean[:, 0:n], op=ALU.subtract)
            P.op(eng, 'tensor_tensor', reads=[Hs, 'p1_rstd'], writes=[Hs], out=H[:, kc, 0:n], in0=H[:, kc, 0:n], in1=rstd[:, 0:n], op=ALU.mult)
            P.op('act', 'activation', reads=[Hs, ms], writes=[Hs], out=H[:, kc, 0:n], in_=H[:, kc, 0:n], func=AF.Identity, bias=mod[:, kc, j:j + 1], scale=mod[:, 16 + kc, j:j + 1])

        def consume(mi, ps, ps_s, t0=t0, n=n):
            st, st_s = st_r.get()
            if mi % 2 == 0:
                P.op('act', 'activation', reads=[ps_s], writes=[st_s], out=st[:, 0:n], in_=ps[:, 0:n], func=AF.Copy)
            else:
                P.op('dve', 'tensor_copy', reads=[ps_s], writes=[st_s], out=st[:, 0:n], in_=ps[:, 0:n])
            P.dma('sp', zT[mi * 128:(mi + 1) * 128, t0:t0 + n], st[:, 0:n], reads=[st_s], writes=[('zT',)])
        proj(P, io[f"w_in{l}"], DIN, H, Hs, n, Wt_r, ps_r, consume)
    P.phase_end()


def phase_outproj(P, l, io, mod, vec, xT, ymT, x1T, T, onesD, t_lo):
    P.phase_begin()
    ms = ('mod', l)
    Y_r = Rot(P, "po_Y", [128, KC, 512], 1)
    X_r = Rot(P, "po_X", [128, KC, 512], 1)
    R = P.sbuf("po_R", [128, KC, 512])
    sq = P.sbuf("po_sq", [128, KC, 512])
    mean = P.sbuf("po_mean", [128, 512]); rstd = P.sbuf("po_rstd", [128, 512]); tmp = P.sbuf("po_tmp", [128, 512])
    Wt_r = Rot(P, "po_W", [128, KC, 512], 2)
    ps_r = Rot(P, "po_ps", [128, 512], 6, psum=True)
    for (t0, n, j) in token_tiles(T):
        if t0 < t_lo:
            continue
        Y, Ys = Y_r.get()
        P.dma('sp', Y[:, :, 0:n], ymT[:, t0:t0 + n].rearrange("(kc p) t -> p kc t", p=128), reads=[('ymT',)], writes=[Ys])
        X, Xs = X_r.get()
        P.dma('sp', X[:, :, 0:n], xT[:, t0:t0 + n].rearrange("(kc p) t -> p kc t", p=128), reads=[('xT',)], writes=[Xs])
        P.op('pool', 'tensor_scalar', reads=[Xs], writes=[Xs], out=X[:, :, 0:n], in0=X[:, :, 0:n], scalar1=float(ALPHA), scalar2=None, op0=ALU.mult)

        def consume(mi, ps, ps_s, n=n, j=j, X=X, Xs=Xs):
            P.op('dve', 'scalar_tensor_tensor', reads=[ps_s, ms, Xs, 'po_R'], writes=['po_R'], out=R[:, mi, 0:n], in0=ps[:, 0:n], scalar=mod[:, 32 + mi, j:j + 1], in1=X[:, mi, 0:n], op0=ALU.mult, op1=ALU.add)
        proj(P, io[f"w_out{l}"], D, Y, Ys, n, Wt_r, ps_r, consume)
        ln_stats(P, R, 'po_R', n, 1e-5, onesD, sq, 'po_sq', ps_r, mean, 'po_mean', rstd, 'po_rstd', tmp, 'po_tmp')
        for kc in range(KC):
            eng = 'dve' if kc % 2 == 0 else 'pool'
            P.op(eng, 'tensor_tensor', reads=['po_R', 'po_mean'], writes=['po_R'], out=R[:, kc, 0:n], in0=R[:, kc, 0:n], in1=mean[:, 0:n], op=ALU.subtract)
            P.op(eng, 'tensor_tensor', reads=['po_R', 'po_rstd'], writes=['po_R'], out=R[:, kc, 0:n], in0=R[:, kc, 0:n], in1=rstd[:, 0:n], op=ALU.mult)
            P.op('act', 'activation', reads=['po_R', 'vec'], writes=['po_R'], out=R[:, kc, 0:n], in_=R[:, kc, 0:n], func=AF.Identity, bias=vec[:, VO["ln_b"] + kc:VO["ln_b"] + kc + 1], scale=vec[:, VO["ln_g"] + kc:VO["ln_g"] + kc + 1])
        P.dma('sp', x1T[:, t0 - t_lo:t0 - t_lo + n].rearrange("(kc p) t -> p kc t", p=128), R[:, :, 0:n], reads=['po_R'], writes=[('x1T', l)])
    P.phase_end()


def phase_rwkv_prep(P, l, io, vec, zT, T, S):
    P.phase_begin()
    blk = P.sbuf("r1_blk", [128, 128])
    P.op('pool', 'memset', writes=['r1_blk'], ap=blk[:], constant=0.0)
    P.op('pool', 'memset', reads=['r1_blk'], writes=['r1_blk'], ap=blk[0:64, 0:64], constant=1.0)
    P.op('pool', 'memset', reads=['r1_blk'], writes=['r1_blk'], ap=blk[64:128, 64:128], constant=1.0)
    w2 = P.sbuf("r1_w2", [128, 1024]); a2 = P.sbuf("r1_a2", [128, 1024])
    P.dma('sp', w2[:], io[f"rw_w2{l}"], writes=['r1_w2'])
    P.dma('sp', a2[:], io[f"rw_a2{l}"], writes=['r1_a2'])
    omka = P.sbuf("r1_omka", [128, 8])
    P.op('dve', 'tensor_scalar', reads=['vec'], writes=['r1_omka'], out=omka[:], in0=vec[:, VO["k_a"]:VO["k_a"] + 8], scalar1=-1.0, scalar2=1.0, op0=ALU.mult, op1=ALU.add)
    TH = P.sbuf("r1_TH", [128, 512]); AC = P.sbuf("r1_AC", [128, 512])
    Z_r = Rot(P, "r1_Z", [128, 514], 3)
    cv = {k: Rot(P, "r1_c" + k, [128, 512], 2) for k in "rkv"}
    t_r = Rot(P, "r1_t", [128, 512], 4)
    o_r = Rot(P, "r1_o", [128, 512], 6)
    ks_r = Rot(P, "r1_ks", [128, 512], 2)
    ps_r = Rot(P, "r1_ps", [128, 512], 4, psum=True)
    for (t0, n, j) in token_tiles(T):
        seg_lo, seg_hi = (0, NCTX) if j == 1 else (NCTX, T)
        P.dma('sp', TH[:, 0:n], zT[6144:6272, t0:t0 + n], reads=[('zT',)], writes=['r1_TH'])
        P.op('act', 'activation', reads=['r1_TH'], writes=['r1_TH'], out=TH[:, 0:n], in_=TH[:, 0:n], func=AF.Tanh)
        P.dma('sp', AC[:, 0:n], zT[6272:6400, t0:t0 + n], reads=[('zT',)], writes=['r1_AC'])
        for hp in range(8):
            res = {}
            for ci, k in enumerate("rkv"):
                Z, Zs = Z_r.get()
                lo = max(t0 - 1, seg_lo); hi = min(t0 + n + 1, seg_hi)
                if lo > t0 - 1:
                    P.op('pool', 'memset', writes=[Zs], ap=Z[:, 0:1], constant=0.0)
                if hi < t0 + n + 1:
                    P.op('pool', 'memset', reads=[Zs], writes=[Zs], ap=Z[:, n + 1:n + 2], constant=0.0)
                row0 = 2048 + ci * 1024 + hp * 128
                P.dma('sp', Z[:, lo - (t0 - 1):hi - (t0 - 1)], zT[row0:row0 + 128, lo:hi], reads=[('zT',), Zs], writes=[Zs])
                o, os_ = cv[k].get()
                c0 = VO["conv"] + (ci * 8 + hp) * 3
                P.op('dve', 'tensor_scalar', reads=[Zs, 'vec'], writes=[os_], out=o[:, 0:n], in0=Z[:, 0:n], scalar1=vec[:, c0:c0 + 1], scalar2=None, op0=ALU.mult)
                P.op('dve', 'scalar_tensor_tensor', reads=[Zs, 'vec', os_], writes=[os_], out=o[:, 0:n], in0=Z[:, 1:n + 1], scalar=vec[:, c0 + 1:c0 + 2], in1=o[:, 0:n], op0=ALU.mult, op1=ALU.add)
                P.op('dve', 'scalar_tensor_tensor', reads=[Zs, 'vec', os_], writes=[os_], out=o[:, 0:n], in0=Z[:, 2:n + 2], scalar=vec[:, c0 + 2:c0 + 3], in1=o[:, 0:n], op0=ALU.mult, op1=ALU.add)
                res[k] = (o, os_)
            (r_, r_s), (k_, k_s), (v_, v_s) = res["r"], res["k"], res["v"]
            rows = slice(hp * 128, (hp + 1) * 128)
            P.dma('sp', S["r"][rows, t0:t0 + n], r_[:, 0:n], reads=[r_s], writes=[('S_r',)])
            P.dma('sp', S["v"][rows, t0:t0 + n], v_[:, 0:n], reads=[v_s], writes=[('S_v',)])
            kk, kk_s = o_r.get()
            t1, t1_s = t_r.get()
            P.op('pool', 'tensor_scalar', reads=[k_s, 'vec'], writes=[kk_s], out=kk[:, 0:n], in0=k_[:, 0:n], scalar1=vec[:, VO["k_k"] + hp:VO["k_k"] + hp + 1], scalar2=None, op0=ALU.mult)
            P.op('act', 'activation', reads=[kk_s], writes=[t1_s], out=t1[:, 0:n], in_=kk[:, 0:n], func=AF.Square)
            ps, ps_s = ps_r.get()
            P.op('pe', 'matmul', reads=[t1_s, 'r1_blk'], writes=[ps_s], out=ps[:, 0:n], lhsT=blk[:], rhs=t1[:, 0:n], start=True, stop=True)
            P.op('act', 'activation', reads=[ps_s], writes=[t1_s], out=t1[:, 0:n], in_=ps[:, 0:n], func=AF.Sqrt)
            P.op('dve', 'tensor_scalar', reads=[t1_s], writes=[t1_s], out=t1[:, 0:n], in0=t1[:, 0:n], scalar1=1e-12, scalar2=None, op0=ALU.max)
            P.op('dve', 'reciprocal', reads=[t1_s], writes=[t1_s], out=t1[:, 0:n], in_=t1[:, 0:n])
            P.op('dve', 'tensor_tensor', reads=[kk_s, t1_s], writes=[kk_s], out=kk[:, 0:n], in0=kk[:, 0:n], in1=t1[:, 0:n], op=ALU.mult)
            P.dma('sp', S["kk"][rows, t0:t0 + n], kk[:, 0:n], reads=[kk_s], writes=[('S_kk',)])
            ks, ks_s = ks_r.get()
            for d in range(2):
                dp = slice(64 * d, 64 * d + 64)
                ps, ps_s = ps_r.get()
                P.op('pe', 'matmul', reads=['r1_w2', 'r1_TH'], writes=[ps_s], out=ps[:, 0:n], lhsT=w2[dp, hp * 128:(hp + 1) * 128], rhs=TH[dp, 0:n], start=True, stop=True)
                lw, lw_s = o_r.get()
                c = VO["w0"] + d * 8 + hp
                P.op('act', 'activation', reads=[ps_s, 'vec'], writes=[lw_s], out=lw[:, 0:n], in_=ps[:, 0:n], func=AF.Sigmoid, bias=vec[:, c:c + 1], scale=1.0)
                P.op('pool', 'tensor_scalar', reads=[lw_s], writes=[lw_s], out=lw[:, 0:n], in0=lw[:, 0:n], scalar1=-0.606531, scalar2=None, op0=ALU.mult)
                P.dma('sp', S["lw"][d][rows, t0:t0 + n], lw[:, 0:n], reads=[lw_s], writes=[('S_lw', d)])
                ps, ps_s = ps_r.get()
                P.op('pe', 'matmul', reads=['r1_a2', 'r1_AC'], writes=[ps_s], out=ps[:, 0:n], lhsT=a2[dp, hp * 128:(hp + 1) * 128], rhs=AC[dp, 0:n], start=True, stop=True)
                a_, a_s = t_r.get()
                c = VO["a0"] + d * 8 + hp
                P.op('act', 'activation', reads=[ps_s, 'vec'], writes=[a_s], out=a_[:, 0:n], in_=ps[:, 0:n], func=AF.Sigmoid, bias=vec[:, c:c + 1], scale=1.0)
                ka, ka_s = o_r.get()
                P.op('dve', 'tensor_tensor', reads=[kk_s, a_s], writes=[ka_s], out=ka[:, 0:n], in0=kk[:, 0:n], in1=a_[:, 0:n], op=ALU.mult)
                P.dma('sp', S["ka"][d][rows, t0:t0 + n], ka[:, 0:n], reads=[ka_s], writes=[('S_ka', d)])
                kd, kd_s = o_r.get()
                P.op('dve', 'tensor_scalar', reads=[a_s, 'vec', 'r1_omka'], writes=[a_s], out=a_[:, 0:n], in0=a_[:, 0:n], scalar1=vec[:, VO["k_a"] + hp:VO["k_a"] + hp + 1], scalar2=omka[:, hp:hp + 1], op0=ALU.mult, op1=ALU.add)
                P.op('dve', 'tensor_tensor', reads=[k_s, a_s], writes=[kd_s], out=kd[:, 0:n], in0=k_[:, 0:n], in1=a_[:, 0:n], op=ALU.mult)
                P.dma('sp', S["kd"][d][rows, t0:t0 + n], kd[:, 0:n], reads=[kd_s], writes=[('S_kd', d)])
                if d == 0:
                    P.op('pool', 'tensor_copy', reads=[kd_s], writes=[ks_s], out=ks[:, 0:n], in_=kd[:, 0:n])
                else:
                    P.op('pool', 'tensor_tensor', reads=[kd_s, ks_s], writes=[ks_s], out=ks[:, 0:n], in0=ks[:, 0:n], in1=kd[:, 0:n], op=ALU.add)
            P.op('dve', 'scalar_tensor_tensor', reads=[r_s, 'vec', ks_s], writes=[ks_s], out=ks[:, 0:n], in0=r_[:, 0:n], scalar=vec[:, VO["r_k"] + hp:VO["r_k"] + hp + 1], in1=ks[:, 0:n], op0=ALU.mult, op1=ALU.mult)
            ps, ps_s = ps_r.get()
            P.op('pe', 'matmul', reads=[ks_s, 'r1_blk'], writes=[ps_s], out=ps[:, 0:n], lhsT=blk[:], rhs=ks[:, 0:n], start=True, stop=True)
            bo, bo_s = o_r.get()
            P.op('dve', 'tensor_tensor', reads=[ps_s, v_s], writes=[bo_s], out=bo[:, 0:n], in0=ps[:, 0:n], in1=v_[:, 0:n], op=ALU.mult)
            P.dma('sp', S["bonus"][rows, t0:t0 + n], bo[:, 0:n], reads=[bo_s], writes=[('S_bonus',)])
    P.phase_end()


def phase_rwkv_post(P, l, vec, zT, T, S, ymT, t_lo):
    P.phase_begin()
    blk = P.sbuf("r3_blk", [128, 128])
    P.op('pool', 'memset', writes=['r3_blk'], ap=blk[:], constant=0.0)
    P.op('pool', 'memset', reads=['r3_blk'], writes=['r3_blk'], ap=blk[0:64, 0:64], constant=1.0 / 64)
    P.op('pool', 'memset', reads=['r3_blk'], writes=['r3_blk'], ap=blk[64:128, 64:128], constant=1.0 / 64)
    i_r = Rot(P, "r3_i", [128, 512], 8)
    t_r = Rot(P, "r3_t", [128, 512], 6)
    ps_r = Rot(P, "r3_ps", [128, 512], 4, psum=True)
    for (t0, n, j) in token_tiles(T):
        if t0 < t_lo:
            continue
        for hp in range(8):
            rows = slice(hp * 128, (hp + 1) * 128)
            ld = {}
            for k, src in (("o0", S["o"][0][rows]), ("o1", S["o"][1][rows]), ("bo", S["bonus"][rows]), ("g", zT[5120 + hp * 128:5120 + (hp + 1) * 128])):
                t, ts = i_r.get()
                P.dma('sp', t[:, 0:n], src[:, t0:t0 + n], reads=[('d_o', 0), ('d_o', 1), ('S_bonus',), ('zT',)], writes=[ts])
                ld[k] = (t, ts)
            (o0, o0_s), (o1, o1_s), (bo, bo_s), (g, g_s) = ld["o0"], ld["o1"], ld["bo"], ld["g"]
            P.op('pool', 'tensor_tensor', reads=[o0_s, o1_s], writes=[o0_s], out=o0[:, 0:n], in0=o0[:, 0:n], in1=o1[:, 0:n], op=ALU.add)
            ps, ps_s = ps_r.get()
            P.op('pe', 'matmul', reads=[o0_s, 'r3_blk'], writes=[ps_s], out=ps[:, 0:n], lhsT=blk[:], rhs=o0[:, 0:n], start=True, stop=True)
            cen, cen_s = t_r.get()
            P.op('dve', 'tensor_tensor', reads=[o0_s, ps_s], writes=[cen_s], out=cen[:, 0:n], in0=o0[:, 0:n], in1=ps[:, 0:n], op=ALU.subtract)
            sq, sq_s = t_r.get()
            P.op('act', 'activation', reads=[cen_s], writes=[sq_s], out=sq[:, 0:n], in_=cen[:, 0:n], func=AF.Square)
            ps, ps_s = ps_r.get()
            P.op('pe', 'matmul', reads=[sq_s, 'r3_blk'], writes=[ps_s], out=ps[:, 0:n], lhsT=blk[:], rhs=sq[:, 0:n], start=True, stop=True)
            P.op('dve', 'tensor_scalar', reads=[ps_s], writes=[sq_s], out=sq[:, 0:n], in0=ps[:, 0:n], scalar1=64e-5, scalar2=None, op0=ALU.add)
            P.op('act', 'activation', reads=[sq_s], writes=[sq_s], out=sq[:, 0:n], in_=sq[:, 0:n], func=AF.Sqrt)
            P.op('dve', 'reciprocal', reads=[sq_s], writes=[sq_s], out=sq[:, 0:n], in_=sq[:, 0:n])
            P.op('dve', 'tensor_tensor', reads=[cen_s, sq_s], writes=[cen_s], out=cen[:, 0:n], in0=cen[:, 0:n], in1=sq[:, 0:n], op=ALU.mult)
            P.op('dve', 'tensor_scalar', reads=[cen_s, 'vec'], writes=[cen_s], out=cen[:, 0:n], in0=cen[:, 0:n], scalar1=vec[:, VO["gn_w"] + hp:VO["gn_w"] + hp + 1], scalar2=vec[:, VO["gn_b"] + hp:VO["gn_b"] + hp + 1], op0=ALU.mult, op1=ALU.add)
            P.op('pool', 'tensor_tensor', reads=[cen_s, bo_s], writes=[cen_s], out=cen[:, 0:n], in0=cen[:, 0:n], in1=bo[:, 0:n], op=ALU.add)
            P.op('act', 'activation', reads=[g_s], writes=[g_s], out=g[:, 0:n], in_=g[:, 0:n], func=AF.Silu)
            P.op('dve', 'tensor_tensor', reads=[cen_s, g_s], writes=[cen_s], out=cen[:, 0:n], in0=cen[:, 0:n], in1=g[:, 0:n], op=ALU.mult)
            P.dma('sp', ymT[1024 + hp * 128:1024 + (hp + 1) * 128, t0:t0 + n], cen[:, 0:n], reads=[cen_s], writes=[('ymT',)])
    P.phase_end()


def sin_red(P, out, z, n, tmpi, tmpf, res_out, res_z, res_t):
    P.op('dve', 'tensor_scalar', reads=[res_z], writes=[res_t], out=tmpi, in0=z, scalar1=1.0 / (2 * PI), scalar2=None, op0=ALU.mult)
    P.op('dve', 'tensor_copy', reads=[res_t], writes=[res_t], out=tmpf, in_=tmpi)
    P.op('dve', 'scalar_tensor_tensor', reads=[res_t, res_z], writes=[res_out], out=out, in0=tmpf, scalar=-2 * PI, in1=z, op0=ALU.mult, op1=ALU.add)
    P.op('dve', 'tensor_scalar', reads=[res_out], writes=[res_t], out=tmpf, in0=out, scalar1=PI, scalar2=None, op0=ALU.is_gt)
    P.op('dve', 'scalar_tensor_tensor', reads=[res_t, res_out], writes=[res_out], out=out, in0=tmpf, scalar=-2 * PI, in1=out, op0=ALU.mult, op1=ALU.add)
    P.op('dve', 'tensor_scalar', reads=[res_out], writes=[res_t], out=tmpf, in0=out, scalar1=-PI, scalar2=None, op0=ALU.is_lt)
    P.op('dve', 'scalar_tensor_tensor', reads=[res_t, res_out], writes=[res_out], out=out, in0=tmpf, scalar=2 * PI, in1=out, op0=ALU.mult, op1=ALU.add)
    P.op('dve', 'tensor_scalar', reads=[res_out], writes=[res_out], out=out, in0=out, scalar1=-3.14159, scalar2=3.14159, op0=ALU.max, op1=ALU.min)
    P.op('act', 'activation', reads=[res_out], writes=[res_out], out=out, in_=out, func=AF.Sin)


def phase_s5(P, l, io, vec, zT, T, S, ymT, t_lo, cst):
    TC = 512
    I32 = mybir.dt.int32
    P.phase_begin()
    ident = cst[:, 0:128]
    Jc = cst[:, 768:896]
    iota = cst[:, 896:896 + TC]
    pp = P.sbuf("s5_pp", [128, 3, 64])
    P.dma('sp', pp[:], io[f"s5p{l}"], writes=['s5_pp'])
    rho = P.sbuf("s5_rho", [128, 64]); theta = P.sbuf("s5_theta", [128, 64])
    P.op('act', 'activation', reads=['s5_pp'], writes=['s5_pp'], out=pp[:, 2, :], in_=pp[:, 2, :], func=AF.Exp)
    P.op('dve', 'tensor_scalar', reads=['s5_pp'], writes=['s5_pp'], out=pp[:, 0, :], in0=pp[:, 0, :], scalar1=-1e-4, scalar2=None, op0=ALU.min)
    P.op('dve', 'tensor_tensor', reads=['s5_pp'], writes=['s5_rho'], out=rho[:], in0=pp[:, 0, :], in1=pp[:, 2, :], op=ALU.mult)
    P.op('act', 'activation', reads=['s5_rho'], writes=['s5_rho'], out=rho[:], in_=rho[:], func=AF.Exp)
    P.op('dve', 'tensor_tensor', reads=['s5_pp'], writes=['s5_theta'], out=theta[:], in0=pp[:, 1, :], in1=pp[:, 2, :], op=ALU.mult)
    W1 = P.sbuf("s5_W1", [16, 64, 128]); W2 = P.sbuf("s5_W2", [16, 64, 128])
    P.phase_begin()
    BT = P.sbuf("s5_BT", [16, 2, 2048])
    P.dma('sp', BT[:], io[f"s5bt{l}"], writes=['s5_BT'])
    rr = P.sbuf("s5_rr", [16, 3, 2048])
    tA = [P.sbuf(f"s5_t{i}", [16, 2048]) for i in range(8)]
    tI = P.sbuf("s5_ti", [16, 2048], I32)
    for d in range(2):
        P.dma('sp', rr[:], io[f"s5r{l}"][:, :, d * 2048:(d + 1) * 2048], reads=['s5_rr'], writes=['s5_rr'])
        re, im, st = rr[:, 0, :], rr[:, 1, :], rr[:, 2, :]
        R_ = ['s5_rr'] + [f"s5_t{i}" for i in range(8)] + ['s5_ti']
        W_ = R_
        def o(eng, name, **kw):
            P.op(eng, name, reads=R_, writes=W_, **kw)
        o('act', 'activation', out=st, in_=st, func=AF.Exp)
        o('dve', 'tensor_scalar', out=re, in0=re, scalar1=-1e-4, scalar2=None, op0=ALU.min)
        er, ang, cs_, sn_, z2 = tA[0][:], tA[1][:], tA[2][:], tA[3][:], tA[4][:]
        o('dve', 'tensor_tensor', out=er, in0=re, in1=st, op=ALU.mult)
        o('act', 'activation', out=er, in_=er, func=AF.Exp)
        o('dve', 'tensor_tensor', out=ang, in0=im, in1=st, op=ALU.mult)
        sin_red(P, sn_, ang, 2048, tI[:], tA[5][:], 's5_t3', 's5_t1', 's5_t5')
        o('dve', 'tensor_scalar', out=z2, in0=ang, scalar1=PI / 2, scalar2=None, op0=ALU.add)
        sin_red(P, cs_, z2, 2048, tI[:], tA[5][:], 's5_t2', 's5_t4', 's5_t5')
        P.barrier()
        nre, nim = tA[2][:], tA[3][:]
        o('dve', 'tensor_tensor', out=nre, in0=er, in1=cs_, op=ALU.mult)
        o('dve', 'tensor_scalar', out=nre, in0=nre, scalar1=-1.0, scalar2=None, op0=ALU.add)
        o('dve', 'tensor_tensor', out=nim, in0=er, in1=sn_, op=ALU.mult)
        den, t5, bsr, bsi = tA[0][:], tA[5][:], tA[6][:], tA[7][:]
        o('dve', 'tensor_tensor', out=den, in0=re, in1=re, op=ALU.mult)
        o('dve', 'tensor_tensor', out=t5, in0=im, in1=im, op=ALU.mult)
        o('dve', 'tensor_tensor', out=den, in0=den, in1=t5, op=ALU.add)
        o('dve', 'reciprocal', out=den, in_=den)
        o('dve', 'tensor_tensor', out=bsr, in0=nre, in1=re, op=ALU.mult)
        o('dve', 'tensor_tensor', out=t5, in0=nim, in1=im, op=ALU.mult)
        o('dve', 'tensor_tensor', out=bsr, in0=bsr, in1=t5, op=ALU.add)
        o('dve', 'tensor_tensor', out=bsr, in0=bsr, in1=den, op=ALU.mult)
        o('dve', 'tensor_tensor', out=bsi, in0=nim, in1=re, op=ALU.mult)
        o('dve', 'tensor_tensor', out=t5, in0=nre, in1=im, op=ALU.mult)
        o('dve', 'tensor_tensor', out=bsi, in0=bsi, in1=t5, op=ALU.subtract)
        o('dve', 'tensor_tensor', out=bsi, in0=bsi, in1=den, op=ALU.mult)
        Br, Bi = BT[:, 0, :], BT[:, 1, :]
        bpr, bpi, t4 = tA[1][:], tA[2][:], tA[4][:]
        Rb = R_ + ['s5_BT']
        P.op('dve', 'tensor_tensor', reads=Rb, writes=W_, out=bpr, in0=bsr, in1=Br, op=ALU.mult)
        P.op('dve', 'tensor_tensor', reads=Rb, writes=W_, out=t4, in0=bsi, in1=Bi, op=ALU.mult)
        o('dve', 'tensor_tensor', out=bpr, in0=bpr, in1=t4, op=ALU.subtract)
        P.op('dve', 'tensor_tensor', reads=Rb, writes=W_, out=bpi, in0=bsr, in1=Bi, op=ALU.mult)
        P.op('dve', 'tensor_tensor', reads=Rb, writes=W_, out=t4, in0=bsi, in1=Br, op=ALU.mult)
        o('dve', 'tensor_tensor', out=bpi, in0=bpi, in1=t4, op=ALU.add)
        g3 = lambda ap: ap.rearrange("p (g s) -> p g s", s=64)
        dg = slice(d * 32, (d + 1) * 32)
        P.op('dve', 'tensor_copy', reads=R_, writes=['s5_W1'], out=W1[:, dg, 0:64], in_=g3(bpr))
        P.op('dve', 'tensor_copy', reads=R_ + ['s5_W1'], writes=['s5_W1'], out=W1[:, dg, 64:128], in_=g3(bpi))
        P.op('dve', 'tensor_copy', reads=R_, writes=['s5_W2'], out=W2[:, dg, 0:64], in_=g3(bpi))
        P.op('dve', 'tensor_scalar', reads=R_ + ['s5_W2'], writes=['s5_W2'], out=W2[:, dg, 64:128], in0=g3(bpr), scalar1=-1.0, scalar2=None, op0=ALU.mult)
        P.barrier()
    P.phase_end()
    CT1 = P.sbuf("s5_CT1", [128, 64, 16]); CT2 = P.sbuf("s5_CT2", [128, 64, 16])
    P.dma('sp', CT1[:], io[f"s5ca{l}"], writes=['s5_CT1'])
    P.dma('sp', CT2[:], io[f"s5cb{l}"], writes=['s5_CT2'])
    P.op('dve', 'tensor_scalar', reads=['s5_CT1'], writes=['s5_CT1'], out=CT1[64:128], in0=CT1[64:128], scalar1=-1.0, scalar2=None, op0=ALU.mult)
    P.op('dve', 'tensor_scalar', reads=['s5_CT2'], writes=['s5_CT2'], out=CT2[:], in0=CT2[:], scalar1=-1.0, scalar2=None, op0=ALU.mult)
    COS = P.sbuf("s5_COS", [128, TC]); SIN = P.sbuf("s5_SIN", [128, TC]); RHO = P.sbuf("s5_RHO", [128, TC])
    zt = P.sbuf("s5_zt", [128, TC]); zi = P.sbuf("s5_zi", [128, TC], I32); zf = P.sbuf("s5_zf", [128, TC])
    u_r = Rot(P, "s5_u", [16, TC], 3)
    t_r = Rot(P, "s5_tt", [128, TC], 8)
    y_r = Rot(P, "s5_y", [16, TC], 3)
    st_r = Rot(P, "s5_st", [128, 1], 3)
    ps_r = Rot(P, "s5_ps", [128, 512], 6, psum=True)
    blocks = [(0, NCTX)] + [(t, 512) for t in range(NCTX, T, 512)]
    for d in range(2):
        order = blocks if d == 0 else [blocks[0]] + blocks[:0:-1]
        for g in range(32):
            dg = d * 32 + g
            P.op('dve', 'tensor_scalar', reads=['rwc', 's5_theta'], writes=['s5_zt'], out=zt[:], in0=iota, scalar1=theta[:, dg:dg + 1], scalar2=None, op0=ALU.mult)
            sin_red(P, SIN[:], zt[:], TC, zi[:], zf[:], 's5_SIN', 's5_zt', 's5_zf')
            P.op('dve', 'tensor_scalar', reads=['s5_zt'], writes=['s5_zt'], out=zt[:], in0=zt[:], scalar1=PI / 2, scalar2=None, op0=ALU.add)
            sin_red(P, COS[:], zt[:], TC, zi[:], zf[:], 's5_COS', 's5_zt', 's5_zf')
            P.op('dve', 'tensor_scalar', reads=['rwc', 's5_rho'], writes=['s5_RHO'], out=RHO[:], in0=iota, scalar1=0.0, scalar2=rho[:, dg:dg + 1], op0=ALU.mult, op1=ALU.add)
            stc = None
            for (t0, n) in order:
                u, u_s = u_r.get()
                P.dma('sp', u[:, 0:n], zT[16 * g:16 * g + 16, t0:t0 + n], reads=[('zT',)], writes=[u_s])
                p1, p1_s = ps_r.get()
                P.op('pe', 'matmul', reads=[u_s, 's5_W1'], writes=[p1_s], out=p1[:, 0:n], lhsT=W1[:, dg, :], rhs=u[:, 0:n], start=True, stop=True)
                p2, p2_s = ps_r.get()
                P.op('pe', 'matmul', reads=[u_s, 's5_W2'], writes=[p2_s], out=p2[:, 0:n], lhsT=W2[:, dg, :], rhs=u[:, 0:n], start=True, stop=True)
                rv = (lambda ap: ap) if d == 0 else (lambda ap: ap[:, ::-1])
                t1, t1_s = t_r.get(); t2, t2_s = t_r.get()
                P.op('dve', 'tensor_tensor', reads=[p1_s, 's5_COS'], writes=[t1_s], out=t1[:, 0:n], in0=rv(p1[:, 0:n]), in1=COS[:, 0:n], op=ALU.mult)
                P.op('dve', 'tensor_tensor', reads=[p2_s, 's5_SIN'], writes=[t2_s], out=t2[:, 0:n], in0=rv(p2[:, 0:n]), in1=SIN[:, 0:n], op=ALU.mult)
                P.op('pool', 'tensor_tensor', reads=[t1_s, t2_s], writes=[t1_s], out=t1[:, 0:n], in0=t1[:, 0:n], in1=t2[:, 0:n], op=ALU.add)
                Wt, Wt_s = t_r.get()
                if stc is None:
                    P.op('dve', 'tensor_tensor_scan', reads=[t1_s, 's5_RHO'], writes=[Wt_s], out=Wt[:, 0:n], data0=RHO[:, 0:n], data1=t1[:, 0:n], initial=0.0, op0=ALU.mult, op1=ALU.add)
                else:
                    P.op('dve', 'tensor_tensor_scan', reads=[t1_s, 's5_RHO', stc[1]], writes=[Wt_s], out=Wt[:, 0:n], data0=RHO[:, 0:n], data1=t1[:, 0:n], initial=stc[0][:, 0:1], op0=ALU.mult, op1=ALU.add)
                t3, t3_s = t_r.get(); t4, t4_s = t_r.get()
                P.op('dve', 'tensor_tensor', reads=[Wt_s, 's5_COS'], writes=[t3_s], out=t3[:, 0:n], in0=Wt[:, 0:n], in1=COS[:, 0:n], op=ALU.mult)
                P.op('pool', 'tensor_tensor', reads=[Wt_s, 's5_SIN'], writes=[t4_s], out=t4[:, 0:n], in0=Wt[:, 0:n], in1=SIN[:, 0:n], op=ALU.mult)
                py, py_s = ps_r.get()
                P.op('pe', 'matmul', reads=[t3_s, 's5_CT1'], writes=[py_s], out=py[0:16, 0:n], lhsT=CT1[:, dg, :], rhs=t3[:, 0:n], start=True, stop=False)
                P.op('pe', 'matmul', reads=[t4_s, 's5_CT2', py_s], writes=[py_s], out=py[0:16, 0:n], lhsT=CT2[:, dg, :], rhs=t4[:, 0:n], start=False, stop=True)
                y, y_s = y_r.get()
                P.op('act', 'activation', reads=[py_s], writes=[y_s], out=rv(y[:, 0:n]), in_=py[0:16, 0:n], func=AF.Copy)
                P.dma('sp', S["ys"][d][16 * g:16 * g + 16, t0:t0 + n], y[:, 0:n], reads=[y_s], writes=[('S_ys', d)])
                px, px_s = ps_r.get()
                P.op('pe', 'matmul', reads=[t3_s, 'rwc'], writes=[px_s], out=px[:, 0:1], lhsT=ident, rhs=t3[:, n - 1:n], start=True, stop=False)
                P.op('pe', 'matmul', reads=[t4_s, 'rwc', px_s], writes=[px_s], out=px[:, 0:1], lhsT=Jc, rhs=t4[:, n - 1:n], start=False, stop=True)
                sn, sn_s = st_r.get()
                P.op('act', 'activation', reads=[px_s], writes=[sn_s], out=sn[:], in_=px[:, 0:1], func=AF.Copy)
                stc = (sn, sn_s)
    P.barrier()
    wg = P.sbuf("s5_wg", [128, 4, 512])
    P.dma('sp', wg[:], io[f"w_glu{l}"].rearrange("(kc p) m -> p kc m", p=128), writes=['s5_wg'])
    YG_r = Rot(P, "s5_YG", [128, 4, 512], 2)
    i_r = Rot(P, "s5_i", [128, 512], 8)
    G_r = Rot(P, "s5_G", [128, 4, 512], 2)
    for (t0, n, j) in token_tiles(T):
        if t0 < t_lo:
            continue
        YG, YG_s = YG_r.get()
        G, G_s = G_r.get()
        gts = []
        for c in range(4):
            rows = slice(c * 128, (c + 1) * 128)
            ld = []
            for src, rs in ((zT[rows], ('zT',)), (S["ys"][0][rows], ('S_ys', 0)), (S["ys"][1][rows], ('S_ys', 1))):
                t, ts = i_r.get()
                P.dma('sp', t[:, 0:n], src[:, t0:t0 + n], reads=[rs], writes=[ts])
                ld.append((t, ts))
            (u, u_s), (y0, y0_s), (y1, y1_s) = ld
            P.dma('sp', G[:, c, 0:n], zT[512 + c * 128:512 + (c + 1) * 128, t0:t0 + n], reads=[('zT',), G_s], writes=[G_s])
            P.op('dve', 'scalar_tensor_tensor', reads=[u_s, 'vec', y0_s], writes=[y0_s], out=y0[:, 0:n], in0=u[:, 0:n], scalar=vec[:, VO["d_skip"] + c:VO["d_skip"] + c + 1], in1=y0[:, 0:n], op0=ALU.mult, op1=ALU.add)
            P.op('pool', 'tensor_tensor', reads=[y0_s, y1_s], writes=[y0_s], out=y0[:, 0:n], in0=y0[:, 0:n], in1=y1[:, 0:n], op=ALU.add)
            P.op('act', 'activation', reads=[y0_s], writes=[y1_s], out=y1[:, 0:n], in_=y0[:, 0:n], func=AF.Square)
            P.op('dve', 'tensor_scalar', reads=[y1_s], writes=[y1_s], out=y1[:, 0:n], in0=y1[:, 0:n], scalar1=0.044715, scalar2=1.0, op0=ALU.mult, op1=ALU.add)
            P.op('dve', 'tensor_tensor', reads=[y1_s, y0_s], writes=[y1_s], out=y1[:, 0:n], in0=y1[:, 0:n], in1=y0[:, 0:n], op=ALU.mult)
            P.op('act', 'activation', reads=[y1_s], writes=[y1_s], out=y1[:, 0:n], in_=y1[:, 0:n], func=AF.Sigmoid, scale=1.5957691216057308)
            P.op('dve', 'tensor_tensor', reads=[y1_s, y0_s, YG_s], writes=[YG_s], out=YG[:, c, 0:n], in0=y1[:, 0:n], in1=y0[:, 0:n], op=ALU.mult)
        P.op('act', 'activation', reads=[G_s], writes=[G_s], out=G[:, :, 0:n], in_=G[:, :, 0:n], func=AF.Silu)
        for m in range(4):
            ps, ps_s = ps_r.get()
            for kc in range(4):
                P.op('pe', 'matmul', reads=[YG_s, 's5_wg', ps_s], writes=[ps_s], out=ps[:, 0:n], lhsT=wg[:, kc, m * 128:(m + 1) * 128], rhs=YG[:, kc, 0:n], start=(kc == 0), stop=(kc == 3))
            sg, sg_s = i_r.get()
            P.op('act', 'activation', reads=[ps_s, 'vec'], writes=[sg_s], out=sg[:, 0:n], in_=ps[:, 0:n], func=AF.Sigmoid, bias=vec[:, VO["b_glu"] + m:VO["b_glu"] + m + 1], scale=1.0)
            P.op('dve', 'tensor_tensor', reads=[sg_s, YG_s], writes=[sg_s], out=sg[:, 0:n], in0=sg[:, 0:n], in1=YG[:, m, 0:n], op=ALU.mult)
            P.op('dve', 'tensor_tensor', reads=[sg_s, G_s], writes=[sg_s], out=sg[:, 0:n], in0=sg[:, 0:n], in1=G[:, m, 0:n], op=ALU.mult)
            P.dma('sp', ymT[m * 128:(m + 1) * 128, t0:t0 + n], sg[:, 0:n], reads=[sg_s], writes=[('ymT',)])
    P.phase_end()


def phase_pool(P, l, io, vec, zT, T, ymT, t_lo):
    P.phase_begin()
    rows_tot = (T - NCTX) // 64
    wp = P.sbuf("pl_wp", [128, 4, 128])
    P.dma('sp', wp[:], io[f"w_pool{l}"], writes=['pl_wp'])
    icn = P.sbuf("pl_icn", [128, 4, rows_tot + 64 + NCTX])
    P.dma('sp', icn[:], io["icnt"], writes=['pl_icn'])
    ps_r = Rot(P, "pl_ps", [128, 512], 4, psum=True)
    g_r = Rot(P, "pl_g", [128, 512], 3)
    regions = []
    RB = min(64, rows_tot)
    for r0 in range(0, rows_tot, RB):
        regions.append((NCTX, rows_tot, 64, r0, RB, 0, rows_tot))
    if t_lo == 0:
        for r0 in range(0, NCTX, 64):
            regions.append((0, NCTX, 1, r0, 64, rows_tot + 64, None))
    bufA = {64: [P.sbuf(f"pl_A{i}", [128, RB + 16, 80]) for i in range(3)], 1: [P.sbuf(f"pl_B{i}", [128, 80, 1]) for i in range(3)]}
    dif = {64: P.sbuf("pl_d64", [128, RB, 64]), 1: P.sbuf("pl_d1", [128, 64, 1])}
    for gi, w in enumerate((2, 4, 8, 16)):
        zr = slice(1024 + gi * 128, 1024 + (gi + 1) * 128)
        for (tok0, Rtot, C, r0, nr, ico, icc) in regions:
            A = bufA[C]
            An = [f"pl_{'A' if C == 64 else 'B'}{i}" for i in range(3)]
            CP = C + 16 if C == 64 else 1
            c0 = 8 if C == 64 else 0
            U, Us = A[0], An[0]
            P.op('pool', 'memset', reads=[Us], writes=[Us], ap=U[:], constant=0.0)
            ra = max(r0 - 8, 0); rb = min(r0 + nr + 8, Rtot)
            P.dma('sp', U[:, 8 + ra - r0:8 + rb - r0, c0:c0 + C], zT[zr, tok0 + ra * C:tok0 + rb * C].rearrange("p (r c) -> p r c", c=C), reads=[('zT',), Us], writes=[Us])
            NR = nr + 16
            cur, cur_s = U, Us
            idx = 0
            s = 1
            while s < w:
                nidx = 1 if idx != 1 else 2
                nxt, nxt_s = A[nidx], An[nidx]
                P.op('dve', 'tensor_tensor', reads=[cur_s, nxt_s], writes=[nxt_s], out=nxt[:, 0:NR - s, :], in0=cur[:, 0:NR - s, :], in1=cur[:, s:NR, :], op=ALU.add)
                cur, cur_s, idx = nxt, nxt_s, nidx
                s *= 2
            nidx = 1 if idx != 1 else 2
            rm, rm_s = A[nidx], An[nidx]
            P.op('dve', 'tensor_tensor', reads=[cur_s, 'pl_icn', rm_s], writes=[rm_s], out=rm[:, 0:nr, :], in0=cur[:, 8 - w // 2:8 - w // 2 + nr, :],
                 in1=icn[:, gi, ico + r0:ico + r0 + nr].unsqueeze(2).to_broadcast([128, nr, CP]), op=ALU.mult)
            cur, cur_s, idx = rm, rm_s, nidx
            if C == 64:
                s = 1
                while s < w:
                    nidx = [i for i in (1, 2) if i != idx][0] if idx != 0 else 1
                    nxt, nxt_s = A[nidx], An[nidx]
                    P.op('dve', 'tensor_tensor', reads=[cur_s, nxt_s], writes=[nxt_s], out=nxt[:, 0:nr, 0:CP - s], in0=cur[:, 0:nr, 0:CP - s], in1=cur[:, 0:nr, s:CP], op=ALU.add)
                    cur, cur_s, idx = nxt, nxt_s, nidx
                    s *= 2
                nidx = [i for i in (1, 2) if i != idx][0]
                cm, cm_s = A[nidx], An[nidx]
                P.op('dve', 'tensor_tensor', reads=[cur_s, 'pl_icn', cm_s], writes=[cm_s], out=cm[:, 0:nr, 0:64], in0=cur[:, 0:nr, 8 - w // 2:8 - w // 2 + 64],
                     in1=icn[:, gi, icc:icc + 64].unsqueeze(1).to_broadcast([128, nr, 64]), op=ALU.mult)
                cur, cur_s = cm, cm_s
            df, df_s = dif[C], ("pl_d", C)
            P.op('dve', 'tensor_tensor', reads=[cur_s, Us, df_s], writes=[df_s], out=df[:, 0:nr, :], in0=cur[:, 0:nr, 0:C], in1=U[:, 8:8 + nr, c0:c0 + C], op=ALU.subtract)
            ntok = nr * C
            dflat = df[:, 0:nr, :].rearrange("p r c -> p (r c)")
            for q0 in range(0, ntok, 512):
                qn = min(512, ntok - q0)
                ps, ps_s = ps_r.get()
                P.op('pe', 'matmul', reads=[df_s, 'pl_wp'], writes=[ps_s], out=ps[:, 0:qn], lhsT=wp[:, gi, :], rhs=dflat[:, q0:q0 + qn], start=True, stop=True)
                tk = tok0 + r0 * C + q0
                gt, gt_s = g_r.get()
                P.dma('sp', gt[:, 0:qn], zT[1536 + gi * 128:1536 + (gi + 1) * 128, tk:tk + qn], reads=[('zT',)], writes=[gt_s])
                P.op('act', 'activation', reads=[gt_s], writes=[gt_s], out=gt[:, 0:qn], in_=gt[:, 0:qn], func=AF.Silu)
                P.op('dve', 'scalar_tensor_tensor', reads=[ps_s, 'vec', gt_s], writes=[gt_s], out=gt[:, 0:qn], in0=ps[:, 0:qn], scalar=vec[:, VO["pool_scale"] + gi:VO["pool_scale"] + gi + 1], in1=gt[:, 0:qn], op0=ALU.mult, op1=ALU.mult)
                P.dma('sp', ymT[512 + gi * 128:512 + (gi + 1) * 128, tk:tk + qn], gt[:, 0:qn], reads=[gt_s], writes=[('ymT',)])
    P.phase_end()


class RowSplit:
    def __init__(self, aps, rows_per):
        self.aps = aps
        self.rp = rows_per

    def __getitem__(self, key):
        if isinstance(key, tuple):
            rs, cs = key
        else:
            rs, cs = key, None
        b = rs.start // self.rp
        assert (rs.stop - 1) // self.rp == b
        ap = self.aps[b][rs.start - b * self.rp:rs.stop - b * self.rp]
        return ap if cs is None else ap[:, cs]

def build(N, nlayers=2, debug=False):
    T = NCTX + N
    nc = bass.Bass("TRN2", target_bir_lowering=False)
    io = {}

    def din(name, shape):
        io[name] = nc.dram_tensor(name, list(shape), F32, kind="ExternalInput").ap()

    def dscr(name, shape):
        return nc.dram_tensor(name, list(shape), F32, kind=("ExternalOutput" if debug else "Internal")).ap()
    rows_tot = N // 64
    din("xT", [D, T]); din("ccT", [128, KC, 2]); din("cst", [128, 1408]); din("icnt", [128, 4, rows_tot + 64 + NCTX])
    for l in range(2):
        din(f"w_ada{l}", [D, 6144]); din(f"w_in{l}", [D, DIN]); din(f"w_out{l}", [D, D]); din(f"vec{l}", [128, VW])
        din(f"s5p{l}", [128, 3, 64]); din(f"s5r{l}", [16, 3, 4096]); din(f"s5bt{l}", [16, 2, 2048])
        din(f"s5ca{l}", [128, 64, 16]); din(f"s5cb{l}", [128, 64, 16]); din(f"w_glu{l}", [512, 512])
        din(f"w_pool{l}", [128, 4, 128]); din(f"rw_w2{l}", [128, 1024]); din(f"rw_a2{l}", [128, 1024])
    yT = nc.dram_tensor("yT", [D, N], F32, kind="ExternalOutput").ap()
    zT = RowSplit([dscr(f"zT{i}", [min(1024, DIN - i * 1024), T]) for i in range(7)], 1024); ymT = dscr("ymT", [D, T]); x1T = dscr("x1T", [D, T])
    S = {"r": dscr("S_r", [1024, T]), "kk": dscr("S_kk", [1024, T]), "v": dscr("S_v", [1024, T]), "bonus": dscr("S_bonus", [1024, T]),
         "lw": [dscr(f"S_lw{d}", [1024, T]) for d in range(2)], "ka": [dscr(f"S_ka{d}", [1024, T]) for d in range(2)],
         "kd": [dscr(f"S_kd{d}", [1024, T]) for d in range(2)], "o": [dscr(f"S_o{d}", [1024, T]) for d in range(2)],
         "ys": [dscr(f"S_ys{d}", [512, T]) for d in range(2)]}
    P = Prog(nc)
    cst = P.sbuf("rwc", [128, 1408])
    P.dma('sp', cst[:], io["cst"], writes=['rwc'])
    onesD = P.sbuf("onesD", [128, 128])
    P.op('pool', 'memset', writes=['onesD'], ap=onesD[:], constant=1.0 / D)
    mod = P.sbuf("mod", [128, 48, 2])
    vec = P.sbuf("vec", [128, VW])
    for l in range(nlayers):
        t_lo = 0 if l == 0 else NCTX
        P.barrier()
        P.dma('sp', vec[:], io[f"vec{l}"], reads=['vec'], writes=['vec'])
        xin = io["xT"] if l == 0 else x1T
        xout = x1T if l == 0 else yT
        phase_mod(P, l, io, mod, vec)
        phase_inproj(P, l, io, mod, xin, zT, T, onesD)
        phase_s5(P, l, io, vec, zT, T, S, ymT, t_lo, cst)
        phase_pool(P, l, io, vec, zT, T, ymT, t_lo)
        phase_rwkv_prep(P, l, io, vec, zT, T, S)
        P.phase_begin()
        rwkv_core(P, cst, 16, [(0, NCTX // L), (NCTX, N // L)], S["r"], S["kk"], S["v"], S["lw"], S["ka"], S["kd"], S["o"])
        P.phase_end()
        phase_rwkv_post(P, l, vec, zT, T, S, ymT, t_lo)
        phase_outproj(P, l, io, mod, vec, xin, ymT, xout, T, onesD, t_lo)
    P.finish([('x1T', nlayers - 1)])
    return nc


def _pm(v):
    v = np.asarray(v, np.float32).reshape(-1)
    return np.ascontiguousarray(v.reshape(-1, 128).T)


def host_layout(inp, b, N):
    f32 = np.float32
    A = lambda a: np.ascontiguousarray(np.asarray(a), dtype=f32)
    m = {}
    m["xT"] = A(np.concatenate([np.asarray(inp["ctx"][b]).T, np.asarray(inp["x"][b, :N]).T], axis=1))
    cc = np.stack([np.asarray(inp["c"][b]), np.asarray(inp["c_ctx"])], 0)
    m["ccT"] = A(cc.reshape(2, KC, 128).transpose(2, 1, 0))
    i = np.arange(128)[:, None]; j = np.arange(128)[None, :]
    Jc = np.zeros((128, 128), f32)
    for p in range(64):
        Jc[64 + p, p] = -1.0
        Jc[p, 64 + p] = 1.0
    iota = np.tile(np.arange(1, 513, dtype=f32)[None, :], (128, 1))
    m["cst"] = A(np.concatenate([np.eye(128), (i < j), (i <= j), (i > j), (i >= j), np.ones((128, 128)), Jc, iota], 1))
    rows_tot = N // 64

    def invcnt(n, w):
        idx = np.arange(n)
        lo = np.clip(idx - w // 2, 0, n - 1); hi = np.clip(idx - w // 2 + w - 1, 0, n - 1)
        return (1.0 / (hi - lo + 1)).astype(f32)
    ic = np.stack([np.concatenate([invcnt(rows_tot, w), invcnt(64, w), invcnt(NCTX, w)]) for w in (2, 4, 8, 16)], 0)
    m["icnt"] = A(np.tile(ic[None], (128, 1, 1)))
    for l in range(2):
        g = lambda k: np.asarray(inp[k][l])
        m[f"w_ada{l}"] = A(g("w_ada")); m[f"w_in{l}"] = A(g("w_in")); m[f"w_out{l}"] = A(g("w_out"))
        w0 = g("rwkv_w0").reshape(2, 8, 128).transpose(2, 0, 1).reshape(128, 16)
        a0 = g("rwkv_a0").reshape(2, 8, 128).transpose(2, 0, 1).reshape(128, 16)
        conv = g("conv_rkv").reshape(3, 24, 128).transpose(2, 1, 0).reshape(128, 72)
        parts = [_pm(g("b_ada")), _pm(g("s5_d")), _pm(g("b_glu")), _pm(g("pool_scale")), _pm(g("rwkv_k_k")), _pm(g("rwkv_k_a")),
                 _pm(g("rwkv_r_k")), _pm(g("gn_w")), _pm(g("gn_b")), _pm(g("ln_g")), _pm(g("ln_b")), w0, a0, conv]
        m[f"vec{l}"] = A(np.concatenate(parts, 1))
        lr = g("s5_lam_re").reshape(64, 64).T; li = g("s5_lam_im").reshape(64, 64).T
        ls = np.tile(g("s5_log_step").reshape(1, 64), (64, 1))
        sp = np.stack([lr, li, ls], 1)
        m[f"s5p{l}"] = A(np.concatenate([sp, sp], 0))
        row = np.stack([g("s5_lam_re").reshape(4096), g("s5_lam_im").reshape(4096), np.repeat(g("s5_log_step").reshape(64), 64)], 0)
        m[f"s5r{l}"] = A(np.tile(row[None], (16, 1, 1)))
        m[f"s5bt{l}"] = A(np.stack([g("s5_b_re").transpose(2, 0, 1).reshape(16, 2048), g("s5_b_im").transpose(2, 0, 1).reshape(16, 2048)], 1))
        cr = g("s5_c_re").transpose(3, 0, 1, 2).reshape(64, 64, 16); ci = g("s5_c_im").transpose(3, 0, 1, 2).reshape(64, 64, 16)
        m[f"s5ca{l}"] = A(np.concatenate([cr, ci], 0)); m[f"s5cb{l}"] = A(np.concatenate([ci, cr], 0))
        m[f"w_glu{l}"] = A(g("w_glu")); m[f"w_pool{l}"] = A(g("w_pool").transpose(1, 0, 2))
        m[f"rw_w2{l}"] = A(g("rwkv_w2").reshape(128, 1024)); m[f"rw_a2{l}"] = A(g("rwkv_a2").reshape(128, 1024))
    return m


_NC_CACHE = {}


def run(inputs, N, ncores=8, nlayers=2, debug=False):
    key = (N, nlayers, debug)
    if key not in _NC_CACHE:
        _NC_CACHE[key] = build(N, nlayers, debug)
    nc = _NC_CACHE[key]
    maps = [host_layout(inputs, b, N) for b in range(2)]
    in_maps = [maps[c % 2] for c in range(ncores)]
    res = run_bass_kernel_spmd(nc, in_maps, core_ids=list(range(ncores)))
    if debug:
        return res
    out = np.stack([np.ascontiguousarray(res.results[b]["yT"].T) for b in range(2)], 0)
    return out.astype(np.float32)


def kernel(**inputs):
    return run(inputs, 16384, ncores=2)
```

```python
import numpy as np
from contextlib import ExitStack
import concourse.bass as bass
import concourse.mybir as mybir
from concourse.bass_utils import run_bass_kernel_spmd

F32 = mybir.dt.float32
BF16 = mybir.dt.bfloat16
AF = mybir.ActivationFunctionType
ALU = mybir.AluOpType
AX = mybir.AxisListType

SEM_CHUNK = 20000
DMA_K = 8
DMA_CHUNK = 1000


class Prog:
    ENG = ('pe', 'dve', 'act', 'pool', 'sp')

    def __init__(self, nc):
        self.nc = nc
        self.st = ExitStack()
        self.sem_st = ExitStack()
        self.ops = {e: [] for e in self.ENG}
        self.n = {e: 0 for e in self.ENG}
        self.sems = {e: [] for e in self.ENG}
        self.waited_c = {e: {x: 0 for x in self.ENG} for e in self.ENG}
        self.waited_d = {e: {} for e in self.ENG}
        self.lastw = {}
        self.readers = {}
        self.dma_n = {e: 0 for e in self.ENG}
        self.dma_sems = {e: {} for e in self.ENG}
        self.nsem = 0
        self.ninst = 0

    def sbuf(self, name, shape, dt=F32):
        self.ninst += 0
        self._uid = getattr(self, '_uid', 0) + 1
        return self.st.enter_context(self.nc.sbuf_tensor(f"{name}_u{self._uid}", list(shape), dt))

    def psum(self, name, shape, dt=F32):
        self._uid = getattr(self, '_uid', 0) + 1
        return self.st.enter_context(self.nc.psum_tensor(f"{name}_u{self._uid}", list(shape), dt))

    def _newsem(self, name):
        self.nsem += 1
        return self.sem_st.enter_context(self.nc.semaphore(name))

    def _csem(self, eng, n):
        idx = (n - 1) // SEM_CHUNK
        while len(self.sems[eng]) <= idx:
            self.sems[eng].append(self._newsem(f"s_{eng}_{len(self.sems[eng])}"))
        return self.sems[eng][idx], (n - 1) % SEM_CHUNK + 1

    def _dsem(self, q, j):
        slot = j % DMA_K
        cnt = j // DMA_K
        key = (slot, cnt // DMA_CHUNK)
        if key not in self.dma_sems[q]:
            self.dma_sems[q][key] = self._newsem(f"d_{q}_{slot}_{cnt // DMA_CHUNK}")
        return self.dma_sems[q][key], 16 * (cnt % DMA_CHUNK + 1), key

    def _wait(self, eng, tok):
        if tok is None:
            return
        if tok[0] == 'c':
            _, e2, n = tok
            if e2 == eng and eng == 'pe':
                return
            if self.waited_c[eng][e2] >= n:
                return
            self.waited_c[eng][e2] = n
            sem, val = self._csem(e2, n)
        else:
            _, q, j = tok
            sem, val, key = self._dsem(q, j)
            k2 = (q, key)
            if self.waited_d[eng].get(k2, 0) >= val:
                return
            self.waited_d[eng][k2] = val
        self.ops[eng].append(lambda e, sem=sem, val=val: e.wait_ge(sem, val))

    def _deps(self, eng, reads, writes):
        toks = []
        for r in reads:
            if r in self.lastw:
                toks.append(self.lastw[r])
        for w in writes:
            if w in self.lastw:
                toks.append(self.lastw[w])
            for t in self.readers.get(w, {}).values():
                toks.append(t)
        for t in toks:
            self._wait(eng, t)

    def _commit(self, tok, reads, writes):
        for w in writes:
            self.lastw[w] = tok
            self.readers[w] = {}
        for r in reads:
            if r in writes:
                continue
            d = self.readers.setdefault(r, {})
            if tok[0] == 'c':
                d[('c', tok[1])] = tok
            else:
                q, j = tok[1], tok[2]
                d[('d', q, j % DMA_K)] = tok

    def op(self, eng, name, reads=(), writes=(), **kw):
        fn = (lambda e, name=name, kw=kw: getattr(e, name)(**kw))
        self._deps(eng, reads, writes)
        self.n[eng] += 1
        n = self.n[eng]
        sem, _ = self._csem(eng, n)
        self.ops[eng].append(lambda e, fn=fn, sem=sem: fn(e).then_inc(sem, 1))
        self._commit(('c', eng, n), reads, writes)
        self.ninst += 1

    def dma(self, q, out, in_, reads=(), writes=(), **kw):
        self.dmaop(q, lambda e, out=out, in_=in_, kw=kw: e.dma_start(out=out, in_=in_, **kw), reads, writes)

    def dmaop(self, q, fn, reads=(), writes=()):
        self._deps(q, reads, writes)
        j = self.dma_n[q]
        self.dma_n[q] += 1
        if j >= DMA_K:
            self._wait(q, ('d', q, j - DMA_K))
        sem, _, _ = self._dsem(q, j)
        self.ops[q].append(lambda e, fn=fn, sem=sem: fn(e).then_inc(sem, 16))
        self._commit(('d', q, j), reads, writes)
        self.ninst += 1

    def barrier(self):
        toks = [('c', e, self.n[e]) for e in self.ENG if self.n[e] > 0]
        for q in self.ENG:
            for j in range(max(0, self.dma_n[q] - DMA_K), self.dma_n[q]):
                toks.append(('d', q, j))
        for e in self.ENG:
            for t in toks:
                self._wait(e, t)

    def phase_begin(self):
        if not hasattr(self, '_stk'):
            self._stk = []
        self._stk.append(self.st)
        self.st = ExitStack()

    def phase_end(self):
        self.barrier()
        self.st.close()
        self.st = self._stk.pop()

    def finish(self, final_res):
        for r in final_res:
            self._wait('sp', self.lastw[r])
        nc = self.nc
        with nc.Block() as block:
            @block.sync
            def _(e):
                for f in self.ops['sp']:
                    f(e)

            @block.tensor
            def _(e):
                for f in self.ops['pe']:
                    f(e)

            @block.vector
            def _(e):
                for f in self.ops['dve']:
                    f(e)

            @block.scalar
            def _(e):
                for f in self.ops['act']:
                    f(e)

            @block.gpsimd
            def _(e):
                for f in self.ops['pool']:
                    f(e)
        self.st.close()


L = 128


class Rot:
    def __init__(self, P, name, shape, n, psum=False, dt=F32):
        self.bufs = []
        for i in range(n):
            t = P.psum(f"{name}{i}", shape, dt) if psum else P.sbuf(f"{name}{i}", shape, dt)
            self.bufs.append((t, (name, i)))
        self.i = 0

    def get(self):
        b = self.bufs[self.i % len(self.bufs)]
        self.i += 1
        return b


def rwkv_consts(P, cst):
    c = P.sbuf("rwc", [128, 6 * 128])
    P.dma('sp', c[:], cst, writes=['rwc'])
    return c


def rwkv_core(P, c, H, segs, d_r, d_kk, d_v, d_lw, d_ka, d_kd, d_o, HB=2, G=4):
    ident = c[:, 0:128]
    ones = c[:, 5 * 128:6 * 128]
    combT = [c[:, 128:384], c[:, 384:640]]
    Mst = [c[:, 384:512], c[:, 128:256]]
    nb = H // HB
    Tst = {}
    for d in range(2):
        for b in range(nb):
            Tst[d, b] = Rot(P, f"T{d}_{b}_", [64, HB, 64], 2)
    B = []
    for sl in range(G):
        q = {}
        for k in ("r", "kk", "v", "lw", "ka", "kd"):
            q["in_" + k] = Rot(P, f"in{sl}_{k}", [64, HB, L], 2)
        for k in ("cs", "g1", "gi", "g0", "nKa", "Kd", "QT", "OL", "O"):
            q[k] = Rot(P, f"{k}{sl}_", [64, HB, L], 1)
        q["KR"] = Rot(P, f"KR{sl}_", [64, HB, 2 * L], 1)
        q["A1"] = Rot(P, f"A1{sl}_", [128, HB, 2 * L], 1)
        q["A2"] = Rot(P, f"A2{sl}_", [128, HB, 2 * L], 1)
        q["PT"] = [Rot(P, f"PT{j}{sl}_", [128, HB, L], 1) for j in range(1, 7)]
        q["Pn"] = Rot(P, f"Pn{sl}_", [128, HB, L], 2)
        q["TOK"] = Rot(P, f"TOK{sl}_", [128, HB, 256], 1)
        q["X"] = Rot(P, f"X{sl}_", [128, HB, 128], 2)
        q["GT"] = Rot(P, f"GT{sl}_", [64, HB, 64], 1)
        q["Hg"] = Rot(P, f"Hg{sl}_", [64, HB, 64], 1)
        B.append(q)
    ps_r = Rot(P, "ps", [128, 512], 8, psum=True)

    def mm(out, lhsT, rhs, reads, wres, start=True, stop=True):
        P.op('pe', 'matmul', reads=reads, writes=[wres], out=out, lhsT=lhsT, rhs=rhs, start=start, stop=stop)

    evac_flip = [0]

    def evac(out, in_, reads, wres):
        evac_flip[0] ^= 1
        if evac_flip[0]:
            P.op('act', 'activation', reads=reads, writes=[wres], out=out, in_=in_, func=AF.Copy)
        else:
            P.op('dve', 'tensor_copy', reads=reads, writes=[wres], out=out, in_=in_)

    def h3(ap, w):
        return ap.rearrange("p (h t) -> p h t", h=HB)

    def item(d, t0, first, b, q):
        rows = slice(b * HB * 64, (b + 1) * HB * 64)

        def src(ap):
            return ap[rows, t0:t0 + L].rearrange("(h p) t -> p h t", p=64)
        tl = {}
        for k, ap in (("r", d_r), ("kk", d_kk), ("v", d_v), ("lw", d_lw[d]), ("ka", d_ka[d]), ("kd", d_kd[d])):
            t, res = q["in_" + k].get()
            P.dma('sp', t[:], src(ap), writes=[res])
            tl[k] = (t, res)
        (r_t, r_s), (kk_t, kk_s), (v_t, v_s) = tl["r"], tl["kk"], tl["v"]
        (lw_t, lw_s), (ka_t, ka_s), (kd_t, kd_s) = tl["lw"], tl["ka"], tl["kd"]
        cs, cs_s = q["cs"].get(); g1, g1_s = q["g1"].get(); gi, gi_s = q["gi"].get(); g0, g0_s = q["g0"].get()
        KR, KR_s = q["KR"].get(); nKa, nKa_s = q["nKa"].get(); Kd, Kd_s = q["Kd"].get()
        yield
        for h in range(HB):
            if d == 0:
                P.op('dve', 'tensor_tensor_scan', reads=[lw_s, 'rwc'], writes=[cs_s], out=cs[:, h, :], data0=ones[0:64, :], data1=lw_t[:, h, :], initial=0.0, op0=ALU.mult, op1=ALU.add)
            else:
                P.op('dve', 'tensor_tensor_scan', reads=[lw_s, 'rwc'], writes=[cs_s], out=cs[:, h, ::-1], data0=ones[0:64, :], data1=lw_t[:, h, ::-1], initial=0.0, op0=ALU.mult, op1=ALU.add)
        yield
        P.op('act', 'activation', reads=[cs_s], writes=[g1_s], out=g1[:], in_=cs[:], func=AF.Exp)
        P.op('act', 'activation', reads=[cs_s], writes=[gi_s], out=gi[:], in_=cs[:], func=AF.Exp, scale=-1.0)
        P.op('pool', 'tensor_tensor', reads=[cs_s, lw_s], writes=[g0_s], out=g0[:], in0=cs[:], in1=lw_t[:], op=ALU.subtract)
        yield
        P.op('act', 'activation', reads=[g0_s], writes=[g0_s], out=g0[:], in_=g0[:], func=AF.Exp)
        P.op('pool', 'tensor_tensor', reads=[r_s, g1_s, KR_s], writes=[KR_s], out=KR[:, :, L:2 * L], in0=r_t[:], in1=g1[:], op=ALU.mult)
        P.op('dve', 'scalar_tensor_tensor', reads=[ka_s, gi_s], writes=[nKa_s], out=nKa[:], in0=ka_t[:], scalar=-1.0, in1=gi[:], op0=ALU.mult, op1=ALU.mult)
        P.op('pool', 'tensor_tensor', reads=[kd_s, gi_s], writes=[Kd_s], out=Kd[:], in0=kd_t[:], in1=gi[:], op=ALU.mult)
        yield
        P.op('dve', 'tensor_tensor', reads=[kk_s, g0_s, KR_s], writes=[KR_s], out=KR[:, :, 0:L], in0=kk_t[:], in1=g0[:], op=ALU.mult)
        yield
        A1, A1_s = q["A1"].get()
        A2, A2_s = q["A2"].get()
        for (Adst, Ares, lh, lres) in ((A1, A1_s, nKa, nKa_s), (A2, A2_s, Kd, Kd_s)):
            for h in range(HB):
                ps, ps_s = ps_r.get()
                mm(ps[:, 0:2 * L], lh[:, h, :], KR[:, h, :], [lres, KR_s], ps_s)
                P.op('dve', 'tensor_tensor', reads=[ps_s, 'rwc', Ares], writes=[Ares], out=Adst[:, h, :], in0=ps[:, 0:2 * L], in1=combT[d], op=ALU.mult)
        Pc, Pc_s = q["Pn"].get()
        ps, ps_s = ps_r.get()
        for h in range(HB):
            mm(ps[:, h * L:(h + 1) * L], KR[:, h, 0:L], nKa[:, h, :], [KR_s, nKa_s, ps_s], ps_s)
        P.op('dve', 'tensor_tensor', reads=[ps_s, 'rwc'], writes=[Pc_s], out=Pc[:], in0=h3(ps[:, 0:HB * L], L), in1=Mst[d].unsqueeze(1).to_broadcast([128, HB, L]), op=ALU.mult)
        yield
        TOK, TOK_s = q["TOK"].get()
        ps, ps_s = ps_r.get()
        for h in range(HB):
            for qi, (srcT, sres) in enumerate(((KR, KR_s), (v_t, v_s), (Kd, Kd_s), (nKa, nKa_s))):
                P.op('pe', 'transpose', reads=[sres, 'rwc', ps_s], writes=[ps_s], out=ps[:, h * 256 + qi * 64:h * 256 + (qi + 1) * 64], in_=srcT[:, h, 0:L], identity=ident[0:64, 0:64])
        evac(TOK[:], h3(ps[:, 0:HB * 256], 256), [ps_s], TOK_s)
        yield
        Xa, Xa_s = q["X"].get()
        ps, ps_s = ps_r.get()
        for h in range(HB):
            mm(ps[:, h * 64:(h + 1) * 64], A2[:, h, 0:L], TOK[:, h, 64:128], [A2_s, TOK_s, ps_s], ps_s)
        P.op('act', 'activation', reads=[ps_s, Xa_s], writes=[Xa_s], out=Xa[:, :, 64:128], in_=h3(ps[:, 0:HB * 64], 64), func=AF.Copy)
        P.op('pool', 'tensor_copy', reads=[TOK_s, Xa_s], writes=[Xa_s], out=Xa[:, :, 0:64], in_=TOK[:, :, 0:64])
        yield
        X, X_s = Xa, Xa_s
        PTp, PTp_s = A1, A1_s
        PTp_f = (lambda h, A1=A1: A1[:, h, 0:L])
        for j in range(7):
            Xn, Xn_s = q["X"].get()
            ps, ps_s = ps_r.get()
            for h in range(HB):
                mm(ps[:, h * 128:(h + 1) * 128], PTp_f(h), X[:, h, :], [PTp_s, X_s, ps_s], ps_s)
            P.op('dve', 'tensor_tensor', reads=[ps_s, X_s, Xn_s], writes=[Xn_s], out=Xn[:], in0=h3(ps[:, 0:HB * 128], 128), in1=X[:], op=ALU.add)
            X, X_s = Xn, Xn_s
            if j < 6:
                PTn, PTn_s = q["PT"][j].get()
                ps, ps_s = ps_r.get()
                for h in range(HB):
                    mm(ps[:, h * L:(h + 1) * L], Pc[:, h, :], PTp_f(h), [Pc_s, PTp_s, ps_s], ps_s)
                evac(PTn[:], h3(ps[:, 0:HB * L], L), [ps_s], PTn_s)
                if j < 5:
                    Pn, Pn_s = q["Pn"].get()
                    ps2, ps2_s = ps_r.get()
                    for h in range(HB):
                        mm(ps2[:, h * L:(h + 1) * L], PTp_f(h), Pc[:, h, :], [Pc_s, PTp_s, ps2_s], ps2_s)
                    evac(Pn[:], h3(ps2[:, 0:HB * L], L), [ps2_s], Pn_s)
                    Pc, Pc_s = Pn, Pn_s
                PTp, PTp_s = PTn, PTn_s
                PTp_f = (lambda h, PTn=PTn: PTn[:, h, :])
            yield
        GT, GT_s = q["GT"].get()
        Hg, Hg_s = q["Hg"].get()
        ps, ps_s = ps_r.get()
        for h in range(HB):
            mm(ps[0:64, h * 64:(h + 1) * 64], X[:, h, 0:64], TOK[:, h, 192:256], [X_s, TOK_s, ps_s], ps_s)
        P.op('dve', 'tensor_tensor', reads=[ps_s, 'rwc'], writes=[GT_s], out=GT[:], in0=h3(ps[0:64, 0:HB * 64], 64), in1=ident[0:64, 0:64].unsqueeze(1).to_broadcast([64, HB, 64]), op=ALU.add)
        ps, ps_s = ps_r.get()
        for h in range(HB):
            mm(ps[0:64, h * 64:(h + 1) * 64], TOK[:, h, 128:192], TOK[:, h, 64:128], [TOK_s, ps_s], ps_s, start=True, stop=False)
            mm(ps[0:64, h * 64:(h + 1) * 64], TOK[:, h, 192:256], X[:, h, 64:128], [TOK_s, X_s, ps_s], ps_s, start=False, stop=True)
        gl = (L - 1) if d == 0 else 0
        for h in range(HB):
            P.op('dve', 'tensor_scalar', reads=[ps_s, g1_s, Hg_s], writes=[Hg_s], out=Hg[:, h, :], in0=ps[0:64, h * 64:(h + 1) * 64], scalar1=g1[:, h, gl:gl + 1], scalar2=None, op0=ALU.mult)
        yield
        QT, QT_s = q["QT"].get()
        OL, OL_s = q["OL"].get()
        ps, ps_s = ps_r.get()
        for h in range(HB):
            mm(ps[0:64, h * L:(h + 1) * L], X[:, h, 0:64], A1[:, h, L:2 * L], [X_s, A1_s, ps_s], ps_s)
        P.op('dve', 'tensor_tensor', reads=[ps_s, KR_s], writes=[QT_s], out=QT[:], in0=h3(ps[0:64, 0:HB * L], L), in1=KR[:, :, L:2 * L], op=ALU.add)
        ps, ps_s = ps_r.get()
        for h in range(HB):
            mm(ps[0:64, h * L:(h + 1) * L], TOK[:, h, 64:128], A2[:, h, L:2 * L], [TOK_s, A2_s, ps_s], ps_s, start=True, stop=False)
            mm(ps[0:64, h * L:(h + 1) * L], X[:, h, 64:128], A1[:, h, L:2 * L], [X_s, A1_s, ps_s], ps_s, start=False, stop=True)
        evac(OL[:], h3(ps[0:64, 0:HB * L], L), [ps_s], OL_s)
        yield
        O, O_s = q["O"].get()
        dst = d_o[d][rows, t0:t0 + L].rearrange("(h p) t -> p h t", p=64)
        if first:
            Tn, Tn_s = Tst[d, b].get()
            P.op('pool', 'tensor_copy', reads=[Hg_s], writes=[Tn_s], out=Tn[:], in_=Hg[:])
            P.dma('sp', dst, OL[:], reads=[OL_s], writes=[('d_o', d)])
            Tst[d, b].cur = (Tn, Tn_s)
        else:
            T0, T0_s = Tst[d, b].cur
            ps, ps_s = ps_r.get()
            for h in range(HB):
                mm(ps[0:64, h * L:(h + 1) * L], T0[:, h, :], QT[:, h, :], [T0_s, QT_s, ps_s], ps_s)
            P.op('dve', 'tensor_tensor', reads=[ps_s, OL_s], writes=[O_s], out=O[:], in0=h3(ps[0:64, 0:HB * L], L), in1=OL[:], op=ALU.add)
            P.dma('sp', dst, O[:], reads=[O_s], writes=[('d_o', d)])
            ps, ps_s = ps_r.get()
            for h in range(HB):
                mm(ps[0:64, h * 64:(h + 1) * 64], GT[:, h, :], T0[:, h, :], [GT_s, T0_s, ps_s], ps_s)
            Tn, Tn_s = Tst[d, b].get()
            for h in range(HB):
                P.op('dve', 'scalar_tensor_tensor', reads=[ps_s, g1_s, Hg_s, Tn_s], writes=[Tn_s], out=Tn[:, h, :], in0=ps[0:64, h * 64:(h + 1) * 64], scalar=g1[:, h, gl:gl + 1], in1=Hg[:, h, :], op0=ALU.mult, op1=ALU.add)
            Tst[d, b].cur = (Tn, Tn_s)
        yield

    nsteps = sum(n for _, n in segs)
    fwd = [(s0 + i * L) for (s0, n) in segs for i in range(n)]
    bwd = [(s0 + i * L) for (s0, n) in segs for i in reversed(range(n))]
    for i in range(nsteps):
        items = [(0, fwd[i], i == 0, b) for b in range(nb)] + [(1, bwd[i], i == 0, b) for b in range(nb)]
        for g0_ in range(0, len(items), G):
            gens = [item(*it, B[sl]) for sl, it in enumerate(items[g0_:g0_ + G])]
            live = list(gens)
            while live:
                nxt = []
                for gen in live:
                    try:
                        next(gen)
                        nxt.append(gen)
                    except StopIteration:
                        pass
                live = nxt

D = 2048
NCTX = 256
KC = 16
DIN = 6400
ALPHA = 4.0 ** 0.25
PI = 3.14159265358979
VO = {}
_o = 0
for _n, _w in (("b_ada", 48), ("d_skip", 4), ("b_glu", 4), ("pool_scale", 4), ("k_k", 8), ("k_a", 8), ("r_k", 8),
               ("gn_w", 8), ("gn_b", 8), ("ln_g", 16), ("ln_b", 16), ("w0", 16), ("a0", 16), ("conv", 72)):
    VO[_n] = _o
    _o += _w
VW = _o


def token_tiles(T):
    tiles = [(0, NCTX, 1)]
    t = NCTX
    while t < T:
        tiles.append((t, 512, 0))
        t += 512
    return tiles


def ln_stats(P, R, Rs, n, eps, onesD, sq, sq_s, ps_r, mean, mean_s, rstd, rstd_s, tmp, tmp_s):
    P.op('act', 'activation', reads=[Rs], writes=[sq_s], out=sq[:, :, 0:n], in_=R[:, :, 0:n], func=AF.Square)
    pm, pm_s = ps_r.get()
    for kc in range(KC):
        P.op('pe', 'matmul', reads=[Rs, 'onesD', pm_s], writes=[pm_s], out=pm[:, 0:n], lhsT=onesD[:], rhs=R[:, kc, 0:n], start=(kc == 0), stop=(kc == KC - 1))
    pq, pq_s = ps_r.get()
    for kc in range(KC):
        P.op('pe', 'matmul', reads=[sq_s, 'onesD', pq_s], writes=[pq_s], out=pq[:, 0:n], lhsT=onesD[:], rhs=sq[:, kc, 0:n], start=(kc == 0), stop=(kc == KC - 1))
    P.op('act', 'activation', reads=[pm_s], writes=[mean_s], out=mean[:, 0:n], in_=pm[:, 0:n], func=AF.Copy)
    P.op('dve', 'tensor_tensor', reads=[mean_s], writes=[tmp_s], out=tmp[:, 0:n], in0=mean[:, 0:n], in1=mean[:, 0:n], op=ALU.mult)
    P.op('dve', 'tensor_tensor', reads=[pq_s, tmp_s], writes=[tmp_s], out=tmp[:, 0:n], in0=pq[:, 0:n], in1=tmp[:, 0:n], op=ALU.subtract)
    P.op('dve', 'tensor_scalar', reads=[tmp_s], writes=[tmp_s], out=tmp[:, 0:n], in0=tmp[:, 0:n], scalar1=float(eps), scalar2=None, op0=ALU.add)
    P.op('act', 'activation', reads=[tmp_s], writes=[tmp_s], out=tmp[:, 0:n], in_=tmp[:, 0:n], func=AF.Sqrt)
    P.op('dve', 'reciprocal', reads=[tmp_s], writes=[rstd_s], out=rstd[:, 0:n], in_=tmp[:, 0:n])


def proj(P, Wd, M, H, Hs, n, Wt_r, ps_r, consume):
    m0 = 0
    while m0 < M:
        w = min(512, M - m0)
        Wt, Wt_s = Wt_r.get()
        P.dma('pool', Wt[:, :, 0:w], Wd[:, m0:m0 + w].rearrange("(kc p) c -> p kc c", p=128), reads=[('wbf',)], writes=[Wt_s])
        for mi in range(w // 128):
            ps, ps_s = ps_r.get()
            for kc in range(KC):
                P.op('pe', 'matmul', reads=[Wt_s, Hs, ps_s], writes=[ps_s], out=ps[:, 0:n], lhsT=Wt[:, kc, mi * 128:(mi + 1) * 128], rhs=H[:, kc, 0:n], start=(kc == 0), stop=(kc == KC - 1))
            consume(m0 // 128 + mi, ps, ps_s)
        m0 += w


def phase_mod(P, l, io, mod, vec, wbf):
    P.phase_begin()
    sc = P.sbuf("m_sc", [128, KC, 2])
    ps_r = Rot(P, "m_ps", [128, 512], 2, psum=True)
    Wt_r = Rot(P, "m_W", [128, KC, 512], 2)
    P.dma('sp', sc[:], io["ccT"], writes=['m_sc'])
    P.op('act', 'activation', reads=['m_sc'], writes=['m_sc'], out=sc[:], in_=sc[:], func=AF.Silu)
    ms = ('mod', l)
    for mg in range(12):
        Wt, Wt_s = Wt_r.get()
        P.dma('pool', Wt[:], io[f"w_ada{l}"][:, mg * 512:(mg + 1) * 512].rearrange("(kc p) c -> p kc c", p=128), writes=[Wt_s])
        ps, ps_s = ps_r.get()
        for mi in range(4):
            for kc in range(KC):
                P.op('pe', 'matmul', reads=[Wt_s, 'm_sc', ps_s], writes=[ps_s], out=ps[:, mi * 2:mi * 2 + 2], lhsT=Wt[:, kc, mi * 128:(mi + 1) * 128], rhs=sc[:, kc, :], start=(kc == 0), stop=(kc == KC - 1))
        P.op('dve', 'tensor_tensor', reads=[ps_s, 'vec', ms], writes=[ms], out=mod[:, mg * 4:mg * 4 + 4, :], in0=ps[:, 0:8].rearrange("p (m j) -> p m j", j=2),
             in1=vec[:, VO["b_ada"] + mg * 4:VO["b_ada"] + mg * 4 + 4].unsqueeze(2).to_broadcast([128, 4, 2]), op=ALU.add)
    P.op('dve', 'tensor_scalar', reads=[ms], writes=[ms], out=mod[:, 16:32, :], in0=mod[:, 16:32, :], scalar1=1.0, scalar2=None, op0=ALU.add)
    Wb_r = Rot(P, "m_Wb", [128, KC, 512], 2, dt=BF16)
    ci = 0
    for (src, dst, M) in ((io[f"w_in{l}"], wbf["in"], DIN), (io[f"w_out{l}"], wbf["out"], D)):
        m0 = 0
        while m0 < M:
            w = min(512, M - m0)
            Wt, Wt_s = Wt_r.get()
            P.dma('pool', Wt[:, :, 0:w], src[:, m0:m0 + w].rearrange("(kc p) c -> p kc c", p=128), writes=[Wt_s])
            Wb, Wb_s = Wb_r.get()
            eng = ('act', 'pool', 'dve')[ci % 3]
            ci += 1
            if eng == 'act':
                P.op('act', 'activation', reads=[Wt_s], writes=[Wb_s], out=Wb[:, :, 0:w], in_=Wt[:, :, 0:w], func=AF.Copy)
            else:
                P.op(eng, 'tensor_copy', reads=[Wt_s], writes=[Wb_s], out=Wb[:, :, 0:w], in_=Wt[:, :, 0:w])
            P.dma('sp', dst[:, m0:m0 + w].rearrange("(kc p) c -> p kc c", p=128), Wb[:, :, 0:w], reads=[Wb_s], writes=[('wbf',)])
            m0 += w
    P.phase_end()


def phase_inproj(P, l, io, mod, xT, zT, T, onesD, wbf):
    P.phase_begin()
    ms = ('mod', l)
    X_r = Rot(P, "p1_X", [128, KC, 512], 2)
    sq = P.sbuf("p1_sq", [128, KC, 512])
    H_r = Rot(P, "p1_H", [128, KC, 512], 2, dt=BF16)
    mean = P.sbuf("p1_mean", [128, 512]); rstd = P.sbuf("p1_rstd", [128, 512]); tmp = P.sbuf("p1_tmp", [128, 512])
    st_r = Rot(P, "p1_st", [128, 512], 4)
    Wt_r = Rot(P, "p1_W", [128, KC, 512], 3, dt=BF16)
    ps_r = Rot(P, "p1_ps", [128, 512], 6, psum=True)
    for (t0, n, j) in token_tiles(T):
        X, Xs = X_r.get()
        P.dma('sp', X[:, :, 0:n], xT[:, t0:t0 + n].rearrange("(kc p) t -> p kc t", p=128), reads=[('xT',)], writes=[Xs])
        H, Hs = H_r.get()
        ln_stats(P, X, Xs, n, 1e-6, onesD, sq, 'p1_sq', ps_r, mean, 'p1_mean', rstd, 'p1_rstd', tmp, 'p1_tmp')
        for kc in range(KC):
            eng = 'dve' if kc % 2 == 0 else 'pool'
            xr = (Xs, kc)
            P.op(eng, 'tensor_tensor', reads=[Xs, 'p1_mean'], writes=[xr], out=X[:, kc, 0:n], in0=X[:, kc, 0:n], in1=mean[:, 0:n], op=ALU.subtract)
            P.op(eng, 'tensor_tensor', reads=[xr, 'p1_rstd'], writes=[xr], out=X[:, kc, 0:n], in0=X[:, kc, 0:n], in1=rstd[:, 0:n], op=ALU.mult)
            P.op('act', 'activation', reads=[xr, ms, Hs], writes=[Hs], out=H[:, kc, 0:n], in_=X[:, kc, 0:n], func=AF.Identity, bias=mod[:, kc, j:j + 1], scale=mod[:, 16 + kc, j:j + 1])
        for kc in range(KC):
            dd = P.readers.setdefault(Xs, {})
            dd[('xw', kc)] = P.lastw[(Xs, kc)]
            for k2, tok in P.readers.get((Xs, kc), {}).items():
                dd[('xr', kc, k2)] = tok

        def consume(mi, ps, ps_s, t0=t0, n=n):
            st, st_s = st_r.get()
            if mi % 2 == 0:
                P.op('act', 'activation', reads=[ps_s], writes=[st_s], out=st[:, 0:n], in_=ps[:, 0:n], func=AF.Copy)
            else:
                P.op('dve', 'tensor_copy', reads=[ps_s], writes=[st_s], out=st[:, 0:n], in_=ps[:, 0:n])
            P.dma('sp', zT[mi * 128:(mi + 1) * 128, t0:t0 + n], st[:, 0:n], reads=[st_s], writes=[('zT',)])
        proj(P, wbf["in"], DIN, H, Hs, n, Wt_r, ps_r, consume)
    P.phase_end()


def phase_outproj(P, l, io, mod, vec, xT, ymT, x1T, T, onesD, t_lo, wbf):
    P.phase_begin()
    ms = ('mod', l)
    Y_r = Rot(P, "po_Y", [128, KC, 512], 2, dt=BF16)
    X_r = Rot(P, "po_X", [128, KC, 512], 2)
    R = P.sbuf("po_R", [128, KC, 512])
    sq = P.sbuf("po_sq", [128, KC, 512])
    mean = P.sbuf("po_mean", [128, 512]); rstd = P.sbuf("po_rstd", [128, 512]); tmp = P.sbuf("po_tmp", [128, 512])
    Wt_r = Rot(P, "po_W", [128, KC, 512], 2, dt=BF16)
    ps_r = Rot(P, "po_ps", [128, 512], 6, psum=True)
    for (t0, n, j) in token_tiles(T):
        if t0 < t_lo:
            continue
        Y, Ys = Y_r.get()
        P.dma('sp', Y[:, :, 0:n], ymT[:, t0:t0 + n].rearrange("(kc p) t -> p kc t", p=128), reads=[('ymT',)], writes=[Ys])
        X, Xs = X_r.get()
        P.dma('sp', X[:, :, 0:n], xT[:, t0:t0 + n].rearrange("(kc p) t -> p kc t", p=128), reads=[('xT',)], writes=[Xs])
        P.op('pool', 'tensor_scalar', reads=[Xs], writes=[Xs], out=X[:, :, 0:n], in0=X[:, :, 0:n], scalar1=float(ALPHA), scalar2=None, op0=ALU.mult)

        def consume(mi, ps, ps_s, n=n, j=j, X=X, Xs=Xs):
            P.op('dve', 'scalar_tensor_tensor', reads=[ps_s, ms, Xs, 'po_R'], writes=['po_R'], out=R[:, mi, 0:n], in0=ps[:, 0:n], scalar=mod[:, 32 + mi, j:j + 1], in1=X[:, mi, 0:n], op0=ALU.mult, op1=ALU.add)
        proj(P, wbf["out"], D, Y, Ys, n, Wt_r, ps_r, consume)
        ln_stats(P, R, 'po_R', n, 1e-5, onesD, sq, 'po_sq', ps_r, mean, 'po_mean', rstd, 'po_rstd', tmp, 'po_tmp')
        for kc in range(KC):
            eng = 'dve' if kc % 2 == 0 else 'pool'
            P.op(eng, 'tensor_tensor', reads=['po_R', 'po_mean'], writes=['po_R'], out=R[:, kc, 0:n], in0=R[:, kc, 0:n], in1=mean[:, 0:n], op=ALU.subtract)
            P.op(eng, 'tensor_tensor', reads=['po_R', 'po_rstd'], writes=['po_R'], out=R[:, kc, 0:n], in0=R[:, kc, 0:n], in1=rstd[:, 0:n], op=ALU.mult)
            P.op('act', 'activation', reads=['po_R', 'vec'], writes=['po_R'], out=R[:, kc, 0:n], in_=R[:, kc, 0:n], func=AF.Identity, bias=vec[:, VO["ln_b"] + kc:VO["ln_b"] + kc + 1], scale=vec[:, VO["ln_g"] + kc:VO["ln_g"] + kc + 1])
        P.dma('sp', x1T[:, t0 - t_lo:t0 - t_lo + n].rearrange("(kc p) t -> p kc t", p=128), R[:, :, 0:n], reads=['po_R'], writes=[('x1T', l)])
    P.phase_end()


def phase_rwkv_prep(P, l, io, vec, zT, T, S):
    P.phase_begin()
    blk = P.sbuf("r1_blk", [128, 128])
    P.op('pool', 'memset', writes=['r1_blk'], ap=blk[:], constant=0.0)
    P.op('pool', 'memset', reads=['r1_blk'], writes=['r1_blk'], ap=blk[0:64, 0:64], constant=1.0)
    P.op('pool', 'memset', reads=['r1_blk'], writes=['r1_blk'], ap=blk[64:128, 64:128], constant=1.0)
    w2 = P.sbuf("r1_w2", [128, 1024]); a2 = P.sbuf("r1_a2", [128, 1024])
    P.dma('sp', w2[:], io[f"rw_w2{l}"], writes=['r1_w2'])
    P.dma('sp', a2[:], io[f"rw_a2{l}"], writes=['r1_a2'])
    omka = P.sbuf("r1_omka", [128, 8])
    P.op('dve', 'tensor_scalar', reads=['vec'], writes=['r1_omka'], out=omka[:], in0=vec[:, VO["k_a"]:VO["k_a"] + 8], scalar1=-1.0, scalar2=1.0, op0=ALU.mult, op1=ALU.add)
    TH = P.sbuf("r1_TH", [128, 512]); AC = P.sbuf("r1_AC", [128, 512])
    Z_r = Rot(P, "r1_Z", [128, 514], 3)
    cv = {k: Rot(P, "r1_c" + k, [128, 512], 2) for k in "rkv"}
    t_r = Rot(P, "r1_t", [128, 512], 4)
    o_r = Rot(P, "r1_o", [128, 512], 6)
    ks_r = Rot(P, "r1_ks", [128, 512], 2)
    ps_r = Rot(P, "r1_ps", [128, 512], 4, psum=True)
    for (t0, n, j) in token_tiles(T):
        seg_lo, seg_hi = (0, NCTX) if j == 1 else (NCTX, T)
        P.dma('sp', TH[:, 0:n], zT[6144:6272, t0:t0 + n], reads=[('zT',)], writes=['r1_TH'])
        P.op('act', 'activation', reads=['r1_TH'], writes=['r1_TH'], out=TH[:, 0:n], in_=TH[:, 0:n], func=AF.Tanh)
        P.dma('sp', AC[:, 0:n], zT[6272:6400, t0:t0 + n], reads=[('zT',)], writes=['r1_AC'])
        for hp in range(8):
            res = {}
            for ci, k in enumerate("rkv"):
                Z, Zs = Z_r.get()
                lo = max(t0 - 1, seg_lo); hi = min(t0 + n + 1, seg_hi)
                if lo > t0 - 1:
                    P.op('pool', 'memset', writes=[Zs], ap=Z[:, 0:1], constant=0.0)
                if hi < t0 + n + 1:
                    P.op('pool', 'memset', reads=[Zs], writes=[Zs], ap=Z[:, n + 1:n + 2], constant=0.0)
                row0 = 2048 + ci * 1024 + hp * 128
                P.dma('sp', Z[:, lo - (t0 - 1):hi - (t0 - 1)], zT[row0:row0 + 128, lo:hi], reads=[('zT',), Zs], writes=[Zs])
                o, os_ = cv[k].get()
                c0 = VO["conv"] + (ci * 8 + hp) * 3
                P.op('dve', 'tensor_scalar', reads=[Zs, 'vec'], writes=[os_], out=o[:, 0:n], in0=Z[:, 0:n], scalar1=vec[:, c0:c0 + 1], scalar2=None, op0=ALU.mult)
                P.op('dve', 'scalar_tensor_tensor', reads=[Zs, 'vec', os_], writes=[os_], out=o[:, 0:n], in0=Z[:, 1:n + 1], scalar=vec[:, c0 + 1:c0 + 2], in1=o[:, 0:n], op0=ALU.mult, op1=ALU.add)
                P.op('dve', 'scalar_tensor_tensor', reads=[Zs, 'vec', os_], writes=[os_], out=o[:, 0:n], in0=Z[:, 2:n + 2], scalar=vec[:, c0 + 2:c0 + 3], in1=o[:, 0:n], op0=ALU.mult, op1=ALU.add)
                res[k] = (o, os_)
            (r_, r_s), (k_, k_s), (v_, v_s) = res["r"], res["k"], res["v"]
            rows = slice(hp * 128, (hp + 1) * 128)
            P.dma('sp', S["r"][rows, t0:t0 + n], r_[:, 0:n], reads=[r_s], writes=[('S_r',)])
            P.dma('sp', S["v"][rows, t0:t0 + n], v_[:, 0:n], reads=[v_s], writes=[('S_v',)])
            kk, kk_s = o_r.get()
            t1, t1_s = t_r.get()
            P.op('pool', 'tensor_scalar', reads=[k_s, 'vec'], writes=[kk_s], out=kk[:, 0:n], in0=k_[:, 0:n], scalar1=vec[:, VO["k_k"] + hp:VO["k_k"] + hp + 1], scalar2=None, op0=ALU.mult)
            P.op('act', 'activation', reads=[kk_s], writes=[t1_s], out=t1[:, 0:n], in_=kk[:, 0:n], func=AF.Square)
            ps, ps_s = ps_r.get()
            P.op('pe', 'matmul', reads=[t1_s, 'r1_blk'], writes=[ps_s], out=ps[:, 0:n], lhsT=blk[:], rhs=t1[:, 0:n], start=True, stop=True)
            P.op('act', 'activation', reads=[ps_s], writes=[t1_s], out=t1[:, 0:n], in_=ps[:, 0:n], func=AF.Sqrt)
            P.op('dve', 'tensor_scalar', reads=[t1_s], writes=[t1_s], out=t1[:, 0:n], in0=t1[:, 0:n], scalar1=1e-12, scalar2=None, op0=ALU.max)
            P.op('dve', 'reciprocal', reads=[t1_s], writes=[t1_s], out=t1[:, 0:n], in_=t1[:, 0:n])
            P.op('dve', 'tensor_tensor', reads=[kk_s, t1_s], writes=[kk_s], out=kk[:, 0:n], in0=kk[:, 0:n], in1=t1[:, 0:n], op=ALU.mult)
            P.dma('sp', S["kk"][rows, t0:t0 + n], kk[:, 0:n], reads=[kk_s], writes=[('S_kk',)])
            ks, ks_s = ks_r.get()
            for d in range(2):
                dp = slice(64 * d, 64 * d + 64)
                ps, ps_s = ps_r.get()
                P.op('pe', 'matmul', reads=['r1_w2', 'r1_TH'], writes=[ps_s], out=ps[:, 0:n], lhsT=w2[dp, hp * 128:(hp + 1) * 128], rhs=TH[dp, 0:n], start=True, stop=True)
                lw, lw_s = o_r.get()
                c = VO["w0"] + d * 8 + hp
                P.op('act', 'activation', reads=[ps_s, 'vec'], writes=[lw_s], out=lw[:, 0:n], in_=ps[:, 0:n], func=AF.Sigmoid, bias=vec[:, c:c + 1], scale=1.0)
                P.op('pool', 'tensor_scalar', reads=[lw_s], writes=[lw_s], out=lw[:, 0:n], in0=lw[:, 0:n], scalar1=-0.606531, scalar2=None, op0=ALU.mult)
                P.dma('sp', S["lw"][d][rows, t0:t0 + n], lw[:, 0:n], reads=[lw_s], writes=[('S_lw', d)])
                ps, ps_s = ps_r.get()
                P.op('pe', 'matmul', reads=['r1_a2', 'r1_AC'], writes=[ps_s], out=ps[:, 0:n], lhsT=a2[dp, hp * 128:(hp + 1) * 128], rhs=AC[dp, 0:n], start=True, stop=True)
                a_, a_s = t_r.get()
                c = VO["a0"] + d * 8 + hp
                P.op('act', 'activation', reads=[ps_s, 'vec'], writes=[a_s], out=a_[:, 0:n], in_=ps[:, 0:n], func=AF.Sigmoid, bias=vec[:, c:c + 1], scale=1.0)
                ka, ka_s = o_r.get()
                P.op('dve', 'tensor_tensor', reads=[kk_s, a_s], writes=[ka_s], out=ka[:, 0:n], in0=kk[:, 0:n], in1=a_[:, 0:n], op=ALU.mult)
                P.dma('sp', S["ka"][d][rows, t0:t0 + n], ka[:, 0:n], reads=[ka_s], writes=[('S_ka', d)])
                kd, kd_s = o_r.get()
                P.op('dve', 'tensor_scalar', reads=[a_s, 'vec', 'r1_omka'], writes=[a_s], out=a_[:, 0:n], in0=a_[:, 0:n], scalar1=vec[:, VO["k_a"] + hp:VO["k_a"] + hp + 1], scalar2=omka[:, hp:hp + 1], op0=ALU.mult, op1=ALU.add)
                P.op('dve', 'tensor_tensor', reads=[k_s, a_s], writes=[kd_s], out=kd[:, 0:n], in0=k_[:, 0:n], in1=a_[:, 0:n], op=ALU.mult)
                P.dma('sp', S["kd"][d][rows, t0:t0 + n], kd[:, 0:n], reads=[kd_s], writes=[('S_kd', d)])
                if d == 0:
                    P.op('pool', 'tensor_copy', reads=[kd_s], writes=[ks_s], out=ks[:, 0:n], in_=kd[:, 0:n])
                else:
                    P.op('pool', 'tensor_tensor', reads=[kd_s, ks_s], writes=[ks_s], out=ks[:, 0:n], in0=ks[:, 0:n], in1=kd[:, 0:n], op=ALU.add)
            P.op('dve', 'scalar_tensor_tensor', reads=[r_s, 'vec', ks_s], writes=[ks_s], out=ks[:, 0:n], in0=r_[:, 0:n], scalar=vec[:, VO["r_k"] + hp:VO["r_k"] + hp + 1], in1=ks[:, 0:n], op0=ALU.mult, op1=ALU.mult)
            ps, ps_s = ps_r.get()
            P.op('pe', 'matmul', reads=[ks_s, 'r1_blk'], writes=[ps_s], out=ps[:, 0:n], lhsT=blk[:], rhs=ks[:, 0:n], start=True, stop=True)
            bo, bo_s = o_r.get()
            P.op('dve', 'tensor_tensor', reads=[ps_s, v_s], writes=[bo_s], out=bo[:, 0:n], in0=ps[:, 0:n], in1=v_[:, 0:n], op=ALU.mult)
            P.dma('sp', S["bonus"][rows, t0:t0 + n], bo[:, 0:n], reads=[bo_s], writes=[('S_bonus',)])
    P.phase_end()


def phase_rwkv_post(P, l, vec, zT, T, S, ymT, t_lo):
    P.phase_begin()
    blk = P.sbuf("r3_blk", [128, 128])
    P.op('pool', 'memset', writes=['r3_blk'], ap=blk[:], constant=0.0)
    P.op('pool', 'memset', reads=['r3_blk'], writes=['r3_blk'], ap=blk[0:64, 0:64], constant=1.0 / 64)
    P.op('pool', 'memset', reads=['r3_blk'], writes=['r3_blk'], ap=blk[64:128, 64:128], constant=1.0 / 64)
    i_r = Rot(P, "r3_i", [128, 512], 8)
    ob_r = Rot(P, "r3_ob", [128, 512], 3, dt=BF16)
    t_r = Rot(P, "r3_t", [128, 512], 6)
    ps_r = Rot(P, "r3_ps", [128, 512], 4, psum=True)
    for (t0, n, j) in token_tiles(T):
        if t0 < t_lo:
            continue
        for hp in range(8):
            rows = slice(hp * 128, (hp + 1) * 128)
            ld = {}
            for k, src in (("o0", S["o"][0][rows]), ("o1", S["o"][1][rows]), ("bo", S["bonus"][rows]), ("g", zT[5120 + hp * 128:5120 + (hp + 1) * 128])):
                t, ts = i_r.get()
                P.dma('sp', t[:, 0:n], src[:, t0:t0 + n], reads=[('d_o', 0), ('d_o', 1), ('S_bonus',), ('zT',)], writes=[ts])
                ld[k] = (t, ts)
            (o0, o0_s), (o1, o1_s), (bo, bo_s), (g, g_s) = ld["o0"], ld["o1"], ld["bo"], ld["g"]
            P.op('pool', 'tensor_tensor', reads=[o0_s, o1_s], writes=[o0_s], out=o0[:, 0:n], in0=o0[:, 0:n], in1=o1[:, 0:n], op=ALU.add)
            ps, ps_s = ps_r.get()
            P.op('pe', 'matmul', reads=[o0_s, 'r3_blk'], writes=[ps_s], out=ps[:, 0:n], lhsT=blk[:], rhs=o0[:, 0:n], start=True, stop=True)
            cen, cen_s = t_r.get()
            P.op('dve', 'tensor_tensor', reads=[o0_s, ps_s], writes=[cen_s], out=cen[:, 0:n], in0=o0[:, 0:n], in1=ps[:, 0:n], op=ALU.subtract)
            sq, sq_s = t_r.get()
            P.op('act', 'activation', reads=[cen_s], writes=[sq_s], out=sq[:, 0:n], in_=cen[:, 0:n], func=AF.Square)
            ps, ps_s = ps_r.get()
            P.op('pe', 'matmul', reads=[sq_s, 'r3_blk'], writes=[ps_s], out=ps[:, 0:n], lhsT=blk[:], rhs=sq[:, 0:n], start=True, stop=True)
            P.op('dve', 'tensor_scalar', reads=[ps_s], writes=[sq_s], out=sq[:, 0:n], in0=ps[:, 0:n], scalar1=64e-5, scalar2=None, op0=ALU.add)
            P.op('act', 'activation', reads=[sq_s], writes=[sq_s], out=sq[:, 0:n], in_=sq[:, 0:n], func=AF.Sqrt)
            P.op('dve', 'reciprocal', reads=[sq_s], writes=[sq_s], out=sq[:, 0:n], in_=sq[:, 0:n])
            P.op('dve', 'tensor_tensor', reads=[cen_s, sq_s], writes=[cen_s], out=cen[:, 0:n], in0=cen[:, 0:n], in1=sq[:, 0:n], op=ALU.mult)
            P.op('dve', 'tensor_scalar', reads=[cen_s, 'vec'], writes=[cen_s], out=cen[:, 0:n], in0=cen[:, 0:n], scalar1=vec[:, VO["gn_w"] + hp:VO["gn_w"] + hp + 1], scalar2=vec[:, VO["gn_b"] + hp:VO["gn_b"] + hp + 1], op0=ALU.mult, op1=ALU.add)
            P.op('pool', 'tensor_tensor', reads=[cen_s, bo_s], writes=[cen_s], out=cen[:, 0:n], in0=cen[:, 0:n], in1=bo[:, 0:n], op=ALU.add)
            P.op('act', 'activation', reads=[g_s], writes=[g_s], out=g[:, 0:n], in_=g[:, 0:n], func=AF.Silu)
            ob, ob_s = ob_r.get()
            P.op('dve', 'tensor_tensor', reads=[cen_s, g_s], writes=[ob_s], out=ob[:, 0:n], in0=cen[:, 0:n], in1=g[:, 0:n], op=ALU.mult)
            P.dma('sp', ymT[1024 + hp * 128:1024 + (hp + 1) * 128, t0:t0 + n], ob[:, 0:n], reads=[ob_s], writes=[('ymT',)])
    P.phase_end()


def sin_red(P, out, z, n, tmpi, tmpf, res_out, res_z, res_t):
    P.op('dve', 'tensor_scalar', reads=[res_z], writes=[res_t], out=tmpi, in0=z, scalar1=1.0 / (2 * PI), scalar2=None, op0=ALU.mult)
    P.op('dve', 'tensor_copy', reads=[res_t], writes=[res_t], out=tmpf, in_=tmpi)
    P.op('dve', 'scalar_tensor_tensor', reads=[res_t, res_z], writes=[res_out], out=out, in0=tmpf, scalar=-2 * PI, in1=z, op0=ALU.mult, op1=ALU.add)
    P.op('dve', 'tensor_scalar', reads=[res_out], writes=[res_t], out=tmpf, in0=out, scalar1=PI, scalar2=None, op0=ALU.is_gt)
    P.op('dve', 'scalar_tensor_tensor', reads=[res_t, res_out], writes=[res_out], out=out, in0=tmpf, scalar=-2 * PI, in1=out, op0=ALU.mult, op1=ALU.add)
    P.op('dve', 'tensor_scalar', reads=[res_out], writes=[res_t], out=tmpf, in0=out, scalar1=-PI, scalar2=None, op0=ALU.is_lt)
    P.op('dve', 'scalar_tensor_tensor', reads=[res_t, res_out], writes=[res_out], out=out, in0=tmpf, scalar=2 * PI, in1=out, op0=ALU.mult, op1=ALU.add)
    P.op('dve', 'tensor_scalar', reads=[res_out], writes=[res_out], out=out, in0=out, scalar1=-3.14159, scalar2=3.14159, op0=ALU.max, op1=ALU.min)
    P.op('act', 'activation', reads=[res_out], writes=[res_out], out=out, in_=out, func=AF.Sin)


def phase_s5(P, l, io, vec, zT, T, S, ymT, t_lo, cst):
    TC = 512
    I32 = mybir.dt.int32
    P.phase_begin()
    ident = cst[:, 0:128]
    Jc = cst[:, 768:896]
    iota = cst[:, 896:896 + TC]
    pp = P.sbuf("s5_pp", [128, 3, 64])
    P.dma('sp', pp[:], io[f"s5p{l}"], writes=['s5_pp'])
    rho = P.sbuf("s5_rho", [128, 64]); theta = P.sbuf("s5_theta", [128, 64])
    P.op('act', 'activation', reads=['s5_pp'], writes=['s5_pp'], out=pp[:, 2, :], in_=pp[:, 2, :], func=AF.Exp)
    P.op('dve', 'tensor_scalar', reads=['s5_pp'], writes=['s5_pp'], out=pp[:, 0, :], in0=pp[:, 0, :], scalar1=-1e-4, scalar2=None, op0=ALU.min)
    P.op('dve', 'tensor_tensor', reads=['s5_pp'], writes=['s5_rho'], out=rho[:], in0=pp[:, 0, :], in1=pp[:, 2, :], op=ALU.mult)
    P.op('act', 'activation', reads=['s5_rho'], writes=['s5_rho'], out=rho[:], in_=rho[:], func=AF.Exp)
    P.op('dve', 'tensor_tensor', reads=['s5_pp'], writes=['s5_theta'], out=theta[:], in0=pp[:, 1, :], in1=pp[:, 2, :], op=ALU.mult)
    W1 = P.sbuf("s5_W1", [16, 64, 128]); W2 = P.sbuf("s5_W2", [16, 64, 128])
    P.phase_begin()
    BT = P.sbuf("s5_BT", [16, 2, 2048])
    P.dma('sp', BT[:], io[f"s5bt{l}"], writes=['s5_BT'])
    rr = P.sbuf("s5_rr", [16, 3, 2048])
    tA = [P.sbuf(f"s5_t{i}", [16, 2048]) for i in range(8)]
    tI = P.sbuf("s5_ti", [16, 2048], I32)
    for d in range(2):
        P.dma('sp', rr[:], io[f"s5r{l}"][:, :, d * 2048:(d + 1) * 2048], reads=['s5_rr'], writes=['s5_rr'])
        re, im, st = rr[:, 0, :], rr[:, 1, :], rr[:, 2, :]
        R_ = ['s5_rr'] + [f"s5_t{i}" for i in range(8)] + ['s5_ti']
        W_ = R_
        def o(eng, name, **kw):
            P.op(eng, name, reads=R_, writes=W_, **kw)
        o('act', 'activation', out=st, in_=st, func=AF.Exp)
        o('dve', 'tensor_scalar', out=re, in0=re, scalar1=-1e-4, scalar2=None, op0=ALU.min)
        er, ang, cs_, sn_, z2 = tA[0][:], tA[1][:], tA[2][:], tA[3][:], tA[4][:]
        o('dve', 'tensor_tensor', out=er, in0=re, in1=st, op=ALU.mult)
        o('act', 'activation', out=er, in_=er, func=AF.Exp)
        o('dve', 'tensor_tensor', out=ang, in0=im, in1=st, op=ALU.mult)
        sin_red(P, sn_, ang, 2048, tI[:], tA[5][:], 's5_t3', 's5_t1', 's5_t5')
        o('dve', 'tensor_scalar', out=z2, in0=ang, scalar1=PI / 2, scalar2=None, op0=ALU.add)
        sin_red(P, cs_, z2, 2048, tI[:], tA[5][:], 's5_t2', 's5_t4', 's5_t5')
        P.barrier()
        nre, nim = tA[2][:], tA[3][:]
        o('dve', 'tensor_tensor', out=nre, in0=er, in1=cs_, op=ALU.mult)
        o('dve', 'tensor_scalar', out=nre, in0=nre, scalar1=-1.0, scalar2=None, op0=ALU.add)
        o('dve', 'tensor_tensor', out=nim, in0=er, in1=sn_, op=ALU.mult)
        den, t5, bsr, bsi = tA[0][:], tA[5][:], tA[6][:], tA[7][:]
        o('dve', 'tensor_tensor', out=den, in0=re, in1=re, op=ALU.mult)
        o('dve', 'tensor_tensor', out=t5, in0=im, in1=im, op=ALU.mult)
        o('dve', 'tensor_tensor', out=den, in0=den, in1=t5, op=ALU.add)
        o('dve', 'reciprocal', out=den, in_=den)
        o('dve', 'tensor_tensor', out=bsr, in0=nre, in1=re, op=ALU.mult)
        o('dve', 'tensor_tensor', out=t5, in0=nim, in1=im, op=ALU.mult)
        o('dve', 'tensor_tensor', out=bsr, in0=bsr, in1=t5, op=ALU.add)
        o('dve', 'tensor_tensor', out=bsr, in0=bsr, in1=den, op=ALU.mult)
        o('dve', 'tensor_tensor', out=bsi, in0=nim, in1=re, op=ALU.mult)
        o('dve', 'tensor_tensor', out=t5, in0=nre, in1=im, op=ALU.mult)
        o('dve', 'tensor_tensor', out=bsi, in0=bsi, in1=t5, op=ALU.subtract)
        o('dve', 'tensor_tensor', out=bsi, in0=bsi, in1=den, op=ALU.mult)
        Br, Bi = BT[:, 0, :], BT[:, 1, :]
        bpr, bpi, t4 = tA[1][:], tA[2][:], tA[4][:]
        Rb = R_ + ['s5_BT']
        P.op('dve', 'tensor_tensor', reads=Rb, writes=W_, out=bpr, in0=bsr, in1=Br, op=ALU.mult)
        P.op('dve', 'tensor_tensor', reads=Rb, writes=W_, out=t4, in0=bsi, in1=Bi, op=ALU.mult)
        o('dve', 'tensor_tensor', out=bpr, in0=bpr, in1=t4, op=ALU.subtract)
        P.op('dve', 'tensor_tensor', reads=Rb, writes=W_, out=bpi, in0=bsr, in1=Bi, op=ALU.mult)
        P.op('dve', 'tensor_tensor', reads=Rb, writes=W_, out=t4, in0=bsi, in1=Br, op=ALU.mult)
        o('dve', 'tensor_tensor', out=bpi, in0=bpi, in1=t4, op=ALU.add)
        g3 = lambda ap: ap.rearrange("p (g s) -> p g s", s=64)
        dg = slice(d * 32, (d + 1) * 32)
        P.op('dve', 'tensor_copy', reads=R_, writes=['s5_W1'], out=W1[:, dg, 0:64], in_=g3(bpr))
        P.op('dve', 'tensor_copy', reads=R_ + ['s5_W1'], writes=['s5_W1'], out=W1[:, dg, 64:128], in_=g3(bpi))
        P.op('dve', 'tensor_copy', reads=R_, writes=['s5_W2'], out=W2[:, dg, 0:64], in_=g3(bpi))
        P.op('dve', 'tensor_scalar', reads=R_ + ['s5_W2'], writes=['s5_W2'], out=W2[:, dg, 64:128], in0=g3(bpr), scalar1=-1.0, scalar2=None, op0=ALU.mult)
        P.barrier()
    P.phase_end()
    CT1 = P.sbuf("s5_CT1", [128, 64, 16]); CT2 = P.sbuf("s5_CT2", [128, 64, 16])
    P.dma('sp', CT1[:], io[f"s5ca{l}"], writes=['s5_CT1'])
    P.dma('sp', CT2[:], io[f"s5cb{l}"], writes=['s5_CT2'])
    P.op('dve', 'tensor_scalar', reads=['s5_CT1'], writes=['s5_CT1'], out=CT1[64:128], in0=CT1[64:128], scalar1=-1.0, scalar2=None, op0=ALU.mult)
    P.op('dve', 'tensor_scalar', reads=['s5_CT2'], writes=['s5_CT2'], out=CT2[:], in0=CT2[:], scalar1=-1.0, scalar2=None, op0=ALU.mult)
    GS = 4
    zt = P.sbuf("s5_zt", [128, TC]); zi = P.sbuf("s5_zi", [128, TC], I32); zf = P.sbuf("s5_zf", [128, TC])
    SB = []
    for sl in range(GS):
        q = {"COS": P.sbuf(f"s5_COS{sl}", [128, TC]), "SIN": P.sbuf(f"s5_SIN{sl}", [128, TC]), "RHO": P.sbuf(f"s5_RHO{sl}", [128, TC]),
             "u": Rot(P, f"s5_u{sl}_", [16, TC], 2), "t": Rot(P, f"s5_tt{sl}_", [128, TC], 5), "y": Rot(P, f"s5_y{sl}_", [16, TC], 2),
             "st": Rot(P, f"s5_st{sl}_", [128, 1], 2), "sl": sl}
        SB.append(q)
    ps_r = Rot(P, "s5_ps", [128, 512], 8, psum=True)
    blocks = [(0, NCTX)] + [(t, 512) for t in range(NCTX, T, 512)]

    def chain(d, g, q):
        sl = q["sl"]
        COS, SIN, RHO = q["COS"], q["SIN"], q["RHO"]
        cr, sr, rr_ = f"s5_COS{sl}", f"s5_SIN{sl}", f"s5_RHO{sl}"
        order = blocks if d == 0 else [blocks[0]] + blocks[:0:-1]
        dg = d * 32 + g
        P.op('dve', 'tensor_scalar', reads=['rwc', 's5_theta'], writes=['s5_zt'], out=zt[:], in0=iota, scalar1=theta[:, dg:dg + 1], scalar2=None, op0=ALU.mult)
        sin_red(P, SIN[:], zt[:], TC, zi[:], zf[:], sr, 's5_zt', 's5_zf')
        P.op('dve', 'tensor_scalar', reads=['s5_zt'], writes=['s5_zt'], out=zt[:], in0=zt[:], scalar1=PI / 2, scalar2=None, op0=ALU.add)
        sin_red(P, COS[:], zt[:], TC, zi[:], zf[:], cr, 's5_zt', 's5_zf')
        P.op('dve', 'tensor_scalar', reads=['rwc', 's5_rho'], writes=[rr_], out=RHO[:], in0=iota, scalar1=0.0, scalar2=rho[:, dg:dg + 1], op0=ALU.mult, op1=ALU.add)
        yield
        stc = None
        rv = (lambda ap: ap) if d == 0 else (lambda ap: ap[:, ::-1])
        for (t0, n) in order:
            u, u_s = q["u"].get()
            P.dma('sp', u[:, 0:n], zT[16 * g:16 * g + 16, t0:t0 + n], reads=[('zT',)], writes=[u_s])
            p1, p1_s = ps_r.get()
            P.op('pe', 'matmul', reads=[u_s, 's5_W1'], writes=[p1_s], out=p1[:, 0:n], lhsT=W1[:, dg, :], rhs=u[:, 0:n], start=True, stop=True)
            p2, p2_s = ps_r.get()
            P.op('pe', 'matmul', reads=[u_s, 's5_W2'], writes=[p2_s], out=p2[:, 0:n], lhsT=W2[:, dg, :], rhs=u[:, 0:n], start=True, stop=True)
            t1, t1_s = q["t"].get(); t2, t2_s = q["t"].get()
            P.op('dve', 'tensor_tensor', reads=[p1_s, cr], writes=[t1_s], out=t1[:, 0:n], in0=rv(p1[:, 0:n]), in1=COS[:, 0:n], op=ALU.mult)
            P.op('dve', 'tensor_tensor', reads=[p2_s, sr], writes=[t2_s], out=t2[:, 0:n], in0=rv(p2[:, 0:n]), in1=SIN[:, 0:n], op=ALU.mult)
            yield
            P.op('pool', 'tensor_tensor', reads=[t1_s, t2_s], writes=[t1_s], out=t1[:, 0:n], in0=t1[:, 0:n], in1=t2[:, 0:n], op=ALU.add)
            yield
            Wt, Wt_s = q["t"].get()
            if stc is None:
                P.op('dve', 'tensor_tensor_scan', reads=[t1_s, rr_], writes=[Wt_s], out=Wt[:, 0:n], data0=RHO[:, 0:n], data1=t1[:, 0:n], initial=0.0, op0=ALU.mult, op1=ALU.add)
            else:
                P.op('dve', 'tensor_tensor_scan', reads=[t1_s, rr_, stc[1]], writes=[Wt_s], out=Wt[:, 0:n], data0=RHO[:, 0:n], data1=t1[:, 0:n], initial=stc[0][:, 0:1], op0=ALU.mult, op1=ALU.add)
            yield
            t3, t3_s = q["t"].get(); t4, t4_s = q["t"].get()
            P.op('dve', 'tensor_tensor', reads=[Wt_s, cr], writes=[t3_s], out=t3[:, 0:n], in0=Wt[:, 0:n], in1=COS[:, 0:n], op=ALU.mult)
            P.op('pool', 'tensor_tensor', reads=[Wt_s, sr], writes=[t4_s], out=t4[:, 0:n], in0=Wt[:, 0:n], in1=SIN[:, 0:n], op=ALU.mult)
            yield
            px, px_s = ps_r.get()
            P.op('pe', 'matmul', reads=[t3_s, 'rwc'], writes=[px_s], out=px[:, 0:1], lhsT=ident, rhs=t3[:, n - 1:n], start=True, stop=False)
            P.op('pe', 'matmul', reads=[t4_s, 'rwc', px_s], writes=[px_s], out=px[:, 0:1], lhsT=Jc, rhs=t4[:, n - 1:n], start=False, stop=True)
            sn, sn_s = q["st"].get()
            P.op('act', 'activation', reads=[px_s], writes=[sn_s], out=sn[:], in_=px[:, 0:1], func=AF.Copy)
            stc = (sn, sn_s)
            py, py_s = ps_r.get()
            P.op('pe', 'matmul', reads=[t3_s, 's5_CT1'], writes=[py_s], out=py[0:16, 0:n], lhsT=CT1[:, dg, :], rhs=t3[:, 0:n], start=True, stop=False)
            P.op('pe', 'matmul', reads=[t4_s, 's5_CT2', py_s], writes=[py_s], out=py[0:16, 0:n], lhsT=CT2[:, dg, :], rhs=t4[:, 0:n], start=False, stop=True)
            y, y_s = q["y"].get()
            P.op('act', 'activation', reads=[py_s], writes=[y_s], out=rv(y[:, 0:n]), in_=py[0:16, 0:n], func=AF.Copy)
            P.dma('sp', S["ys"][d][16 * g:16 * g + 16, t0:t0 + n], y[:, 0:n], reads=[y_s], writes=[('S_ys', d)])
            yield

    for d in range(2):
        for gb in range(0, 32, GS):
            live = [chain(d, gb + k, SB[k]) for k in range(GS)]
            while live:
                nxt = []
                for gen in live:
                    try:
                        next(gen)
                        nxt.append(gen)
                    except StopIteration:
                        pass
                live = nxt
    P.phase_end()
    P.phase_begin()
    ps_r = Rot(P, "s5c_ps", [128, 512], 4, psum=True)
    wg = P.sbuf("s5_wg", [128, 4, 512])
    P.dma('sp', wg[:], io[f"w_glu{l}"].rearrange("(kc p) m -> p kc m", p=128), writes=['s5_wg'])
    YG_r = Rot(P, "s5_YG", [128, 4, 512], 2)
    i_r = Rot(P, "s5_i", [128, 512], 8)
    G_r = Rot(P, "s5_G", [128, 4, 512], 2)
    ob_r = Rot(P, "s5_ob", [128, 512], 3, dt=BF16)
    for (t0, n, j) in token_tiles(T):
        if t0 < t_lo:
            continue
        YG, YG_s = YG_r.get()
        G, G_s = G_r.get()
        gts = []
        for c in range(4):
            rows = slice(c * 128, (c + 1) * 128)
            ld = []
            for src, rs in ((zT[rows], ('zT',)), (S["ys"][0][rows], ('S_ys', 0)), (S["ys"][1][rows], ('S_ys', 1))):
                t, ts = i_r.get()
                P.dma('sp', t[:, 0:n], src[:, t0:t0 + n], reads=[rs], writes=[ts])
                ld.append((t, ts))
            (u, u_s), (y0, y0_s), (y1, y1_s) = ld
            P.dma('sp', G[:, c, 0:n], zT[512 + c * 128:512 + (c + 1) * 128, t0:t0 + n], reads=[('zT',), G_s], writes=[G_s])
            P.op('dve', 'scalar_tensor_tensor', reads=[u_s, 'vec', y0_s], writes=[y0_s], out=y0[:, 0:n], in0=u[:, 0:n], scalar=vec[:, VO["d_skip"] + c:VO["d_skip"] + c + 1], in1=y0[:, 0:n], op0=ALU.mult, op1=ALU.add)
            P.op('pool', 'tensor_tensor', reads=[y0_s, y1_s], writes=[y0_s], out=y0[:, 0:n], in0=y0[:, 0:n], in1=y1[:, 0:n], op=ALU.add)
            P.op('act', 'activation', reads=[y0_s], writes=[y1_s], out=y1[:, 0:n], in_=y0[:, 0:n], func=AF.Square)
            P.op('dve', 'tensor_scalar', reads=[y1_s], writes=[y1_s], out=y1[:, 0:n], in0=y1[:, 0:n], scalar1=0.044715, scalar2=1.0, op0=ALU.mult, op1=ALU.add)
            P.op('dve', 'tensor_tensor', reads=[y1_s, y0_s], writes=[y1_s], out=y1[:, 0:n], in0=y1[:, 0:n], in1=y0[:, 0:n], op=ALU.mult)
            P.op('act', 'activation', reads=[y1_s], writes=[y1_s], out=y1[:, 0:n], in_=y1[:, 0:n], func=AF.Sigmoid, scale=1.5957691216057308)
            P.op('dve', 'tensor_tensor', reads=[y1_s, y0_s, YG_s], writes=[YG_s], out=YG[:, c, 0:n], in0=y1[:, 0:n], in1=y0[:, 0:n], op=ALU.mult)
        P.op('act', 'activation', reads=[G_s], writes=[G_s], out=G[:, :, 0:n], in_=G[:, :, 0:n], func=AF.Silu)
        for m in range(4):
            ps, ps_s = ps_r.get()
            for kc in range(4):
                P.op('pe', 'matmul', reads=[YG_s, 's5_wg', ps_s], writes=[ps_s], out=ps[:, 0:n], lhsT=wg[:, kc, m * 128:(m + 1) * 128], rhs=YG[:, kc, 0:n], start=(kc == 0), stop=(kc == 3))
            sg, sg_s = i_r.get()
            P.op('act', 'activation', reads=[ps_s, 'vec'], writes=[sg_s], out=sg[:, 0:n], in_=ps[:, 0:n], func=AF.Sigmoid, bias=vec[:, VO["b_glu"] + m:VO["b_glu"] + m + 1], scale=1.0)
            P.op('dve', 'tensor_tensor', reads=[sg_s, YG_s], writes=[sg_s], out=sg[:, 0:n], in0=sg[:, 0:n], in1=YG[:, m, 0:n], op=ALU.mult)
            ob, ob_s = ob_r.get()
            P.op('dve', 'tensor_tensor', reads=[sg_s, G_s], writes=[ob_s], out=ob[:, 0:n], in0=sg[:, 0:n], in1=G[:, m, 0:n], op=ALU.mult)
            P.dma('sp', ymT[m * 128:(m + 1) * 128, t0:t0 + n], ob[:, 0:n], reads=[ob_s], writes=[('ymT',)])
    P.phase_end()


def phase_pool(P, l, io, vec, zT, T, ymT, t_lo):
    P.phase_begin()
    rows_tot = (T - NCTX) // 64
    wp = P.sbuf("pl_wp", [128, 4, 128])
    P.dma('sp', wp[:], io[f"w_pool{l}"], writes=['pl_wp'])
    icn = P.sbuf("pl_icn", [128, 4, rows_tot + 64 + NCTX])
    P.dma('sp', icn[:], io["icnt"], writes=['pl_icn'])
    ps_r = Rot(P, "pl_ps", [128, 512], 4, psum=True)
    g_r = Rot(P, "pl_g", [128, 512], 3)
    ob_r = Rot(P, "pl_ob", [128, 512], 3, dt=BF16)
    regions = []
    RB = min(64, rows_tot)
    for r0 in range(0, rows_tot, RB):
        regions.append((NCTX, rows_tot, 64, r0, RB, 0, rows_tot))
    if t_lo == 0:
        for r0 in range(0, NCTX, 64):
            regions.append((0, NCTX, 1, r0, 64, rows_tot + 64, None))
    bufA = {64: [P.sbuf(f"pl_A{i}", [128, RB + 16, 80]) for i in range(3)], 1: [P.sbuf(f"pl_B{i}", [128, 80, 1]) for i in range(3)]}
    dif = {64: P.sbuf("pl_d64", [128, RB, 64]), 1: P.sbuf("pl_d1", [128, 64, 1])}
    for gi, w in enumerate((2, 4, 8, 16)):
        zr = slice(1024 + gi * 128, 1024 + (gi + 1) * 128)
        for (tok0, Rtot, C, r0, nr, ico, icc) in regions:
            A = bufA[C]
            An = [f"pl_{'A' if C == 64 else 'B'}{i}" for i in range(3)]
            CP = C + 16 if C == 64 else 1
            c0 = 8 if C == 64 else 0
            U, Us = A[0], An[0]
            P.op('pool', 'memset', reads=[Us], writes=[Us], ap=U[:], constant=0.0)
            ra = max(r0 - 8, 0); rb = min(r0 + nr + 8, Rtot)
            P.dma('sp', U[:, 8 + ra - r0:8 + rb - r0, c0:c0 + C], zT[zr, tok0 + ra * C:tok0 + rb * C].rearrange("p (r c) -> p r c", c=C), reads=[('zT',), Us], writes=[Us])
            NR = nr + 16
            cur, cur_s = U, Us
            idx = 0
            s = 1
            while s < w:
                nidx = 1 if idx != 1 else 2
                nxt, nxt_s = A[nidx], An[nidx]
                P.op('dve', 'tensor_tensor', reads=[cur_s, nxt_s], writes=[nxt_s], out=nxt[:, 0:NR - s, :], in0=cur[:, 0:NR - s, :], in1=cur[:, s:NR, :], op=ALU.add)
                cur, cur_s, idx = nxt, nxt_s, nidx
                s *= 2
            nidx = 1 if idx != 1 else 2
            rm, rm_s = A[nidx], An[nidx]
            P.op('dve', 'tensor_tensor', reads=[cur_s, 'pl_icn', rm_s], writes=[rm_s], out=rm[:, 0:nr, :], in0=cur[:, 8 - w // 2:8 - w // 2 + nr, :],
                 in1=icn[:, gi, ico + r0:ico + r0 + nr].unsqueeze(2).to_broadcast([128, nr, CP]), op=ALU.mult)
            cur, cur_s, idx = rm, rm_s, nidx
            if C == 64:
                s = 1
                while s < w:
                    nidx = [i for i in (1, 2) if i != idx][0] if idx != 0 else 1
                    nxt, nxt_s = A[nidx], An[nidx]
                    P.op('dve', 'tensor_tensor', reads=[cur_s, nxt_s], writes=[nxt_s], out=nxt[:, 0:nr, 0:CP - s], in0=cur[:, 0:nr, 0:CP - s], in1=cur[:, 0:nr, s:CP], op=ALU.add)
                    cur, cur_s, idx = nxt, nxt_s, nidx
                    s *= 2
                nidx = [i for i in (1, 2) if i != idx][0]
                cm, cm_s = A[nidx], An[nidx]
                P.op('dve', 'tensor_tensor', reads=[cur_s, 'pl_icn', cm_s], writes=[cm_s], out=cm[:, 0:nr, 0:64], in0=cur[:, 0:nr, 8 - w // 2:8 - w // 2 + 64],
                     in1=icn[:, gi, icc:icc + 64].unsqueeze(1).to_broadcast([128, nr, 64]), op=ALU.mult)
                cur, cur_s = cm, cm_s
            df, df_s = dif[C], ("pl_d", C)
            P.op('dve', 'tensor_tensor', reads=[cur_s, Us, df_s], writes=[df_s], out=df[:, 0:nr, :], in0=cur[:, 0:nr, 0:C], in1=U[:, 8:8 + nr, c0:c0 + C], op=ALU.subtract)
            ntok = nr * C
            dflat = df[:, 0:nr, :].rearrange("p r c -> p (r c)")
            for q0 in range(0, ntok, 512):
                qn = min(512, ntok - q0)
                ps, ps_s = ps_r.get()
                P.op('pe', 'matmul', reads=[df_s, 'pl_wp'], writes=[ps_s], out=ps[:, 0:qn], lhsT=wp[:, gi, :], rhs=dflat[:, q0:q0 + qn], start=True, stop=True)
                tk = tok0 + r0 * C + q0
                gt, gt_s = g_r.get()
                P.dma('sp', gt[:, 0:qn], zT[1536 + gi * 128:1536 + (gi + 1) * 128, tk:tk + qn], reads=[('zT',)], writes=[gt_s])
                P.op('act', 'activation', reads=[gt_s], writes=[gt_s], out=gt[:, 0:qn], in_=gt[:, 0:qn], func=AF.Silu)
                ob, ob_s = ob_r.get()
                P.op('dve', 'scalar_tensor_tensor', reads=[ps_s, 'vec', gt_s], writes=[ob_s], out=ob[:, 0:qn], in0=ps[:, 0:qn], scalar=vec[:, VO["pool_scale"] + gi:VO["pool_scale"] + gi + 1], in1=gt[:, 0:qn], op0=ALU.mult, op1=ALU.mult)
                P.dma('sp', ymT[512 + gi * 128:512 + (gi + 1) * 128, tk:tk + qn], ob[:, 0:qn], reads=[ob_s], writes=[('ymT',)])
    P.phase_end()


class RowSplit:
    def __init__(self, aps, rows_per):
        self.aps = aps
        self.rp = rows_per

    def __getitem__(self, key):
        if isinstance(key, tuple):
            rs, cs = key
        else:
            rs, cs = key, None
        b = rs.start // self.rp
        assert (rs.stop - 1) // self.rp == b
        ap = self.aps[b][rs.start - b * self.rp:rs.stop - b * self.rp]
        return ap if cs is None else ap[:, cs]

def build(N, nlayers=2, debug=False):
    T = NCTX + N
    nc = bass.Bass("TRN2", target_bir_lowering=False)
    io = {}

    def din(name, shape):
        io[name] = nc.dram_tensor(name, list(shape), F32, kind="ExternalInput").ap()

    def dscr(name, shape):
        return nc.dram_tensor(name, list(shape), F32, kind=("ExternalOutput" if debug else "Internal")).ap()
    rows_tot = N // 64
    din("xT", [D, T]); din("ccT", [128, KC, 2]); din("cst", [128, 1408]); din("icnt", [128, 4, rows_tot + 64 + NCTX])
    for l in range(2):
        din(f"w_ada{l}", [D, 6144]); din(f"w_in{l}", [D, DIN]); din(f"w_out{l}", [D, D]); din(f"vec{l}", [128, VW])
        din(f"s5p{l}", [128, 3, 64]); din(f"s5r{l}", [16, 3, 4096]); din(f"s5bt{l}", [16, 2, 2048])
        din(f"s5ca{l}", [128, 64, 16]); din(f"s5cb{l}", [128, 64, 16]); din(f"w_glu{l}", [512, 512])
        din(f"w_pool{l}", [128, 4, 128]); din(f"rw_w2{l}", [128, 1024]); din(f"rw_a2{l}", [128, 1024])
    yT = nc.dram_tensor("yT", [D, N], F32, kind="ExternalOutput").ap()
    zT = RowSplit([dscr(f"zT{i}", [min(1024, DIN - i * 1024), T]) for i in range(7)], 1024); ymT = nc.dram_tensor("ymT", [D, T], BF16, kind="Internal").ap(); x1T = dscr("x1T", [D, T])
    wbf = {"in": nc.dram_tensor("winb", [D, DIN], BF16, kind="Internal").ap(), "out": nc.dram_tensor("woutb", [D, D], BF16, kind="Internal").ap()}
    S = {"r": dscr("S_r", [1024, T]), "kk": dscr("S_kk", [1024, T]), "v": dscr("S_v", [1024, T]), "bonus": dscr("S_bonus", [1024, T]),
         "lw": [dscr(f"S_lw{d}", [1024, T]) for d in range(2)], "ka": [dscr(f"S_ka{d}", [1024, T]) for d in range(2)],
         "kd": [dscr(f"S_kd{d}", [1024, T]) for d in range(2)], "o": [dscr(f"S_o{d}", [1024, T]) for d in range(2)],
         "ys": [dscr(f"S_ys{d}", [512, T]) for d in range(2)]}
    P = Prog(nc)
    cst = P.sbuf("rwc", [128, 1408])
    P.dma('sp', cst[:], io["cst"], writes=['rwc'])
    onesD = P.sbuf("onesD", [128, 128])
    P.op('pool', 'memset', writes=['onesD'], ap=onesD[:], constant=1.0 / D)
    mod = P.sbuf("mod", [128, 48, 2])
    vec = P.sbuf("vec", [128, VW])
    for l in range(nlayers):
        t_lo = 0 if l == 0 else NCTX
        P.barrier()
        P.dma('sp', vec[:], io[f"vec{l}"], reads=['vec'], writes=['vec'])
        xin = io["xT"] if l == 0 else x1T
        xout = x1T if l == 0 else yT
        SK = []
        phase_mod(P, l, io, mod, vec, wbf)
        if "inproj" not in SK:
            phase_inproj(P, l, io, mod, xin, zT, T, onesD, wbf)
        if "s5" not in SK:
            phase_s5(P, l, io, vec, zT, T, S, ymT, t_lo, cst)
        if "pool" not in SK:
            phase_pool(P, l, io, vec, zT, T, ymT, t_lo)
        if "prep" not in SK:
            phase_rwkv_prep(P, l, io, vec, zT, T, S)
        if "core" not in SK:
            P.phase_begin()
            rwkv_core(P, cst, 16, [(0, NCTX // L), (NCTX, N // L)], S["r"], S["kk"], S["v"], S["lw"], S["ka"], S["kd"], S["o"])
            P.phase_end()
        if "post" not in SK:
            phase_rwkv_post(P, l, vec, zT, T, S, ymT, t_lo)
        phase_outproj(P, l, io, mod, vec, xin, ymT, xout, T, onesD, t_lo, wbf)
    P.finish([('x1T', nlayers - 1)])
    return nc


def _pm(v):
    v = np.asarray(v, np.float32).reshape(-1)
    return np.ascontiguousarray(v.reshape(-1, 128).T)


def host_layout(inp, b, N):
    f32 = np.float32
    A = lambda a: np.ascontiguousarray(np.asarray(a), dtype=f32)
    m = {}
    m["xT"] = A(np.concatenate([np.asarray(inp["ctx"][b]).T, np.asarray(inp["x"][b, :N]).T], axis=1))
    cc = np.stack([np.asarray(inp["c"][b]), np.asarray(inp["c_ctx"])], 0)
    m["ccT"] = A(cc.reshape(2, KC, 128).transpose(2, 1, 0))
    i = np.arange(128)[:, None]; j = np.arange(128)[None, :]
    Jc = np.zeros((128, 128), f32)
    for p in range(64):
        Jc[64 + p, p] = -1.0
        Jc[p, 64 + p] = 1.0
    iota = np.tile(np.arange(1, 513, dtype=f32)[None, :], (128, 1))
    m["cst"] = A(np.concatenate([np.eye(128), (i < j), (i <= j), (i > j), (i >= j), np.ones((128, 128)), Jc, iota], 1))
    rows_tot = N // 64

    def invcnt(n, w):
        idx = np.arange(n)
        lo = np.clip(idx - w // 2, 0, n - 1); hi = np.clip(idx - w // 2 + w - 1, 0, n - 1)
        return (1.0 / (hi - lo + 1)).astype(f32)
    ic = np.stack([np.concatenate([invcnt(rows_tot, w), invcnt(64, w), invcnt(NCTX, w)]) for w in (2, 4, 8, 16)], 0)
    m["icnt"] = A(np.tile(ic[None], (128, 1, 1)))
    for l in range(2):
        g = lambda k: np.asarray(inp[k][l])
        m[f"w_ada{l}"] = A(g("w_ada")); m[f"w_in{l}"] = A(g("w_in")); m[f"w_out{l}"] = A(g("w_out"))
        w0 = g("rwkv_w0").reshape(2, 8, 128).transpose(2, 0, 1).reshape(128, 16)
        a0 = g("rwkv_a0").reshape(2, 8, 128).transpose(2, 0, 1).reshape(128, 16)
        conv = g("conv_rkv").reshape(3, 24, 128).transpose(2, 1, 0).reshape(128, 72)
        parts = [_pm(g("b_ada")), _pm(g("s5_d")), _pm(g("b_glu")), _pm(g("pool_scale")), _pm(g("rwkv_k_k")), _pm(g("rwkv_k_a")),
                 _pm(g("rwkv_r_k")), _pm(g("gn_w")), _pm(g("gn_b")), _pm(g("ln_g")), _pm(g("ln_b")), w0, a0, conv]
        m[f"vec{l}"] = A(np.concatenate(parts, 1))
        lr = g("s5_lam_re").reshape(64, 64).T; li = g("s5_lam_im").reshape(64, 64).T
        ls = np.tile(g("s5_log_step").reshape(1, 64), (64, 1))
        sp = np.stack([lr, li, ls], 1)
        m[f"s5p{l}"] = A(np.concatenate([sp, sp], 0))
        row = np.stack([g("s5_lam_re").reshape(4096), g("s5_lam_im").reshape(4096), np.repeat(g("s5_log_step").reshape(64), 64)], 0)
        m[f"s5r{l}"] = A(np.tile(row[None], (16, 1, 1)))
        m[f"s5bt{l}"] = A(np.stack([g("s5_b_re").transpose(2, 0, 1).reshape(16, 2048), g("s5_b_im").transpose(2, 0, 1).reshape(16, 2048)], 1))
        cr = g("s5_c_re").transpose(3, 0, 1, 2).reshape(64, 64, 16); ci = g("s5_c_im").transpose(3, 0, 1, 2).reshape(64, 64, 16)
        m[f"s5ca{l}"] = A(np.concatenate([cr, ci], 0)); m[f"s5cb{l}"] = A(np.concatenate([ci, cr], 0))
        m[f"w_glu{l}"] = A(g("w_glu")); m[f"w_pool{l}"] = A(g("w_pool").transpose(1, 0, 2))
        m[f"rw_w2{l}"] = A(g("rwkv_w2").reshape(128, 1024)); m[f"rw_a2{l}"] = A(g("rwkv_a2").reshape(128, 1024))
    return m


_NC_CACHE = {}


def run(inputs, N, ncores=8, nlayers=2, debug=False, trace=False):
    key = (N, nlayers, debug)
    if key not in _NC_CACHE:
        _NC_CACHE[key] = build(N, nlayers, debug)
    nc = _NC_CACHE[key]
    maps = [host_layout(inputs, b, N) for b in range(2)]
    in_maps = [maps[c % 2] for c in range(ncores)]
    if trace:
        res = run_bass_kernel_spmd(nc, in_maps, core_ids=list(range(ncores)), trace=True)
        print("EXEC_TIME_NS", res.exec_time_ns)
    else:
        res = run_bass_kernel_spmd(nc, in_maps, core_ids=list(range(ncores)))
    if debug:
        return res
    out = np.stack([np.ascontiguousarray(res.results[b]["yT"].T) for b in range(2)], 0)
    return out.astype(np.float32)


def kernel(**inputs):
    return run(inputs, 16384, ncores=2)
```

```python
import numpy as np
from contextlib import ExitStack
import concourse.bass as bass
import concourse.mybir as mybir
from concourse.bass_utils import run_bass_kernel_spmd

F32 = mybir.dt.float32
BF16 = mybir.dt.bfloat16
AF = mybir.ActivationFunctionType
ALU = mybir.AluOpType
AX = mybir.AxisListType

SEM_CHUNK = 20000
DMA_K = 8
DMA_CHUNK = 1000


class Prog:
    ENG = ('pe', 'dve', 'act', 'pool', 'sp')

    def __init__(self, nc):
        self.nc = nc
        self.st = ExitStack()
        self.sem_st = ExitStack()
        self.ops = {e: [] for e in self.ENG}
        self.n = {e: 0 for e in self.ENG}
        self.sems = {e: [] for e in self.ENG}
        self.waited_c = {e: {x: 0 for x in self.ENG} for e in self.ENG}
        self.waited_d = {e: {} for e in self.ENG}
        self.lastw = {}
        self.readers = {}
        self.dma_n = {e: 0 for e in self.ENG}
        self.dma_sems = {e: {} for e in self.ENG}
        self.nsem = 0
        self.ninst = 0

    def sbuf(self, name, shape, dt=F32):
        self.ninst += 0
        self._uid = getattr(self, '_uid', 0) + 1
        return self.st.enter_context(self.nc.sbuf_tensor(f"{name}_u{self._uid}", list(shape), dt))

    def psum(self, name, shape, dt=F32):
        self._uid = getattr(self, '_uid', 0) + 1
        return self.st.enter_context(self.nc.psum_tensor(f"{name}_u{self._uid}", list(shape), dt))

    def _newsem(self, name):
        self.nsem += 1
        return self.sem_st.enter_context(self.nc.semaphore(name))

    def _csem(self, eng, n):
        idx = (n - 1) // SEM_CHUNK
        while len(self.sems[eng]) <= idx:
            self.sems[eng].append(self._newsem(f"s_{eng}_{len(self.sems[eng])}"))
        return self.sems[eng][idx], (n - 1) % SEM_CHUNK + 1

    def _dsem(self, q, j):
        slot = j % DMA_K
        cnt = j // DMA_K
        key = (slot, cnt // DMA_CHUNK)
        if key not in self.dma_sems[q]:
            self.dma_sems[q][key] = self._newsem(f"d_{q}_{slot}_{cnt // DMA_CHUNK}")
        return self.dma_sems[q][key], 16 * (cnt % DMA_CHUNK + 1), key

    def _wait(self, eng, tok):
        if tok is None:
            return
        if tok[0] == 'c':
            _, e2, n = tok
            if e2 == eng and eng == 'pe':
                return
            if self.waited_c[eng][e2] >= n:
                return
            self.waited_c[eng][e2] = n
            sem, val = self._csem(e2, n)
        else:
            _, q, j = tok
            sem, val, key = self._dsem(q, j)
            k2 = (q, key)
            if self.waited_d[eng].get(k2, 0) >= val:
                return
            self.waited_d[eng][k2] = val
        self.ops[eng].append(lambda e, sem=sem, val=val: e.wait_ge(sem, val))

    def _deps(self, eng, reads, writes):
        toks = []
        for r in reads:
            if r in self.lastw:
                toks.append(self.lastw[r])
        for w in writes:
            if w in self.lastw:
                toks.append(self.lastw[w])
            for t in self.readers.get(w, {}).values():
                toks.append(t)
        for t in toks:
            self._wait(eng, t)

    def _commit(self, tok, reads, writes):
        for w in writes:
            self.lastw[w] = tok
            self.readers[w] = {}
        for r in reads:
            if r in writes:
                continue
            d = self.readers.setdefault(r, {})
            if tok[0] == 'c':
                d[('c', tok[1])] = tok
            else:
                q, j = tok[1], tok[2]
                d[('d', q, j % DMA_K)] = tok

    def op(self, eng, name, reads=(), writes=(), **kw):
        fn = (lambda e, name=name, kw=kw: getattr(e, name)(**kw))
        self._deps(eng, reads, writes)
        self.n[eng] += 1
        n = self.n[eng]
        sem, _ = self._csem(eng, n)
        self.ops[eng].append(lambda e, fn=fn, sem=sem: fn(e).then_inc(sem, 1))
        self._commit(('c', eng, n), reads, writes)
        self.ninst += 1

    def dma(self, q, out, in_, reads=(), writes=(), **kw):
        self.dmaop(q, lambda e, out=out, in_=in_, kw=kw: e.dma_start(out=out, in_=in_, **kw), reads, writes)

    def dmaop(self, q, fn, reads=(), writes=()):
        self._deps(q, reads, writes)
        j = self.dma_n[q]
        self.dma_n[q] += 1
        if j >= DMA_K:
            self._wait(q, ('d', q, j - DMA_K))
        sem, _, _ = self._dsem(q, j)
        self.ops[q].append(lambda e, fn=fn, sem=sem: fn(e).then_inc(sem, 16))
        self._commit(('d', q, j), reads, writes)
        self.ninst += 1

    def barrier(self):
        toks = [('c', e, self.n[e]) for e in self.ENG if self.n[e] > 0]
        for q in self.ENG:
            for j in range(max(0, self.dma_n[q] - DMA_K), self.dma_n[q]):
                toks.append(('d', q, j))
        for e in self.ENG:
            for t in toks:
                self._wait(e, t)

    def phase_begin(self):
        if not hasattr(self, '_stk'):
            self._stk = []
        self._stk.append(self.st)
        self.st = ExitStack()

    def phase_end(self):
        self.barrier()
        self.st.close()
        self.st = self._stk.pop()

    def finish(self, final_res):
        for r in final_res:
            self._wait('sp', self.lastw[r])
        nc = self.nc
        with nc.Block() as block:
            @block.sync
            def _(e):
                for f in self.ops['sp']:
                    f(e)

            @block.tensor
            def _(e):
                for f in self.ops['pe']:
                    f(e)

            @block.vector
            def _(e):
                for f in self.ops['dve']:
                    f(e)

            @block.scalar
            def _(e):
                for f in self.ops['act']:
                    f(e)

            @block.gpsimd
            def _(e):
                for f in self.ops['pool']:
                    f(e)
        self.st.close()


L = 128


class Rot:
    def __init__(self, P, name, shape, n, psum=False, dt=F32):
        self.bufs = []
        for i in range(n):
            t = P.psum(f"{name}{i}", shape, dt) if psum else P.sbuf(f"{name}{i}", shape, dt)
            self.bufs.append((t, (name, i)))
        self.i = 0

    def get(self):
        b = self.bufs[self.i % len(self.bufs)]
        self.i += 1
        return b


def rwkv_consts(P, cst):
    c = P.sbuf("rwc", [128, 6 * 128])
    P.dma('sp', c[:], cst, writes=['rwc'])
    return c


def rwkv_core(P, c, H, segs, d_r, d_kk, d_v, d_lw, d_ka, d_kd, d_o, HB=2, G=4):
    ident = c[:, 0:128]
    ones = c[:, 5 * 128:6 * 128]
    combT = [c[:, 128:384], c[:, 384:640]]
    Mst = [c[:, 384:512], c[:, 128:256]]
    nb = H // HB
    Tst = {}
    for d in range(2):
        for b in range(nb):
            Tst[d, b] = Rot(P, f"T{d}_{b}_", [64, HB, 64], 2)
    B = []
    for sl in range(G):
        q = {}
        for k in ("r", "kk", "v", "lw", "ka", "kd"):
            q["in_" + k] = Rot(P, f"in{sl}_{k}", [64, HB, L], 2)
        for k in ("cs", "g1", "gi", "g0", "QT", "OL", "O"):
            q[k] = Rot(P, f"{k}{sl}_", [64, HB, L], 1)
        for k in ("nKa", "Kd"):
            q[k] = Rot(P, f"{k}{sl}_", [64, HB, L], 1, dt=BF16)
        q["KR"] = Rot(P, f"KR{sl}_", [64, HB, 2 * L], 1, dt=BF16)
        q["A1"] = Rot(P, f"A1{sl}_", [128, HB, 2 * L], 1, dt=BF16)
        q["A2"] = Rot(P, f"A2{sl}_", [128, HB, 2 * L], 1, dt=BF16)
        q["PT"] = [Rot(P, f"PT{j}{sl}_", [128, HB, L], 1, dt=BF16) for j in range(1, 7)]
        q["Pn"] = Rot(P, f"Pn{sl}_", [128, HB, L], 2, dt=BF16)
        q["TOK"] = Rot(P, f"TOK{sl}_", [128, HB, 256], 1, dt=BF16)
        q["X"] = Rot(P, f"X{sl}_", [128, HB, 128], 2, dt=BF16)
        q["vb"] = Rot(P, f"vb{sl}_", [64, HB, L], 1, dt=BF16)
        q["Rf"] = Rot(P, f"Rf{sl}_", [64, HB, L], 1)
        q["GT"] = Rot(P, f"GT{sl}_", [64, HB, 64], 1)
        q["Hg"] = Rot(P, f"Hg{sl}_", [64, HB, 64], 1)
        B.append(q)
    ps_r = Rot(P, "ps", [128, 512], 6, psum=True)
    psb_r = Rot(P, "psb", [128, 1024], 2, psum=True, dt=BF16)
    identb = P.sbuf("identb", [64, 64], BF16)
    P.op('pool', 'tensor_copy', reads=['rwc'], writes=['identb'], out=identb[:], in_=ident[0:64, 0:64])

    def mm(out, lhsT, rhs, reads, wres, start=True, stop=True):
        P.op('pe', 'matmul', reads=reads, writes=[wres], out=out, lhsT=lhsT, rhs=rhs, start=start, stop=stop)

    evac_flip = [0]

    def evac(out, in_, reads, wres):
        evac_flip[0] ^= 1
        if evac_flip[0]:
            P.op('act', 'activation', reads=reads, writes=[wres], out=out, in_=in_, func=AF.Copy)
        else:
            P.op('dve', 'tensor_copy', reads=reads, writes=[wres], out=out, in_=in_)

    def h3(ap, w):
        return ap.rearrange("p (h t) -> p h t", h=HB)

    def item(d, t0, first, b, q):
        rows = slice(b * HB * 64, (b + 1) * HB * 64)

        def src(ap):
            return ap[rows, t0:t0 + L].rearrange("(h p) t -> p h t", p=64)
        tl = {}
        for k, ap in (("r", d_r), ("kk", d_kk), ("v", d_v), ("lw", d_lw[d]), ("ka", d_ka[d]), ("kd", d_kd[d])):
            t, res = q["in_" + k].get()
            P.dma('sp', t[:], src(ap), writes=[res])
            tl[k] = (t, res)
        (r_t, r_s), (kk_t, kk_s), (v_t, v_s) = tl["r"], tl["kk"], tl["v"]
        (lw_t, lw_s), (ka_t, ka_s), (kd_t, kd_s) = tl["lw"], tl["ka"], tl["kd"]
        cs, cs_s = q["cs"].get(); g1, g1_s = q["g1"].get(); gi, gi_s = q["gi"].get(); g0, g0_s = q["g0"].get()
        KR, KR_s = q["KR"].get(); nKa, nKa_s = q["nKa"].get(); Kd, Kd_s = q["Kd"].get()
        yield
        for h in range(HB):
            if d == 0:
                P.op('dve', 'tensor_tensor_scan', reads=[lw_s, 'rwc'], writes=[cs_s], out=cs[:, h, :], data0=ones[0:64, :], data1=lw_t[:, h, :], initial=0.0, op0=ALU.mult, op1=ALU.add)
            else:
                P.op('dve', 'tensor_tensor_scan', reads=[lw_s, 'rwc'], writes=[cs_s], out=cs[:, h, ::-1], data0=ones[0:64, :], data1=lw_t[:, h, ::-1], initial=0.0, op0=ALU.mult, op1=ALU.add)
        yield
        P.op('act', 'activation', reads=[cs_s], writes=[g1_s], out=g1[:], in_=cs[:], func=AF.Exp)
        P.op('act', 'activation', reads=[cs_s], writes=[gi_s], out=gi[:], in_=cs[:], func=AF.Exp, scale=-1.0)
        P.op('pool', 'tensor_tensor', reads=[cs_s, lw_s], writes=[g0_s], out=g0[:], in0=cs[:], in1=lw_t[:], op=ALU.subtract)
        yield
        P.op('act', 'activation', reads=[g0_s], writes=[g0_s], out=g0[:], in_=g0[:], func=AF.Exp)
        Rf, Rf_s = q["Rf"].get()
        vb, vb_s = q["vb"].get()
        P.op('pool', 'tensor_tensor', reads=[r_s, g1_s], writes=[Rf_s], out=Rf[:], in0=r_t[:], in1=g1[:], op=ALU.mult)
        P.op('pool', 'tensor_copy', reads=[Rf_s, KR_s], writes=[KR_s], out=KR[:, :, L:2 * L], in_=Rf[:])
        P.op('pool', 'tensor_copy', reads=[v_s], writes=[vb_s], out=vb[:], in_=v_t[:])
        P.op('dve', 'scalar_tensor_tensor', reads=[ka_s, gi_s], writes=[nKa_s], out=nKa[:], in0=ka_t[:], scalar=-1.0, in1=gi[:], op0=ALU.mult, op1=ALU.mult)
        P.op('pool', 'tensor_tensor', reads=[kd_s, gi_s], writes=[Kd_s], out=Kd[:], in0=kd_t[:], in1=gi[:], op=ALU.mult)
        yield
        P.op('dve', 'tensor_tensor', reads=[kk_s, g0_s, KR_s], writes=[KR_s], out=KR[:, :, 0:L], in0=kk_t[:], in1=g0[:], op=ALU.mult)
        yield
        A1, A1_s = q["A1"].get()
        A2, A2_s = q["A2"].get()
        for (Adst, Ares, lh, lres) in ((A1, A1_s, nKa, nKa_s), (A2, A2_s, Kd, Kd_s)):
            for h in range(HB):
                ps, ps_s = ps_r.get()
                mm(ps[:, 0:2 * L], lh[:, h, :], KR[:, h, :], [lres, KR_s], ps_s)
                P.op('dve', 'tensor_tensor', reads=[ps_s, 'rwc', Ares], writes=[Ares], out=Adst[:, h, :], in0=ps[:, 0:2 * L], in1=combT[d], op=ALU.mult)
        Pc, Pc_s = q["Pn"].get()
        ps, ps_s = ps_r.get()
        for h in range(HB):
            mm(ps[:, h * L:(h + 1) * L], KR[:, h, 0:L], nKa[:, h, :], [KR_s, nKa_s, ps_s], ps_s)
        P.op('dve', 'tensor_tensor', reads=[ps_s, 'rwc'], writes=[Pc_s], out=Pc[:], in0=h3(ps[:, 0:HB * L], L), in1=Mst[d].unsqueeze(1).to_broadcast([128, HB, L]), op=ALU.mult)
        yield
        TOK, TOK_s = q["TOK"].get()
        ps, ps_s = psb_r.get()
        for h in range(HB):
            for qi, (srcT, sres) in enumerate(((KR, KR_s), (vb, vb_s), (Kd, Kd_s), (nKa, nKa_s))):
                P.op('pe', 'transpose', reads=[sres, 'identb', ps_s], writes=[ps_s], out=ps[:, h * 256 + qi * 64:h * 256 + (qi + 1) * 64], in_=srcT[:, h, 0:L], identity=identb[:])
        evac(TOK[:], h3(ps[:, 0:HB * 256], 256), [ps_s], TOK_s)
        yield
        Xa, Xa_s = q["X"].get()
        ps, ps_s = ps_r.get()
        for h in range(HB):
            mm(ps[:, h * 64:(h + 1) * 64], A2[:, h, 0:L], TOK[:, h, 64:128], [A2_s, TOK_s, ps_s], ps_s)
        P.op('act', 'activation', reads=[ps_s, Xa_s], writes=[Xa_s], out=Xa[:, :, 64:128], in_=h3(ps[:, 0:HB * 64], 64), func=AF.Copy)
        P.op('pool', 'tensor_copy', reads=[TOK_s, Xa_s], writes=[Xa_s], out=Xa[:, :, 0:64], in_=TOK[:, :, 0:64])
        yield
        X, X_s = Xa, Xa_s
        PTp, PTp_s = A1, A1_s
        PTp_f = (lambda h, A1=A1: A1[:, h, 0:L])
        for j in range(7):
            Xn, Xn_s = q["X"].get()
            ps, ps_s = ps_r.get()
            for h in range(HB):
                mm(ps[:, h * 128:(h + 1) * 128], PTp_f(h), X[:, h, :], [PTp_s, X_s, ps_s], ps_s)
            P.op('dve', 'tensor_tensor', reads=[ps_s, X_s, Xn_s], writes=[Xn_s], out=Xn[:], in0=h3(ps[:, 0:HB * 128], 128), in1=X[:], op=ALU.add)
            X, X_s = Xn, Xn_s
            if j < 6:
                PTn, PTn_s = q["PT"][j].get()
                ps, ps_s = ps_r.get()
                for h in range(HB):
                    mm(ps[:, h * L:(h + 1) * L], Pc[:, h, :], PTp_f(h), [Pc_s, PTp_s, ps_s], ps_s)
                evac(PTn[:], h3(ps[:, 0:HB * L], L), [ps_s], PTn_s)
                if j < 5:
                    Pn, Pn_s = q["Pn"].get()
                    ps2, ps2_s = ps_r.get()
                    for h in range(HB):
                        mm(ps2[:, h * L:(h + 1) * L], PTp_f(h), Pc[:, h, :], [Pc_s, PTp_s, ps2_s], ps2_s)
                    evac(Pn[:], h3(ps2[:, 0:HB * L], L), [ps2_s], Pn_s)
                    Pc, Pc_s = Pn, Pn_s
                PTp, PTp_s = PTn, PTn_s
                PTp_f = (lambda h, PTn=PTn: PTn[:, h, :])
            yield
        GT, GT_s = q["GT"].get()
        Hg, Hg_s = q["Hg"].get()
        ps, ps_s = ps_r.get()
        for h in range(HB):
            mm(ps[0:64, h * 64:(h + 1) * 64], X[:, h, 0:64], TOK[:, h, 192:256], [X_s, TOK_s, ps_s], ps_s)
        P.op('dve', 'tensor_tensor', reads=[ps_s, 'rwc'], writes=[GT_s], out=GT[:], in0=h3(ps[0:64, 0:HB * 64], 64), in1=ident[0:64, 0:64].unsqueeze(1).to_broadcast([64, HB, 64]), op=ALU.add)
        ps, ps_s = ps_r.get()
        for h in range(HB):
            mm(ps[0:64, h * 64:(h + 1) * 64], TOK[:, h, 128:192], TOK[:, h, 64:128], [TOK_s, ps_s], ps_s, start=True, stop=False)
            mm(ps[0:64, h * 64:(h + 1) * 64], TOK[:, h, 192:256], X[:, h, 64:128], [TOK_s, X_s, ps_s], ps_s, start=False, stop=True)
        gl = (L - 1) if d == 0 else 0
        for h in range(HB):
            P.op('dve', 'tensor_scalar', reads=[ps_s, g1_s, Hg_s], writes=[Hg_s], out=Hg[:, h, :], in0=ps[0:64, h * 64:(h + 1) * 64], scalar1=g1[:, h, gl:gl + 1], scalar2=None, op0=ALU.mult)
        yield
        QT, QT_s = q["QT"].get()
        OL, OL_s = q["OL"].get()
        ps, ps_s = ps_r.get()
        for h in range(HB):
            mm(ps[0:64, h * L:(h + 1) * L], X[:, h, 0:64], A1[:, h, L:2 * L], [X_s, A1_s, ps_s], ps_s)
        P.op('dve', 'tensor_tensor', reads=[ps_s, Rf_s], writes=[QT_s], out=QT[:], in0=h3(ps[0:64, 0:HB * L], L), in1=Rf[:], op=ALU.add)
        ps, ps_s = ps_r.get()
        for h in range(HB):
            mm(ps[0:64, h * L:(h + 1) * L], TOK[:, h, 64:128], A2[:, h, L:2 * L], [TOK_s, A2_s, ps_s], ps_s, start=True, stop=False)
            mm(ps[0:64, h * L:(h + 1) * L], X[:, h, 64:128], A1[:, h, L:2 * L], [X_s, A1_s, ps_s], ps_s, start=False, stop=True)
        evac(OL[:], h3(ps[0:64, 0:HB * L], L), [ps_s], OL_s)
        yield
        O, O_s = q["O"].get()
        dst = d_o[d][rows, t0:t0 + L].rearrange("(h p) t -> p h t", p=64)
        if first:
            Tn, Tn_s = Tst[d, b].get()
            P.op('pool', 'tensor_copy', reads=[Hg_s], writes=[Tn_s], out=Tn[:], in_=Hg[:])
            P.dma('sp', dst, OL[:], reads=[OL_s], writes=[('d_o', d)])
            Tst[d, b].cur = (Tn, Tn_s)
        else:
            T0, T0_s = Tst[d, b].cur
            ps, ps_s = ps_r.get()
            for h in range(HB):
                mm(ps[0:64, h * L:(h + 1) * L], T0[:, h, :], QT[:, h, :], [T0_s, QT_s, ps_s], ps_s)
            P.op('dve', 'tensor_tensor', reads=[ps_s, OL_s], writes=[O_s], out=O[:], in0=h3(ps[0:64, 0:HB * L], L), in1=OL[:], op=ALU.add)
            P.dma('sp', dst, O[:], reads=[O_s], writes=[('d_o', d)])
            ps, ps_s = ps_r.get()
            for h in range(HB):
                mm(ps[0:64, h * 64:(h + 1) * 64], GT[:, h, :], T0[:, h, :], [GT_s, T0_s, ps_s], ps_s)
            Tn, Tn_s = Tst[d, b].get()
            for h in range(HB):
                P.op('dve', 'scalar_tensor_tensor', reads=[ps_s, g1_s, Hg_s, Tn_s], writes=[Tn_s], out=Tn[:, h, :], in0=ps[0:64, h * 64:(h + 1) * 64], scalar=g1[:, h, gl:gl + 1], in1=Hg[:, h, :], op0=ALU.mult, op1=ALU.add)
            Tst[d, b].cur = (Tn, Tn_s)
        yield

    nsteps = sum(n for _, n in segs)
    fwd = [(s0 + i * L) for (s0, n) in segs for i in range(n)]
    bwd = [(s0 + i * L) for (s0, n) in segs for i in reversed(range(n))]
    for i in range(nsteps):
        items = [(0, fwd[i], i == 0, b) for b in range(nb)] + [(1, bwd[i], i == 0, b) for b in range(nb)]
        for g0_ in range(0, len(items), G):
            gens = [item(*it, B[sl]) for sl, it in enumerate(items[g0_:g0_ + G])]
            live = list(gens)
            while live:
                nxt = []
                for gen in live:
                    try:
                        next(gen)
                        nxt.append(gen)
                    except StopIteration:
                        pass
                live = nxt

D = 2048
NCTX = 256
KC = 16
DIN = 6400
ALPHA = 4.0 ** 0.25
PI = 3.14159265358979
VO = {}
_o = 0
for _n, _w in (("b_ada", 48), ("d_skip", 4), ("b_glu", 4), ("pool_scale", 4), ("k_k", 8), ("k_a", 8), ("r_k", 8),
               ("gn_w", 8), ("gn_b", 8), ("ln_g", 16), ("ln_b", 16), ("w0", 16), ("a0", 16), ("conv", 72)):
    VO[_n] = _o
    _o += _w
VW = _o


def token_tiles(T):
    tiles = [(0, NCTX, 1)]
    t = NCTX
    while t < T:
        tiles.append((t, 512, 0))
        t += 512
    return tiles


def ln_stats(P, R, Rs, n, eps, onesD, sq, sq_s, ps_r, mean, mean_s, rstd, rstd_s, tmp, tmp_s):
    P.op('act', 'activation', reads=[Rs], writes=[sq_s], out=sq[:, :, 0:n], in_=R[:, :, 0:n], func=AF.Square)
    pm, pm_s = ps_r.get()
    for kc in range(KC):
        P.op('pe', 'matmul', reads=[Rs, 'onesD', pm_s], writes=[pm_s], out=pm[:, 0:n], lhsT=onesD[:], rhs=R[:, kc, 0:n], start=(kc == 0), stop=(kc == KC - 1))
    pq, pq_s = ps_r.get()
    for kc in range(KC):
        P.op('pe', 'matmul', reads=[sq_s, 'onesD', pq_s], writes=[pq_s], out=pq[:, 0:n], lhsT=onesD[:], rhs=sq[:, kc, 0:n], start=(kc == 0), stop=(kc == KC - 1))
    P.op('act', 'activation', reads=[pm_s], writes=[mean_s], out=mean[:, 0:n], in_=pm[:, 0:n], func=AF.Copy)
    P.op('dve', 'tensor_tensor', reads=[mean_s], writes=[tmp_s], out=tmp[:, 0:n], in0=mean[:, 0:n], in1=mean[:, 0:n], op=ALU.mult)
    P.op('dve', 'tensor_tensor', reads=[pq_s, tmp_s], writes=[tmp_s], out=tmp[:, 0:n], in0=pq[:, 0:n], in1=tmp[:, 0:n], op=ALU.subtract)
    P.op('dve', 'tensor_scalar', reads=[tmp_s], writes=[tmp_s], out=tmp[:, 0:n], in0=tmp[:, 0:n], scalar1=float(eps), scalar2=None, op0=ALU.add)
    P.op('act', 'activation', reads=[tmp_s], writes=[tmp_s], out=tmp[:, 0:n], in_=tmp[:, 0:n], func=AF.Sqrt)
    P.op('dve', 'reciprocal', reads=[tmp_s], writes=[rstd_s], out=rstd[:, 0:n], in_=tmp[:, 0:n])


def proj(P, Wd, M, H, Hs, n, Wt_r, ps_r, consume):
    m0 = 0
    while m0 < M:
        w = min(512, M - m0)
        Wt, Wt_s = Wt_r.get()
        P.dma('pool', Wt[:, :, 0:w], Wd[:, m0:m0 + w].rearrange("(kc p) c -> p kc c", p=128), reads=[('wbf',)], writes=[Wt_s])
        for mi in range(w // 128):
            ps, ps_s = ps_r.get()
            for kc in range(KC):
                P.op('pe', 'matmul', reads=[Wt_s, Hs, ps_s], writes=[ps_s], out=ps[:, 0:n], lhsT=Wt[:, kc, mi * 128:(mi + 1) * 128], rhs=H[:, kc, 0:n], start=(kc == 0), stop=(kc == KC - 1))
            consume(m0 // 128 + mi, ps, ps_s)
        m0 += w


def phase_mod(P, l, io, mod, vec, wbf):
    P.phase_begin()
    sc = P.sbuf("m_sc", [128, KC, 2])
    ps_r = Rot(P, "m_ps", [128, 512], 2, psum=True)
    Wt_r = Rot(P, "m_W", [128, KC, 512], 2)
    P.dma('sp', sc[:], io["ccT"], writes=['m_sc'])
    P.op('act', 'activation', reads=['m_sc'], writes=['m_sc'], out=sc[:], in_=sc[:], func=AF.Silu)
    ms = ('mod', l)
    for mg in range(12):
        Wt, Wt_s = Wt_r.get()
        P.dma('pool', Wt[:], io[f"w_ada{l}"][:, mg * 512:(mg + 1) * 512].rearrange("(kc p) c -> p kc c", p=128), writes=[Wt_s])
        ps, ps_s = ps_r.get()
        for mi in range(4):
            for kc in range(KC):
                P.op('pe', 'matmul', reads=[Wt_s, 'm_sc', ps_s], writes=[ps_s], out=ps[:, mi * 2:mi * 2 + 2], lhsT=Wt[:, kc, mi * 128:(mi + 1) * 128], rhs=sc[:, kc, :], start=(kc == 0), stop=(kc == KC - 1))
        P.op('dve', 'tensor_tensor', reads=[ps_s, 'vec', ms], writes=[ms], out=mod[:, mg * 4:mg * 4 + 4, :], in0=ps[:, 0:8].rearrange("p (m j) -> p m j", j=2),
             in1=vec[:, VO["b_ada"] + mg * 4:VO["b_ada"] + mg * 4 + 4].unsqueeze(2).to_broadcast([128, 4, 2]), op=ALU.add)
    P.op('dve', 'tensor_scalar', reads=[ms], writes=[ms], out=mod[:, 16:32, :], in0=mod[:, 16:32, :], scalar1=1.0, scalar2=None, op0=ALU.add)
    Wb_r = Rot(P, "m_Wb", [128, KC, 512], 2, dt=BF16)
    ci = 0
    for (src, dst, M) in ((io[f"w_in{l}"], wbf["in"], DIN), (io[f"w_out{l}"], wbf["out"], D)):
        m0 = 0
        while m0 < M:
            w = min(512, M - m0)
            Wt, Wt_s = Wt_r.get()
            P.dma('pool', Wt[:, :, 0:w], src[:, m0:m0 + w].rearrange("(kc p) c -> p kc c", p=128), writes=[Wt_s])
            Wb, Wb_s = Wb_r.get()
            eng = ('act', 'pool', 'dve')[ci % 3]
            ci += 1
            if eng == 'act':
                P.op('act', 'activation', reads=[Wt_s], writes=[Wb_s], out=Wb[:, :, 0:w], in_=Wt[:, :, 0:w], func=AF.Copy)
            else:
                P.op(eng, 'tensor_copy', reads=[Wt_s], writes=[Wb_s], out=Wb[:, :, 0:w], in_=Wt[:, :, 0:w])
            P.dma('sp', dst[:, m0:m0 + w].rearrange("(kc p) c -> p kc c", p=128), Wb[:, :, 0:w], reads=[Wb_s], writes=[('wbf',)])
            m0 += w
    P.phase_end()


def phase_inproj(P, l, io, mod, xT, zT, T, onesD, wbf):
    P.phase_begin()
    ms = ('mod', l)
    X_r = Rot(P, "p1_X", [128, KC, 512], 2)
    sq = P.sbuf("p1_sq", [128, KC, 512])
    H_r = Rot(P, "p1_H", [128, KC, 512], 2, dt=BF16)
    mean = P.sbuf("p1_mean", [128, 512]); rstd = P.sbuf("p1_rstd", [128, 512]); tmp = P.sbuf("p1_tmp", [128, 512])
    st_r = Rot(P, "p1_st", [128, 512], 4)
    Wt_r = Rot(P, "p1_W", [128, KC, 512], 3, dt=BF16)
    ps_r = Rot(P, "p1_ps", [128, 512], 6, psum=True)
    for (t0, n, j) in token_tiles(T):
        X, Xs = X_r.get()
        P.dma('sp', X[:, :, 0:n], xT[:, t0:t0 + n].rearrange("(kc p) t -> p kc t", p=128), reads=[('xT',)], writes=[Xs])
        H, Hs = H_r.get()
        ln_stats(P, X, Xs, n, 1e-6, onesD, sq, 'p1_sq', ps_r, mean, 'p1_mean', rstd, 'p1_rstd', tmp, 'p1_tmp')
        for kc in range(KC):
            eng = 'dve' if kc % 2 == 0 else 'pool'
            xr = (Xs, kc)
            P.op(eng, 'tensor_tensor', reads=[Xs, 'p1_mean'], writes=[xr], out=X[:, kc, 0:n], in0=X[:, kc, 0:n], in1=mean[:, 0:n], op=ALU.subtract)
            P.op(eng, 'tensor_tensor', reads=[xr, 'p1_rstd'], writes=[xr], out=X[:, kc, 0:n], in0=X[:, kc, 0:n], in1=rstd[:, 0:n], op=ALU.mult)
            P.op('act', 'activation', reads=[xr, ms, Hs], writes=[Hs], out=H[:, kc, 0:n], in_=X[:, kc, 0:n], func=AF.Identity, bias=mod[:, kc, j:j + 1], scale=mod[:, 16 + kc, j:j + 1])
        for kc in range(KC):
            dd = P.readers.setdefault(Xs, {})
            dd[('xw', kc)] = P.lastw[(Xs, kc)]
            for k2, tok in P.readers.get((Xs, kc), {}).items():
                dd[('xr', kc, k2)] = tok

        def consume(mi, ps, ps_s, t0=t0, n=n):
            st, st_s = st_r.get()
            if mi % 2 == 0:
                P.op('act', 'activation', reads=[ps_s], writes=[st_s], out=st[:, 0:n], in_=ps[:, 0:n], func=AF.Copy)
            else:
                P.op('dve', 'tensor_copy', reads=[ps_s], writes=[st_s], out=st[:, 0:n], in_=ps[:, 0:n])
            P.dma('sp', zT[mi * 128:(mi + 1) * 128, t0:t0 + n], st[:, 0:n], reads=[st_s], writes=[('zT',)])
        proj(P, wbf["in"], DIN, H, Hs, n, Wt_r, ps_r, consume)
    P.phase_end()


def phase_outproj(P, l, io, mod, vec, xT, ymT, x1T, T, onesD, t_lo, wbf):
    P.phase_begin()
    ms = ('mod', l)
    Y_r = Rot(P, "po_Y", [128, KC, 512], 2, dt=BF16)
    X_r = Rot(P, "po_X", [128, KC, 512], 2)
    R = P.sbuf("po_R", [128, KC, 512])
    sq = P.sbuf("po_sq", [128, KC, 512])
    mean = P.sbuf("po_mean", [128, 512]); rstd = P.sbuf("po_rstd", [128, 512]); tmp = P.sbuf("po_tmp", [128, 512])
    Wt_r = Rot(P, "po_W", [128, KC, 512], 2, dt=BF16)
    ps_r = Rot(P, "po_ps", [128, 512], 6, psum=True)
    for (t0, n, j) in token_tiles(T):
        if t0 < t_lo:
            continue
        Y, Ys = Y_r.get()
        P.dma('sp', Y[:, :, 0:n], ymT[:, t0:t0 + n].rearrange("(kc p) t -> p kc t", p=128), reads=[('ymT',)], writes=[Ys])
        X, Xs = X_r.get()
        P.dma('sp', X[:, :, 0:n], xT[:, t0:t0 + n].rearrange("(kc p) t -> p kc t", p=128), reads=[('xT',)], writes=[Xs])
        P.op('pool', 'tensor_scalar', reads=[Xs], writes=[Xs], out=X[:, :, 0:n], in0=X[:, :, 0:n], scalar1=float(ALPHA), scalar2=None, op0=ALU.mult)

        def consume(mi, ps, ps_s, n=n, j=j, X=X, Xs=Xs):
            P.op('dve', 'scalar_tensor_tensor', reads=[ps_s, ms, Xs, 'po_R'], writes=['po_R'], out=R[:, mi, 0:n], in0=ps[:, 0:n], scalar=mod[:, 32 + mi, j:j + 1], in1=X[:, mi, 0:n], op0=ALU.mult, op1=ALU.add)
        proj(P, wbf["out"], D, Y, Ys, n, Wt_r, ps_r, consume)
        ln_stats(P, R, 'po_R', n, 1e-5, onesD, sq, 'po_sq', ps_r, mean, 'po_mean', rstd, 'po_rstd', tmp, 'po_tmp')
        for kc in range(KC):
            eng = 'dve' if kc % 2 == 0 else 'pool'
            P.op(eng, 'tensor_tensor', reads=['po_R', 'po_mean'], writes=['po_R'], out=R[:, kc, 0:n], in0=R[:, kc, 0:n], in1=mean[:, 0:n], op=ALU.subtract)
            P.op(eng, 'tensor_tensor', reads=['po_R', 'po_rstd'], writes=['po_R'], out=R[:, kc, 0:n], in0=R[:, kc, 0:n], in1=rstd[:, 0:n], op=ALU.mult)
            P.op('act', 'activation', reads=['po_R', 'vec'], writes=['po_R'], out=R[:, kc, 0:n], in_=R[:, kc, 0:n], func=AF.Identity, bias=vec[:, VO["ln_b"] + kc:VO["ln_b"] + kc + 1], scale=vec[:, VO["ln_g"] + kc:VO["ln_g"] + kc + 1])
        P.dma('sp', x1T[:, t0 - t_lo:t0 - t_lo + n].rearrange("(kc p) t -> p kc t", p=128), R[:, :, 0:n], reads=['po_R'], writes=[('x1T', l)])
    P.phase_end()


def phase_rwkv_prep(P, l, io, vec, zT, T, S):
    P.phase_begin()
    blk = P.sbuf("r1_blk", [128, 128])
    P.op('pool', 'memset', writes=['r1_blk'], ap=blk[:], constant=0.0)
    P.op('pool', 'memset', reads=['r1_blk'], writes=['r1_blk'], ap=blk[0:64, 0:64], constant=1.0)
    P.op('pool', 'memset', reads=['r1_blk'], writes=['r1_blk'], ap=blk[64:128, 64:128], constant=1.0)
    w2 = P.sbuf("r1_w2", [128, 1024]); a2 = P.sbuf("r1_a2", [128, 1024])
    P.dma('sp', w2[:], io[f"rw_w2{l}"], writes=['r1_w2'])
    P.dma('sp', a2[:], io[f"rw_a2{l}"], writes=['r1_a2'])
    omka = P.sbuf("r1_omka", [128, 8])
    P.op('dve', 'tensor_scalar', reads=['vec'], writes=['r1_omka'], out=omka[:], in0=vec[:, VO["k_a"]:VO["k_a"] + 8], scalar1=-1.0, scalar2=1.0, op0=ALU.mult, op1=ALU.add)
    TH = P.sbuf("r1_TH", [128, 512]); AC = P.sbuf("r1_AC", [128, 512])
    Z_r = Rot(P, "r1_Z", [128, 514], 3)
    cv = {k: Rot(P, "r1_c" + k, [128, 512], 2) for k in "rkv"}
    t_r = Rot(P, "r1_t", [128, 512], 4)
    o_r = Rot(P, "r1_o", [128, 512], 6)
    ks_r = Rot(P, "r1_ks", [128, 512], 2)
    ps_r = Rot(P, "r1_ps", [128, 512], 4, psum=True)
    for (t0, n, j) in token_tiles(T):
        seg_lo, seg_hi = (0, NCTX) if j == 1 else (NCTX, T)
        P.dma('sp', TH[:, 0:n], zT[6144:6272, t0:t0 + n], reads=[('zT',)], writes=['r1_TH'])
        P.op('act', 'activation', reads=['r1_TH'], writes=['r1_TH'], out=TH[:, 0:n], in_=TH[:, 0:n], func=AF.Tanh)
        P.dma('sp', AC[:, 0:n], zT[6272:6400, t0:t0 + n], reads=[('zT',)], writes=['r1_AC'])
        for hp in range(8):
            res = {}
            for ci, k in enumerate("rkv"):
                Z, Zs = Z_r.get()
                lo = max(t0 - 1, seg_lo); hi = min(t0 + n + 1, seg_hi)
                if lo > t0 - 1:
                    P.op('pool', 'memset', writes=[Zs], ap=Z[:, 0:1], constant=0.0)
                if hi < t0 + n + 1:
                    P.op('pool', 'memset', reads=[Zs], writes=[Zs], ap=Z[:, n + 1:n + 2], constant=0.0)
                row0 = 2048 + ci * 1024 + hp * 128
                P.dma('sp', Z[:, lo - (t0 - 1):hi - (t0 - 1)], zT[row0:row0 + 128, lo:hi], reads=[('zT',), Zs], writes=[Zs])
                o, os_ = cv[k].get()
                c0 = VO["conv"] + (ci * 8 + hp) * 3
                P.op('dve', 'tensor_scalar', reads=[Zs, 'vec'], writes=[os_], out=o[:, 0:n], in0=Z[:, 0:n], scalar1=vec[:, c0:c0 + 1], scalar2=None, op0=ALU.mult)
                P.op('dve', 'scalar_tensor_tensor', reads=[Zs, 'vec', os_], writes=[os_], out=o[:, 0:n], in0=Z[:, 1:n + 1], scalar=vec[:, c0 + 1:c0 + 2], in1=o[:, 0:n], op0=ALU.mult, op1=ALU.add)
                P.op('dve', 'scalar_tensor_tensor', reads=[Zs, 'vec', os_], writes=[os_], out=o[:, 0:n], in0=Z[:, 2:n + 2], scalar=vec[:, c0 + 2:c0 + 3], in1=o[:, 0:n], op0=ALU.mult, op1=ALU.add)
                res[k] = (o, os_)
            (r_, r_s), (k_, k_s), (v_, v_s) = res["r"], res["k"], res["v"]
            rows = slice(hp * 128, (hp + 1) * 128)
            P.dma('sp', S["r"][rows, t0:t0 + n], r_[:, 0:n], reads=[r_s], writes=[('S_r',)])
            P.dma('sp', S["v"][rows, t0:t0 + n], v_[:, 0:n], reads=[v_s], writes=[('S_v',)])
            kk, kk_s = o_r.get()
            t1, t1_s = t_r.get()
            P.op('pool', 'tensor_scalar', reads=[k_s, 'vec'], writes=[kk_s], out=kk[:, 0:n], in0=k_[:, 0:n], scalar1=vec[:, VO["k_k"] + hp:VO["k_k"] + hp + 1], scalar2=None, op0=ALU.mult)
            P.op('act', 'activation', reads=[kk_s], writes=[t1_s], out=t1[:, 0:n], in_=kk[:, 0:n], func=AF.Square)
            ps, ps_s = ps_r.get()
            P.op('pe', 'matmul', reads=[t1_s, 'r1_blk'], writes=[ps_s], out=ps[:, 0:n], lhsT=blk[:], rhs=t1[:, 0:n], start=True, stop=True)
            P.op('act', 'activation', reads=[ps_s], writes=[t1_s], out=t1[:, 0:n], in_=ps[:, 0:n], func=AF.Sqrt)
            P.op('dve', 'tensor_scalar', reads=[t1_s], writes=[t1_s], out=t1[:, 0:n], in0=t1[:, 0:n], scalar1=1e-12, scalar2=None, op0=ALU.max)
            P.op('dve', 'reciprocal', reads=[t1_s], writes=[t1_s], out=t1[:, 0:n], in_=t1[:, 0:n])
            P.op('dve', 'tensor_tensor', reads=[kk_s, t1_s], writes=[kk_s], out=kk[:, 0:n], in0=kk[:, 0:n], in1=t1[:, 0:n], op=ALU.mult)
            P.dma('sp', S["kk"][rows, t0:t0 + n], kk[:, 0:n], reads=[kk_s], writes=[('S_kk',)])
            ks, ks_s = ks_r.get()
            for d in range(2):
                dp = slice(64 * d, 64 * d + 64)
                ps, ps_s = ps_r.get()
                P.op('pe', 'matmul', reads=['r1_w2', 'r1_TH'], writes=[ps_s], out=ps[:, 0:n], lhsT=w2[dp, hp * 128:(hp + 1) * 128], rhs=TH[dp, 0:n], start=True, stop=True)
                lw, lw_s = o_r.get()
                c = VO["w0"] + d * 8 + hp
                P.op('act', 'activation', reads=[ps_s, 'vec'], writes=[lw_s], out=lw[:, 0:n], in_=ps[:, 0:n], func=AF.Sigmoid, bias=vec[:, c:c + 1], scale=1.0)
                P.op('pool', 'tensor_scalar', reads=[lw_s], writes=[lw_s], out=lw[:, 0:n], in0=lw[:, 0:n], scalar1=-0.606531, scalar2=None, op0=ALU.mult)
                P.dma('sp', S["lw"][d][rows, t0:t0 + n], lw[:, 0:n], reads=[lw_s], writes=[('S_lw', d)])
                ps, ps_s = ps_r.get()
                P.op('pe', 'matmul', reads=['r1_a2', 'r1_AC'], writes=[ps_s], out=ps[:, 0:n], lhsT=a2[dp, hp * 128:(hp + 1) * 128], rhs=AC[dp, 0:n], start=True, stop=True)
                a_, a_s = t_r.get()
                c = VO["a0"] + d * 8 + hp
                P.op('act', 'activation', reads=[ps_s, 'vec'], writes=[a_s], out=a_[:, 0:n], in_=ps[:, 0:n], func=AF.Sigmoid, bias=vec[:, c:c + 1], scale=1.0)
                ka, ka_s = o_r.get()
                P.op('dve', 'tensor_tensor', reads=[kk_s, a_s], writes=[ka_s], out=ka[:, 0:n], in0=kk[:, 0:n], in1=a_[:, 0:n], op=ALU.mult)
                P.dma('sp', S["ka"][d][rows, t0:t0 + n], ka[:, 0:n], reads=[ka_s], writes=[('S_ka', d)])
                kd, kd_s = o_r.get()
                P.op('dve', 'tensor_scalar', reads=[a_s, 'vec', 'r1_omka'], writes=[a_s], out=a_[:, 0:n], in0=a_[:, 0:n], scalar1=vec[:, VO["k_a"] + hp:VO["k_a"] + hp + 1], scalar2=omka[:, hp:hp + 1], op0=ALU.mult, op1=ALU.add)
                P.op('dve', 'tensor_tensor', reads=[k_s, a_s], writes=[kd_s], out=kd[:, 0:n], in0=k_[:, 0:n], in1=a_[:, 0:n], op=ALU.mult)
                P.dma('sp', S["kd"][d][rows, t0:t0 + n], kd[:, 0:n], reads=[kd_s], writes=[('S_kd', d)])
                if d == 0:
                    P.op('pool', 'tensor_copy', reads=[kd_s], writes=[ks_s], out=ks[:, 0:n], in_=kd[:, 0:n])
                else:
                    P.op('pool', 'tensor_tensor', reads=[kd_s, ks_s], writes=[ks_s], out=ks[:, 0:n], in0=ks[:, 0:n], in1=kd[:, 0:n], op=ALU.add)
            P.op('dve', 'scalar_tensor_tensor', reads=[r_s, 'vec', ks_s], writes=[ks_s], out=ks[:, 0:n], in0=r_[:, 0:n], scalar=vec[:, VO["r_k"] + hp:VO["r_k"] + hp + 1], in1=ks[:, 0:n], op0=ALU.mult, op1=ALU.mult)
            ps, ps_s = ps_r.get()
            P.op('pe', 'matmul', reads=[ks_s, 'r1_blk'], writes=[ps_s], out=ps[:, 0:n], lhsT=blk[:], rhs=ks[:, 0:n], start=True, stop=True)
            bo, bo_s = o_r.get()
            P.op('dve', 'tensor_tensor', reads=[ps_s, v_s], writes=[bo_s], out=bo[:, 0:n], in0=ps[:, 0:n], in1=v_[:, 0:n], op=ALU.mult)
            P.dma('sp', S["bonus"][rows, t0:t0 + n], bo[:, 0:n], reads=[bo_s], writes=[('S_bonus',)])
    P.phase_end()


def phase_rwkv_post(P, l, vec, zT, T, S, ymT, t_lo):
    P.phase_begin()
    blk = P.sbuf("r3_blk", [128, 128])
    P.op('pool', 'memset', writes=['r3_blk'], ap=blk[:], constant=0.0)
    P.op('pool', 'memset', reads=['r3_blk'], writes=['r3_blk'], ap=blk[0:64, 0:64], constant=1.0 / 64)
    P.op('pool', 'memset', reads=['r3_blk'], writes=['r3_blk'], ap=blk[64:128, 64:128], constant=1.0 / 64)
    i_r = Rot(P, "r3_i", [128, 512], 8)
    ob_r = Rot(P, "r3_ob", [128, 512], 3, dt=BF16)
    t_r = Rot(P, "r3_t", [128, 512], 6)
    ps_r = Rot(P, "r3_ps", [128, 512], 4, psum=True)
    for (t0, n, j) in token_tiles(T):
        if t0 < t_lo:
            continue
        for hp in range(8):
            rows = slice(hp * 128, (hp + 1) * 128)
            ld = {}
            for k, src in (("o0", S["o"][0][rows]), ("o1", S["o"][1][rows]), ("bo", S["bonus"][rows]), ("g", zT[5120 + hp * 128:5120 + (hp + 1) * 128])):
                t, ts = i_r.get()
                P.dma('sp', t[:, 0:n], src[:, t0:t0 + n], reads=[('d_o', 0), ('d_o', 1), ('S_bonus',), ('zT',)], writes=[ts])
                ld[k] = (t, ts)
            (o0, o0_s), (o1, o1_s), (bo, bo_s), (g, g_s) = ld["o0"], ld["o1"], ld["bo"], ld["g"]
            P.op('pool', 'tensor_tensor', reads=[o0_s, o1_s], writes=[o0_s], out=o0[:, 0:n], in0=o0[:, 0:n], in1=o1[:, 0:n], op=ALU.add)
            ps, ps_s = ps_r.get()
            P.op('pe', 'matmul', reads=[o0_s, 'r3_blk'], writes=[ps_s], out=ps[:, 0:n], lhsT=blk[:], rhs=o0[:, 0:n], start=True, stop=True)
            cen, cen_s = t_r.get()
            P.op('dve', 'tensor_tensor', reads=[o0_s, ps_s], writes=[cen_s], out=cen[:, 0:n], in0=o0[:, 0:n], in1=ps[:, 0:n], op=ALU.subtract)
            sq, sq_s = t_r.get()
            P.op('act', 'activation', reads=[cen_s], writes=[sq_s], out=sq[:, 0:n], in_=cen[:, 0:n], func=AF.Square)
            ps, ps_s = ps_r.get()
            P.op('pe', 'matmul', reads=[sq_s, 'r3_blk'], writes=[ps_s], out=ps[:, 0:n], lhsT=blk[:], rhs=sq[:, 0:n], start=True, stop=True)
            P.op('dve', 'tensor_scalar', reads=[ps_s], writes=[sq_s], out=sq[:, 0:n], in0=ps[:, 0:n], scalar1=64e-5, scalar2=None, op0=ALU.add)
            P.op('act', 'activation', reads=[sq_s], writes=[sq_s], out=sq[:, 0:n], in_=sq[:, 0:n], func=AF.Sqrt)
            P.op('dve', 'reciprocal', reads=[sq_s], writes=[sq_s], out=sq[:, 0:n], in_=sq[:, 0:n])
            P.op('dve', 'tensor_tensor', reads=[cen_s, sq_s], writes=[cen_s], out=cen[:, 0:n], in0=cen[:, 0:n], in1=sq[:, 0:n], op=ALU.mult)
            P.op('dve', 'tensor_scalar', reads=[cen_s, 'vec'], writes=[cen_s], out=cen[:, 0:n], in0=cen[:, 0:n], scalar1=vec[:, VO["gn_w"] + hp:VO["gn_w"] + hp + 1], scalar2=vec[:, VO["gn_b"] + hp:VO["gn_b"] + hp + 1], op0=ALU.mult, op1=ALU.add)
            P.op('pool', 'tensor_tensor', reads=[cen_s, bo_s], writes=[cen_s], out=cen[:, 0:n], in0=cen[:, 0:n], in1=bo[:, 0:n], op=ALU.add)
            P.op('act', 'activation', reads=[g_s], writes=[g_s], out=g[:, 0:n], in_=g[:, 0:n], func=AF.Silu)
            ob, ob_s = ob_r.get()
            P.op('dve', 'tensor_tensor', reads=[cen_s, g_s], writes=[ob_s], out=ob[:, 0:n], in0=cen[:, 0:n], in1=g[:, 0:n], op=ALU.mult)
            P.dma('sp', ymT[1024 + hp * 128:1024 + (hp + 1) * 128, t0:t0 + n], ob[:, 0:n], reads=[ob_s], writes=[('ymT',)])
    P.phase_end()


def sin_red(P, out, z, n, tmpi, tmpf, res_out, res_z, res_t):
    P.op('dve', 'tensor_scalar', reads=[res_z], writes=[res_t], out=tmpi, in0=z, scalar1=1.0 / (2 * PI), scalar2=None, op0=ALU.mult)
    P.op('dve', 'tensor_copy', reads=[res_t], writes=[res_t], out=tmpf, in_=tmpi)
    P.op('dve', 'scalar_tensor_tensor', reads=[res_t, res_z], writes=[res_out], out=out, in0=tmpf, scalar=-2 * PI, in1=z, op0=ALU.mult, op1=ALU.add)
    P.op('dve', 'tensor_scalar', reads=[res_out], writes=[res_t], out=tmpf, in0=out, scalar1=PI, scalar2=None, op0=ALU.is_gt)
    P.op('dve', 'scalar_tensor_tensor', reads=[res_t, res_out], writes=[res_out], out=out, in0=tmpf, scalar=-2 * PI, in1=out, op0=ALU.mult, op1=ALU.add)
    P.op('dve', 'tensor_scalar', reads=[res_out], writes=[res_t], out=tmpf, in0=out, scalar1=-PI, scalar2=None, op0=ALU.is_lt)
    P.op('dve', 'scalar_tensor_tensor', reads=[res_t, res_out], writes=[res_out], out=out, in0=tmpf, scalar=2 * PI, in1=out, op0=ALU.mult, op1=ALU.add)
    P.op('dve', 'tensor_scalar', reads=[res_out], writes=[res_out], out=out, in0=out, scalar1=-3.14159, scalar2=3.14159, op0=ALU.max, op1=ALU.min)
    P.op('act', 'activation', reads=[res_out], writes=[res_out], out=out, in_=out, func=AF.Sin)


def phase_s5(P, l, io, vec, zT, T, S, ymT, t_lo, cst):
    TC = 512
    I32 = mybir.dt.int32
    P.phase_begin()
    ident = cst[:, 0:128]
    Jc = cst[:, 768:896]
    iota = cst[:, 896:896 + TC]
    pp = P.sbuf("s5_pp", [128, 3, 64])
    P.dma('sp', pp[:], io[f"s5p{l}"], writes=['s5_pp'])
    rho = P.sbuf("s5_rho", [128, 64]); theta = P.sbuf("s5_theta", [128, 64])
    P.op('act', 'activation', reads=['s5_pp'], writes=['s5_pp'], out=pp[:, 2, :], in_=pp[:, 2, :], func=AF.Exp)
    P.op('dve', 'tensor_scalar', reads=['s5_pp'], writes=['s5_pp'], out=pp[:, 0, :], in0=pp[:, 0, :], scalar1=-1e-4, scalar2=None, op0=ALU.min)
    P.op('dve', 'tensor_tensor', reads=['s5_pp'], writes=['s5_rho'], out=rho[:], in0=pp[:, 0, :], in1=pp[:, 2, :], op=ALU.mult)
    P.op('act', 'activation', reads=['s5_rho'], writes=['s5_rho'], out=rho[:], in_=rho[:], func=AF.Exp)
    P.op('dve', 'tensor_tensor', reads=['s5_pp'], writes=['s5_theta'], out=theta[:], in0=pp[:, 1, :], in1=pp[:, 2, :], op=ALU.mult)
    W1 = P.sbuf("s5_W1", [16, 64, 128], BF16); W2 = P.sbuf("s5_W2", [16, 64, 128], BF16)
    P.phase_begin()
    BT = P.sbuf("s5_BT", [16, 2, 2048])
    P.dma('sp', BT[:], io[f"s5bt{l}"], writes=['s5_BT'])
    rr = P.sbuf("s5_rr", [16, 3, 2048])
    tA = [P.sbuf(f"s5_t{i}", [16, 2048]) for i in range(8)]
    tI = P.sbuf("s5_ti", [16, 2048], I32)
    for d in range(2):
        P.dma('sp', rr[:], io[f"s5r{l}"][:, :, d * 2048:(d + 1) * 2048], reads=['s5_rr'], writes=['s5_rr'])
        re, im, st = rr[:, 0, :], rr[:, 1, :], rr[:, 2, :]
        R_ = ['s5_rr'] + [f"s5_t{i}" for i in range(8)] + ['s5_ti']
        W_ = R_
        def o(eng, name, **kw):
            P.op(eng, name, reads=R_, writes=W_, **kw)
        o('act', 'activation', out=st, in_=st, func=AF.Exp)
        o('dve', 'tensor_scalar', out=re, in0=re, scalar1=-1e-4, scalar2=None, op0=ALU.min)
        er, ang, cs_, sn_, z2 = tA[0][:], tA[1][:], tA[2][:], tA[3][:], tA[4][:]
        o('dve', 'tensor_tensor', out=er, in0=re, in1=st, op=ALU.mult)
        o('act', 'activation', out=er, in_=er, func=AF.Exp)
        o('dve', 'tensor_tensor', out=ang, in0=im, in1=st, op=ALU.mult)
        sin_red(P, sn_, ang, 2048, tI[:], tA[5][:], 's5_t3', 's5_t1', 's5_t5')
        o('dve', 'tensor_scalar', out=z2, in0=ang, scalar1=PI / 2, scalar2=None, op0=ALU.add)
        sin_red(P, cs_, z2, 2048, tI[:], tA[5][:], 's5_t2', 's5_t4', 's5_t5')
        P.barrier()
        nre, nim = tA[2][:], tA[3][:]
        o('dve', 'tensor_tensor', out=nre, in0=er, in1=cs_, op=ALU.mult)
        o('dve', 'tensor_scalar', out=nre, in0=nre, scalar1=-1.0, scalar2=None, op0=ALU.add)
        o('dve', 'tensor_tensor', out=nim, in0=er, in1=sn_, op=ALU.mult)
        den, t5, bsr, bsi = tA[0][:], tA[5][:], tA[6][:], tA[7][:]
        o('dve', 'tensor_tensor', out=den, in0=re, in1=re, op=ALU.mult)
        o('dve', 'tensor_tensor', out=t5, in0=im, in1=im, op=ALU.mult)
        o('dve', 'tensor_tensor', out=den, in0=den, in1=t5, op=ALU.add)
        o('dve', 'reciprocal', out=den, in_=den)
        o('dve', 'tensor_tensor', out=bsr, in0=nre, in1=re, op=ALU.mult)
        o('dve', 'tensor_tensor', out=t5, in0=nim, in1=im, op=ALU.mult)
        o('dve', 'tensor_tensor', out=bsr, in0=bsr, in1=t5, op=ALU.add)
        o('dve', 'tensor_tensor', out=bsr, in0=bsr, in1=den, op=ALU.mult)
        o('dve', 'tensor_tensor', out=bsi, in0=nim, in1=re, op=ALU.mult)
        o('dve', 'tensor_tensor', out=t5, in0=nre, in1=im, op=ALU.mult)
        o('dve', 'tensor_tensor', out=bsi, in0=bsi, in1=t5, op=ALU.subtract)
        o('dve', 'tensor_tensor', out=bsi, in0=bsi, in1=den, op=ALU.mult)
        Br, Bi = BT[:, 0, :], BT[:, 1, :]
        bpr, bpi, t4 = tA[1][:], tA[2][:], tA[4][:]
        Rb = R_ + ['s5_BT']
        P.op('dve', 'tensor_tensor', reads=Rb, writes=W_, out=bpr, in0=bsr, in1=Br, op=ALU.mult)
        P.op('dve', 'tensor_tensor', reads=Rb, writes=W_, out=t4, in0=bsi, in1=Bi, op=ALU.mult)
        o('dve', 'tensor_tensor', out=bpr, in0=bpr, in1=t4, op=ALU.subtract)
        P.op('dve', 'tensor_tensor', reads=Rb, writes=W_, out=bpi, in0=bsr, in1=Bi, op=ALU.mult)
        P.op('dve', 'tensor_tensor', reads=Rb, writes=W_, out=t4, in0=bsi, in1=Br, op=ALU.mult)
        o('dve', 'tensor_tensor', out=bpi, in0=bpi, in1=t4, op=ALU.add)
        g3 = lambda ap: ap.rearrange("p (g s) -> p g s", s=64)
        dg = slice(d * 32, (d + 1) * 32)
        P.op('dve', 'tensor_copy', reads=R_, writes=['s5_W1'], out=W1[:, dg, 0:64], in_=g3(bpr))
        P.op('dve', 'tensor_copy', reads=R_ + ['s5_W1'], writes=['s5_W1'], out=W1[:, dg, 64:128], in_=g3(bpi))
        P.op('dve', 'tensor_copy', reads=R_, writes=['s5_W2'], out=W2[:, dg, 0:64], in_=g3(bpi))
        P.op('dve', 'tensor_scalar', reads=R_ + ['s5_W2'], writes=['s5_W2'], out=W2[:, dg, 64:128], in0=g3(bpr), scalar1=-1.0, scalar2=None, op0=ALU.mult)
        P.barrier()
    P.phase_end()
    CT1f = P.sbuf("s5_CT1f", [128, 64, 16]); CT2f = P.sbuf("s5_CT2f", [128, 64, 16])
    CT1 = P.sbuf("s5_CT1", [128, 64, 16], BF16); CT2 = P.sbuf("s5_CT2", [128, 64, 16], BF16)
    P.dma('sp', CT1f[:], io[f"s5ca{l}"], writes=['s5_CT1f'])
    P.dma('sp', CT2f[:], io[f"s5cb{l}"], writes=['s5_CT2f'])
    P.op('dve', 'tensor_copy', reads=['s5_CT1f'], writes=['s5_CT1'], out=CT1[0:64], in_=CT1f[0:64])
    P.op('dve', 'tensor_scalar', reads=['s5_CT1f', 's5_CT1'], writes=['s5_CT1'], out=CT1[64:128], in0=CT1f[64:128], scalar1=-1.0, scalar2=None, op0=ALU.mult)
    P.op('dve', 'tensor_scalar', reads=['s5_CT2f'], writes=['s5_CT2'], out=CT2[:], in0=CT2f[:], scalar1=-1.0, scalar2=None, op0=ALU.mult)
    GS = 4
    zt = P.sbuf("s5_zt", [128, TC]); zi = P.sbuf("s5_zi", [128, TC], I32); zf = P.sbuf("s5_zf", [128, TC])
    SB = []
    for sl in range(GS):
        q = {"COS": P.sbuf(f"s5_COS{sl}", [128, TC]), "SIN": P.sbuf(f"s5_SIN{sl}", [128, TC]), "RHO": P.sbuf(f"s5_RHO{sl}", [128, TC]),
             "u": Rot(P, f"s5_u{sl}_", [16, TC], 2), "ub": Rot(P, f"s5_ub{sl}_", [16, TC], 2, dt=BF16), "t": Rot(P, f"s5_tt{sl}_", [128, TC], 3),
             "tb": Rot(P, f"s5_tb{sl}_", [128, TC], 2, dt=BF16), "l": Rot(P, f"s5_l{sl}_", [128, 2], 2), "y": Rot(P, f"s5_y{sl}_", [16, TC], 2),
             "st": Rot(P, f"s5_st{sl}_", [128, 1], 2), "sl": sl}
        SB.append(q)
    ps_r = Rot(P, "s5_ps", [128, 512], 8, psum=True)
    blocks = [(0, NCTX)] + [(t, 512) for t in range(NCTX, T, 512)]

    def chain(d, g, q):
        sl = q["sl"]
        COS, SIN, RHO = q["COS"], q["SIN"], q["RHO"]
        cr, sr, rr_ = f"s5_COS{sl}", f"s5_SIN{sl}", f"s5_RHO{sl}"
        order = blocks if d == 0 else [blocks[0]] + blocks[:0:-1]
        dg = d * 32 + g
        P.op('dve', 'tensor_scalar', reads=['rwc', 's5_theta'], writes=['s5_zt'], out=zt[:], in0=iota, scalar1=theta[:, dg:dg + 1], scalar2=None, op0=ALU.mult)
        sin_red(P, SIN[:], zt[:], TC, zi[:], zf[:], sr, 's5_zt', 's5_zf')
        P.op('dve', 'tensor_scalar', reads=['s5_zt'], writes=['s5_zt'], out=zt[:], in0=zt[:], scalar1=PI / 2, scalar2=None, op0=ALU.add)
        sin_red(P, COS[:], zt[:], TC, zi[:], zf[:], cr, 's5_zt', 's5_zf')
        P.op('dve', 'tensor_scalar', reads=['rwc', 's5_rho'], writes=[rr_], out=RHO[:], in0=iota, scalar1=0.0, scalar2=rho[:, dg:dg + 1], op0=ALU.mult, op1=ALU.add)
        yield
        stc = None
        rv = (lambda ap: ap) if d == 0 else (lambda ap: ap[:, ::-1])
        for (t0, n) in order:
            u, u_s = q["u"].get()
            P.dma('sp', u[:, 0:n], zT[16 * g:16 * g + 16, t0:t0 + n], reads=[('zT',)], writes=[u_s])
            ub, ub_s = q["ub"].get()
            P.op('pool', 'tensor_copy', reads=[u_s], writes=[ub_s], out=ub[:, 0:n], in_=u[:, 0:n])
            p1, p1_s = ps_r.get()
            P.op('pe', 'matmul', reads=[ub_s, 's5_W1'], writes=[p1_s], out=p1[:, 0:n], lhsT=W1[:, dg, :], rhs=ub[:, 0:n], start=True, stop=True)
            p2, p2_s = ps_r.get()
            P.op('pe', 'matmul', reads=[ub_s, 's5_W2'], writes=[p2_s], out=p2[:, 0:n], lhsT=W2[:, dg, :], rhs=ub[:, 0:n], start=True, stop=True)
            t1, t1_s = q["t"].get(); t2, t2_s = q["t"].get()
            P.op('dve', 'tensor_tensor', reads=[p1_s, cr], writes=[t1_s], out=t1[:, 0:n], in0=rv(p1[:, 0:n]), in1=COS[:, 0:n], op=ALU.mult)
            P.op('dve', 'tensor_tensor', reads=[p2_s, sr], writes=[t2_s], out=t2[:, 0:n], in0=rv(p2[:, 0:n]), in1=SIN[:, 0:n], op=ALU.mult)
            yield
            P.op('pool', 'tensor_tensor', reads=[t1_s, t2_s], writes=[t1_s], out=t1[:, 0:n], in0=t1[:, 0:n], in1=t2[:, 0:n], op=ALU.add)
            yield
            Wt, Wt_s = q["t"].get()
            if stc is None:
                P.op('dve', 'tensor_tensor_scan', reads=[t1_s, rr_], writes=[Wt_s], out=Wt[:, 0:n], data0=RHO[:, 0:n], data1=t1[:, 0:n], initial=0.0, op0=ALU.mult, op1=ALU.add)
            else:
                P.op('dve', 'tensor_tensor_scan', reads=[t1_s, rr_, stc[1]], writes=[Wt_s], out=Wt[:, 0:n], data0=RHO[:, 0:n], data1=t1[:, 0:n], initial=stc[0][:, 0:1], op0=ALU.mult, op1=ALU.add)
            yield
            t3, t3_s = q["tb"].get(); t4, t4_s = q["tb"].get()
            ll, ll_s = q["l"].get()
            P.op('dve', 'tensor_tensor', reads=[Wt_s, cr], writes=[ll_s], out=ll[:, 0:1], in0=Wt[:, n - 1:n], in1=COS[:, n - 1:n], op=ALU.mult)
            P.op('dve', 'tensor_tensor', reads=[Wt_s, sr, ll_s], writes=[ll_s], out=ll[:, 1:2], in0=Wt[:, n - 1:n], in1=SIN[:, n - 1:n], op=ALU.mult)
            P.op('dve', 'tensor_tensor', reads=[Wt_s, cr], writes=[t3_s], out=t3[:, 0:n], in0=Wt[:, 0:n], in1=COS[:, 0:n], op=ALU.mult)
            P.op('pool', 'tensor_tensor', reads=[Wt_s, sr], writes=[t4_s], out=t4[:, 0:n], in0=Wt[:, 0:n], in1=SIN[:, 0:n], op=ALU.mult)
            yield
            px, px_s = ps_r.get()
            P.op('pe', 'matmul', reads=[ll_s, 'rwc'], writes=[px_s], out=px[:, 0:1], lhsT=ident, rhs=ll[:, 0:1], start=True, stop=False)
            P.op('pe', 'matmul', reads=[ll_s, 'rwc', px_s], writes=[px_s], out=px[:, 0:1], lhsT=Jc, rhs=ll[:, 1:2], start=False, stop=True)
            sn, sn_s = q["st"].get()
            P.op('act', 'activation', reads=[px_s], writes=[sn_s], out=sn[:], in_=px[:, 0:1], func=AF.Copy)
            stc = (sn, sn_s)
            py, py_s = ps_r.get()
            P.op('pe', 'matmul', reads=[t3_s, 's5_CT1'], writes=[py_s], out=py[0:16, 0:n], lhsT=CT1[:, dg, :], rhs=t3[:, 0:n], start=True, stop=False)
            P.op('pe', 'matmul', reads=[t4_s, 's5_CT2', py_s], writes=[py_s], out=py[0:16, 0:n], lhsT=CT2[:, dg, :], rhs=t4[:, 0:n], start=False, stop=True)
            y, y_s = q["y"].get()
            P.op('act', 'activation', reads=[py_s], writes=[y_s], out=rv(y[:, 0:n]), in_=py[0:16, 0:n], func=AF.Copy)
            P.dma('sp', S["ys"][d][16 * g:16 * g + 16, t0:t0 + n], y[:, 0:n], reads=[y_s], writes=[('S_ys', d)])
            yield

    for d in range(2):
        for gb in range(0, 32, GS):
            live = [chain(d, gb + k, SB[k]) for k in range(GS)]
            while live:
                nxt = []
                for gen in live:
                    try:
                        next(gen)
                        nxt.append(gen)
                    except StopIteration:
                        pass
                live = nxt
    P.phase_end()
    P.phase_begin()
    ps_r = Rot(P, "s5c_ps", [128, 512], 4, psum=True)
    wg = P.sbuf("s5_wg", [128, 4, 512])
    P.dma('sp', wg[:], io[f"w_glu{l}"].rearrange("(kc p) m -> p kc m", p=128), writes=['s5_wg'])
    YG_r = Rot(P, "s5_YG", [128, 4, 512], 2)
    i_r = Rot(P, "s5_i", [128, 512], 8)
    G_r = Rot(P, "s5_G", [128, 4, 512], 2)
    ob_r = Rot(P, "s5_ob", [128, 512], 3, dt=BF16)
    for (t0, n, j) in token_tiles(T):
        if t0 < t_lo:
            continue
        YG, YG_s = YG_r.get()
        G, G_s = G_r.get()
        gts = []
        for c in range(4):
            rows = slice(c * 128, (c + 1) * 128)
            ld = []
            for src, rs in ((zT[rows], ('zT',)), (S["ys"][0][rows], ('S_ys', 0)), (S["ys"][1][rows], ('S_ys', 1))):
                t, ts = i_r.get()
                P.dma('sp', t[:, 0:n], src[:, t0:t0 + n], reads=[rs], writes=[ts])
                ld.append((t, ts))
            (u, u_s), (y0, y0_s), (y1, y1_s) = ld
            P.dma('sp', G[:, c, 0:n], zT[512 + c * 128:512 + (c + 1) * 128, t0:t0 + n], reads=[('zT',), G_s], writes=[G_s])
            P.op('dve', 'scalar_tensor_tensor', reads=[u_s, 'vec', y0_s], writes=[y0_s], out=y0[:, 0:n], in0=u[:, 0:n], scalar=vec[:, VO["d_skip"] + c:VO["d_skip"] + c + 1], in1=y0[:, 0:n], op0=ALU.mult, op1=ALU.add)
            P.op('pool', 'tensor_tensor', reads=[y0_s, y1_s], writes=[y0_s], out=y0[:, 0:n], in0=y0[:, 0:n], in1=y1[:, 0:n], op=ALU.add)
            P.op('act', 'activation', reads=[y0_s], writes=[y1_s], out=y1[:, 0:n], in_=y0[:, 0:n], func=AF.Square)
            P.op('dve', 'tensor_scalar', reads=[y1_s], writes=[y1_s], out=y1[:, 0:n], in0=y1[:, 0:n], scalar1=0.044715, scalar2=1.0, op0=ALU.mult, op1=ALU.add)
            P.op('dve', 'tensor_tensor', reads=[y1_s, y0_s], writes=[y1_s], out=y1[:, 0:n], in0=y1[:, 0:n], in1=y0[:, 0:n], op=ALU.mult)
            P.op('act', 'activation', reads=[y1_s], writes=[y1_s], out=y1[:, 0:n], in_=y1[:, 0:n], func=AF.Sigmoid, scale=1.5957691216057308)
            P.op('dve', 'tensor_tensor', reads=[y1_s, y0_s, YG_s], writes=[YG_s], out=YG[:, c, 0:n], in0=y1[:, 0:n], in1=y0[:, 0:n], op=ALU.mult)
        P.op('act', 'activation', reads=[G_s], writes=[G_s], out=G[:, :, 0:n], in_=G[:, :, 0:n], func=AF.Silu)
        for m in range(4):
            ps, ps_s = ps_r.get()
            for kc in range(4):
                P.op('pe', 'matmul', reads=[YG_s, 's5_wg', ps_s], writes=[ps_s], out=ps[:, 0:n], lhsT=wg[:, kc, m * 128:(m + 1) * 128], rhs=YG[:, kc, 0:n], start=(kc == 0), stop=(kc == 3))
            sg, sg_s = i_r.get()
            P.op('act', 'activation', reads=[ps_s, 'vec'], writes=[sg_s], out=sg[:, 0:n], in_=ps[:, 0:n], func=AF.Sigmoid, bias=vec[:, VO["b_glu"] + m:VO["b_glu"] + m + 1], scale=1.0)
            P.op('dve', 'tensor_tensor', reads=[sg_s, YG_s], writes=[sg_s], out=sg[:, 0:n], in0=sg[:, 0:n], in1=YG[:, m, 0:n], op=ALU.mult)
            ob, ob_s = ob_r.get()
            P.op('dve', 'tensor_tensor', reads=[sg_s, G_s], writes=[ob_s], out=ob[:, 0:n], in0=sg[:, 0:n], in1=G[:, m, 0:n], op=ALU.mult)
            P.dma('sp', ymT[m * 128:(m + 1) * 128, t0:t0 + n], ob[:, 0:n], reads=[ob_s], writes=[('ymT',)])
    P.phase_end()


def phase_pool(P, l, io, vec, zT, T, ymT, t_lo):
    P.phase_begin()
    rows_tot = (T - NCTX) // 64
    wp = P.sbuf("pl_wp", [128, 4, 128])
    P.dma('sp', wp[:], io[f"w_pool{l}"], writes=['pl_wp'])
    icn = P.sbuf("pl_icn", [128, 4, rows_tot + 64 + NCTX])
    P.dma('sp', icn[:], io["icnt"], writes=['pl_icn'])
    ps_r = Rot(P, "pl_ps", [128, 512], 4, psum=True)
    g_r = Rot(P, "pl_g", [128, 512], 3)
    ob_r = Rot(P, "pl_ob", [128, 512], 3, dt=BF16)
    regions = []
    RB = min(64, rows_tot)
    for r0 in range(0, rows_tot, RB):
        regions.append((NCTX, rows_tot, 64, r0, RB, 0, rows_tot))
    if t_lo == 0:
        for r0 in range(0, NCTX, 64):
            regions.append((0, NCTX, 1, r0, 64, rows_tot + 64, None))
    bufA = {64: [P.sbuf(f"pl_A{i}", [128, RB + 16, 80]) for i in range(3)], 1: [P.sbuf(f"pl_B{i}", [128, 80, 1]) for i in range(3)]}
    dif = {64: P.sbuf("pl_d64", [128, RB, 64]), 1: P.sbuf("pl_d1", [128, 64, 1])}
    for gi, w in enumerate((2, 4, 8, 16)):
        zr = slice(1024 + gi * 128, 1024 + (gi + 1) * 128)
        for (tok0, Rtot, C, r0, nr, ico, icc) in regions:
            A = bufA[C]
            An = [f"pl_{'A' if C == 64 else 'B'}{i}" for i in range(3)]
            CP = C + 16 if C == 64 else 1
            c0 = 8 if C == 64 else 0
            U, Us = A[0], An[0]
            P.op('pool', 'memset', reads=[Us], writes=[Us], ap=U[:], constant=0.0)
            ra = max(r0 - 8, 0); rb = min(r0 + nr + 8, Rtot)
            P.dma('sp', U[:, 8 + ra - r0:8 + rb - r0, c0:c0 + C], zT[zr, tok0 + ra * C:tok0 + rb * C].rearrange("p (r c) -> p r c", c=C), reads=[('zT',), Us], writes=[Us])
            NR = nr + 16
            cur, cur_s = U, Us
            idx = 0
            s = 1
            while s < w:
                nidx = 1 if idx != 1 else 2
                nxt, nxt_s = A[nidx], An[nidx]
                P.op('dve', 'tensor_tensor', reads=[cur_s, nxt_s], writes=[nxt_s], out=nxt[:, 0:NR - s, :], in0=cur[:, 0:NR - s, :], in1=cur[:, s:NR, :], op=ALU.add)
                cur, cur_s, idx = nxt, nxt_s, nidx
                s *= 2
            nidx = 1 if idx != 1 else 2
            rm, rm_s = A[nidx], An[nidx]
            P.op('dve', 'tensor_tensor', reads=[cur_s, 'pl_icn', rm_s], writes=[rm_s], out=rm[:, 0:nr, :], in0=cur[:, 8 - w // 2:8 - w // 2 + nr, :],
                 in1=icn[:, gi, ico + r0:ico + r0 + nr].unsqueeze(2).to_broadcast([128, nr, CP]), op=ALU.mult)
            cur, cur_s, idx = rm, rm_s, nidx
            if C == 64:
                s = 1
                while s < w:
                    nidx = [i for i in (1, 2) if i != idx][0] if idx != 0 else 1
                    nxt, nxt_s = A[nidx], An[nidx]
                    P.op('dve', 'tensor_tensor', reads=[cur_s, nxt_s], writes=[nxt_s], out=nxt[:, 0:nr, 0:CP - s], in0=cur[:, 0:nr, 0:CP - s], in1=cur[:, 0:nr, s:CP], op=ALU.add)
                    cur, cur_s, idx = nxt, nxt_s, nidx
                    s *= 2
                nidx = [i for i in (1, 2) if i != idx][0]
                cm, cm_s = A[nidx], An[nidx]
                P.op('dve', 'tensor_tensor', reads=[cur_s, 'pl_icn', cm_s], writes=[cm_s], out=cm[:, 0:nr, 0:64], in0=cur[:, 0:nr, 8 - w // 2:8 - w // 2 + 64],
                     in1=icn[:, gi, icc:icc + 64].unsqueeze(1).to_broadcast([128, nr, 64]), op=ALU.mult)
                cur, cur_s = cm, cm_s
            df, df_s = dif[C], ("pl_d", C)
            P.op('dve', 'tensor_tensor', reads=[cur_s, Us, df_s], writes=[df_s], out=df[:, 0:nr, :], in0=cur[:, 0:nr, 0:C], in1=U[:, 8:8 + nr, c0:c0 + C], op=ALU.subtract)
            ntok = nr * C
            dflat = df[:, 0:nr, :].rearrange("p r c -> p (r c)")
            for q0 in range(0, ntok, 512):
                qn = min(512, ntok - q0)
                ps, ps_s = ps_r.get()
                P.op('pe', 'matmul', reads=[df_s, 'pl_wp'], writes=[ps_s], out=ps[:, 0:qn], lhsT=wp[:, gi, :], rhs=dflat[:, q0:q0 + qn], start=True, stop=True)
                tk = tok0 + r0 * C + q0
                gt, gt_s = g_r.get()
                P.dma('sp', gt[:, 0:qn], zT[1536 + gi * 128:1536 + (gi + 1) * 128, tk:tk + qn], reads=[('zT',)], writes=[gt_s])
                P.op('act', 'activation', reads=[gt_s], writes=[gt_s], out=gt[:, 0:qn], in_=gt[:, 0:qn], func=AF.Silu)
                ob, ob_s = ob_r.get()
                P.op('dve', 'scalar_tensor_tensor', reads=[ps_s, 'vec', gt_s], writes=[ob_s], out=ob[:, 0:qn], in0=ps[:, 0:qn], scalar=vec[:, VO["pool_scale"] + gi:VO["pool_scale"] + gi + 1], in1=gt[:, 0:qn], op0=ALU.mult, op1=ALU.mult)
                P.dma('sp', ymT[512 + gi * 128:512 + (gi + 1) * 128, tk:tk + qn], ob[:, 0:qn], reads=[ob_s], writes=[('ymT',)])
    P.phase_end()


class RowSplit:
    def __init__(self, aps, rows_per):
        self.aps = aps
        self.rp = rows_per

    def __getitem__(self, key):
        if isinstance(key, tuple):
            rs, cs = key
        else:
            rs, cs = key, None
        b = rs.start // self.rp
        assert (rs.stop - 1) // self.rp == b
        ap = self.aps[b][rs.start - b * self.rp:rs.stop - b * self.rp]
        return ap if cs is None else ap[:, cs]

def build(N, nlayers=2, debug=False):
    T = NCTX + N
    nc = bass.Bass("TRN2", target_bir_lowering=False)
    io = {}

    def din(name, shape):
        io[name] = nc.dram_tensor(name, list(shape), F32, kind="ExternalInput").ap()

    def dscr(name, shape):
        return nc.dram_tensor(name, list(shape), F32, kind=("ExternalOutput" if debug else "Internal")).ap()
    rows_tot = N // 64
    din("xT", [D, T]); din("ccT", [128, KC, 2]); din("cst", [128, 1408]); din("icnt", [128, 4, rows_tot + 64 + NCTX])
    for l in range(2):
        din(f"w_ada{l}", [D, 6144]); din(f"w_in{l}", [D, DIN]); din(f"w_out{l}", [D, D]); din(f"vec{l}", [128, VW])
        din(f"s5p{l}", [128, 3, 64]); din(f"s5r{l}", [16, 3, 4096]); din(f"s5bt{l}", [16, 2, 2048])
        din(f"s5ca{l}", [128, 64, 16]); din(f"s5cb{l}", [128, 64, 16]); din(f"w_glu{l}", [512, 512])
        din(f"w_pool{l}", [128, 4, 128]); din(f"rw_w2{l}", [128, 1024]); din(f"rw_a2{l}", [128, 1024])
    yT = nc.dram_tensor("yT", [D, N], F32, kind="ExternalOutput").ap()
    zT = RowSplit([dscr(f"zT{i}", [min(1024, DIN - i * 1024), T]) for i in range(7)], 1024); ymT = nc.dram_tensor("ymT", [D, T], BF16, kind="Internal").ap(); x1T = dscr("x1T", [D, T])
    wbf = {"in": nc.dram_tensor("winb", [D, DIN], BF16, kind="Internal").ap(), "out": nc.dram_tensor("woutb", [D, D], BF16, kind="Internal").ap()}
    S = {"r": dscr("S_r", [1024, T]), "kk": dscr("S_kk", [1024, T]), "v": dscr("S_v", [1024, T]), "bonus": dscr("S_bonus", [1024, T]),
         "lw": [dscr(f"S_lw{d}", [1024, T]) for d in range(2)], "ka": [dscr(f"S_ka{d}", [1024, T]) for d in range(2)],
         "kd": [dscr(f"S_kd{d}", [1024, T]) for d in range(2)], "o": [dscr(f"S_o{d}", [1024, T]) for d in range(2)],
         "ys": [dscr(f"S_ys{d}", [512, T]) for d in range(2)]}
    P = Prog(nc)
    cst = P.sbuf("rwc", [128, 1408])
    P.dma('sp', cst[:], io["cst"], writes=['rwc'])
    onesD = P.sbuf("onesD", [128, 128])
    P.op('pool', 'memset', writes=['onesD'], ap=onesD[:], constant=1.0 / D)
    mod = P.sbuf("mod", [128, 48, 2])
    vec = P.sbuf("vec", [128, VW])
    for l in range(nlayers):
        t_lo = 0 if l == 0 else NCTX
        P.barrier()
        P.dma('sp', vec[:], io[f"vec{l}"], reads=['vec'], writes=['vec'])
        xin = io["xT"] if l == 0 else x1T
        xout = x1T if l == 0 else yT
        SK = []
        phase_mod(P, l, io, mod, vec, wbf)
        if "inproj" not in SK:
            phase_inproj(P, l, io, mod, xin, zT, T, onesD, wbf)
        if "s5" not in SK:
            phase_s5(P, l, io, vec, zT, T, S, ymT, t_lo, cst)
        if "pool" not in SK:
            phase_pool(P, l, io, vec, zT, T, ymT, t_lo)
        if "prep" not in SK:
            phase_rwkv_prep(P, l, io, vec, zT, T, S)
        if "core" not in SK:
            P.phase_begin()
            rwkv_core(P, cst, 16, [(0, NCTX // L), (NCTX, N // L)], S["r"], S["kk"], S["v"], S["lw"], S["ka"], S["kd"], S["o"])
            P.phase_end()
        if "post" not in SK:
            phase_rwkv_post(P, l, vec, zT, T, S, ymT, t_lo)
        phase_outproj(P, l, io, mod, vec, xin, ymT, xout, T, onesD, t_lo, wbf)
    P.finish([('x1T', nlayers - 1)])
    return nc


def _pm(v):
    v = np.asarray(v, np.float32).reshape(-1)
    return np.ascontiguousarray(v.reshape(-1, 128).T)


def host_layout(inp, b, N):
    f32 = np.float32
    A = lambda a: np.ascontiguousarray(np.asarray(a), dtype=f32)
    m = {}
    m["xT"] = A(np.concatenate([np.asarray(inp["ctx"][b]).T, np.asarray(inp["x"][b, :N]).T], axis=1))
    cc = np.stack([np.asarray(inp["c"][b]), np.asarray(inp["c_ctx"])], 0)
    m["ccT"] = A(cc.reshape(2, KC, 128).transpose(2, 1, 0))
    i = np.arange(128)[:, None]; j = np.arange(128)[None, :]
    Jc = np.zeros((128, 128), f32)
    for p in range(64):
        Jc[64 + p, p] = -1.0
        Jc[p, 64 + p] = 1.0
    iota = np.tile(np.arange(1, 513, dtype=f32)[None, :], (128, 1))
    m["cst"] = A(np.concatenate([np.eye(128), (i < j), (i <= j), (i > j), (i >= j), np.ones((128, 128)), Jc, iota], 1))
    rows_tot = N // 64

    def invcnt(n, w):
        idx = np.arange(n)
        lo = np.clip(idx - w // 2, 0, n - 1); hi = np.clip(idx - w // 2 + w - 1, 0, n - 1)
        return (1.0 / (hi - lo + 1)).astype(f32)
    ic = np.stack([np.concatenate([invcnt(rows_tot, w), invcnt(64, w), invcnt(NCTX, w)]) for w in (2, 4, 8, 16)], 0)
    m["icnt"] = A(np.tile(ic[None], (128, 1, 1)))
    for l in range(2):
        g = lambda k: np.asarray(inp[k][l])
        m[f"w_ada{l}"] = A(g("w_ada")); m[f"w_in{l}"] = A(g("w_in")); m[f"w_out{l}"] = A(g("w_out"))
        w0 = g("rwkv_w0").reshape(2, 8, 128).transpose(2, 0, 1).reshape(128, 16)
        a0 = g("rwkv_a0").reshape(2, 8, 128).transpose(2, 0, 1).reshape(128, 16)
        conv = g("conv_rkv").reshape(3, 24, 128).transpose(2, 1, 0).reshape(128, 72)
        parts = [_pm(g("b_ada")), _pm(g("s5_d")), _pm(g("b_glu")), _pm(g("pool_scale")), _pm(g("rwkv_k_k")), _pm(g("rwkv_k_a")),
                 _pm(g("rwkv_r_k")), _pm(g("gn_w")), _pm(g("gn_b")), _pm(g("ln_g")), _pm(g("ln_b")), w0, a0, conv]
        m[f"vec{l}"] = A(np.concatenate(parts, 1))
        lr = g("s5_lam_re").reshape(64, 64).T; li = g("s5_lam_im").reshape(64, 64).T
        ls = np.tile(g("s5_log_step").reshape(1, 64), (64, 1))
        sp = np.stack([lr, li, ls], 1)
        m[f"s5p{l}"] = A(np.concatenate([sp, sp], 0))
        row = np.stack([g("s5_lam_re").reshape(4096), g("s5_lam_im").reshape(4096), np.repeat(g("s5_log_step").reshape(64), 64)], 0)
        m[f"s5r{l}"] = A(np.tile(row[None], (16, 1, 1)))
        m[f"s5bt{l}"] = A(np.stack([g("s5_b_re").transpose(2, 0, 1).reshape(16, 2048), g("s5_b_im").transpose(2, 0, 1).reshape(16, 2048)], 1))
        cr = g("s5_c_re").transpose(3, 0, 1, 2).reshape(64, 64, 16); ci = g("s5_c_im").transpose(3, 0, 1, 2).reshape(64, 64, 16)
        m[f"s5ca{l}"] = A(np.concatenate([cr, ci], 0)); m[f"s5cb{l}"] = A(np.concatenate([ci, cr], 0))
        m[f"w_glu{l}"] = A(g("w_glu")); m[f"w_pool{l}"] = A(g("w_pool").transpose(1, 0, 2))
        m[f"rw_w2{l}"] = A(g("rwkv_w2").reshape(128, 1024)); m[f"rw_a2{l}"] = A(g("rwkv_a2").reshape(128, 1024))
    return m


_NC_CACHE = {}


def run(inputs, N, ncores=8, nlayers=2, debug=False, trace=False):
    key = (N, nlayers, debug)
    if key not in _NC_CACHE:
        _NC_CACHE[key] = build(N, nlayers, debug)
    nc = _NC_CACHE[key]
    maps = [host_layout(inputs, b, N) for b in range(2)]
    in_maps = [maps[c % 2] for c in range(ncores)]
    if trace:
        res = run_bass_kernel_spmd(nc, in_maps, core_ids=list(range(ncores)), trace=True)
        print("EXEC_TIME_NS", res.exec_time_ns)
    else:
        res = run_bass_kernel_spmd(nc, in_maps, core_ids=list(range(ncores)))
    if debug:
        return res
    out = np.stack([np.ascontiguousarray(res.results[b]["yT"].T) for b in range(2)], 0)
    return out.astype(np.float32)


def kernel(**inputs):
    return run(inputs, 16384, ncores=2)
```

```python
import numpy as np
from contextlib import ExitStack
import concourse.bass as bass
import concourse.mybir as mybir
from concourse.bass_utils import run_bass_kernel_spmd

F32 = mybir.dt.float32
BF16 = mybir.dt.bfloat16
AF = mybir.ActivationFunctionType
ALU = mybir.AluOpType
AX = mybir.AxisListType

SEM_CHUNK = 20000
DMA_K = 8
DMA_CHUNK = 1000


class Prog:
    ENG = ('pe', 'dve', 'act', 'pool', 'sp')

    def __init__(self, nc):
        self.nc = nc
        self.st = ExitStack()
        self.sem_st = ExitStack()
        self.ops = {e: [] for e in self.ENG}
        self.n = {e: 0 for e in self.ENG}
        self.sems = {e: [] for e in self.ENG}
        self.waited_c = {e: {x: 0 for x in self.ENG} for e in self.ENG}
        self.waited_d = {e: {} for e in self.ENG}
        self.lastw = {}
        self.readers = {}
        self.dma_n = {e: 0 for e in self.ENG}
        self.dma_sems = {e: {} for e in self.ENG}
        self.nsem = 0
        self.ninst = 0

    def sbuf(self, name, shape, dt=F32):
        self.ninst += 0
        self._uid = getattr(self, '_uid', 0) + 1
        return self.st.enter_context(self.nc.sbuf_tensor(f"{name}_u{self._uid}", list(shape), dt))

    def psum(self, name, shape, dt=F32):
        self._uid = getattr(self, '_uid', 0) + 1
        return self.st.enter_context(self.nc.psum_tensor(f"{name}_u{self._uid}", list(shape), dt))

    def _newsem(self, name):
        self.nsem += 1
        return self.sem_st.enter_context(self.nc.semaphore(name))

    def _csem(self, eng, n):
        idx = (n - 1) // SEM_CHUNK
        while len(self.sems[eng]) <= idx:
            self.sems[eng].append(self._newsem(f"s_{eng}_{len(self.sems[eng])}"))
        return self.sems[eng][idx], (n - 1) % SEM_CHUNK + 1

    def _dsem(self, q, j):
        slot = j % DMA_K
        cnt = j // DMA_K
        key = (slot, cnt // DMA_CHUNK)
        if key not in self.dma_sems[q]:
            self.dma_sems[q][key] = self._newsem(f"d_{q}_{slot}_{cnt // DMA_CHUNK}")
        return self.dma_sems[q][key], 16 * (cnt % DMA_CHUNK + 1), key

    def _wait(self, eng, tok):
        if tok is None:
            return
        if tok[0] == 'c':
            _, e2, n = tok
            if e2 == eng and eng == 'pe':
                return
            if self.waited_c[eng][e2] >= n:
                return
            self.waited_c[eng][e2] = n
            sem, val = self._csem(e2, n)
        else:
            _, q, j = tok
            sem, val, key = self._dsem(q, j)
            k2 = (q, key)
            if self.waited_d[eng].get(k2, 0) >= val:
                return
            self.waited_d[eng][k2] = val
        self.ops[eng].append(lambda e, sem=sem, val=val: e.wait_ge(sem, val))

    def _deps(self, eng, reads, writes):
        toks = []
        for r in reads:
            if r in self.lastw:
                toks.append(self.lastw[r])
        for w in writes:
            if w in self.lastw:
                toks.append(self.lastw[w])
            for t in self.readers.get(w, {}).values():
                toks.append(t)
        for t in toks:
            self._wait(eng, t)

    def _commit(self, tok, reads, writes):
        for w in writes:
            self.lastw[w] = tok
            self.readers[w] = {}
        for r in reads:
            if r in writes:
                continue
            d = self.readers.setdefault(r, {})
            if tok[0] == 'c':
                d[('c', tok[1])] = tok
            else:
                q, j = tok[1], tok[2]
                d[('d', q, j % DMA_K)] = tok

    def op(self, eng, name, reads=(), writes=(), **kw):
        fn = (lambda e, name=name, kw=kw: getattr(e, name)(**kw))
        self._deps(eng, reads, writes)
        self.n[eng] += 1
        n = self.n[eng]
        sem, _ = self._csem(eng, n)
        self.ops[eng].append(lambda e, fn=fn, sem=sem: fn(e).then_inc(sem, 1))
        self._commit(('c', eng, n), reads, writes)
        self.ninst += 1

    def dma(self, q, out, in_, reads=(), writes=(), **kw):
        self.dmaop(q, lambda e, out=out, in_=in_, kw=kw: e.dma_start(out=out, in_=in_, **kw), reads, writes)

    def dmaop(self, q, fn, reads=(), writes=()):
        self._deps(q, reads, writes)
        j = self.dma_n[q]
        self.dma_n[q] += 1
        if j >= DMA_K:
            self._wait(q, ('d', q, j - DMA_K))
        sem, _, _ = self._dsem(q, j)
        self.ops[q].append(lambda e, fn=fn, sem=sem: fn(e).then_inc(sem, 16))
        self._commit(('d', q, j), reads, writes)
        self.ninst += 1

    def barrier(self):
        toks = [('c', e, self.n[e]) for e in self.ENG if self.n[e] > 0]
        for q in self.ENG:
            for j in range(max(0, self.dma_n[q] - DMA_K), self.dma_n[q]):
                toks.append(('d', q, j))
        for e in self.ENG:
            for t in toks:
                self._wait(e, t)

    def phase_begin(self):
        if not hasattr(self, '_stk'):
            self._stk = []
        self._stk.append(self.st)
        self.st = ExitStack()

    def phase_end(self):
        self.barrier()
        self.st.close()
        self.st = self._stk.pop()

    def finish(self, final_res):
        for r in final_res:
            self._wait('sp', self.lastw[r])
        nc = self.nc
        with nc.Block() as block:
            @block.sync
            def _(e):
                for f in self.ops['sp']:
                    f(e)

            @block.tensor
            def _(e):
                for f in self.ops['pe']:
                    f(e)

            @block.vector
            def _(e):
                for f in self.ops['dve']:
                    f(e)

            @block.scalar
            def _(e):
                for f in self.ops['act']:
                    f(e)

            @block.gpsimd
            def _(e):
                for f in self.ops['pool']:
                    f(e)
        self.st.close()


L = 128


class Rot:
    def __init__(self, P, name, shape, n, psum=False, dt=F32):
        self.bufs = []
        for i in range(n):
            t = P.psum(f"{name}{i}", shape, dt) if psum else P.sbuf(f"{name}{i}", shape, dt)
            self.bufs.append((t, (name, i)))
        self.i = 0

    def get(self):
        b = self.bufs[self.i % len(self.bufs)]
        self.i += 1
        return b


def rwkv_consts(P, cst):
    c = P.sbuf("rwc", [128, 6 * 128])
    P.dma('sp', c[:], cst, writes=['rwc'])
    return c


def rwkv_core(P, c, H, segs, d_r, d_kk, d_v, d_lw, d_ka, d_kd, d_o, HB=2, G=4):
    ident = c[:, 0:128]
    ones = c[:, 5 * 128:6 * 128]
    combT = [c[:, 128:384], c[:, 384:640]]
    Mst = [c[:, 384:512], c[:, 128:256]]
    nb = H // HB
    Tst = {}
    for d in range(2):
        for b in range(nb):
            Tst[d, b] = Rot(P, f"T{d}_{b}_", [64, HB, 64], 2)
    B = []
    for sl in range(G):
        q = {}
        for k in ("r", "kk", "v", "lw", "ka", "kd"):
            q["in_" + k] = Rot(P, f"in{sl}_{k}", [64, HB, L], 2)
        for k in ("cs", "g1", "gi", "g0", "QT", "OL", "O"):
            q[k] = Rot(P, f"{k}{sl}_", [64, HB, L], 1)
        for k in ("nKa", "Kd"):
            q[k] = Rot(P, f"{k}{sl}_", [64, HB, L], 1, dt=BF16)
        q["KR"] = Rot(P, f"KR{sl}_", [64, HB, 2 * L], 1, dt=BF16)
        q["A1"] = Rot(P, f"A1{sl}_", [128, HB, 2 * L], 1, dt=BF16)
        q["A2"] = Rot(P, f"A2{sl}_", [128, HB, 2 * L], 1, dt=BF16)
        q["PT"] = [Rot(P, f"PT{j}{sl}_", [128, HB, L], 1, dt=BF16) for j in range(1, 7)]
        q["Pn"] = Rot(P, f"Pn{sl}_", [128, HB, L], 2, dt=BF16)
        q["TOK"] = Rot(P, f"TOK{sl}_", [128, HB, 256], 1, dt=BF16)
        q["X"] = Rot(P, f"X{sl}_", [128, HB, 128], 2, dt=BF16)
        q["vb"] = Rot(P, f"vb{sl}_", [64, HB, L], 1, dt=BF16)
        q["Rf"] = Rot(P, f"Rf{sl}_", [64, HB, L], 1)
        q["GT"] = Rot(P, f"GT{sl}_", [64, HB, 64], 1)
        q["Hg"] = Rot(P, f"Hg{sl}_", [64, HB, 64], 1)
        B.append(q)
    ps_r = Rot(P, "ps", [128, 512], 6, psum=True)
    psb_r = Rot(P, "psb", [128, 1024], 2, psum=True, dt=BF16)
    identb = P.sbuf("identb", [64, 64], BF16)
    P.op('pool', 'tensor_copy', reads=['rwc'], writes=['identb'], out=identb[:], in_=ident[0:64, 0:64])

    def mm(out, lhsT, rhs, reads, wres, start=True, stop=True):
        P.op('pe', 'matmul', reads=reads, writes=[wres], out=out, lhsT=lhsT, rhs=rhs, start=start, stop=stop)

    evac_flip = [0]

    def evac(out, in_, reads, wres):
        evac_flip[0] ^= 1
        if evac_flip[0]:
            P.op('act', 'activation', reads=reads, writes=[wres], out=out, in_=in_, func=AF.Copy)
        else:
            P.op('dve', 'tensor_copy', reads=reads, writes=[wres], out=out, in_=in_)

    def h3(ap, w):
        return ap.rearrange("p (h t) -> p h t", h=HB)

    def item(d, t0, first, b, q):
        rows = slice(b * HB * 64, (b + 1) * HB * 64)

        def src(ap):
            return ap[rows, t0:t0 + L].rearrange("(h p) t -> p h t", p=64)
        tl = {}
        for k, ap in (("r", d_r), ("kk", d_kk), ("v", d_v), ("lw", d_lw[d]), ("ka", d_ka[d]), ("kd", d_kd[d])):
            t, res = q["in_" + k].get()
            P.dma('sp', t[:], src(ap), writes=[res])
            tl[k] = (t, res)
        (r_t, r_s), (kk_t, kk_s), (v_t, v_s) = tl["r"], tl["kk"], tl["v"]
        (lw_t, lw_s), (ka_t, ka_s), (kd_t, kd_s) = tl["lw"], tl["ka"], tl["kd"]
        cs, cs_s = q["cs"].get(); g1, g1_s = q["g1"].get(); gi, gi_s = q["gi"].get(); g0, g0_s = q["g0"].get()
        KR, KR_s = q["KR"].get(); nKa, nKa_s = q["nKa"].get(); Kd, Kd_s = q["Kd"].get()
        yield
        for h in range(HB):
            if d == 0:
                P.op('dve', 'tensor_tensor_scan', reads=[lw_s, 'rwc'], writes=[cs_s], out=cs[:, h, :], data0=ones[0:64, :], data1=lw_t[:, h, :], initial=0.0, op0=ALU.mult, op1=ALU.add)
            else:
                P.op('dve', 'tensor_tensor_scan', reads=[lw_s, 'rwc'], writes=[cs_s], out=cs[:, h, ::-1], data0=ones[0:64, :], data1=lw_t[:, h, ::-1], initial=0.0, op0=ALU.mult, op1=ALU.add)
        yield
        P.op('act', 'activation', reads=[cs_s], writes=[g1_s], out=g1[:], in_=cs[:], func=AF.Exp)
        P.op('act', 'activation', reads=[cs_s], writes=[gi_s], out=gi[:], in_=cs[:], func=AF.Exp, scale=-1.0)
        P.op('pool', 'tensor_tensor', reads=[cs_s, lw_s], writes=[g0_s], out=g0[:], in0=cs[:], in1=lw_t[:], op=ALU.subtract)
        yield
        P.op('act', 'activation', reads=[g0_s], writes=[g0_s], out=g0[:], in_=g0[:], func=AF.Exp)
        Rf, Rf_s = q["Rf"].get()
        vb, vb_s = q["vb"].get()
        P.op('pool', 'tensor_tensor', reads=[r_s, g1_s], writes=[Rf_s], out=Rf[:], in0=r_t[:], in1=g1[:], op=ALU.mult)
        P.op('pool', 'tensor_copy', reads=[Rf_s, KR_s], writes=[KR_s], out=KR[:, :, L:2 * L], in_=Rf[:])
        P.op('pool', 'tensor_copy', reads=[v_s], writes=[vb_s], out=vb[:], in_=v_t[:])
        P.op('dve', 'scalar_tensor_tensor', reads=[ka_s, gi_s], writes=[nKa_s], out=nKa[:], in0=ka_t[:], scalar=-1.0, in1=gi[:], op0=ALU.mult, op1=ALU.mult)
        P.op('pool', 'tensor_tensor', reads=[kd_s, gi_s], writes=[Kd_s], out=Kd[:], in0=kd_t[:], in1=gi[:], op=ALU.mult)
        yield
        P.op('dve', 'tensor_tensor', reads=[kk_s, g0_s, KR_s], writes=[KR_s], out=KR[:, :, 0:L], in0=kk_t[:], in1=g0[:], op=ALU.mult)
        yield
        A1, A1_s = q["A1"].get()
        A2, A2_s = q["A2"].get()
        for (Adst, Ares, lh, lres) in ((A1, A1_s, nKa, nKa_s), (A2, A2_s, Kd, Kd_s)):
            for h in range(HB):
                ps, ps_s = ps_r.get()
                mm(ps[:, 0:2 * L], lh[:, h, :], KR[:, h, :], [lres, KR_s], ps_s)
                P.op('dve', 'tensor_tensor', reads=[ps_s, 'rwc', Ares], writes=[Ares], out=Adst[:, h, :], in0=ps[:, 0:2 * L], in1=combT[d], op=ALU.mult)
        Pc, Pc_s = q["Pn"].get()
        ps, ps_s = ps_r.get()
        for h in range(HB):
            mm(ps[:, h * L:(h + 1) * L], KR[:, h, 0:L], nKa[:, h, :], [KR_s, nKa_s, ps_s], ps_s)
        P.op('dve', 'tensor_tensor', reads=[ps_s, 'rwc'], writes=[Pc_s], out=Pc[:], in0=h3(ps[:, 0:HB * L], L), in1=Mst[d].unsqueeze(1).to_broadcast([128, HB, L]), op=ALU.mult)
        yield
        TOK, TOK_s = q["TOK"].get()
        ps, ps_s = psb_r.get()
        for h in range(HB):
            for qi, (srcT, sres) in enumerate(((KR, KR_s), (vb, vb_s), (Kd, Kd_s), (nKa, nKa_s))):
                P.op('pe', 'transpose', reads=[sres, 'identb', ps_s], writes=[ps_s], out=ps[:, h * 256 + qi * 64:h * 256 + (qi + 1) * 64], in_=srcT[:, h, 0:L], identity=identb[:])
        evac(TOK[:], h3(ps[:, 0:HB * 256], 256), [ps_s], TOK_s)
        yield
        Xa, Xa_s = q["X"].get()
        ps, ps_s = ps_r.get()
        for h in range(HB):
            mm(ps[:, h * 64:(h + 1) * 64], A2[:, h, 0:L], TOK[:, h, 64:128], [A2_s, TOK_s, ps_s], ps_s)
        P.op('act', 'activation', reads=[ps_s, Xa_s], writes=[Xa_s], out=Xa[:, :, 64:128], in_=h3(ps[:, 0:HB * 64], 64), func=AF.Copy)
        P.op('pool', 'tensor_copy', reads=[TOK_s, Xa_s], writes=[Xa_s], out=Xa[:, :, 0:64], in_=TOK[:, :, 0:64])
        yield
        X, X_s = Xa, Xa_s
        PTp, PTp_s = A1, A1_s
        PTp_f = (lambda h, A1=A1: A1[:, h, 0:L])
        for j in range(7):
            Xn, Xn_s = q["X"].get()
            ps, ps_s = ps_r.get()
            for h in range(HB):
                mm(ps[:, h * 128:(h + 1) * 128], PTp_f(h), X[:, h, :], [PTp_s, X_s, ps_s], ps_s)
            P.op('dve', 'tensor_tensor', reads=[ps_s, X_s, Xn_s], writes=[Xn_s], out=Xn[:], in0=h3(ps[:, 0:HB * 128], 128), in1=X[:], op=ALU.add)
            X, X_s = Xn, Xn_s
            if j < 6:
                PTn, PTn_s = q["PT"][j].get()
                ps, ps_s = ps_r.get()
                for h in range(HB):
                    mm(ps[:, h * L:(h + 1) * L], Pc[:, h, :], PTp_f(h), [Pc_s, PTp_s, ps_s], ps_s)
                evac(PTn[:], h3(ps[:, 0:HB * L], L), [ps_s], PTn_s)
                if j < 5:
                    Pn, Pn_s = q["Pn"].get()
                    ps2, ps2_s = ps_r.get()
                    for h in range(HB):
                        mm(ps2[:, h * L:(h + 1) * L], PTp_f(h), Pc[:, h, :], [Pc_s, PTp_s, ps2_s], ps2_s)
                    evac(Pn[:], h3(ps2[:, 0:HB * L], L), [ps2_s], Pn_s)
                    Pc, Pc_s = Pn, Pn_s
                PTp, PTp_s = PTn, PTn_s
                PTp_f = (lambda h, PTn=PTn: PTn[:, h, :])
            yield
        GT, GT_s = q["GT"].get()
        Hg, Hg_s = q["Hg"].get()
        ps, ps_s = ps_r.get()
        for h in range(HB):
            mm(ps[0:64, h * 64:(h + 1) * 64], X[:, h, 0:64], TOK[:, h, 192:256], [X_s, TOK_s, ps_s], ps_s)
        P.op('dve', 'tensor_tensor', reads=[ps_s, 'rwc'], writes=[GT_s], out=GT[:], in0=h3(ps[0:64, 0:HB * 64], 64), in1=ident[0:64, 0:64].unsqueeze(1).to_broadcast([64, HB, 64]), op=ALU.add)
        ps, ps_s = ps_r.get()
        for h in range(HB):
            mm(ps[0:64, h * 64:(h + 1) * 64], TOK[:, h, 128:192], TOK[:, h, 64:128], [TOK_s, ps_s], ps_s, start=True, stop=False)
            mm(ps[0:64, h * 64:(h + 1) * 64], TOK[:, h, 192:256], X[:, h, 64:128], [TOK_s, X_s, ps_s], ps_s, start=False, stop=True)
        gl = (L - 1) if d == 0 else 0
        for h in range(HB):
            P.op('dve', 'tensor_scalar', reads=[ps_s, g1_s, Hg_s], writes=[Hg_s], out=Hg[:, h, :], in0=ps[0:64, h * 64:(h + 1) * 64], scalar1=g1[:, h, gl:gl + 1], scalar2=None, op0=ALU.mult)
        yield
        QT, QT_s = q["QT"].get()
        OL, OL_s = q["OL"].get()
        ps, ps_s = ps_r.get()
        for h in range(HB):
            mm(ps[0:64, h * L:(h + 1) * L], X[:, h, 0:64], A1[:, h, L:2 * L], [X_s, A1_s, ps_s], ps_s)
        P.op('dve', 'tensor_tensor', reads=[ps_s, Rf_s], writes=[QT_s], out=QT[:], in0=h3(ps[0:64, 0:HB * L], L), in1=Rf[:], op=ALU.add)
        ps, ps_s = ps_r.get()
        for h in range(HB):
            mm(ps[0:64, h * L:(h + 1) * L], TOK[:, h, 64:128], A2[:, h, L:2 * L], [TOK_s, A2_s, ps_s], ps_s, start=True, stop=False)
            mm(ps[0:64, h * L:(h + 1) * L], X[:, h, 64:128], A1[:, h, L:2 * L], [X_s, A1_s, ps_s], ps_s, start=False, stop=True)
        evac(OL[:], h3(ps[0:64, 0:HB * L], L), [ps_s], OL_s)
        yield
        O, O_s = q["O"].get()
        dst = d_o[d][rows, t0:t0 + L].rearrange("(h p) t -> p h t", p=64)
        if first:
            Tn, Tn_s = Tst[d, b].get()
            P.op('pool', 'tensor_copy', reads=[Hg_s], writes=[Tn_s], out=Tn[:], in_=Hg[:])
            P.dma('sp', dst, OL[:], reads=[OL_s], writes=[('d_o', d)])
            Tst[d, b].cur = (Tn, Tn_s)
        else:
            T0, T0_s = Tst[d, b].cur
            ps, ps_s = ps_r.get()
            for h in range(HB):
                mm(ps[0:64, h * L:(h + 1) * L], T0[:, h, :], QT[:, h, :], [T0_s, QT_s, ps_s], ps_s)
            P.op('dve', 'tensor_tensor', reads=[ps_s, OL_s], writes=[O_s], out=O[:], in0=h3(ps[0:64, 0:HB * L], L), in1=OL[:], op=ALU.add)
            P.dma('sp', dst, O[:], reads=[O_s], writes=[('d_o', d)])
            ps, ps_s = ps_r.get()
            for h in range(HB):
                mm(ps[0:64, h * 64:(h + 1) * 64], GT[:, h, :], T0[:, h, :], [GT_s, T0_s, ps_s], ps_s)
            Tn, Tn_s = Tst[d, b].get()
            for h in range(HB):
                P.op('dve', 'scalar_tensor_tensor', reads=[ps_s, g1_s, Hg_s, Tn_s], writes=[Tn_s], out=Tn[:, h, :], in0=ps[0:64, h * 64:(h + 1) * 64], scalar=g1[:, h, gl:gl + 1], in1=Hg[:, h, :], op0=ALU.mult, op1=ALU.add)
            Tst[d, b].cur = (Tn, Tn_s)
        yield

    nsteps = sum(n for _, n in segs)
    fwd = [(s0 + i * L) for (s0, n) in segs for i in range(n)]
    bwd = [(s0 + i * L) for (s0, n) in segs for i in reversed(range(n))]
    for i in range(nsteps):
        items = [(0, fwd[i], i == 0, b) for b in range(nb)] + [(1, bwd[i], i == 0, b) for b in range(nb)]
        for g0_ in range(0, len(items), G):
            gens = [item(*it, B[sl]) for sl, it in enumerate(items[g0_:g0_ + G])]
            live = list(gens)
            while live:
                nxt = []
                for gen in live:
                    try:
                        next(gen)
                        nxt.append(gen)
                    except StopIteration:
                        pass
                live = nxt

D = 2048
NCTX = 256
KC = 16
DIN = 6400
ALPHA = 4.0 ** 0.25
PI = 3.14159265358979
VO = {}
_o = 0
for _n, _w in (("b_ada", 48), ("d_skip", 4), ("b_glu", 4), ("pool_scale", 4), ("k_k", 8), ("k_a", 8), ("r_k", 8),
               ("gn_w", 8), ("gn_b", 8), ("ln_g", 16), ("ln_b", 16), ("w0", 16), ("a0", 16), ("conv", 72)):
    VO[_n] = _o
    _o += _w
VW = _o


def token_tiles(T):
    tiles = [(0, NCTX, 1)]
    t = NCTX
    while t < T:
        tiles.append((t, 512, 0))
        t += 512
    return tiles


def ln_stats(P, R, Rs, n, eps, onesD, sq, sq_s, ps_r, mean, mean_s, rstd, rstd_s, tmp, tmp_s):
    P.op('act', 'activation', reads=[Rs], writes=[sq_s], out=sq[:, :, 0:n], in_=R[:, :, 0:n], func=AF.Square)
    pm, pm_s = ps_r.get()
    for kc in range(KC):
        P.op('pe', 'matmul', reads=[Rs, 'onesD', pm_s], writes=[pm_s], out=pm[:, 0:n], lhsT=onesD[:], rhs=R[:, kc, 0:n], start=(kc == 0), stop=(kc == KC - 1))
    pq, pq_s = ps_r.get()
    for kc in range(KC):
        P.op('pe', 'matmul', reads=[sq_s, 'onesD', pq_s], writes=[pq_s], out=pq[:, 0:n], lhsT=onesD[:], rhs=sq[:, kc, 0:n], start=(kc == 0), stop=(kc == KC - 1))
    P.op('act', 'activation', reads=[pm_s], writes=[mean_s], out=mean[:, 0:n], in_=pm[:, 0:n], func=AF.Copy)
    P.op('dve', 'tensor_tensor', reads=[mean_s], writes=[tmp_s], out=tmp[:, 0:n], in0=mean[:, 0:n], in1=mean[:, 0:n], op=ALU.mult)
    P.op('dve', 'tensor_tensor', reads=[pq_s, tmp_s], writes=[tmp_s], out=tmp[:, 0:n], in0=pq[:, 0:n], in1=tmp[:, 0:n], op=ALU.subtract)
    P.op('dve', 'tensor_scalar', reads=[tmp_s], writes=[tmp_s], out=tmp[:, 0:n], in0=tmp[:, 0:n], scalar1=float(eps), scalar2=None, op0=ALU.add)
    P.op('act', 'activation', reads=[tmp_s], writes=[tmp_s], out=tmp[:, 0:n], in_=tmp[:, 0:n], func=AF.Sqrt)
    P.op('dve', 'reciprocal', reads=[tmp_s], writes=[rstd_s], out=rstd[:, 0:n], in_=tmp[:, 0:n])


def proj(P, Wd, M, H, Hs, n, Wt_r, ps_r, consume):
    m0 = 0
    while m0 < M:
        w = min(512, M - m0)
        Wt, Wt_s = Wt_r.get()
        P.dma('pool', Wt[:, :, 0:w], Wd[:, m0:m0 + w].rearrange("(kc p) c -> p kc c", p=128), reads=[('wbf',)], writes=[Wt_s])
        for mi in range(w // 128):
            ps, ps_s = ps_r.get()
            for kc in range(KC):
                P.op('pe', 'matmul', reads=[Wt_s, Hs, ps_s], writes=[ps_s], out=ps[:, 0:n], lhsT=Wt[:, kc, mi * 128:(mi + 1) * 128], rhs=H[:, kc, 0:n], start=(kc == 0), stop=(kc == KC - 1))
            consume(m0 // 128 + mi, ps, ps_s)
        m0 += w


def phase_mod(P, l, io, mod, vec, wbf):
    P.phase_begin()
    sc = P.sbuf("m_sc", [128, KC, 2])
    ps_r = Rot(P, "m_ps", [128, 512], 2, psum=True)
    Wt_r = Rot(P, "m_W", [128, KC, 512], 2)
    P.dma('sp', sc[:], io["ccT"], writes=['m_sc'])
    P.op('act', 'activation', reads=['m_sc'], writes=['m_sc'], out=sc[:], in_=sc[:], func=AF.Silu)
    ms = ('mod', l)
    for mg in range(12):
        Wt, Wt_s = Wt_r.get()
        P.dma('pool', Wt[:], io[f"w_ada{l}"][:, mg * 512:(mg + 1) * 512].rearrange("(kc p) c -> p kc c", p=128), writes=[Wt_s])
        ps, ps_s = ps_r.get()
        for mi in range(4):
            for kc in range(KC):
                P.op('pe', 'matmul', reads=[Wt_s, 'm_sc', ps_s], writes=[ps_s], out=ps[:, mi * 2:mi * 2 + 2], lhsT=Wt[:, kc, mi * 128:(mi + 1) * 128], rhs=sc[:, kc, :], start=(kc == 0), stop=(kc == KC - 1))
        P.op('dve', 'tensor_tensor', reads=[ps_s, 'vec', ms], writes=[ms], out=mod[:, mg * 4:mg * 4 + 4, :], in0=ps[:, 0:8].rearrange("p (m j) -> p m j", j=2),
             in1=vec[:, VO["b_ada"] + mg * 4:VO["b_ada"] + mg * 4 + 4].unsqueeze(2).to_broadcast([128, 4, 2]), op=ALU.add)
    P.op('dve', 'tensor_scalar', reads=[ms], writes=[ms], out=mod[:, 16:32, :], in0=mod[:, 16:32, :], scalar1=1.0, scalar2=None, op0=ALU.add)
    Wb_r = Rot(P, "m_Wb", [128, KC, 512], 2, dt=BF16)
    ci = 0
    for (src, dst, M) in ((io[f"w_in{l}"], wbf["in"], DIN), (io[f"w_out{l}"], wbf["out"], D)):
        m0 = 0
        while m0 < M:
            w = min(512, M - m0)
            Wt, Wt_s = Wt_r.get()
            P.dma('pool', Wt[:, :, 0:w], src[:, m0:m0 + w].rearrange("(kc p) c -> p kc c", p=128), writes=[Wt_s])
            Wb, Wb_s = Wb_r.get()
            eng = ('act', 'pool', 'dve')[ci % 3]
            ci += 1
            if eng == 'act':
                P.op('act', 'activation', reads=[Wt_s], writes=[Wb_s], out=Wb[:, :, 0:w], in_=Wt[:, :, 0:w], func=AF.Copy)
            else:
                P.op(eng, 'tensor_copy', reads=[Wt_s], writes=[Wb_s], out=Wb[:, :, 0:w], in_=Wt[:, :, 0:w])
            P.dma('sp', dst[:, m0:m0 + w].rearrange("(kc p) c -> p kc c", p=128), Wb[:, :, 0:w], reads=[Wb_s], writes=[('wbf',)])
            m0 += w
    P.phase_end()


def phase_inproj(P, l, io, mod, xT, zT, T, onesD, wbf):
    P.phase_begin()
    ms = ('mod', l)
    X_r = Rot(P, "p1_X", [128, KC, 512], 2)
    sq = P.sbuf("p1_sq", [128, KC, 512])
    H_r = Rot(P, "p1_H", [128, KC, 512], 2, dt=BF16)
    mean = P.sbuf("p1_mean", [128, 512]); rstd = P.sbuf("p1_rstd", [128, 512]); tmp = P.sbuf("p1_tmp", [128, 512])
    st_r = Rot(P, "p1_st", [128, 512], 4)
    Wt_r = Rot(P, "p1_W", [128, KC, 512], 3, dt=BF16)
    ps_r = Rot(P, "p1_ps", [128, 512], 6, psum=True)
    for (t0, n, j) in token_tiles(T):
        X, Xs = X_r.get()
        P.dma('sp', X[:, :, 0:n], xT[:, t0:t0 + n].rearrange("(kc p) t -> p kc t", p=128), reads=[('xT',)], writes=[Xs])
        H, Hs = H_r.get()
        ln_stats(P, X, Xs, n, 1e-6, onesD, sq, 'p1_sq', ps_r, mean, 'p1_mean', rstd, 'p1_rstd', tmp, 'p1_tmp')
        for kc in range(KC):
            eng = 'dve' if kc % 2 == 0 else 'pool'
            xr = (Xs, kc)
            P.op(eng, 'tensor_tensor', reads=[Xs, 'p1_mean'], writes=[xr], out=X[:, kc, 0:n], in0=X[:, kc, 0:n], in1=mean[:, 0:n], op=ALU.subtract)
            P.op(eng, 'tensor_tensor', reads=[xr, 'p1_rstd'], writes=[xr], out=X[:, kc, 0:n], in0=X[:, kc, 0:n], in1=rstd[:, 0:n], op=ALU.mult)
            P.op('act', 'activation', reads=[xr, ms, Hs], writes=[Hs], out=H[:, kc, 0:n], in_=X[:, kc, 0:n], func=AF.Identity, bias=mod[:, kc, j:j + 1], scale=mod[:, 16 + kc, j:j + 1])
        for kc in range(KC):
            dd = P.readers.setdefault(Xs, {})
            dd[('xw', kc)] = P.lastw[(Xs, kc)]
            for k2, tok in P.readers.get((Xs, kc), {}).items():
                dd[('xr', kc, k2)] = tok

        def consume(mi, ps, ps_s, t0=t0, n=n):
            st, st_s = st_r.get()
            if mi % 2 == 0:
                P.op('act', 'activation', reads=[ps_s], writes=[st_s], out=st[:, 0:n], in_=ps[:, 0:n], func=AF.Copy)
            else:
                P.op('dve', 'tensor_copy', reads=[ps_s], writes=[st_s], out=st[:, 0:n], in_=ps[:, 0:n])
            P.dma('sp', zT[mi * 128:(mi + 1) * 128, t0:t0 + n], st[:, 0:n], reads=[st_s], writes=[('zT',)])
        proj(P, wbf["in"], DIN, H, Hs, n, Wt_r, ps_r, consume)
    P.phase_end()


def phase_outproj(P, l, io, mod, vec, xT, ymT, x1T, T, onesD, t_lo, wbf):
    P.phase_begin()
    ms = ('mod', l)
    Y_r = Rot(P, "po_Y", [128, KC, 512], 2, dt=BF16)
    X_r = Rot(P, "po_X", [128, KC, 512], 2)
    R = P.sbuf("po_R", [128, KC, 512])
    sq = P.sbuf("po_sq", [128, KC, 512])
    mean = P.sbuf("po_mean", [128, 512]); rstd = P.sbuf("po_rstd", [128, 512]); tmp = P.sbuf("po_tmp", [128, 512])
    Wt_r = Rot(P, "po_W", [128, KC, 512], 2, dt=BF16)
    ps_r = Rot(P, "po_ps", [128, 512], 6, psum=True)
    for (t0, n, j) in token_tiles(T):
        if t0 < t_lo:
            continue
        Y, Ys = Y_r.get()
        P.dma('sp', Y[:, :, 0:n], ymT[:, t0:t0 + n].rearrange("(kc p) t -> p kc t", p=128), reads=[('ymT',)], writes=[Ys])
        X, Xs = X_r.get()
        P.dma('sp', X[:, :, 0:n], xT[:, t0:t0 + n].rearrange("(kc p) t -> p kc t", p=128), reads=[('xT',)], writes=[Xs])
        P.op('pool', 'tensor_scalar', reads=[Xs], writes=[Xs], out=X[:, :, 0:n], in0=X[:, :, 0:n], scalar1=float(ALPHA), scalar2=None, op0=ALU.mult)

        def consume(mi, ps, ps_s, n=n, j=j, X=X, Xs=Xs):
            P.op('dve', 'scalar_tensor_tensor', reads=[ps_s, ms, Xs, 'po_R'], writes=['po_R'], out=R[:, mi, 0:n], in0=ps[:, 0:n], scalar=mod[:, 32 + mi, j:j + 1], in1=X[:, mi, 0:n], op0=ALU.mult, op1=ALU.add)
        proj(P, wbf["out"], D, Y, Ys, n, Wt_r, ps_r, consume)
        ln_stats(P, R, 'po_R', n, 1e-5, onesD, sq, 'po_sq', ps_r, mean, 'po_mean', rstd, 'po_rstd', tmp, 'po_tmp')
        for kc in range(KC):
            eng = 'dve' if kc % 2 == 0 else 'pool'
            P.op(eng, 'tensor_tensor', reads=['po_R', 'po_mean'], writes=['po_R'], out=R[:, kc, 0:n], in0=R[:, kc, 0:n], in1=mean[:, 0:n], op=ALU.subtract)
            P.op(eng, 'tensor_tensor', reads=['po_R', 'po_rstd'], writes=['po_R'], out=R[:, kc, 0:n], in0=R[:, kc, 0:n], in1=rstd[:, 0:n], op=ALU.mult)
            P.op('act', 'activation', reads=['po_R', 'vec'], writes=['po_R'], out=R[:, kc, 0:n], in_=R[:, kc, 0:n], func=AF.Identity, bias=vec[:, VO["ln_b"] + kc:VO["ln_b"] + kc + 1], scale=vec[:, VO["ln_g"] + kc:VO["ln_g"] + kc + 1])
        P.dma('sp', x1T[:, t0 - t_lo:t0 - t_lo + n].rearrange("(kc p) t -> p kc t", p=128), R[:, :, 0:n], reads=['po_R'], writes=[('x1T', l)])
    P.phase_end()


def _rr(gens):
    live = list(gens)
    while live:
        nxt = []
        for g in live:
            try:
                next(g)
                nxt.append(g)
            except StopIteration:
                pass
        live = nxt


def phase_rwkv_prep(P, l, io, vec, zT, T, S):
    P.phase_begin()
    NS = 4
    blk = P.sbuf("r1_blk", [128, 128])
    P.op('pool', 'memset', writes=['r1_blk'], ap=blk[:], constant=0.0)
    P.op('pool', 'memset', reads=['r1_blk'], writes=['r1_blk'], ap=blk[0:64, 0:64], constant=1.0)
    P.op('pool', 'memset', reads=['r1_blk'], writes=['r1_blk'], ap=blk[64:128, 64:128], constant=1.0)
    w2 = P.sbuf("r1_w2", [128, 1024]); a2 = P.sbuf("r1_a2", [128, 1024])
    P.dma('sp', w2[:], io[f"rw_w2{l}"], writes=['r1_w2'])
    P.dma('sp', a2[:], io[f"rw_a2{l}"], writes=['r1_a2'])
    omka = P.sbuf("r1_omka", [128, 8])
    P.op('dve', 'tensor_scalar', reads=['vec'], writes=['r1_omka'], out=omka[:], in0=vec[:, VO["k_a"]:VO["k_a"] + 8], scalar1=-1.0, scalar2=1.0, op0=ALU.mult, op1=ALU.add)
    TH_r = Rot(P, "r1_TH", [128, 512], 2); AC_r = Rot(P, "r1_AC", [128, 512], 2)
    SL = []
    for sl in range(NS):
        SL.append({"Z": Rot(P, f"r1_Z{sl}_", [128, 514], 3), "cv": {k: Rot(P, f"r1_c{k}{sl}_", [128, 512], 1) for k in "rkv"},
                   "t": Rot(P, f"r1_t{sl}_", [128, 512], 3), "o": Rot(P, f"r1_o{sl}_", [128, 512], 6), "ks": Rot(P, f"r1_ks{sl}_", [128, 512], 1)})
    ps_r = Rot(P, "r1_ps", [128, 512], 8, psum=True)

    def body(hp, q, t0, n, j, TH, TH_s, AC, AC_s):
        seg_lo, seg_hi = (0, NCTX) if j == 1 else (NCTX, T)
        res = {}
        for ci, k in enumerate("rkv"):
            Z, Zs = q["Z"].get()
            lo = max(t0 - 1, seg_lo); hi = min(t0 + n + 1, seg_hi)
            if lo > t0 - 1:
                P.op('pool', 'memset', reads=[Zs], writes=[Zs], ap=Z[:, 0:1], constant=0.0)
            if hi < t0 + n + 1:
                P.op('pool', 'memset', reads=[Zs], writes=[Zs], ap=Z[:, n + 1:n + 2], constant=0.0)
            row0 = 2048 + ci * 1024 + hp * 128
            P.dma('sp', Z[:, lo - (t0 - 1):hi - (t0 - 1)], zT[row0:row0 + 128, lo:hi], reads=[('zT',), Zs], writes=[Zs])
            res[k] = (Z, Zs)
        yield
        cvs = {}
        for ci, k in enumerate("rkv"):
            Z, Zs = res[k]
            o, os_ = q["cv"][k].get()
            c0 = VO["conv"] + (ci * 8 + hp) * 3
            P.op('dve', 'tensor_scalar', reads=[Zs, 'vec', os_], writes=[os_], out=o[:, 0:n], in0=Z[:, 0:n], scalar1=vec[:, c0:c0 + 1], scalar2=None, op0=ALU.mult)
            cvs[k] = (o, os_, c0)
        yield
        for step in (1, 2):
            for k in "rkv":
                Z, Zs = res[k]
                o, os_, c0 = cvs[k]
                P.op('dve', 'scalar_tensor_tensor', reads=[Zs, 'vec', os_], writes=[os_], out=o[:, 0:n], in0=Z[:, step:n + step], scalar=vec[:, c0 + step:c0 + step + 1], in1=o[:, 0:n], op0=ALU.mult, op1=ALU.add)
            yield
        (r_, r_s, _), (k_, k_s, _), (v_, v_s, _) = cvs["r"], cvs["k"], cvs["v"]
        rows = slice(hp * 128, (hp + 1) * 128)
        P.dma('sp', S["r"][rows, t0:t0 + n], r_[:, 0:n], reads=[r_s], writes=[('S_r',)])
        P.dma('sp', S["v"][rows, t0:t0 + n], v_[:, 0:n], reads=[v_s], writes=[('S_v',)])
        kk, kk_s = q["o"].get()
        t1, t1_s = q["t"].get()
        P.op('pool', 'tensor_scalar', reads=[k_s, 'vec'], writes=[kk_s], out=kk[:, 0:n], in0=k_[:, 0:n], scalar1=vec[:, VO["k_k"] + hp:VO["k_k"] + hp + 1], scalar2=None, op0=ALU.mult)
        yield
        P.op('act', 'activation', reads=[kk_s], writes=[t1_s], out=t1[:, 0:n], in_=kk[:, 0:n], func=AF.Square)
        yield
        ps, ps_s = ps_r.get()
        P.op('pe', 'matmul', reads=[t1_s, 'r1_blk'], writes=[ps_s], out=ps[:, 0:n], lhsT=blk[:], rhs=t1[:, 0:n], start=True, stop=True)
        pw = []
        for d in range(2):
            dp = slice(64 * d, 64 * d + 64)
            p1, p1_s = ps_r.get()
            P.op('pe', 'matmul', reads=['r1_w2', TH_s], writes=[p1_s], out=p1[:, 0:n], lhsT=w2[dp, hp * 128:(hp + 1) * 128], rhs=TH[dp, 0:n], start=True, stop=True)
            p2, p2_s = ps_r.get()
            P.op('pe', 'matmul', reads=['r1_a2', AC_s], writes=[p2_s], out=p2[:, 0:n], lhsT=a2[dp, hp * 128:(hp + 1) * 128], rhs=AC[dp, 0:n], start=True, stop=True)
            pw.append((p1, p1_s, p2, p2_s))
        P.op('act', 'activation', reads=[ps_s], writes=[t1_s], out=t1[:, 0:n], in_=ps[:, 0:n], func=AF.Sqrt)
        lws, as_ = [], []
        for d in range(2):
            p1, p1_s, p2, p2_s = pw[d]
            lw, lw_s = q["o"].get()
            c = VO["w0"] + d * 8 + hp
            P.op('act', 'activation', reads=[p1_s, 'vec'], writes=[lw_s], out=lw[:, 0:n], in_=p1[:, 0:n], func=AF.Sigmoid, bias=vec[:, c:c + 1], scale=1.0)
            a_, a_s = q["t"].get()
            c = VO["a0"] + d * 8 + hp
            P.op('act', 'activation', reads=[p2_s, 'vec'], writes=[a_s], out=a_[:, 0:n], in_=p2[:, 0:n], func=AF.Sigmoid, bias=vec[:, c:c + 1], scale=1.0)
            lws.append((lw, lw_s)); as_.append((a_, a_s))
        yield
        P.op('dve', 'tensor_scalar', reads=[t1_s], writes=[t1_s], out=t1[:, 0:n], in0=t1[:, 0:n], scalar1=1e-12, scalar2=None, op0=ALU.max)
        for d in range(2):
            lw, lw_s = lws[d]
            P.op('pool', 'tensor_scalar', reads=[lw_s], writes=[lw_s], out=lw[:, 0:n], in0=lw[:, 0:n], scalar1=-0.606531, scalar2=None, op0=ALU.mult)
            P.dma('sp', S["lw"][d][rows, t0:t0 + n], lw[:, 0:n], reads=[lw_s], writes=[('S_lw', d)])
        yield
        P.op('dve', 'reciprocal', reads=[t1_s], writes=[t1_s], out=t1[:, 0:n], in_=t1[:, 0:n])
        yield
        P.op('dve', 'tensor_tensor', reads=[kk_s, t1_s], writes=[kk_s], out=kk[:, 0:n], in0=kk[:, 0:n], in1=t1[:, 0:n], op=ALU.mult)
        P.dma('sp', S["kk"][rows, t0:t0 + n], kk[:, 0:n], reads=[kk_s], writes=[('S_kk',)])
        yield
        ks, ks_s = q["ks"].get()
        kds = []
        for d in range(2):
            a_, a_s = as_[d]
            ka, ka_s = q["o"].get()
            P.op('dve', 'tensor_tensor', reads=[kk_s, a_s], writes=[ka_s], out=ka[:, 0:n], in0=kk[:, 0:n], in1=a_[:, 0:n], op=ALU.mult)
            P.dma('sp', S["ka"][d][rows, t0:t0 + n], ka[:, 0:n], reads=[ka_s], writes=[('S_ka', d)])
        yield
        for d in range(2):
            a_, a_s = as_[d]
            P.op('dve', 'tensor_scalar', reads=[a_s, 'vec', 'r1_omka'], writes=[a_s], out=a_[:, 0:n], in0=a_[:, 0:n], scalar1=vec[:, VO["k_a"] + hp:VO["k_a"] + hp + 1], scalar2=omka[:, hp:hp + 1], op0=ALU.mult, op1=ALU.add)
        yield
        for d in range(2):
            a_, a_s = as_[d]
            kd, kd_s = q["o"].get()
            P.op('dve', 'tensor_tensor', reads=[k_s, a_s], writes=[kd_s], out=kd[:, 0:n], in0=k_[:, 0:n], in1=a_[:, 0:n], op=ALU.mult)
            P.dma('sp', S["kd"][d][rows, t0:t0 + n], kd[:, 0:n], reads=[kd_s], writes=[('S_kd', d)])
            kds.append((kd, kd_s))
        yield
        P.op('pool', 'tensor_tensor', reads=[kds[0][1], kds[1][1], ks_s], writes=[ks_s], out=ks[:, 0:n], in0=kds[0][0][:, 0:n], in1=kds[1][0][:, 0:n], op=ALU.add)
        yield
        P.op('dve', 'scalar_tensor_tensor', reads=[r_s, 'vec', ks_s], writes=[ks_s], out=ks[:, 0:n], in0=r_[:, 0:n], scalar=vec[:, VO["r_k"] + hp:VO["r_k"] + hp + 1], in1=ks[:, 0:n], op0=ALU.mult, op1=ALU.mult)
        yield
        ps, ps_s = ps_r.get()
        P.op('pe', 'matmul', reads=[ks_s, 'r1_blk'], writes=[ps_s], out=ps[:, 0:n], lhsT=blk[:], rhs=ks[:, 0:n], start=True, stop=True)
        bo, bo_s = q["o"].get()
        P.op('dve', 'tensor_tensor', reads=[ps_s, v_s], writes=[bo_s], out=bo[:, 0:n], in0=ps[:, 0:n], in1=v_[:, 0:n], op=ALU.mult)
        P.dma('sp', S["bonus"][rows, t0:t0 + n], bo[:, 0:n], reads=[bo_s], writes=[('S_bonus',)])
        yield

    for (t0, n, j) in token_tiles(T):
        TH, TH_s = TH_r.get(); AC, AC_s = AC_r.get()
        P.dma('sp', TH[:, 0:n], zT[6144:6272, t0:t0 + n], reads=[('zT',)], writes=[TH_s])
        P.op('act', 'activation', reads=[TH_s], writes=[TH_s], out=TH[:, 0:n], in_=TH[:, 0:n], func=AF.Tanh)
        P.dma('sp', AC[:, 0:n], zT[6272:6400, t0:t0 + n], reads=[('zT',)], writes=[AC_s])
        for h0 in range(0, 8, NS):
            _rr([body(h0 + k, SL[k], t0, n, j, TH, TH_s, AC, AC_s) for k in range(NS)])
    P.phase_end()


def phase_rwkv_post(P, l, vec, zT, T, S, ymT, t_lo):
    P.phase_begin()
    NS = 4
    blk = P.sbuf("r3_blk", [128, 128])
    P.op('pool', 'memset', writes=['r3_blk'], ap=blk[:], constant=0.0)
    P.op('pool', 'memset', reads=['r3_blk'], writes=['r3_blk'], ap=blk[0:64, 0:64], constant=1.0 / 64)
    P.op('pool', 'memset', reads=['r3_blk'], writes=['r3_blk'], ap=blk[64:128, 64:128], constant=1.0 / 64)
    SL = [{"i": Rot(P, f"r3_i{sl}_", [128, 512], 8), "t": Rot(P, f"r3_t{sl}_", [128, 512], 2), "ob": Rot(P, f"r3_ob{sl}_", [128, 512], 2, dt=BF16)} for sl in range(NS)]
    ps_r = Rot(P, "r3_ps", [128, 512], 8, psum=True)

    def body(hp, q, t0, n):
        rows = slice(hp * 128, (hp + 1) * 128)
        ld = {}
        for k, src in (("o0", S["o"][0][rows]), ("o1", S["o"][1][rows]), ("bo", S["bonus"][rows]), ("g", zT[5120 + hp * 128:5120 + (hp + 1) * 128])):
            t, ts = q["i"].get()
            P.dma('sp', t[:, 0:n], src[:, t0:t0 + n], reads=[('d_o', 0), ('d_o', 1), ('S_bonus',), ('zT',)], writes=[ts])
            ld[k] = (t, ts)
        (o0, o0_s), (o1, o1_s), (bo, bo_s), (g, g_s) = ld["o0"], ld["o1"], ld["bo"], ld["g"]
        yield
        P.op('pool', 'tensor_tensor', reads=[o0_s, o1_s], writes=[o0_s], out=o0[:, 0:n], in0=o0[:, 0:n], in1=o1[:, 0:n], op=ALU.add)
        P.op('act', 'activation', reads=[g_s], writes=[g_s], out=g[:, 0:n], in_=g[:, 0:n], func=AF.Silu)
        yield
        ps, ps_s = ps_r.get()
        P.op('pe', 'matmul', reads=[o0_s, 'r3_blk'], writes=[ps_s], out=ps[:, 0:n], lhsT=blk[:], rhs=o0[:, 0:n], start=True, stop=True)
        cen, cen_s = q["t"].get()
        P.op('dve', 'tensor_tensor', reads=[o0_s, ps_s], writes=[cen_s], out=cen[:, 0:n], in0=o0[:, 0:n], in1=ps[:, 0:n], op=ALU.subtract)
        yield
        sq, sq_s = q["t"].get()
        P.op('act', 'activation', reads=[cen_s], writes=[sq_s], out=sq[:, 0:n], in_=cen[:, 0:n], func=AF.Square)
        yield
        ps, ps_s = ps_r.get()
        P.op('pe', 'matmul', reads=[sq_s, 'r3_blk'], writes=[ps_s], out=ps[:, 0:n], lhsT=blk[:], rhs=sq[:, 0:n], start=True, stop=True)
        P.op('dve', 'tensor_scalar', reads=[ps_s], writes=[sq_s], out=sq[:, 0:n], in0=ps[:, 0:n], scalar1=64e-5, scalar2=None, op0=ALU.add)
        yield
        P.op('act', 'activation', reads=[sq_s], writes=[sq_s], out=sq[:, 0:n], in_=sq[:, 0:n], func=AF.Sqrt)
        yield
        P.op('dve', 'reciprocal', reads=[sq_s], writes=[sq_s], out=sq[:, 0:n], in_=sq[:, 0:n])
        yield
        P.op('dve', 'tensor_tensor', reads=[cen_s, sq_s], writes=[cen_s], out=cen[:, 0:n], in0=cen[:, 0:n], in1=sq[:, 0:n], op=ALU.mult)
        yield
        P.op('dve', 'tensor_scalar', reads=[cen_s, 'vec'], writes=[cen_s], out=cen[:, 0:n], in0=cen[:, 0:n], scalar1=vec[:, VO["gn_w"] + hp:VO["gn_w"] + hp + 1], scalar2=vec[:, VO["gn_b"] + hp:VO["gn_b"] + hp + 1], op0=ALU.mult, op1=ALU.add)
        yield
        P.op('pool', 'tensor_tensor', reads=[cen_s, bo_s], writes=[cen_s], out=cen[:, 0:n], in0=cen[:, 0:n], in1=bo[:, 0:n], op=ALU.add)
        yield
        ob, ob_s = q["ob"].get()
        P.op('dve', 'tensor_tensor', reads=[cen_s, g_s], writes=[ob_s], out=ob[:, 0:n], in0=cen[:, 0:n], in1=g[:, 0:n], op=ALU.mult)
        P.dma('sp', ymT[1024 + hp * 128:1024 + (hp + 1) * 128, t0:t0 + n], ob[:, 0:n], reads=[ob_s], writes=[('ymT',)])
        yield

    for (t0, n, j) in token_tiles(T):
        if t0 < t_lo:
            continue
        for h0 in range(0, 8, NS):
            _rr([body(h0 + k, SL[k], t0, n) for k in range(NS)])
    P.phase_end()


def sin_red(P, out, z, n, tmpi, tmpf, res_out, res_z, res_t):
    P.op('dve', 'tensor_scalar', reads=[res_z], writes=[res_t], out=tmpi, in0=z, scalar1=1.0 / (2 * PI), scalar2=None, op0=ALU.mult)
    P.op('dve', 'tensor_copy', reads=[res_t], writes=[res_t], out=tmpf, in_=tmpi)
    P.op('dve', 'scalar_tensor_tensor', reads=[res_t, res_z], writes=[res_out], out=out, in0=tmpf, scalar=-2 * PI, in1=z, op0=ALU.mult, op1=ALU.add)
    P.op('dve', 'tensor_scalar', reads=[res_out], writes=[res_t], out=tmpf, in0=out, scalar1=PI, scalar2=None, op0=ALU.is_gt)
    P.op('dve', 'scalar_tensor_tensor', reads=[res_t, res_out], writes=[res_out], out=out, in0=tmpf, scalar=-2 * PI, in1=out, op0=ALU.mult, op1=ALU.add)
    P.op('dve', 'tensor_scalar', reads=[res_out], writes=[res_t], out=tmpf, in0=out, scalar1=-PI, scalar2=None, op0=ALU.is_lt)
    P.op('dve', 'scalar_tensor_tensor', reads=[res_t, res_out], writes=[res_out], out=out, in0=tmpf, scalar=2 * PI, in1=out, op0=ALU.mult, op1=ALU.add)
    P.op('dve', 'tensor_scalar', reads=[res_out], writes=[res_out], out=out, in0=out, scalar1=-3.14159, scalar2=3.14159, op0=ALU.max, op1=ALU.min)
    P.op('act', 'activation', reads=[res_out], writes=[res_out], out=out, in_=out, func=AF.Sin)


def phase_s5(P, l, io, vec, zT, T, S, ymT, t_lo, cst):
    TC = 512
    I32 = mybir.dt.int32
    P.phase_begin()
    ident = cst[:, 0:128]
    Jc = cst[:, 768:896]
    iota = cst[:, 896:896 + TC]
    pp = P.sbuf("s5_pp", [128, 3, 64])
    P.dma('sp', pp[:], io[f"s5p{l}"], writes=['s5_pp'])
    rho = P.sbuf("s5_rho", [128, 64]); theta = P.sbuf("s5_theta", [128, 64])
    P.op('act', 'activation', reads=['s5_pp'], writes=['s5_pp'], out=pp[:, 2, :], in_=pp[:, 2, :], func=AF.Exp)
    P.op('dve', 'tensor_scalar', reads=['s5_pp'], writes=['s5_pp'], out=pp[:, 0, :], in0=pp[:, 0, :], scalar1=-1e-4, scalar2=None, op0=ALU.min)
    P.op('dve', 'tensor_tensor', reads=['s5_pp'], writes=['s5_rho'], out=rho[:], in0=pp[:, 0, :], in1=pp[:, 2, :], op=ALU.mult)
    P.op('act', 'activation', reads=['s5_rho'], writes=['s5_rho'], out=rho[:], in_=rho[:], func=AF.Exp)
    P.op('dve', 'tensor_tensor', reads=['s5_pp'], writes=['s5_theta'], out=theta[:], in0=pp[:, 1, :], in1=pp[:, 2, :], op=ALU.mult)
    W1 = P.sbuf("s5_W1", [16, 64, 128], BF16); W2 = P.sbuf("s5_W2", [16, 64, 128], BF16)
    P.phase_begin()
    BT = P.sbuf("s5_BT", [16, 2, 2048])
    P.dma('sp', BT[:], io[f"s5bt{l}"], writes=['s5_BT'])
    rr = P.sbuf("s5_rr", [16, 3, 2048])
    tA = [P.sbuf(f"s5_t{i}", [16, 2048]) for i in range(8)]
    tI = P.sbuf("s5_ti", [16, 2048], I32)
    for d in range(2):
        P.dma('sp', rr[:], io[f"s5r{l}"][:, :, d * 2048:(d + 1) * 2048], reads=['s5_rr'], writes=['s5_rr'])
        re, im, st = rr[:, 0, :], rr[:, 1, :], rr[:, 2, :]
        R_ = ['s5_rr'] + [f"s5_t{i}" for i in range(8)] + ['s5_ti']
        W_ = R_
        def o(eng, name, **kw):
            P.op(eng, name, reads=R_, writes=W_, **kw)
        o('act', 'activation', out=st, in_=st, func=AF.Exp)
        o('dve', 'tensor_scalar', out=re, in0=re, scalar1=-1e-4, scalar2=None, op0=ALU.min)
        er, ang, cs_, sn_, z2 = tA[0][:], tA[1][:], tA[2][:], tA[3][:], tA[4][:]
        o('dve', 'tensor_tensor', out=er, in0=re, in1=st, op=ALU.mult)
        o('act', 'activation', out=er, in_=er, func=AF.Exp)
        o('dve', 'tensor_tensor', out=ang, in0=im, in1=st, op=ALU.mult)
        sin_red(P, sn_, ang, 2048, tI[:], tA[5][:], 's5_t3', 's5_t1', 's5_t5')
        o('dve', 'tensor_scalar', out=z2, in0=ang, scalar1=PI / 2, scalar2=None, op0=ALU.add)
        sin_red(P, cs_, z2, 2048, tI[:], tA[5][:], 's5_t2', 's5_t4', 's5_t5')
        P.barrier()
        nre, nim = tA[2][:], tA[3][:]
        o('dve', 'tensor_tensor', out=nre, in0=er, in1=cs_, op=ALU.mult)
        o('dve', 'tensor_scalar', out=nre, in0=nre, scalar1=-1.0, scalar2=None, op0=ALU.add)
        o('dve', 'tensor_tensor', out=nim, in0=er, in1=sn_, op=ALU.mult)
        den, t5, bsr, bsi = tA[0][:], tA[5][:], tA[6][:], tA[7][:]
        o('dve', 'tensor_tensor', out=den, in0=re, in1=re, op=ALU.mult)
        o('dve', 'tensor_tensor', out=t5, in0=im, in1=im, op=ALU.mult)
        o('dve', 'tensor_tensor', out=den, in0=den, in1=t5, op=ALU.add)
        o('dve', 'reciprocal', out=den, in_=den)
        o('dve', 'tensor_tensor', out=bsr, in0=nre, in1=re, op=ALU.mult)
        o('dve', 'tensor_tensor', out=t5, in0=nim, in1=im, op=ALU.mult)
        o('dve', 'tensor_tensor', out=bsr, in0=bsr, in1=t5, op=ALU.add)
        o('dve', 'tensor_tensor', out=bsr, in0=bsr, in1=den, op=ALU.mult)
        o('dve', 'tensor_tensor', out=bsi, in0=nim, in1=re, op=ALU.mult)
        o('dve', 'tensor_tensor', out=t5, in0=nre, in1=im, op=ALU.mult)
        o('dve', 'tensor_tensor', out=bsi, in0=bsi, in1=t5, op=ALU.subtract)
        o('dve', 'tensor_tensor', out=bsi, in0=bsi, in1=den, op=ALU.mult)
        Br, Bi = BT[:, 0, :], BT[:, 1, :]
        bpr, bpi, t4 = tA[1][:], tA[2][:], tA[4][:]
        Rb = R_ + ['s5_BT']
        P.op('dve', 'tensor_tensor', reads=Rb, writes=W_, out=bpr, in0=bsr, in1=Br, op=ALU.mult)
        P.op('dve', 'tensor_tensor', reads=Rb, writes=W_, out=t4, in0=bsi, in1=Bi, op=ALU.mult)
        o('dve', 'tensor_tensor', out=bpr, in0=bpr, in1=t4, op=ALU.subtract)
        P.op('dve', 'tensor_tensor', reads=Rb, writes=W_, out=bpi, in0=bsr, in1=Bi, op=ALU.mult)
        P.op('dve', 'tensor_tensor', reads=Rb, writes=W_, out=t4, in0=bsi, in1=Br, op=ALU.mult)
        o('dve', 'tensor_tensor', out=bpi, in0=bpi, in1=t4, op=ALU.add)
        g3 = lambda ap: ap.rearrange("p (g s) -> p g s", s=64)
        dg = slice(d * 32, (d + 1) * 32)
        P.op('dve', 'tensor_copy', reads=R_, writes=['s5_W1'], out=W1[:, dg, 0:64], in_=g3(bpr))
        P.op('dve', 'tensor_copy', reads=R_ + ['s5_W1'], writes=['s5_W1'], out=W1[:, dg, 64:128], in_=g3(bpi))
        P.op('dve', 'tensor_copy', reads=R_, writes=['s5_W2'], out=W2[:, dg, 0:64], in_=g3(bpi))
        P.op('dve', 'tensor_scalar', reads=R_ + ['s5_W2'], writes=['s5_W2'], out=W2[:, dg, 64:128], in0=g3(bpr), scalar1=-1.0, scalar2=None, op0=ALU.mult)
        P.barrier()
    P.phase_end()
    CT1f = P.sbuf("s5_CT1f", [128, 64, 16]); CT2f = P.sbuf("s5_CT2f", [128, 64, 16])
    CT1 = P.sbuf("s5_CT1", [128, 64, 16], BF16); CT2 = P.sbuf("s5_CT2", [128, 64, 16], BF16)
    P.dma('sp', CT1f[:], io[f"s5ca{l}"], writes=['s5_CT1f'])
    P.dma('sp', CT2f[:], io[f"s5cb{l}"], writes=['s5_CT2f'])
    P.op('dve', 'tensor_copy', reads=['s5_CT1f'], writes=['s5_CT1'], out=CT1[0:64], in_=CT1f[0:64])
    P.op('dve', 'tensor_scalar', reads=['s5_CT1f', 's5_CT1'], writes=['s5_CT1'], out=CT1[64:128], in0=CT1f[64:128], scalar1=-1.0, scalar2=None, op0=ALU.mult)
    P.op('dve', 'tensor_scalar', reads=['s5_CT2f'], writes=['s5_CT2'], out=CT2[:], in0=CT2f[:], scalar1=-1.0, scalar2=None, op0=ALU.mult)
    GS = 4
    zt = P.sbuf("s5_zt", [128, TC]); zi = P.sbuf("s5_zi", [128, TC], I32); zf = P.sbuf("s5_zf", [128, TC])
    SB = []
    for sl in range(GS):
        q = {"COS": P.sbuf(f"s5_COS{sl}", [128, TC]), "SIN": P.sbuf(f"s5_SIN{sl}", [128, TC]), "RHO": P.sbuf(f"s5_RHO{sl}", [128, TC]),
             "u": Rot(P, f"s5_u{sl}_", [16, TC], 2), "ub": Rot(P, f"s5_ub{sl}_", [16, TC], 2, dt=BF16), "t": Rot(P, f"s5_tt{sl}_", [128, TC], 3),
             "tb": Rot(P, f"s5_tb{sl}_", [128, TC], 2, dt=BF16), "l": Rot(P, f"s5_l{sl}_", [128, 2], 2), "y": Rot(P, f"s5_y{sl}_", [16, TC], 2),
             "st": Rot(P, f"s5_st{sl}_", [128, 1], 2), "sl": sl}
        SB.append(q)
    ps_r = Rot(P, "s5_ps", [128, 512], 8, psum=True)
    blocks = [(0, NCTX)] + [(t, 512) for t in range(NCTX, T, 512)]

    def chain(d, g, q):
        sl = q["sl"]
        COS, SIN, RHO = q["COS"], q["SIN"], q["RHO"]
        cr, sr, rr_ = f"s5_COS{sl}", f"s5_SIN{sl}", f"s5_RHO{sl}"
        order = blocks if d == 0 else [blocks[0]] + blocks[:0:-1]
        dg = d * 32 + g
        P.op('dve', 'tensor_scalar', reads=['rwc', 's5_theta'], writes=['s5_zt'], out=zt[:], in0=iota, scalar1=theta[:, dg:dg + 1], scalar2=None, op0=ALU.mult)
        sin_red(P, SIN[:], zt[:], TC, zi[:], zf[:], sr, 's5_zt', 's5_zf')
        P.op('dve', 'tensor_scalar', reads=['s5_zt'], writes=['s5_zt'], out=zt[:], in0=zt[:], scalar1=PI / 2, scalar2=None, op0=ALU.add)
        sin_red(P, COS[:], zt[:], TC, zi[:], zf[:], cr, 's5_zt', 's5_zf')
        P.op('dve', 'tensor_scalar', reads=['rwc', 's5_rho'], writes=[rr_], out=RHO[:], in0=iota, scalar1=0.0, scalar2=rho[:, dg:dg + 1], op0=ALU.mult, op1=ALU.add)
        yield
        stc = None
        rv = (lambda ap: ap) if d == 0 else (lambda ap: ap[:, ::-1])
        for (t0, n) in order:
            u, u_s = q["u"].get()
            P.dma('sp', u[:, 0:n], zT[16 * g:16 * g + 16, t0:t0 + n], reads=[('zT',)], writes=[u_s])
            ub, ub_s = q["ub"].get()
            P.op('pool', 'tensor_copy', reads=[u_s], writes=[ub_s], out=ub[:, 0:n], in_=u[:, 0:n])
            p1, p1_s = ps_r.get()
            P.op('pe', 'matmul', reads=[ub_s, 's5_W1'], writes=[p1_s], out=p1[:, 0:n], lhsT=W1[:, dg, :], rhs=ub[:, 0:n], start=True, stop=True)
            p2, p2_s = ps_r.get()
            P.op('pe', 'matmul', reads=[ub_s, 's5_W2'], writes=[p2_s], out=p2[:, 0:n], lhsT=W2[:, dg, :], rhs=ub[:, 0:n], start=True, stop=True)
            t1, t1_s = q["t"].get(); t2, t2_s = q["t"].get()
            P.op('dve', 'tensor_tensor', reads=[p1_s, cr], writes=[t1_s], out=t1[:, 0:n], in0=rv(p1[:, 0:n]), in1=COS[:, 0:n], op=ALU.mult)
            P.op('dve', 'tensor_tensor', reads=[p2_s, sr], writes=[t2_s], out=t2[:, 0:n], in0=rv(p2[:, 0:n]), in1=SIN[:, 0:n], op=ALU.mult)
            yield
            P.op('pool', 'tensor_tensor', reads=[t1_s, t2_s], writes=[t1_s], out=t1[:, 0:n], in0=t1[:, 0:n], in1=t2[:, 0:n], op=ALU.add)
            yield
            Wt, Wt_s = q["t"].get()
            if stc is None:
                P.op('dve', 'tensor_tensor_scan', reads=[t1_s, rr_], writes=[Wt_s], out=Wt[:, 0:n], data0=RHO[:, 0:n], data1=t1[:, 0:n], initial=0.0, op0=ALU.mult, op1=ALU.add)
            else:
                P.op('dve', 'tensor_tensor_scan', reads=[t1_s, rr_, stc[1]], writes=[Wt_s], out=Wt[:, 0:n], data0=RHO[:, 0:n], data1=t1[:, 0:n], initial=stc[0][:, 0:1], op0=ALU.mult, op1=ALU.add)
            yield
            t3, t3_s = q["tb"].get(); t4, t4_s = q["tb"].get()
            ll, ll_s = q["l"].get()
            P.op('dve', 'tensor_tensor', reads=[Wt_s, cr], writes=[ll_s], out=ll[:, 0:1], in0=Wt[:, n - 1:n], in1=COS[:, n - 1:n], op=ALU.mult)
            P.op('dve', 'tensor_tensor', reads=[Wt_s, sr, ll_s], writes=[ll_s], out=ll[:, 1:2], in0=Wt[:, n - 1:n], in1=SIN[:, n - 1:n], op=ALU.mult)
            P.op('dve', 'tensor_tensor', reads=[Wt_s, cr], writes=[t3_s], out=t3[:, 0:n], in0=Wt[:, 0:n], in1=COS[:, 0:n], op=ALU.mult)
            P.op('pool', 'tensor_tensor', reads=[Wt_s, sr], writes=[t4_s], out=t4[:, 0:n], in0=Wt[:, 0:n], in1=SIN[:, 0:n], op=ALU.mult)
            yield
            px, px_s = ps_r.get()
            P.op('pe', 'matmul', reads=[ll_s, 'rwc'], writes=[px_s], out=px[:, 0:1], lhsT=ident, rhs=ll[:, 0:1], start=True, stop=False)
            P.op('pe', 'matmul', reads=[ll_s, 'rwc', px_s], writes=[px_s], out=px[:, 0:1], lhsT=Jc, rhs=ll[:, 1:2], start=False, stop=True)
            sn, sn_s = q["st"].get()
            P.op('act', 'activation', reads=[px_s], writes=[sn_s], out=sn[:], in_=px[:, 0:1], func=AF.Copy)
            stc = (sn, sn_s)
            py, py_s = ps_r.get()
            P.op('pe', 'matmul', reads=[t3_s, 's5_CT1'], writes=[py_s], out=py[0:16, 0:n], lhsT=CT1[:, dg, :], rhs=t3[:, 0:n], start=True, stop=False)
            P.op('pe', 'matmul', reads=[t4_s, 's5_CT2', py_s], writes=[py_s], out=py[0:16, 0:n], lhsT=CT2[:, dg, :], rhs=t4[:, 0:n], start=False, stop=True)
            y, y_s = q["y"].get()
            P.op('act', 'activation', reads=[py_s], writes=[y_s], out=rv(y[:, 0:n]), in_=py[0:16, 0:n], func=AF.Copy)
            P.dma('sp', S["ys"][d][16 * g:16 * g + 16, t0:t0 + n], y[:, 0:n], reads=[y_s], writes=[('S_ys', d)])
            yield

    for d in range(2):
        for gb in range(0, 32, GS):
            live = [chain(d, gb + k, SB[k]) for k in range(GS)]
            while live:
                nxt = []
                for gen in live:
                    try:
                        next(gen)
                        nxt.append(gen)
                    except StopIteration:
                        pass
                live = nxt
    P.phase_end()
    P.phase_begin()
    ps_r = Rot(P, "s5c_ps", [128, 512], 4, psum=True)
    wg = P.sbuf("s5_wg", [128, 4, 512])
    P.dma('sp', wg[:], io[f"w_glu{l}"].rearrange("(kc p) m -> p kc m", p=128), writes=['s5_wg'])
    YG_r = Rot(P, "s5_YG", [128, 4, 512], 2)
    i_r = Rot(P, "s5_i", [128, 512], 8)
    G_r = Rot(P, "s5_G", [128, 4, 512], 2)
    ob_r = Rot(P, "s5_ob", [128, 512], 3, dt=BF16)
    for (t0, n, j) in token_tiles(T):
        if t0 < t_lo:
            continue
        YG, YG_s = YG_r.get()
        G, G_s = G_r.get()
        gts = []
        for c in range(4):
            rows = slice(c * 128, (c + 1) * 128)
            ld = []
            for src, rs in ((zT[rows], ('zT',)), (S["ys"][0][rows], ('S_ys', 0)), (S["ys"][1][rows], ('S_ys', 1))):
                t, ts = i_r.get()
                P.dma('sp', t[:, 0:n], src[:, t0:t0 + n], reads=[rs], writes=[ts])
                ld.append((t, ts))
            (u, u_s), (y0, y0_s), (y1, y1_s) = ld
            P.dma('sp', G[:, c, 0:n], zT[512 + c * 128:512 + (c + 1) * 128, t0:t0 + n], reads=[('zT',), G_s], writes=[G_s])
            P.op('dve', 'scalar_tensor_tensor', reads=[u_s, 'vec', y0_s], writes=[y0_s], out=y0[:, 0:n], in0=u[:, 0:n], scalar=vec[:, VO["d_skip"] + c:VO["d_skip"] + c + 1], in1=y0[:, 0:n], op0=ALU.mult, op1=ALU.add)
            P.op('pool', 'tensor_tensor', reads=[y0_s, y1_s], writes=[y0_s], out=y0[:, 0:n], in0=y0[:, 0:n], in1=y1[:, 0:n], op=ALU.add)
            P.op('act', 'activation', reads=[y0_s], writes=[y1_s], out=y1[:, 0:n], in_=y0[:, 0:n], func=AF.Square)
            P.op('dve', 'tensor_scalar', reads=[y1_s], writes=[y1_s], out=y1[:, 0:n], in0=y1[:, 0:n], scalar1=0.044715, scalar2=1.0, op0=ALU.mult, op1=ALU.add)
            P.op('dve', 'tensor_tensor', reads=[y1_s, y0_s], writes=[y1_s], out=y1[:, 0:n], in0=y1[:, 0:n], in1=y0[:, 0:n], op=ALU.mult)
            P.op('act', 'activation', reads=[y1_s], writes=[y1_s], out=y1[:, 0:n], in_=y1[:, 0:n], func=AF.Sigmoid, scale=1.5957691216057308)
            P.op('dve', 'tensor_tensor', reads=[y1_s, y0_s, YG_s], writes=[YG_s], out=YG[:, c, 0:n], in0=y1[:, 0:n], in1=y0[:, 0:n], op=ALU.mult)
        P.op('act', 'activation', reads=[G_s], writes=[G_s], out=G[:, :, 0:n], in_=G[:, :, 0:n], func=AF.Silu)
        for m in range(4):
            ps, ps_s = ps_r.get()
            for kc in range(4):
                P.op('pe', 'matmul', reads=[YG_s, 's5_wg', ps_s], writes=[ps_s], out=ps[:, 0:n], lhsT=wg[:, kc, m * 128:(m + 1) * 128], rhs=YG[:, kc, 0:n], start=(kc == 0), stop=(kc == 3))
            sg, sg_s = i_r.get()
            P.op('act', 'activation', reads=[ps_s, 'vec'], writes=[sg_s], out=sg[:, 0:n], in_=ps[:, 0:n], func=AF.Sigmoid, bias=vec[:, VO["b_glu"] + m:VO["b_glu"] + m + 1], scale=1.0)
            P.op('dve', 'tensor_tensor', reads=[sg_s, YG_s], writes=[sg_s], out=sg[:, 0:n], in0=sg[:, 0:n], in1=YG[:, m, 0:n], op=ALU.mult)
            ob, ob_s = ob_r.get()
            P.op('dve', 'tensor_tensor', reads=[sg_s, G_s], writes=[ob_s], out=ob[:, 0:n], in0=sg[:, 0:n], in1=G[:, m, 0:n], op=ALU.mult)
            P.dma('sp', ymT[m * 128:(m + 1) * 128, t0:t0 + n], ob[:, 0:n], reads=[ob_s], writes=[('ymT',)])
    P.phase_end()


def phase_pool(P, l, io, vec, zT, T, ymT, t_lo):
    P.phase_begin()
    rows_tot = (T - NCTX) // 64
    wp = P.sbuf("pl_wp", [128, 4, 128])
    P.dma('sp', wp[:], io[f"w_pool{l}"], writes=['pl_wp'])
    icn = P.sbuf("pl_icn", [128, 4, rows_tot + 64 + NCTX])
    P.dma('sp', icn[:], io["icnt"], writes=['pl_icn'])
    ps_r = Rot(P, "pl_ps", [128, 512], 4, psum=True)
    g_r = Rot(P, "pl_g", [128, 512], 3)
    ob_r = Rot(P, "pl_ob", [128, 512], 3, dt=BF16)
    regions = []
    RB = min(64, rows_tot)
    for r0 in range(0, rows_tot, RB):
        regions.append((NCTX, rows_tot, 64, r0, RB, 0, rows_tot))
    if t_lo == 0:
        for r0 in range(0, NCTX, 64):
            regions.append((0, NCTX, 1, r0, 64, rows_tot + 64, None))
    bufA = {64: [P.sbuf(f"pl_A{i}", [128, RB + 16, 80]) for i in range(3)], 1: [P.sbuf(f"pl_B{i}", [128, 80, 1]) for i in range(3)]}
    dif = {64: P.sbuf("pl_d64", [128, RB, 64]), 1: P.sbuf("pl_d1", [128, 64, 1])}
    for gi, w in enumerate((2, 4, 8, 16)):
        zr = slice(1024 + gi * 128, 1024 + (gi + 1) * 128)
        for (tok0, Rtot, C, r0, nr, ico, icc) in regions:
            A = bufA[C]
            An = [f"pl_{'A' if C == 64 else 'B'}{i}" for i in range(3)]
            CP = C + 16 if C == 64 else 1
            c0 = 8 if C == 64 else 0
            U, Us = A[0], An[0]
            P.op('pool', 'memset', reads=[Us], writes=[Us], ap=U[:], constant=0.0)
            ra = max(r0 - 8, 0); rb = min(r0 + nr + 8, Rtot)
            P.dma('sp', U[:, 8 + ra - r0:8 + rb - r0, c0:c0 + C], zT[zr, tok0 + ra * C:tok0 + rb * C].rearrange("p (r c) -> p r c", c=C), reads=[('zT',), Us], writes=[Us])
            NR = nr + 16
            cur, cur_s = U, Us
            idx = 0
            s = 1
            while s < w:
                nidx = 1 if idx != 1 else 2
                nxt, nxt_s = A[nidx], An[nidx]
                P.op('dve', 'tensor_tensor', reads=[cur_s, nxt_s], writes=[nxt_s], out=nxt[:, 0:NR - s, :], in0=cur[:, 0:NR - s, :], in1=cur[:, s:NR, :], op=ALU.add)
                cur, cur_s, idx = nxt, nxt_s, nidx
                s *= 2
            nidx = 1 if idx != 1 else 2
            rm, rm_s = A[nidx], An[nidx]
            P.op('dve', 'tensor_tensor', reads=[cur_s, 'pl_icn', rm_s], writes=[rm_s], out=rm[:, 0:nr, :], in0=cur[:, 8 - w // 2:8 - w // 2 + nr, :],
                 in1=icn[:, gi, ico + r0:ico + r0 + nr].unsqueeze(2).to_broadcast([128, nr, CP]), op=ALU.mult)
            cur, cur_s, idx = rm, rm_s, nidx
            if C == 64:
                s = 1
                while s < w:
                    nidx = [i for i in (1, 2) if i != idx][0] if idx != 0 else 1
                    nxt, nxt_s = A[nidx], An[nidx]
                    P.op('dve', 'tensor_tensor', reads=[cur_s, nxt_s], writes=[nxt_s], out=nxt[:, 0:nr, 0:CP - s], in0=cur[:, 0:nr, 0:CP - s], in1=cur[:, 0:nr, s:CP], op=ALU.add)
                    cur, cur_s, idx = nxt, nxt_s, nidx
                    s *= 2
                nidx = [i for i in (1, 2) if i != idx][0]
                cm, cm_s = A[nidx], An[nidx]
                P.op('dve', 'tensor_tensor', reads=[cur_s, 'pl_icn', cm_s], writes=[cm_s], out=cm[:, 0:nr, 0:64], in0=cur[:, 0:nr, 8 - w // 2:8 - w // 2 + 64],
                     in1=icn[:, gi, icc:icc + 64].unsqueeze(1).to_broadcast([128, nr, 64]), op=ALU.mult)
                cur, cur_s = cm, cm_s
            df, df_s = dif[C], ("pl_d", C)
            P.op('dve', 'tensor_tensor', reads=[cur_s, Us, df_s], writes=[df_s], out=df[:, 0:nr, :], in0=cur[:, 0:nr, 0:C], in1=U[:, 8:8 + nr, c0:c0 + C], op=ALU.subtract)
            ntok = nr * C
            dflat = df[:, 0:nr, :].rearrange("p r c -> p (r c)")
            for q0 in range(0, ntok, 512):
                qn = min(512, ntok - q0)
                ps, ps_s = ps_r.get()
                P.op('pe', 'matmul', reads=[df_s, 'pl_wp'], writes=[ps_s], out=ps[:, 0:qn], lhsT=wp[:, gi, :], rhs=dflat[:, q0:q0 + qn], start=True, stop=True)
                tk = tok0 + r0 * C + q0
                gt, gt_s = g_r.get()
                P.dma('sp', gt[:, 0:qn], zT[1536 + gi * 128:1536 + (gi + 1) * 128, tk:tk + qn], reads=[('zT',)], writes=[gt_s])
                P.op('act', 'activation', reads=[gt_s], writes=[gt_s], out=gt[:, 0:qn], in_=gt[:, 0:qn], func=AF.Silu)
                ob, ob_s = ob_r.get()
                P.op('dve', 'scalar_tensor_tensor', reads=[ps_s, 'vec', gt_s], writes=[ob_s], out=ob[:, 0:qn], in0=ps[:, 0:qn], scalar=vec[:, VO["pool_scale"] + gi:VO["pool_scale"] + gi + 1], in1=gt[:, 0:qn], op0=ALU.mult, op1=ALU.mult)
                P.dma('sp', ymT[512 + gi * 128:512 + (gi + 1) * 128, tk:tk + qn], ob[:, 0:qn], reads=[ob_s], writes=[('ymT',)])
    P.phase_end()


class RowSplit:
    def __init__(self, aps, rows_per):
        self.aps = aps
        self.rp = rows_per

    def __getitem__(self, key):
        if isinstance(key, tuple):
            rs, cs = key
        else:
            rs, cs = key, None
        b = rs.start // self.rp
        assert (rs.stop - 1) // self.rp == b
        ap = self.aps[b][rs.start - b * self.rp:rs.stop - b * self.rp]
        return ap if cs is None else ap[:, cs]

def build(N, nlayers=2, debug=False):
    T = NCTX + N
    nc = bass.Bass("TRN2", target_bir_lowering=False)
    io = {}

    def din(name, shape):
        io[name] = nc.dram_tensor(name, list(shape), F32, kind="ExternalInput").ap()

    def dscr(name, shape):
        return nc.dram_tensor(name, list(shape), F32, kind=("ExternalOutput" if debug else "Internal")).ap()
    rows_tot = N // 64
    din("xT", [D, T]); din("ccT", [128, KC, 2]); din("cst", [128, 1408]); din("icnt", [128, 4, rows_tot + 64 + NCTX])
    for l in range(2):
        din(f"w_ada{l}", [D, 6144]); din(f"w_in{l}", [D, DIN]); din(f"w_out{l}", [D, D]); din(f"vec{l}", [128, VW])
        din(f"s5p{l}", [128, 3, 64]); din(f"s5r{l}", [16, 3, 4096]); din(f"s5bt{l}", [16, 2, 2048])
        din(f"s5ca{l}", [128, 64, 16]); din(f"s5cb{l}", [128, 64, 16]); din(f"w_glu{l}", [512, 512])
        din(f"w_pool{l}", [128, 4, 128]); din(f"rw_w2{l}", [128, 1024]); din(f"rw_a2{l}", [128, 1024])
    yT = nc.dram_tensor("yT", [D, N], F32, kind="ExternalOutput").ap()
    zT = RowSplit([dscr(f"zT{i}", [min(1024, DIN - i * 1024), T]) for i in range(7)], 1024); ymT = nc.dram_tensor("ymT", [D, T], BF16, kind="Internal").ap(); x1T = dscr("x1T", [D, T])
    wbf = {"in": nc.dram_tensor("winb", [D, DIN], BF16, kind="Internal").ap(), "out": nc.dram_tensor("woutb", [D, D], BF16, kind="Internal").ap()}
    S = {"r": dscr("S_r", [1024, T]), "kk": dscr("S_kk", [1024, T]), "v": dscr("S_v", [1024, T]), "bonus": dscr("S_bonus", [1024, T]),
         "lw": [dscr(f"S_lw{d}", [1024, T]) for d in range(2)], "ka": [dscr(f"S_ka{d}", [1024, T]) for d in range(2)],
         "kd": [dscr(f"S_kd{d}", [1024, T]) for d in range(2)], "o": [dscr(f"S_o{d}", [1024, T]) for d in range(2)],
         "ys": [dscr(f"S_ys{d}", [512, T]) for d in range(2)]}
    P = Prog(nc)
    cst = P.sbuf("rwc", [128, 1408])
    P.dma('sp', cst[:], io["cst"], writes=['rwc'])
    onesD = P.sbuf("onesD", [128, 128])
    P.op('pool', 'memset', writes=['onesD'], ap=onesD[:], constant=1.0 / D)
    mod = P.sbuf("mod", [128, 48, 2])
    vec = P.sbuf("vec", [128, VW])
    for l in range(nlayers):
        t_lo = 0 if l == 0 else NCTX
        P.barrier()
        P.dma('sp', vec[:], io[f"vec{l}"], reads=['vec'], writes=['vec'])
        xin = io["xT"] if l == 0 else x1T
        xout = x1T if l == 0 else yT
        SK = []
        phase_mod(P, l, io, mod, vec, wbf)
        if "inproj" not in SK:
            phase_inproj(P, l, io, mod, xin, zT, T, onesD, wbf)
        if "s5" not in SK:
            phase_s5(P, l, io, vec, zT, T, S, ymT, t_lo, cst)
        if "pool" not in SK:
            phase_pool(P, l, io, vec, zT, T, ymT, t_lo)
        if "prep" not in SK:
            phase_rwkv_prep(P, l, io, vec, zT, T, S)
        if "core" not in SK:
            P.phase_begin()
            rwkv_core(P, cst, 16, [(0, NCTX // L), (NCTX, N // L)], S["r"], S["kk"], S["v"], S["lw"], S["ka"], S["kd"], S["o"])
            P.phase_end()
        if "post" not in SK:
            phase_rwkv_post(P, l, vec, zT, T, S, ymT, t_lo)
        phase_outproj(P, l, io, mod, vec, xin, ymT, xout, T, onesD, t_lo, wbf)
    P.finish([('x1T', nlayers - 1)])
    return nc


def _pm(v):
    v = np.asarray(v, np.float32).reshape(-1)
    return np.ascontiguousarray(v.reshape(-1, 128).T)


def host_layout(inp, b, N):
    f32 = np.float32
    A = lambda a: np.ascontiguousarray(np.asarray(a), dtype=f32)
    m = {}
    m["xT"] = A(np.concatenate([np.asarray(inp["ctx"][b]).T, np.asarray(inp["x"][b, :N]).T], axis=1))
    cc = np.stack([np.asarray(inp["c"][b]), np.asarray(inp["c_ctx"])], 0)
    m["ccT"] = A(cc.reshape(2, KC, 128).transpose(2, 1, 0))
    i = np.arange(128)[:, None]; j = np.arange(128)[None, :]
    Jc = np.zeros((128, 128), f32)
    for p in range(64):
        Jc[64 + p, p] = -1.0
        Jc[p, 64 + p] = 1.0
    iota = np.tile(np.arange(1, 513, dtype=f32)[None, :], (128, 1))
    m["cst"] = A(np.concatenate([np.eye(128), (i < j), (i <= j), (i > j), (i >= j), np.ones((128, 128)), Jc, iota], 1))
    rows_tot = N // 64

    def invcnt(n, w):
        idx = np.arange(n)
        lo = np.clip(idx - w // 2, 0, n - 1); hi = np.clip(idx - w // 2 + w - 1, 0, n - 1)
        return (1.0 / (hi - lo + 1)).astype(f32)
    ic = np.stack([np.concatenate([invcnt(rows_tot, w), invcnt(64, w), invcnt(NCTX, w)]) for w in (2, 4, 8, 16)], 0)
    m["icnt"] = A(np.tile(ic[None], (128, 1, 1)))
    for l in range(2):
        g = lambda k: np.asarray(inp[k][l])
        m[f"w_ada{l}"] = A(g("w_ada")); m[f"w_in{l}"] = A(g("w_in")); m[f"w_out{l}"] = A(g("w_out"))
        w0 = g("rwkv_w0").reshape(2, 8, 128).transpose(2, 0, 1).reshape(128, 16)
        a0 = g("rwkv_a0").reshape(2, 8, 128).transpose(2, 0, 1).reshape(128, 16)
        conv = g("conv_rkv").reshape(3, 24, 128).transpose(2, 1, 0).reshape(128, 72)
        parts = [_pm(g("b_ada")), _pm(g("s5_d")), _pm(g("b_glu")), _pm(g("pool_scale")), _pm(g("rwkv_k_k")), _pm(g("rwkv_k_a")),
                 _pm(g("rwkv_r_k")), _pm(g("gn_w")), _pm(g("gn_b")), _pm(g("ln_g")), _pm(g("ln_b")), w0, a0, conv]
        m[f"vec{l}"] = A(np.concatenate(parts, 1))
        lr = g("s5_lam_re").reshape(64, 64).T; li = g("s5_lam_im").reshape(64, 64).T
        ls = np.tile(g("s5_log_step").reshape(1, 64), (64, 1))
        sp = np.stack([lr, li, ls], 1)
        m[f"s5p{l}"] = A(np.concatenate([sp, sp], 0))
        row = np.stack([g("s5_lam_re").reshape(4096), g("s5_lam_im").reshape(4096), np.repeat(g("s5_log_step").reshape(64), 64)], 0)
        m[f"s5r{l}"] = A(np.tile(row[None], (16, 1, 1)))
        m[f"s5bt{l}"] = A(np.stack([g("s5_b_re").transpose(2, 0, 1).reshape(16, 2048), g("s5_b_im").transpose(2, 0, 1).reshape(16, 2048)], 1))
        cr = g("s5_c_re").transpose(3, 0, 1, 2).reshape(64, 64, 16); ci = g("s5_c_im").transpose(3, 0, 1, 2).reshape(64, 64, 16)
        m[f"s5ca{l}"] = A(np.concatenate([cr, ci], 0)); m[f"s5cb{l}"] = A(np.concatenate([ci, cr], 0))
        m[f"w_glu{l}"] = A(g("w_glu")); m[f"w_pool{l}"] = A(g("w_pool").transpose(1, 0, 2))
        m[f"rw_w2{l}"] = A(g("rwkv_w2").reshape(128, 1024)); m[f"rw_a2{l}"] = A(g("rwkv_a2").reshape(128, 1024))
    return m


_NC_CACHE = {}


def run(inputs, N, ncores=8, nlayers=2, debug=False, trace=False):
    key = (N, nlayers, debug)
    if key not in _NC_CACHE:
        _NC_CACHE[key] = build(N, nlayers, debug)
    nc = _NC_CACHE[key]
    maps = [host_layout(inputs, b, N) for b in range(2)]
    in_maps = [maps[c % 2] for c in range(ncores)]
    if trace:
        res = run_bass_kernel_spmd(nc, in_maps, core_ids=list(range(ncores)), trace=True)
        print("EXEC_TIME_NS", res.exec_time_ns)
    else:
        res = run_bass_kernel_spmd(nc, in_maps, core_ids=list(range(ncores)))
    if debug:
        return res
    out = np.stack([np.ascontiguousarray(res.results[b]["yT"].T) for b in range(2)], 0)
    return out.astype(np.float32)


def kernel(**inputs):
    return run(inputs, 16384, ncores=2)
```

```python
import numpy as np
from contextlib import ExitStack
import concourse.bass as bass
import concourse.mybir as mybir
from concourse.bass_utils import run_bass_kernel_spmd

F32 = mybir.dt.float32
BF16 = mybir.dt.bfloat16
AF = mybir.ActivationFunctionType
ALU = mybir.AluOpType
AX = mybir.AxisListType

SEM_CHUNK = 20000
DMA_K = 8
DMA_CHUNK = 2000


class Prog:
    ENG = ('pe', 'dve', 'act', 'pool', 'sp')

    def __init__(self, nc):
        self.nc = nc
        self.st = ExitStack()
        self.sem_st = ExitStack()
        self.ops = {e: [] for e in self.ENG}
        self.n = {e: 0 for e in self.ENG}
        self.sems = {e: [] for e in self.ENG}
        self.waited_c = {e: {x: 0 for x in self.ENG} for e in self.ENG}
        self.waited_d = {e: {} for e in self.ENG}
        self.lastw = {}
        self.readers = {}
        self.dma_n = {e: 0 for e in self.ENG}
        self.dma_sems = {e: {} for e in self.ENG}
        self.nsem = 0
        self.ninst = 0

    def sbuf(self, name, shape, dt=F32):
        self.ninst += 0
        self._uid = getattr(self, '_uid', 0) + 1
        return self.st.enter_context(self.nc.sbuf_tensor(f"{name}_u{self._uid}", list(shape), dt))

    def psum(self, name, shape, dt=F32):
        self._uid = getattr(self, '_uid', 0) + 1
        return self.st.enter_context(self.nc.psum_tensor(f"{name}_u{self._uid}", list(shape), dt))

    def _newsem(self, name):
        self.nsem += 1
        return self.sem_st.enter_context(self.nc.semaphore(name))

    def _csem(self, eng, n):
        idx = (n - 1) // SEM_CHUNK
        while len(self.sems[eng]) <= idx:
            self.sems[eng].append(self._newsem(f"s_{eng}_{len(self.sems[eng])}"))
        return self.sems[eng][idx], (n - 1) % SEM_CHUNK + 1

    def _dsem(self, q, j):
        slot = j % DMA_K
        cnt = j // DMA_K
        key = (slot, cnt // DMA_CHUNK)
        if key not in self.dma_sems[q]:
            self.dma_sems[q][key] = self._newsem(f"d_{q}_{slot}_{cnt // DMA_CHUNK}")
        return self.dma_sems[q][key], 16 * (cnt % DMA_CHUNK + 1), key

    def _wait(self, eng, tok):
        if tok is None:
            return
        if tok[0] == 'c':
            _, e2, n = tok
            if e2 == eng and eng == 'pe':
                return
            if self.waited_c[eng][e2] >= n:
                return
            self.waited_c[eng][e2] = n
            sem, val = self._csem(e2, n)
        else:
            _, q, j = tok
            sem, val, key = self._dsem(q, j)
            k2 = (q, key)
            if self.waited_d[eng].get(k2, 0) >= val:
                return
            self.waited_d[eng][k2] = val
        self.ops[eng].append(lambda e, sem=sem, val=val: e.wait_ge(sem, val))

    def _deps(self, eng, reads, writes):
        toks = []
        for r in reads:
            if r in self.lastw:
                toks.append(self.lastw[r])
        for w in writes:
            if w in self.lastw:
                toks.append(self.lastw[w])
            for t in self.readers.get(w, {}).values():
                toks.append(t)
        for t in toks:
            self._wait(eng, t)

    def _commit(self, tok, reads, writes):
        for w in writes:
            self.lastw[w] = tok
            self.readers[w] = {}
        for r in reads:
            if r in writes:
                continue
            d = self.readers.setdefault(r, {})
            if tok[0] == 'c':
                d[('c', tok[1])] = tok
            else:
                q, j = tok[1], tok[2]
                d[('d', q, j % DMA_K)] = tok

    def op(self, eng, name, reads=(), writes=(), **kw):
        fn = (lambda e, name=name, kw=kw: getattr(e, name)(**kw))
        self._deps(eng, reads, writes)
        self.n[eng] += 1
        n = self.n[eng]
        sem, _ = self._csem(eng, n)
        self.ops[eng].append(lambda e, fn=fn, sem=sem: fn(e).then_inc(sem, 1))
        self._commit(('c', eng, n), reads, writes)
        self.ninst += 1

    def dma(self, q, out, in_, reads=(), writes=(), **kw):
        self.dmaop(q, lambda e, out=out, in_=in_, kw=kw: e.dma_start(out=out, in_=in_, **kw), reads, writes)

    def dmaop(self, q, fn, reads=(), writes=()):
        self._deps(q, reads, writes)
        j = self.dma_n[q]
        self.dma_n[q] += 1
        if j >= DMA_K:
            self._wait(q, ('d', q, j - DMA_K))
        sem, _, _ = self._dsem(q, j)
        self.ops[q].append(lambda e, fn=fn, sem=sem: fn(e).then_inc(sem, 16))
        self._commit(('d', q, j), reads, writes)
        self.ninst += 1

    def barrier(self):
        toks = [('c', e, self.n[e]) for e in self.ENG if self.n[e] > 0]
        for q in self.ENG:
            for j in range(max(0, self.dma_n[q] - DMA_K), self.dma_n[q]):
                toks.append(('d', q, j))
        for e in self.ENG:
            for t in toks:
                self._wait(e, t)

    def phase_begin(self):
        if not hasattr(self, '_stk'):
            self._stk = []
        self._stk.append(self.st)
        self.st = ExitStack()

    def phase_end(self):
        self.barrier()
        self.st.close()
        self.st = self._stk.pop()

    def finish(self, final_res):
        for r in final_res:
            self._wait('sp', self.lastw[r])
        nc = self.nc
        with nc.Block() as block:
            @block.sync
            def _(e):
                for f in self.ops['sp']:
                    f(e)

            @block.tensor
            def _(e):
                for f in self.ops['pe']:
                    f(e)

            @block.vector
            def _(e):
                for f in self.ops['dve']:
                    f(e)

            @block.scalar
            def _(e):
                for f in self.ops['act']:
                    f(e)

            @block.gpsimd
            def _(e):
                for f in self.ops['pool']:
                    f(e)
        self.st.close()


L = 128


class Rot:
    def __init__(self, P, name, shape, n, psum=False, dt=F32):
        self.bufs = []
        for i in range(n):
            t = P.psum(f"{name}{i}", shape, dt) if psum else P.sbuf(f"{name}{i}", shape, dt)
            self.bufs.append((t, (name, i)))
        self.i = 0

    def get(self):
        b = self.bufs[self.i % len(self.bufs)]
        self.i += 1
        return b


def rwkv_consts(P, cst):
    c = P.sbuf("rwc", [128, 6 * 128])
    P.dma('sp', c[:], cst, writes=['rwc'])
    return c


def rwkv_core(P, c, H, segs, d_r, d_kk, d_v, d_lw, d_ka, d_kd, d_o, HB=2, G=8):
    ident = c[:, 0:128]
    ones = c[:, 5 * 128:6 * 128]
    combT = [c[:, 128:384], c[:, 384:640]]
    Mst = [c[:, 384:512], c[:, 128:256]]
    nb = H // HB
    Tst = {}
    for d in range(2):
        for b in range(nb):
            Tst[d, b] = Rot(P, f"T{d}_{b}_", [64, HB, 64], 2)
    B = []
    for sl in range(G):
        q = {}
        for k in ("r", "kk", "v", "lw", "ka", "kd"):
            q["in_" + k] = Rot(P, f"in{sl}_{k}", [64, HB, L], 1)
        for k in ("cs", "g1", "gi", "g0"):
            q[k] = Rot(P, f"{k}{sl}_", [64, HB, L], 1)
        q["QT"], q["OL"], q["O"] = q["gi"], q["g0"], q["cs"]
        for k in ("nKa", "Kd"):
            q[k] = Rot(P, f"{k}{sl}_", [64, HB, L], 1, dt=BF16)
        q["KR"] = Rot(P, f"KR{sl}_", [64, HB, 2 * L], 1, dt=BF16)
        q["A1"] = Rot(P, f"A1{sl}_", [128, HB, 2 * L], 1, dt=BF16)
        q["A2"] = Rot(P, f"A2{sl}_", [128, HB, 2 * L], 1, dt=BF16)
        q["PT"] = Rot(P, f"PT{sl}_", [128, HB, L], 2, dt=BF16)
        q["Pn"] = Rot(P, f"Pn{sl}_", [128, HB, L], 2, dt=BF16)
        q["TOK"] = Rot(P, f"TOK{sl}_", [128, HB, 256], 1, dt=BF16)
        q["X"] = Rot(P, f"X{sl}_", [128, HB, 128], 2, dt=BF16)
        q["vb"] = Rot(P, f"vb{sl}_", [64, HB, L], 1, dt=BF16)
        q["Rf"] = Rot(P, f"Rf{sl}_", [64, HB, L], 1)
        q["GT"] = Rot(P, f"GT{sl}_", [64, HB, 64], 1)
        q["Hg"] = Rot(P, f"Hg{sl}_", [64, HB, 64], 1)
        B.append(q)
    ps_r = Rot(P, "ps", [128, 512], 6, psum=True)
    psb_r = Rot(P, "psb", [128, 1024], 2, psum=True, dt=BF16)
    identb = P.sbuf("identb", [64, 64], BF16)
    P.op('pool', 'tensor_copy', reads=['rwc'], writes=['identb'], out=identb[:], in_=ident[0:64, 0:64])

    def mm(out, lhsT, rhs, reads, wres, start=True, stop=True):
        P.op('pe', 'matmul', reads=reads, writes=[wres], out=out, lhsT=lhsT, rhs=rhs, start=start, stop=stop)

    evac_flip = [0]

    def evac(out, in_, reads, wres):
        evac_flip[0] ^= 1
        if evac_flip[0]:
            P.op('act', 'activation', reads=reads, writes=[wres], out=out, in_=in_, func=AF.Copy)
        else:
            P.op('dve', 'tensor_copy', reads=reads, writes=[wres], out=out, in_=in_)

    def h3(ap, w):
        return ap.rearrange("p (h t) -> p h t", h=HB)

    def item(d, t0, first, b, q):
        rows = slice(b * HB * 64, (b + 1) * HB * 64)

        def src(ap):
            return ap[rows, t0:t0 + L].rearrange("(h p) t -> p h t", p=64)
        tl = {}
        for k, ap in (("r", d_r), ("kk", d_kk), ("v", d_v), ("lw", d_lw[d]), ("ka", d_ka[d]), ("kd", d_kd[d])):
            t, res = q["in_" + k].get()
            P.dma('sp', t[:], src(ap), writes=[res])
            tl[k] = (t, res)
        (r_t, r_s), (kk_t, kk_s), (v_t, v_s) = tl["r"], tl["kk"], tl["v"]
        (lw_t, lw_s), (ka_t, ka_s), (kd_t, kd_s) = tl["lw"], tl["ka"], tl["kd"]
        cs, cs_s = q["cs"].get(); g1, g1_s = q["g1"].get(); gi, gi_s = q["gi"].get(); g0, g0_s = q["g0"].get()
        KR, KR_s = q["KR"].get(); nKa, nKa_s = q["nKa"].get(); Kd, Kd_s = q["Kd"].get()
        yield
        for h in range(HB):
            if d == 0:
                P.op('dve', 'tensor_tensor_scan', reads=[lw_s, 'rwc'], writes=[cs_s], out=cs[:, h, :], data0=ones[0:64, :], data1=lw_t[:, h, :], initial=0.0, op0=ALU.mult, op1=ALU.add)
            else:
                P.op('dve', 'tensor_tensor_scan', reads=[lw_s, 'rwc'], writes=[cs_s], out=cs[:, h, ::-1], data0=ones[0:64, :], data1=lw_t[:, h, ::-1], initial=0.0, op0=ALU.mult, op1=ALU.add)
        yield
        P.op('act', 'activation', reads=[cs_s], writes=[g1_s], out=g1[:], in_=cs[:], func=AF.Exp)
        P.op('act', 'activation', reads=[cs_s], writes=[gi_s], out=gi[:], in_=cs[:], func=AF.Exp, scale=-1.0)
        P.op('pool', 'tensor_tensor', reads=[cs_s, lw_s], writes=[g0_s], out=g0[:], in0=cs[:], in1=lw_t[:], op=ALU.subtract)
        yield
        P.op('act', 'activation', reads=[g0_s], writes=[g0_s], out=g0[:], in_=g0[:], func=AF.Exp)
        Rf, Rf_s = q["Rf"].get()
        vb, vb_s = q["vb"].get()
        P.op('pool', 'tensor_tensor', reads=[r_s, g1_s], writes=[Rf_s], out=Rf[:], in0=r_t[:], in1=g1[:], op=ALU.mult)
        P.op('pool', 'tensor_copy', reads=[Rf_s, KR_s], writes=[KR_s], out=KR[:, :, L:2 * L], in_=Rf[:])
        P.op('pool', 'tensor_copy', reads=[v_s], writes=[vb_s], out=vb[:], in_=v_t[:])
        P.op('dve', 'scalar_tensor_tensor', reads=[ka_s, gi_s], writes=[nKa_s], out=nKa[:], in0=ka_t[:], scalar=-1.0, in1=gi[:], op0=ALU.mult, op1=ALU.mult)
        P.op('pool', 'tensor_tensor', reads=[kd_s, gi_s], writes=[Kd_s], out=Kd[:], in0=kd_t[:], in1=gi[:], op=ALU.mult)
        yield
        P.op('dve', 'tensor_tensor', reads=[kk_s, g0_s, KR_s], writes=[KR_s], out=KR[:, :, 0:L], in0=kk_t[:], in1=g0[:], op=ALU.mult)
        yield
        A1, A1_s = q["A1"].get()
        A2, A2_s = q["A2"].get()
        for (Adst, Ares, lh, lres) in ((A1, A1_s, nKa, nKa_s), (A2, A2_s, Kd, Kd_s)):
            for h in range(HB):
                ps, ps_s = ps_r.get()
                mm(ps[:, 0:2 * L], lh[:, h, :], KR[:, h, :], [lres, KR_s], ps_s)
                P.op('dve', 'tensor_tensor', reads=[ps_s, 'rwc', Ares], writes=[Ares], out=Adst[:, h, :], in0=ps[:, 0:2 * L], in1=combT[d], op=ALU.mult)
        Pc, Pc_s = q["Pn"].get()
        ps, ps_s = ps_r.get()
        for h in range(HB):
            mm(ps[:, h * L:(h + 1) * L], KR[:, h, 0:L], nKa[:, h, :], [KR_s, nKa_s, ps_s], ps_s)
        P.op('dve', 'tensor_tensor', reads=[ps_s, 'rwc'], writes=[Pc_s], out=Pc[:], in0=h3(ps[:, 0:HB * L], L), in1=Mst[d].unsqueeze(1).to_broadcast([128, HB, L]), op=ALU.mult)
        yield
        TOK, TOK_s = q["TOK"].get()
        ps, ps_s = psb_r.get()
        for h in range(HB):
            for qi, (srcT, sres) in enumerate(((KR, KR_s), (vb, vb_s), (Kd, Kd_s), (nKa, nKa_s))):
                P.op('pe', 'transpose', reads=[sres, 'identb', ps_s], writes=[ps_s], out=ps[:, h * 256 + qi * 64:h * 256 + (qi + 1) * 64], in_=srcT[:, h, 0:L], identity=identb[:])
        evac(TOK[:], h3(ps[:, 0:HB * 256], 256), [ps_s], TOK_s)
        yield
        Xa, Xa_s = q["X"].get()
        ps, ps_s = ps_r.get()
        for h in range(HB):
            mm(ps[:, h * 64:(h + 1) * 64], A2[:, h, 0:L], TOK[:, h, 64:128], [A2_s, TOK_s, ps_s], ps_s)
        P.op('act', 'activation', reads=[ps_s, Xa_s], writes=[Xa_s], out=Xa[:, :, 64:128], in_=h3(ps[:, 0:HB * 64], 64), func=AF.Copy)
        P.op('pool', 'tensor_copy', reads=[TOK_s, Xa_s], writes=[Xa_s], out=Xa[:, :, 0:64], in_=TOK[:, :, 0:64])
        yield
        X, X_s = Xa, Xa_s
        PTp, PTp_s = A1, A1_s
        PTp_f = (lambda h, A1=A1: A1[:, h, 0:L])
        for j in range(7):
            Xn, Xn_s = q["X"].get()
            ps, ps_s = ps_r.get()
            for h in range(HB):
                mm(ps[:, h * 128:(h + 1) * 128], PTp_f(h), X[:, h, :], [PTp_s, X_s, ps_s], ps_s)
            P.op('dve', 'tensor_tensor', reads=[ps_s, X_s, Xn_s], writes=[Xn_s], out=Xn[:], in0=h3(ps[:, 0:HB * 128], 128), in1=X[:], op=ALU.add)
            X, X_s = Xn, Xn_s
            if j < 6:
                PTn, PTn_s = q["PT"].get()
                ps, ps_s = ps_r.get()
                for h in range(HB):
                    mm(ps[:, h * L:(h + 1) * L], Pc[:, h, :], PTp_f(h), [Pc_s, PTp_s, ps_s], ps_s)
                evac(PTn[:], h3(ps[:, 0:HB * L], L), [ps_s], PTn_s)
                if j < 5:
                    Pn, Pn_s = q["Pn"].get()
                    ps2, ps2_s = ps_r.get()
                    for h in range(HB):
                        mm(ps2[:, h * L:(h + 1) * L], PTp_f(h), Pc[:, h, :], [Pc_s, PTp_s, ps2_s], ps2_s)
                    evac(Pn[:], h3(ps2[:, 0:HB * L], L), [ps2_s], Pn_s)
                    Pc, Pc_s = Pn, Pn_s
                PTp, PTp_s = PTn, PTn_s
                PTp_f = (lambda h, PTn=PTn: PTn[:, h, :])
            yield
        GT, GT_s = q["GT"].get()
        Hg, Hg_s = q["Hg"].get()
        ps, ps_s = ps_r.get()
        for h in range(HB):
            mm(ps[0:64, h * 64:(h + 1) * 64], X[:, h, 0:64], TOK[:, h, 192:256], [X_s, TOK_s, ps_s], ps_s)
        P.op('dve', 'tensor_tensor', reads=[ps_s, 'rwc'], writes=[GT_s], out=GT[:], in0=h3(ps[0:64, 0:HB * 64], 64), in1=ident[0:64, 0:64].unsqueeze(1).to_broadcast([64, HB, 64]), op=ALU.add)
        ps, ps_s = ps_r.get()
        for h in range(HB):
            mm(ps[0:64, h * 64:(h + 1) * 64], TOK[:, h, 128:192], TOK[:, h, 64:128], [TOK_s, ps_s], ps_s, start=True, stop=False)
            mm(ps[0:64, h * 64:(h + 1) * 64], TOK[:, h, 192:256], X[:, h, 64:128], [TOK_s, X_s, ps_s], ps_s, start=False, stop=True)
        gl = (L - 1) if d == 0 else 0
        for h in range(HB):
            P.op('dve', 'tensor_scalar', reads=[ps_s, g1_s, Hg_s], writes=[Hg_s], out=Hg[:, h, :], in0=ps[0:64, h * 64:(h + 1) * 64], scalar1=g1[:, h, gl:gl + 1], scalar2=None, op0=ALU.mult)
        yield
        QT, QT_s = q["QT"].get()
        OL, OL_s = q["OL"].get()
        ps, ps_s = ps_r.get()
        for h in range(HB):
            mm(ps[0:64, h * L:(h + 1) * L], X[:, h, 0:64], A1[:, h, L:2 * L], [X_s, A1_s, ps_s], ps_s)
        P.op('dve', 'tensor_tensor', reads=[ps_s, Rf_s], writes=[QT_s], out=QT[:], in0=h3(ps[0:64, 0:HB * L], L), in1=Rf[:], op=ALU.add)
        ps, ps_s = ps_r.get()
        for h in range(HB):
            mm(ps[0:64, h * L:(h + 1) * L], TOK[:, h, 64:128], A2[:, h, L:2 * L], [TOK_s, A2_s, ps_s], ps_s, start=True, stop=False)
            mm(ps[0:64, h * L:(h + 1) * L], X[:, h, 64:128], A1[:, h, L:2 * L], [X_s, A1_s, ps_s], ps_s, start=False, stop=True)
        evac(OL[:], h3(ps[0:64, 0:HB * L], L), [ps_s], OL_s)
        yield
        O, O_s = q["O"].get()
        dst = d_o[d][rows, t0:t0 + L].rearrange("(h p) t -> p h t", p=64)
        if first:
            Tn, Tn_s = Tst[d, b].get()
            P.op('pool', 'tensor_copy', reads=[Hg_s], writes=[Tn_s], out=Tn[:], in_=Hg[:])
            P.dma('sp', dst, OL[:], reads=[OL_s], writes=[('d_o', d)])
            Tst[d, b].cur = (Tn, Tn_s)
        else:
            T0, T0_s = Tst[d, b].cur
            ps, ps_s = ps_r.get()
            for h in range(HB):
                mm(ps[0:64, h * L:(h + 1) * L], T0[:, h, :], QT[:, h, :], [T0_s, QT_s, ps_s], ps_s)
            P.op('dve', 'tensor_tensor', reads=[ps_s, OL_s], writes=[O_s], out=O[:], in0=h3(ps[0:64, 0:HB * L], L), in1=OL[:], op=ALU.add)
            P.dma('sp', dst, O[:], reads=[O_s], writes=[('d_o', d)])
            ps, ps_s = ps_r.get()
            for h in range(HB):
                mm(ps[0:64, h * 64:(h + 1) * 64], GT[:, h, :], T0[:, h, :], [GT_s, T0_s, ps_s], ps_s)
            Tn, Tn_s = Tst[d, b].get()
            for h in range(HB):
                P.op('dve', 'scalar_tensor_tensor', reads=[ps_s, g1_s, Hg_s, Tn_s], writes=[Tn_s], out=Tn[:, h, :], in0=ps[0:64, h * 64:(h + 1) * 64], scalar=g1[:, h, gl:gl + 1], in1=Hg[:, h, :], op0=ALU.mult, op1=ALU.add)
            Tst[d, b].cur = (Tn, Tn_s)
        yield

    nsteps = sum(n for _, n in segs)
    fwd = [(s0 + i * L) for (s0, n) in segs for i in range(n)]
    bwd = [(s0 + i * L) for (s0, n) in segs for i in reversed(range(n))]
    for i in range(nsteps):
        items = [(0, fwd[i], i == 0, b) for b in range(nb)] + [(1, bwd[i], i == 0, b) for b in range(nb)]
        for g0_ in range(0, len(items), G):
            gens = [item(*it, B[sl]) for sl, it in enumerate(items[g0_:g0_ + G])]
            live = list(gens)
            while live:
                nxt = []
                for gen in live:
                    try:
                        next(gen)
                        nxt.append(gen)
                    except StopIteration:
                        pass
                live = nxt

D = 2048
NCTX = 256
KC = 16
DIN = 6400
ALPHA = 4.0 ** 0.25
PI = 3.14159265358979
VO = {}
_o = 0
for _n, _w in (("b_ada", 48), ("d_skip", 4), ("b_glu", 4), ("pool_scale", 4), ("k_k", 8), ("k_a", 8), ("r_k", 8),
               ("gn_w", 8), ("gn_b", 8), ("ln_g", 16), ("ln_b", 16), ("w0", 16), ("a0", 16), ("conv", 72)):
    VO[_n] = _o
    _o += _w
VW = _o


def token_tiles(T):
    tiles = [(0, NCTX, 1)]
    t = NCTX
    while t < T:
        tiles.append((t, 512, 0))
        t += 512
    return tiles


def ln_stats(P, R, Rs, n, eps, onesD, sq, sq_s, ps_r, mean, mean_s, rstd, rstd_s, tmp, tmp_s):
    P.op('act', 'activation', reads=[Rs], writes=[sq_s], out=sq[:, :, 0:n], in_=R[:, :, 0:n], func=AF.Square)
    pm, pm_s = ps_r.get()
    for kc in range(KC):
        P.op('pe', 'matmul', reads=[Rs, 'onesD', pm_s], writes=[pm_s], out=pm[:, 0:n], lhsT=onesD[:], rhs=R[:, kc, 0:n], start=(kc == 0), stop=(kc == KC - 1))
    pq, pq_s = ps_r.get()
    for kc in range(KC):
        P.op('pe', 'matmul', reads=[sq_s, 'onesD', pq_s], writes=[pq_s], out=pq[:, 0:n], lhsT=onesD[:], rhs=sq[:, kc, 0:n], start=(kc == 0), stop=(kc == KC - 1))
    P.op('act', 'activation', reads=[pm_s], writes=[mean_s], out=mean[:, 0:n], in_=pm[:, 0:n], func=AF.Copy)
    P.op('dve', 'tensor_tensor', reads=[mean_s], writes=[tmp_s], out=tmp[:, 0:n], in0=mean[:, 0:n], in1=mean[:, 0:n], op=ALU.mult)
    P.op('dve', 'tensor_tensor', reads=[pq_s, tmp_s], writes=[tmp_s], out=tmp[:, 0:n], in0=pq[:, 0:n], in1=tmp[:, 0:n], op=ALU.subtract)
    P.op('dve', 'tensor_scalar', reads=[tmp_s], writes=[tmp_s], out=tmp[:, 0:n], in0=tmp[:, 0:n], scalar1=float(eps), scalar2=None, op0=ALU.add)
    P.op('act', 'activation', reads=[tmp_s], writes=[tmp_s], out=tmp[:, 0:n], in_=tmp[:, 0:n], func=AF.Sqrt)
    P.op('dve', 'reciprocal', reads=[tmp_s], writes=[rstd_s], out=rstd[:, 0:n], in_=tmp[:, 0:n])


def proj(P, Wd, M, H, Hs, n, Wt_r, ps_r, consume):
    m0 = 0
    while m0 < M:
        w = min(512, M - m0)
        Wt, Wt_s = Wt_r.get()
        P.dma('pool', Wt[:, :, 0:w], Wd[:, m0:m0 + w].rearrange("(kc p) c -> p kc c", p=128), reads=[('wbf',)], writes=[Wt_s])
        for mi in range(w // 128):
            ps, ps_s = ps_r.get()
            for kc in range(KC):
                P.op('pe', 'matmul', reads=[Wt_s, Hs, ps_s], writes=[ps_s], out=ps[:, 0:n], lhsT=Wt[:, kc, mi * 128:(mi + 1) * 128], rhs=H[:, kc, 0:n], start=(kc == 0), stop=(kc == KC - 1))
            consume(m0 // 128 + mi, ps, ps_s)
        m0 += w


def phase_mod(P, l, io, mod, vec, wbf):
    P.phase_begin()
    sc = P.sbuf("m_sc", [128, KC, 2])
    ps_r = Rot(P, "m_ps", [128, 512], 2, psum=True)
    Wt_r = Rot(P, "m_W", [128, KC, 512], 2)
    P.dma('sp', sc[:], io["ccT"], writes=['m_sc'])
    P.op('act', 'activation', reads=['m_sc'], writes=['m_sc'], out=sc[:], in_=sc[:], func=AF.Silu)
    ms = ('mod', l)
    for mg in range(12):
        Wt, Wt_s = Wt_r.get()
        P.dma('pool', Wt[:], io[f"w_ada{l}"][:, mg * 512:(mg + 1) * 512].rearrange("(kc p) c -> p kc c", p=128), writes=[Wt_s])
        ps, ps_s = ps_r.get()
        for mi in range(4):
            for kc in range(KC):
                P.op('pe', 'matmul', reads=[Wt_s, 'm_sc', ps_s], writes=[ps_s], out=ps[:, mi * 2:mi * 2 + 2], lhsT=Wt[:, kc, mi * 128:(mi + 1) * 128], rhs=sc[:, kc, :], start=(kc == 0), stop=(kc == KC - 1))
        P.op('dve', 'tensor_tensor', reads=[ps_s, 'vec', ms], writes=[ms], out=mod[:, mg * 4:mg * 4 + 4, :], in0=ps[:, 0:8].rearrange("p (m j) -> p m j", j=2),
             in1=vec[:, VO["b_ada"] + mg * 4:VO["b_ada"] + mg * 4 + 4].unsqueeze(2).to_broadcast([128, 4, 2]), op=ALU.add)
    P.op('dve', 'tensor_scalar', reads=[ms], writes=[ms], out=mod[:, 16:32, :], in0=mod[:, 16:32, :], scalar1=1.0, scalar2=None, op0=ALU.add)
    Wb_r = Rot(P, "m_Wb", [128, KC, 512], 2, dt=BF16)
    ci = 0
    for (src, dst, M) in ((io[f"w_in{l}"], wbf["in"], DIN), (io[f"w_out{l}"], wbf["out"], D)):
        m0 = 0
        while m0 < M:
            w = min(512, M - m0)
            Wt, Wt_s = Wt_r.get()
            P.dma('pool', Wt[:, :, 0:w], src[:, m0:m0 + w].rearrange("(kc p) c -> p kc c", p=128), writes=[Wt_s])
            Wb, Wb_s = Wb_r.get()
            eng = ('act', 'pool', 'dve')[ci % 3]
            ci += 1
            if eng == 'act':
                P.op('act', 'activation', reads=[Wt_s], writes=[Wb_s], out=Wb[:, :, 0:w], in_=Wt[:, :, 0:w], func=AF.Copy)
            else:
                P.op(eng, 'tensor_copy', reads=[Wt_s], writes=[Wb_s], out=Wb[:, :, 0:w], in_=Wt[:, :, 0:w])
            P.dma('sp', dst[:, m0:m0 + w].rearrange("(kc p) c -> p kc c", p=128), Wb[:, :, 0:w], reads=[Wb_s], writes=[('wbf',)])
            m0 += w
    P.phase_end()


def phase_inproj(P, l, io, mod, xT, zT, T, onesD, wbf):
    P.phase_begin()
    ms = ('mod', l)
    X_r = Rot(P, "p1_X", [128, KC, 512], 2)
    sq = P.sbuf("p1_sq", [128, KC, 512])
    H_r = Rot(P, "p1_H", [128, KC, 512], 2, dt=BF16)
    mean = P.sbuf("p1_mean", [128, 512]); rstd = P.sbuf("p1_rstd", [128, 512]); tmp = P.sbuf("p1_tmp", [128, 512])
    st_r = Rot(P, "p1_st", [128, 512], 4)
    Wt_r = Rot(P, "p1_W", [128, KC, 512], 3, dt=BF16)
    ps_r = Rot(P, "p1_ps", [128, 512], 6, psum=True)
    for (t0, n, j) in token_tiles(T):
        X, Xs = X_r.get()
        P.dma('sp', X[:, :, 0:n], xT[:, t0:t0 + n].rearrange("(kc p) t -> p kc t", p=128), reads=[('xT',)], writes=[Xs])
        H, Hs = H_r.get()
        ln_stats(P, X, Xs, n, 1e-6, onesD, sq, 'p1_sq', ps_r, mean, 'p1_mean', rstd, 'p1_rstd', tmp, 'p1_tmp')
        for kc in range(KC):
            eng = 'dve' if kc % 2 == 0 else 'pool'
            xr = (Xs, kc)
            P.op(eng, 'tensor_tensor', reads=[Xs, 'p1_mean'], writes=[xr], out=X[:, kc, 0:n], in0=X[:, kc, 0:n], in1=mean[:, 0:n], op=ALU.subtract)
            P.op(eng, 'tensor_tensor', reads=[xr, 'p1_rstd'], writes=[xr], out=X[:, kc, 0:n], in0=X[:, kc, 0:n], in1=rstd[:, 0:n], op=ALU.mult)
            P.op('act', 'activation', reads=[xr, ms, Hs], writes=[Hs], out=H[:, kc, 0:n], in_=X[:, kc, 0:n], func=AF.Identity, bias=mod[:, kc, j:j + 1], scale=mod[:, 16 + kc, j:j + 1])
        for kc in range(KC):
            dd = P.readers.setdefault(Xs, {})
            dd[('xw', kc)] = P.lastw[(Xs, kc)]
            for k2, tok in P.readers.get((Xs, kc), {}).items():
                dd[('xr', kc, k2)] = tok

        def consume(mi, ps, ps_s, t0=t0, n=n):
            st, st_s = st_r.get()
            if mi % 2 == 0:
                P.op('act', 'activation', reads=[ps_s], writes=[st_s], out=st[:, 0:n], in_=ps[:, 0:n], func=AF.Copy)
            else:
                P.op('dve', 'tensor_copy', reads=[ps_s], writes=[st_s], out=st[:, 0:n], in_=ps[:, 0:n])
            P.dma('sp', zT[mi * 128:(mi + 1) * 128, t0:t0 + n], st[:, 0:n], reads=[st_s], writes=[('zT',)])
        proj(P, wbf["in"], DIN, H, Hs, n, Wt_r, ps_r, consume)
    P.phase_end()


def phase_outproj(P, l, io, mod, vec, xT, ymT, x1T, T, onesD, t_lo, wbf):
    P.phase_begin()
    ms = ('mod', l)
    Y_r = Rot(P, "po_Y", [128, KC, 512], 2, dt=BF16)
    X_r = Rot(P, "po_X", [128, KC, 512], 2)
    R = P.sbuf("po_R", [128, KC, 512])
    sq = P.sbuf("po_sq", [128, KC, 512])
    mean = P.sbuf("po_mean", [128, 512]); rstd = P.sbuf("po_rstd", [128, 512]); tmp = P.sbuf("po_tmp", [128, 512])
    Wt_r = Rot(P, "po_W", [128, KC, 512], 2, dt=BF16)
    ps_r = Rot(P, "po_ps", [128, 512], 6, psum=True)
    for (t0, n, j) in token_tiles(T):
        if t0 < t_lo:
            continue
        Y, Ys = Y_r.get()
        P.dma('sp', Y[:, :, 0:n], ymT[:, t0:t0 + n].rearrange("(kc p) t -> p kc t", p=128), reads=[('ymT',)], writes=[Ys])
        X, Xs = X_r.get()
        P.dma('sp', X[:, :, 0:n], xT[:, t0:t0 + n].rearrange("(kc p) t -> p kc t", p=128), reads=[('xT',)], writes=[Xs])
        P.op('pool', 'tensor_scalar', reads=[Xs], writes=[Xs], out=X[:, :, 0:n], in0=X[:, :, 0:n], scalar1=float(ALPHA), scalar2=None, op0=ALU.mult)

        def consume(mi, ps, ps_s, n=n, j=j, X=X, Xs=Xs):
            P.op('dve', 'scalar_tensor_tensor', reads=[ps_s, ms, Xs, 'po_R'], writes=['po_R'], out=R[:, mi, 0:n], in0=ps[:, 0:n], scalar=mod[:, 32 + mi, j:j + 1], in1=X[:, mi, 0:n], op0=ALU.mult, op1=ALU.add)
        proj(P, wbf["out"], D, Y, Ys, n, Wt_r, ps_r, consume)
        ln_stats(P, R, 'po_R', n, 1e-5, onesD, sq, 'po_sq', ps_r, mean, 'po_mean', rstd, 'po_rstd', tmp, 'po_tmp')
        for kc in range(KC):
            eng = 'dve' if kc % 2 == 0 else 'pool'
            P.op(eng, 'tensor_tensor', reads=['po_R', 'po_mean'], writes=['po_R'], out=R[:, kc, 0:n], in0=R[:, kc, 0:n], in1=mean[:, 0:n], op=ALU.subtract)
            P.op(eng, 'tensor_tensor', reads=['po_R', 'po_rstd'], writes=['po_R'], out=R[:, kc, 0:n], in0=R[:, kc, 0:n], in1=rstd[:, 0:n], op=ALU.mult)
            P.op('act', 'activation', reads=['po_R', 'vec'], writes=['po_R'], out=R[:, kc, 0:n], in_=R[:, kc, 0:n], func=AF.Identity, bias=vec[:, VO["ln_b"] + kc:VO["ln_b"] + kc + 1], scale=vec[:, VO["ln_g"] + kc:VO["ln_g"] + kc + 1])
        P.dma('sp', x1T[:, t0 - t_lo:t0 - t_lo + n].rearrange("(kc p) t -> p kc t", p=128), R[:, :, 0:n], reads=['po_R'], writes=[('x1T', l)])
    P.phase_end()


def _rr(gens):
    live = list(gens)
    while live:
        nxt = []
        for g in live:
            try:
                next(g)
                nxt.append(g)
            except StopIteration:
                pass
        live = nxt


def phase_rwkv_prep(P, l, io, vec, zT, T, S):
    P.phase_begin()
    NS = 4
    blk = P.sbuf("r1_blk", [128, 128])
    P.op('pool', 'memset', writes=['r1_blk'], ap=blk[:], constant=0.0)
    P.op('pool', 'memset', reads=['r1_blk'], writes=['r1_blk'], ap=blk[0:64, 0:64], constant=1.0)
    P.op('pool', 'memset', reads=['r1_blk'], writes=['r1_blk'], ap=blk[64:128, 64:128], constant=1.0)
    w2 = P.sbuf("r1_w2", [128, 1024]); a2 = P.sbuf("r1_a2", [128, 1024])
    P.dma('sp', w2[:], io[f"rw_w2{l}"], writes=['r1_w2'])
    P.dma('sp', a2[:], io[f"rw_a2{l}"], writes=['r1_a2'])
    omka = P.sbuf("r1_omka", [128, 8])
    P.op('dve', 'tensor_scalar', reads=['vec'], writes=['r1_omka'], out=omka[:], in0=vec[:, VO["k_a"]:VO["k_a"] + 8], scalar1=-1.0, scalar2=1.0, op0=ALU.mult, op1=ALU.add)
    TH_r = Rot(P, "r1_TH", [128, 512], 2); AC_r = Rot(P, "r1_AC", [128, 512], 2)
    SL = []
    for sl in range(NS):
        SL.append({"Z": Rot(P, f"r1_Z{sl}_", [128, 514], 3), "cv": {k: Rot(P, f"r1_c{k}{sl}_", [128, 512], 1) for k in "rkv"},
                   "t": Rot(P, f"r1_t{sl}_", [128, 512], 3), "o": Rot(P, f"r1_o{sl}_", [128, 512], 6), "ks": Rot(P, f"r1_ks{sl}_", [128, 512], 1)})
    ps_r = Rot(P, "r1_ps", [128, 512], 8, psum=True)

    def body(hp, q, t0, n, j, TH, TH_s, AC, AC_s):
        seg_lo, seg_hi = (0, NCTX) if j == 1 else (NCTX, T)
        res = {}
        for ci, k in enumerate("rkv"):
            Z, Zs = q["Z"].get()
            lo = max(t0 - 1, seg_lo); hi = min(t0 + n + 1, seg_hi)
            if lo > t0 - 1:
                P.op('pool', 'memset', reads=[Zs], writes=[Zs], ap=Z[:, 0:1], constant=0.0)
            if hi < t0 + n + 1:
                P.op('pool', 'memset', reads=[Zs], writes=[Zs], ap=Z[:, n + 1:n + 2], constant=0.0)
            row0 = 2048 + ci * 1024 + hp * 128
            P.dma('sp', Z[:, lo - (t0 - 1):hi - (t0 - 1)], zT[row0:row0 + 128, lo:hi], reads=[('zT',), Zs], writes=[Zs])
            res[k] = (Z, Zs)
        yield
        cvs = {}
        for ci, k in enumerate("rkv"):
            Z, Zs = res[k]
            o, os_ = q["cv"][k].get()
            c0 = VO["conv"] + (ci * 8 + hp) * 3
            P.op('dve', 'tensor_scalar', reads=[Zs, 'vec', os_], writes=[os_], out=o[:, 0:n], in0=Z[:, 0:n], scalar1=vec[:, c0:c0 + 1], scalar2=None, op0=ALU.mult)
            cvs[k] = (o, os_, c0)
        yield
        for step in (1, 2):
            for k in "rkv":
                Z, Zs = res[k]
                o, os_, c0 = cvs[k]
                P.op('dve', 'scalar_tensor_tensor', reads=[Zs, 'vec', os_], writes=[os_], out=o[:, 0:n], in0=Z[:, step:n + step], scalar=vec[:, c0 + step:c0 + step + 1], in1=o[:, 0:n], op0=ALU.mult, op1=ALU.add)
            yield
        (r_, r_s, _), (k_, k_s, _), (v_, v_s, _) = cvs["r"], cvs["k"], cvs["v"]
        rows = slice(hp * 128, (hp + 1) * 128)
        P.dma('sp', S["r"][rows, t0:t0 + n], r_[:, 0:n], reads=[r_s], writes=[('S_r',)])
        P.dma('sp', S["v"][rows, t0:t0 + n], v_[:, 0:n], reads=[v_s], writes=[('S_v',)])
        kk, kk_s = q["o"].get()
        t1, t1_s = q["t"].get()
        P.op('pool', 'tensor_scalar', reads=[k_s, 'vec'], writes=[kk_s], out=kk[:, 0:n], in0=k_[:, 0:n], scalar1=vec[:, VO["k_k"] + hp:VO["k_k"] + hp + 1], scalar2=None, op0=ALU.mult)
        yield
        P.op('act', 'activation', reads=[kk_s], writes=[t1_s], out=t1[:, 0:n], in_=kk[:, 0:n], func=AF.Square)
        yield
        ps, ps_s = ps_r.get()
        P.op('pe', 'matmul', reads=[t1_s, 'r1_blk'], writes=[ps_s], out=ps[:, 0:n], lhsT=blk[:], rhs=t1[:, 0:n], start=True, stop=True)
        pw = []
        for d in range(2):
            dp = slice(64 * d, 64 * d + 64)
            p1, p1_s = ps_r.get()
            P.op('pe', 'matmul', reads=['r1_w2', TH_s], writes=[p1_s], out=p1[:, 0:n], lhsT=w2[dp, hp * 128:(hp + 1) * 128], rhs=TH[dp, 0:n], start=True, stop=True)
            p2, p2_s = ps_r.get()
            P.op('pe', 'matmul', reads=['r1_a2', AC_s], writes=[p2_s], out=p2[:, 0:n], lhsT=a2[dp, hp * 128:(hp + 1) * 128], rhs=AC[dp, 0:n], start=True, stop=True)
            pw.append((p1, p1_s, p2, p2_s))
        P.op('act', 'activation', reads=[ps_s], writes=[t1_s], out=t1[:, 0:n], in_=ps[:, 0:n], func=AF.Sqrt)
        lws, as_ = [], []
        for d in range(2):
            p1, p1_s, p2, p2_s = pw[d]
            lw, lw_s = q["o"].get()
            c = VO["w0"] + d * 8 + hp
            P.op('act', 'activation', reads=[p1_s, 'vec'], writes=[lw_s], out=lw[:, 0:n], in_=p1[:, 0:n], func=AF.Sigmoid, bias=vec[:, c:c + 1], scale=1.0)
            a_, a_s = q["t"].get()
            c = VO["a0"] + d * 8 + hp
            P.op('act', 'activation', reads=[p2_s, 'vec'], writes=[a_s], out=a_[:, 0:n], in_=p2[:, 0:n], func=AF.Sigmoid, bias=vec[:, c:c + 1], scale=1.0)
            lws.append((lw, lw_s)); as_.append((a_, a_s))
        yield
        P.op('dve', 'tensor_scalar', reads=[t1_s], writes=[t1_s], out=t1[:, 0:n], in0=t1[:, 0:n], scalar1=1e-12, scalar2=None, op0=ALU.max)
        for d in range(2):
            lw, lw_s = lws[d]
            P.op('pool', 'tensor_scalar', reads=[lw_s], writes=[lw_s], out=lw[:, 0:n], in0=lw[:, 0:n], scalar1=-0.606531, scalar2=None, op0=ALU.mult)
            P.dma('sp', S["lw"][d][rows, t0:t0 + n], lw[:, 0:n], reads=[lw_s], writes=[('S_lw', d)])
        yield
        P.op('dve', 'reciprocal', reads=[t1_s], writes=[t1_s], out=t1[:, 0:n], in_=t1[:, 0:n])
        yield
        P.op('dve', 'tensor_tensor', reads=[kk_s, t1_s], writes=[kk_s], out=kk[:, 0:n], in0=kk[:, 0:n], in1=t1[:, 0:n], op=ALU.mult)
        P.dma('sp', S["kk"][rows, t0:t0 + n], kk[:, 0:n], reads=[kk_s], writes=[('S_kk',)])
        yield
        ks, ks_s = q["ks"].get()
        kds = []
        for d in range(2):
            a_, a_s = as_[d]
            ka, ka_s = q["o"].get()
            P.op('dve', 'tensor_tensor', reads=[kk_s, a_s], writes=[ka_s], out=ka[:, 0:n], in0=kk[:, 0:n], in1=a_[:, 0:n], op=ALU.mult)
            P.dma('sp', S["ka"][d][rows, t0:t0 + n], ka[:, 0:n], reads=[ka_s], writes=[('S_ka', d)])
        yield
        for d in range(2):
            a_, a_s = as_[d]
            P.op('dve', 'tensor_scalar', reads=[a_s, 'vec', 'r1_omka'], writes=[a_s], out=a_[:, 0:n], in0=a_[:, 0:n], scalar1=vec[:, VO["k_a"] + hp:VO["k_a"] + hp + 1], scalar2=omka[:, hp:hp + 1], op0=ALU.mult, op1=ALU.add)
        yield
        for d in range(2):
            a_, a_s = as_[d]
            kd, kd_s = q["o"].get()
            P.op('dve', 'tensor_tensor', reads=[k_s, a_s], writes=[kd_s], out=kd[:, 0:n], in0=k_[:, 0:n], in1=a_[:, 0:n], op=ALU.mult)
            P.dma('sp', S["kd"][d][rows, t0:t0 + n], kd[:, 0:n], reads=[kd_s], writes=[('S_kd', d)])
            kds.append((kd, kd_s))
        yield
        P.op('pool', 'tensor_tensor', reads=[kds[0][1], kds[1][1], ks_s], writes=[ks_s], out=ks[:, 0:n], in0=kds[0][0][:, 0:n], in1=kds[1][0][:, 0:n], op=ALU.add)
        yield
        P.op('dve', 'scalar_tensor_tensor', reads=[r_s, 'vec', ks_s], writes=[ks_s], out=ks[:, 0:n], in0=r_[:, 0:n], scalar=vec[:, VO["r_k"] + hp:VO["r_k"] + hp + 1], in1=ks[:, 0:n], op0=ALU.mult, op1=ALU.mult)
        yield
        ps, ps_s = ps_r.get()
        P.op('pe', 'matmul', reads=[ks_s, 'r1_blk'], writes=[ps_s], out=ps[:, 0:n], lhsT=blk[:], rhs=ks[:, 0:n], start=True, stop=True)
        bo, bo_s = q["o"].get()
        P.op('dve', 'tensor_tensor', reads=[ps_s, v_s], writes=[bo_s], out=bo[:, 0:n], in0=ps[:, 0:n], in1=v_[:, 0:n], op=ALU.mult)
        P.dma('sp', S["bonus"][rows, t0:t0 + n], bo[:, 0:n], reads=[bo_s], writes=[('S_bonus',)])
        yield

    for (t0, n, j) in token_tiles(T):
        TH, TH_s = TH_r.get(); AC, AC_s = AC_r.get()
        P.dma('sp', TH[:, 0:n], zT[6144:6272, t0:t0 + n], reads=[('zT',)], writes=[TH_s])
        P.op('act', 'activation', reads=[TH_s], writes=[TH_s], out=TH[:, 0:n], in_=TH[:, 0:n], func=AF.Tanh)
        P.dma('sp', AC[:, 0:n], zT[6272:6400, t0:t0 + n], reads=[('zT',)], writes=[AC_s])
        for h0 in range(0, 8, NS):
            _rr([body(h0 + k, SL[k], t0, n, j, TH, TH_s, AC, AC_s) for k in range(NS)])
    P.phase_end()


def phase_rwkv_post(P, l, vec, zT, T, S, ymT, t_lo):
    P.phase_begin()
    NS = 4
    blk = P.sbuf("r3_blk", [128, 128])
    P.op('pool', 'memset', writes=['r3_blk'], ap=blk[:], constant=0.0)
    P.op('pool', 'memset', reads=['r3_blk'], writes=['r3_blk'], ap=blk[0:64, 0:64], constant=1.0 / 64)
    P.op('pool', 'memset', reads=['r3_blk'], writes=['r3_blk'], ap=blk[64:128, 64:128], constant=1.0 / 64)
    SL = [{"i": Rot(P, f"r3_i{sl}_", [128, 512], 8), "t": Rot(P, f"r3_t{sl}_", [128, 512], 2), "ob": Rot(P, f"r3_ob{sl}_", [128, 512], 2, dt=BF16)} for sl in range(NS)]
    ps_r = Rot(P, "r3_ps", [128, 512], 8, psum=True)

    def body(hp, q, t0, n):
        rows = slice(hp * 128, (hp + 1) * 128)
        ld = {}
        for k, src in (("o0", S["o"][0][rows]), ("o1", S["o"][1][rows]), ("bo", S["bonus"][rows]), ("g", zT[5120 + hp * 128:5120 + (hp + 1) * 128])):
            t, ts = q["i"].get()
            P.dma('sp', t[:, 0:n], src[:, t0:t0 + n], reads=[('d_o', 0), ('d_o', 1), ('S_bonus',), ('zT',)], writes=[ts])
            ld[k] = (t, ts)
        (o0, o0_s), (o1, o1_s), (bo, bo_s), (g, g_s) = ld["o0"], ld["o1"], ld["bo"], ld["g"]
        yield
        P.op('pool', 'tensor_tensor', reads=[o0_s, o1_s], writes=[o0_s], out=o0[:, 0:n], in0=o0[:, 0:n], in1=o1[:, 0:n], op=ALU.add)
        P.op('act', 'activation', reads=[g_s], writes=[g_s], out=g[:, 0:n], in_=g[:, 0:n], func=AF.Silu)
        yield
        ps, ps_s = ps_r.get()
        P.op('pe', 'matmul', reads=[o0_s, 'r3_blk'], writes=[ps_s], out=ps[:, 0:n], lhsT=blk[:], rhs=o0[:, 0:n], start=True, stop=True)
        cen, cen_s = q["t"].get()
        P.op('dve', 'tensor_tensor', reads=[o0_s, ps_s], writes=[cen_s], out=cen[:, 0:n], in0=o0[:, 0:n], in1=ps[:, 0:n], op=ALU.subtract)
        yield
        sq, sq_s = q["t"].get()
        P.op('act', 'activation', reads=[cen_s], writes=[sq_s], out=sq[:, 0:n], in_=cen[:, 0:n], func=AF.Square)
        yield
        ps, ps_s = ps_r.get()
        P.op('pe', 'matmul', reads=[sq_s, 'r3_blk'], writes=[ps_s], out=ps[:, 0:n], lhsT=blk[:], rhs=sq[:, 0:n], start=True, stop=True)
        P.op('dve', 'tensor_scalar', reads=[ps_s], writes=[sq_s], out=sq[:, 0:n], in0=ps[:, 0:n], scalar1=64e-5, scalar2=None, op0=ALU.add)
        yield
        P.op('act', 'activation', reads=[sq_s], writes=[sq_s], out=sq[:, 0:n], in_=sq[:, 0:n], func=AF.Sqrt)
        yield
        P.op('dve', 'reciprocal', reads=[sq_s], writes=[sq_s], out=sq[:, 0:n], in_=sq[:, 0:n])
        yield
        P.op('dve', 'tensor_tensor', reads=[cen_s, sq_s], writes=[cen_s], out=cen[:, 0:n], in0=cen[:, 0:n], in1=sq[:, 0:n], op=ALU.mult)
        yield
        P.op('dve', 'tensor_scalar', reads=[cen_s, 'vec'], writes=[cen_s], out=cen[:, 0:n], in0=cen[:, 0:n], scalar1=vec[:, VO["gn_w"] + hp:VO["gn_w"] + hp + 1], scalar2=vec[:, VO["gn_b"] + hp:VO["gn_b"] + hp + 1], op0=ALU.mult, op1=ALU.add)
        yield
        P.op('pool', 'tensor_tensor', reads=[cen_s, bo_s], writes=[cen_s], out=cen[:, 0:n], in0=cen[:, 0:n], in1=bo[:, 0:n], op=ALU.add)
        yield
        ob, ob_s = q["ob"].get()
        P.op('dve', 'tensor_tensor', reads=[cen_s, g_s], writes=[ob_s], out=ob[:, 0:n], in0=cen[:, 0:n], in1=g[:, 0:n], op=ALU.mult)
        P.dma('sp', ymT[1024 + hp * 128:1024 + (hp + 1) * 128, t0:t0 + n], ob[:, 0:n], reads=[ob_s], writes=[('ymT',)])
        yield

    for (t0, n, j) in token_tiles(T):
        if t0 < t_lo:
            continue
        for h0 in range(0, 8, NS):
            _rr([body(h0 + k, SL[k], t0, n) for k in range(NS)])
    P.phase_end()


def sin_red(P, out, z, n, tmpi, tmpf, res_out, res_z, res_t):
    P.op('dve', 'tensor_scalar', reads=[res_z], writes=[res_t], out=tmpi, in0=z, scalar1=1.0 / (2 * PI), scalar2=None, op0=ALU.mult)
    P.op('dve', 'tensor_copy', reads=[res_t], writes=[res_t], out=tmpf, in_=tmpi)
    P.op('dve', 'scalar_tensor_tensor', reads=[res_t, res_z], writes=[res_out], out=out, in0=tmpf, scalar=-2 * PI, in1=z, op0=ALU.mult, op1=ALU.add)
    P.op('dve', 'tensor_scalar', reads=[res_out], writes=[res_t], out=tmpf, in0=out, scalar1=PI, scalar2=None, op0=ALU.is_gt)
    P.op('dve', 'scalar_tensor_tensor', reads=[res_t, res_out], writes=[res_out], out=out, in0=tmpf, scalar=-2 * PI, in1=out, op0=ALU.mult, op1=ALU.add)
    P.op('dve', 'tensor_scalar', reads=[res_out], writes=[res_t], out=tmpf, in0=out, scalar1=-PI, scalar2=None, op0=ALU.is_lt)
    P.op('dve', 'scalar_tensor_tensor', reads=[res_t, res_out], writes=[res_out], out=out, in0=tmpf, scalar=2 * PI, in1=out, op0=ALU.mult, op1=ALU.add)
    P.op('dve', 'tensor_scalar', reads=[res_out], writes=[res_out], out=out, in0=out, scalar1=-3.14159, scalar2=3.14159, op0=ALU.max, op1=ALU.min)
    P.op('act', 'activation', reads=[res_out], writes=[res_out], out=out, in_=out, func=AF.Sin)


def phase_s5(P, l, io, vec, zT, T, S, ymT, t_lo, cst):
    TC = 512
    I32 = mybir.dt.int32
    P.phase_begin()
    ident = cst[:, 0:128]
    Jc = cst[:, 768:896]
    iota = cst[:, 896:896 + TC]
    pp = P.sbuf("s5_pp", [128, 3, 64])
    P.dma('sp', pp[:], io[f"s5p{l}"], writes=['s5_pp'])
    rho = P.sbuf("s5_rho", [128, 64]); theta = P.sbuf("s5_theta", [128, 64])
    P.op('act', 'activation', reads=['s5_pp'], writes=['s5_pp'], out=pp[:, 2, :], in_=pp[:, 2, :], func=AF.Exp)
    P.op('dve', 'tensor_scalar', reads=['s5_pp'], writes=['s5_pp'], out=pp[:, 0, :], in0=pp[:, 0, :], scalar1=-1e-4, scalar2=None, op0=ALU.min)
    P.op('dve', 'tensor_tensor', reads=['s5_pp'], writes=['s5_rho'], out=rho[:], in0=pp[:, 0, :], in1=pp[:, 2, :], op=ALU.mult)
    P.op('act', 'activation', reads=['s5_rho'], writes=['s5_rho'], out=rho[:], in_=rho[:], func=AF.Exp)
    P.op('dve', 'tensor_tensor', reads=['s5_pp'], writes=['s5_theta'], out=theta[:], in0=pp[:, 1, :], in1=pp[:, 2, :], op=ALU.mult)
    W1 = P.sbuf("s5_W1", [16, 64, 128], BF16); W2 = P.sbuf("s5_W2", [16, 64, 128], BF16)
    P.phase_begin()
    BT = P.sbuf("s5_BT", [16, 2, 2048])
    P.dma('sp', BT[:], io[f"s5bt{l}"], writes=['s5_BT'])
    rr = P.sbuf("s5_rr", [16, 3, 2048])
    tA = [P.sbuf(f"s5_t{i}", [16, 2048]) for i in range(8)]
    tI = P.sbuf("s5_ti", [16, 2048], I32)
    for d in range(2):
        P.dma('sp', rr[:], io[f"s5r{l}"][:, :, d * 2048:(d + 1) * 2048], reads=['s5_rr'], writes=['s5_rr'])
        re, im, st = rr[:, 0, :], rr[:, 1, :], rr[:, 2, :]
        R_ = ['s5_rr'] + [f"s5_t{i}" for i in range(8)] + ['s5_ti']
        W_ = R_
        def o(eng, name, **kw):
            P.op(eng, name, reads=R_, writes=W_, **kw)
        o('act', 'activation', out=st, in_=st, func=AF.Exp)
        o('dve', 'tensor_scalar', out=re, in0=re, scalar1=-1e-4, scalar2=None, op0=ALU.min)
        er, ang, cs_, sn_, z2 = tA[0][:], tA[1][:], tA[2][:], tA[3][:], tA[4][:]
        o('dve', 'tensor_tensor', out=er, in0=re, in1=st, op=ALU.mult)
        o('act', 'activation', out=er, in_=er, func=AF.Exp)
        o('dve', 'tensor_tensor', out=ang, in0=im, in1=st, op=ALU.mult)
        sin_red(P, sn_, ang, 2048, tI[:], tA[5][:], 's5_t3', 's5_t1', 's5_t5')
        o('dve', 'tensor_scalar', out=z2, in0=ang, scalar1=PI / 2, scalar2=None, op0=ALU.add)
        sin_red(P, cs_, z2, 2048, tI[:], tA[5][:], 's5_t2', 's5_t4', 's5_t5')
        P.barrier()
        nre, nim = tA[2][:], tA[3][:]
        o('dve', 'tensor_tensor', out=nre, in0=er, in1=cs_, op=ALU.mult)
        o('dve', 'tensor_scalar', out=nre, in0=nre, scalar1=-1.0, scalar2=None, op0=ALU.add)
        o('dve', 'tensor_tensor', out=nim, in0=er, in1=sn_, op=ALU.mult)
        den, t5, bsr, bsi = tA[0][:], tA[5][:], tA[6][:], tA[7][:]
        o('dve', 'tensor_tensor', out=den, in0=re, in1=re, op=ALU.mult)
        o('dve', 'tensor_tensor', out=t5, in0=im, in1=im, op=ALU.mult)
        o('dve', 'tensor_tensor', out=den, in0=den, in1=t5, op=ALU.add)
        o('dve', 'reciprocal', out=den, in_=den)
        o('dve', 'tensor_tensor', out=bsr, in0=nre, in1=re, op=ALU.mult)
        o('dve', 'tensor_tensor', out=t5, in0=nim, in1=im, op=ALU.mult)
        o('dve', 'tensor_tensor', out=bsr, in0=bsr, in1=t5, op=ALU.add)
        o('dve', 'tensor_tensor', out=bsr, in0=bsr, in1=den, op=ALU.mult)
        o('dve', 'tensor_tensor', out=bsi, in0=nim, in1=re, op=ALU.mult)
        o('dve', 'tensor_tensor', out=t5, in0=nre, in1=im, op=ALU.mult)
        o('dve', 'tensor_tensor', out=bsi, in0=bsi, in1=t5, op=ALU.subtract)
        o('dve', 'tensor_tensor', out=bsi, in0=bsi, in1=den, op=ALU.mult)
        Br, Bi = BT[:, 0, :], BT[:, 1, :]
        bpr, bpi, t4 = tA[1][:], tA[2][:], tA[4][:]
        Rb = R_ + ['s5_BT']
        P.op('dve', 'tensor_tensor', reads=Rb, writes=W_, out=bpr, in0=bsr, in1=Br, op=ALU.mult)
        P.op('dve', 'tensor_tensor', reads=Rb, writes=W_, out=t4, in0=bsi, in1=Bi, op=ALU.mult)
        o('dve', 'tensor_tensor', out=bpr, in0=bpr, in1=t4, op=ALU.subtract)
        P.op('dve', 'tensor_tensor', reads=Rb, writes=W_, out=bpi, in0=bsr, in1=Bi, op=ALU.mult)
        P.op('dve', 'tensor_tensor', reads=Rb, writes=W_, out=t4, in0=bsi, in1=Br, op=ALU.mult)
        o('dve', 'tensor_tensor', out=bpi, in0=bpi, in1=t4, op=ALU.add)
        g3 = lambda ap: ap.rearrange("p (g s) -> p g s", s=64)
        dg = slice(d * 32, (d + 1) * 32)
        P.op('dve', 'tensor_copy', reads=R_, writes=['s5_W1'], out=W1[:, dg, 0:64], in_=g3(bpr))
        P.op('dve', 'tensor_copy', reads=R_ + ['s5_W1'], writes=['s5_W1'], out=W1[:, dg, 64:128], in_=g3(bpi))
        P.op('dve', 'tensor_copy', reads=R_, writes=['s5_W2'], out=W2[:, dg, 0:64], in_=g3(bpi))
        P.op('dve', 'tensor_scalar', reads=R_ + ['s5_W2'], writes=['s5_W2'], out=W2[:, dg, 64:128], in0=g3(bpr), scalar1=-1.0, scalar2=None, op0=ALU.mult)
        P.barrier()
    P.phase_end()
    CT1f = P.sbuf("s5_CT1f", [128, 64, 16]); CT2f = P.sbuf("s5_CT2f", [128, 64, 16])
    CT1 = P.sbuf("s5_CT1", [128, 64, 16], BF16); CT2 = P.sbuf("s5_CT2", [128, 64, 16], BF16)
    P.dma('sp', CT1f[:], io[f"s5ca{l}"], writes=['s5_CT1f'])
    P.dma('sp', CT2f[:], io[f"s5cb{l}"], writes=['s5_CT2f'])
    P.op('dve', 'tensor_copy', reads=['s5_CT1f'], writes=['s5_CT1'], out=CT1[0:64], in_=CT1f[0:64])
    P.op('dve', 'tensor_scalar', reads=['s5_CT1f', 's5_CT1'], writes=['s5_CT1'], out=CT1[64:128], in0=CT1f[64:128], scalar1=-1.0, scalar2=None, op0=ALU.mult)
    P.op('dve', 'tensor_scalar', reads=['s5_CT2f'], writes=['s5_CT2'], out=CT2[:], in0=CT2f[:], scalar1=-1.0, scalar2=None, op0=ALU.mult)
    GS = 4
    zt = P.sbuf("s5_zt", [128, TC]); zi = P.sbuf("s5_zi", [128, TC], I32); zf = P.sbuf("s5_zf", [128, TC])
    SB = []
    for sl in range(GS):
        q = {"COS": P.sbuf(f"s5_COS{sl}", [128, TC]), "SIN": P.sbuf(f"s5_SIN{sl}", [128, TC]), "RHO": P.sbuf(f"s5_RHO{sl}", [128, TC]),
             "u": Rot(P, f"s5_u{sl}_", [16, TC], 2), "ub": Rot(P, f"s5_ub{sl}_", [16, TC], 2, dt=BF16), "t": Rot(P, f"s5_tt{sl}_", [128, TC], 3),
             "tb": Rot(P, f"s5_tb{sl}_", [128, TC], 2, dt=BF16), "l": Rot(P, f"s5_l{sl}_", [128, 2], 2), "y": Rot(P, f"s5_y{sl}_", [16, TC], 2),
             "st": Rot(P, f"s5_st{sl}_", [128, 1], 2), "sl": sl}
        SB.append(q)
    ps_r = Rot(P, "s5_ps", [128, 512], 8, psum=True)
    blocks = [(0, NCTX)] + [(t, 512) for t in range(NCTX, T, 512)]

    def chain(d, g, q):
        sl = q["sl"]
        COS, SIN, RHO = q["COS"], q["SIN"], q["RHO"]
        cr, sr, rr_ = f"s5_COS{sl}", f"s5_SIN{sl}", f"s5_RHO{sl}"
        order = blocks if d == 0 else [blocks[0]] + blocks[:0:-1]
        dg = d * 32 + g
        P.op('dve', 'tensor_scalar', reads=['rwc', 's5_theta'], writes=['s5_zt'], out=zt[:], in0=iota, scalar1=theta[:, dg:dg + 1], scalar2=None, op0=ALU.mult)
        sin_red(P, SIN[:], zt[:], TC, zi[:], zf[:], sr, 's5_zt', 's5_zf')
        P.op('dve', 'tensor_scalar', reads=['s5_zt'], writes=['s5_zt'], out=zt[:], in0=zt[:], scalar1=PI / 2, scalar2=None, op0=ALU.add)
        sin_red(P, COS[:], zt[:], TC, zi[:], zf[:], cr, 's5_zt', 's5_zf')
        P.op('dve', 'tensor_scalar', reads=['rwc', 's5_rho'], writes=[rr_], out=RHO[:], in0=iota, scalar1=0.0, scalar2=rho[:, dg:dg + 1], op0=ALU.mult, op1=ALU.add)
        yield
        stc = None
        rv = (lambda ap: ap) if d == 0 else (lambda ap: ap[:, ::-1])
        def load(t0, n):
            u, u_s = q["u"].get()
            P.dma('sp', u[:, 0:n], zT[16 * g:16 * g + 16, t0:t0 + n], reads=[('zT',)], writes=[u_s])
            return (u, u_s)
        cur = load(*order[0])
        for bi, (t0, n) in enumerate(order):
            u, u_s = cur
            if bi + 1 < len(order):
                cur = load(*order[bi + 1])
            ub, ub_s = q["ub"].get()
            P.op('pool', 'tensor_copy', reads=[u_s], writes=[ub_s], out=ub[:, 0:n], in_=u[:, 0:n])
            p1, p1_s = ps_r.get()
            P.op('pe', 'matmul', reads=[ub_s, 's5_W1'], writes=[p1_s], out=p1[:, 0:n], lhsT=W1[:, dg, :], rhs=ub[:, 0:n], start=True, stop=True)
            p2, p2_s = ps_r.get()
            P.op('pe', 'matmul', reads=[ub_s, 's5_W2'], writes=[p2_s], out=p2[:, 0:n], lhsT=W2[:, dg, :], rhs=ub[:, 0:n], start=True, stop=True)
            t1, t1_s = q["t"].get(); t2, t2_s = q["t"].get()
            P.op('dve', 'tensor_tensor', reads=[p1_s, cr], writes=[t1_s], out=t1[:, 0:n], in0=rv(p1[:, 0:n]), in1=COS[:, 0:n], op=ALU.mult)
            P.op('dve', 'tensor_tensor', reads=[p2_s, sr], writes=[t2_s], out=t2[:, 0:n], in0=rv(p2[:, 0:n]), in1=SIN[:, 0:n], op=ALU.mult)
            yield
            P.op('pool', 'tensor_tensor', reads=[t1_s, t2_s], writes=[t1_s], out=t1[:, 0:n], in0=t1[:, 0:n], in1=t2[:, 0:n], op=ALU.add)
            yield
            Wt, Wt_s = q["t"].get()
            if stc is None:
                P.op('dve', 'tensor_tensor_scan', reads=[t1_s, rr_], writes=[Wt_s], out=Wt[:, 0:n], data0=RHO[:, 0:n], data1=t1[:, 0:n], initial=0.0, op0=ALU.mult, op1=ALU.add)
            else:
                P.op('dve', 'tensor_tensor_scan', reads=[t1_s, rr_, stc[1]], writes=[Wt_s], out=Wt[:, 0:n], data0=RHO[:, 0:n], data1=t1[:, 0:n], initial=stc[0][:, 0:1], op0=ALU.mult, op1=ALU.add)
            yield
            t3, t3_s = q["tb"].get(); t4, t4_s = q["tb"].get()
            ll, ll_s = q["l"].get()
            P.op('dve', 'tensor_tensor', reads=[Wt_s, cr], writes=[ll_s], out=ll[:, 0:1], in0=Wt[:, n - 1:n], in1=COS[:, n - 1:n], op=ALU.mult)
            P.op('dve', 'tensor_tensor', reads=[Wt_s, sr, ll_s], writes=[ll_s], out=ll[:, 1:2], in0=Wt[:, n - 1:n], in1=SIN[:, n - 1:n], op=ALU.mult)
            P.op('dve', 'tensor_tensor', reads=[Wt_s, cr], writes=[t3_s], out=t3[:, 0:n], in0=Wt[:, 0:n], in1=COS[:, 0:n], op=ALU.mult)
            P.op('pool', 'tensor_tensor', reads=[Wt_s, sr], writes=[t4_s], out=t4[:, 0:n], in0=Wt[:, 0:n], in1=SIN[:, 0:n], op=ALU.mult)
            yield
            px, px_s = ps_r.get()
            P.op('pe', 'matmul', reads=[ll_s, 'rwc'], writes=[px_s], out=px[:, 0:1], lhsT=ident, rhs=ll[:, 0:1], start=True, stop=False)
            P.op('pe', 'matmul', reads=[ll_s, 'rwc', px_s], writes=[px_s], out=px[:, 0:1], lhsT=Jc, rhs=ll[:, 1:2], start=False, stop=True)
            sn, sn_s = q["st"].get()
            P.op('act', 'activation', reads=[px_s], writes=[sn_s], out=sn[:], in_=px[:, 0:1], func=AF.Copy)
            stc = (sn, sn_s)
            py, py_s = ps_r.get()
            P.op('pe', 'matmul', reads=[t3_s, 's5_CT1'], writes=[py_s], out=py[0:16, 0:n], lhsT=CT1[:, dg, :], rhs=t3[:, 0:n], start=True, stop=False)
            P.op('pe', 'matmul', reads=[t4_s, 's5_CT2', py_s], writes=[py_s], out=py[0:16, 0:n], lhsT=CT2[:, dg, :], rhs=t4[:, 0:n], start=False, stop=True)
            y, y_s = q["y"].get()
            P.op('act', 'activation', reads=[py_s], writes=[y_s], out=rv(y[:, 0:n]), in_=py[0:16, 0:n], func=AF.Copy)
            P.dma('act', S["ys"][d][16 * g:16 * g + 16, t0:t0 + n], y[:, 0:n], reads=[y_s], writes=[('S_ys', d, g)])
            yield

    for d in range(2):
        for gb in range(0, 32, GS):
            live = [chain(d, gb + k, SB[k]) for k in range(GS)]
            while live:
                nxt = []
                for gen in live:
                    try:
                        next(gen)
                        nxt.append(gen)
                    except StopIteration:
                        pass
                live = nxt
    P.phase_end()
    P.phase_begin()
    ps_r = Rot(P, "s5c_ps", [128, 512], 4, psum=True)
    wg = P.sbuf("s5_wg", [128, 4, 512])
    P.dma('sp', wg[:], io[f"w_glu{l}"].rearrange("(kc p) m -> p kc m", p=128), writes=['s5_wg'])
    YG_r = Rot(P, "s5_YG", [128, 4, 512], 2)
    i_r = Rot(P, "s5_i", [128, 512], 8)
    G_r = Rot(P, "s5_G", [128, 4, 512], 2)
    ob_r = Rot(P, "s5_ob", [128, 512], 3, dt=BF16)
    for (t0, n, j) in token_tiles(T):
        if t0 < t_lo:
            continue
        YG, YG_s = YG_r.get()
        G, G_s = G_r.get()
        gts = []
        for c in range(4):
            rows = slice(c * 128, (c + 1) * 128)
            ld = []
            for src, rs in ((zT[rows], ('zT',)), (S["ys"][0][rows], ('S_ys', 0)), (S["ys"][1][rows], ('S_ys', 1))):
                t, ts = i_r.get()
                P.dma('sp', t[:, 0:n], src[:, t0:t0 + n], reads=[rs], writes=[ts])
                ld.append((t, ts))
            (u, u_s), (y0, y0_s), (y1, y1_s) = ld
            P.dma('sp', G[:, c, 0:n], zT[512 + c * 128:512 + (c + 1) * 128, t0:t0 + n], reads=[('zT',), G_s], writes=[G_s])
            P.op('dve', 'scalar_tensor_tensor', reads=[u_s, 'vec', y0_s], writes=[y0_s], out=y0[:, 0:n], in0=u[:, 0:n], scalar=vec[:, VO["d_skip"] + c:VO["d_skip"] + c + 1], in1=y0[:, 0:n], op0=ALU.mult, op1=ALU.add)
            P.op('pool', 'tensor_tensor', reads=[y0_s, y1_s], writes=[y0_s], out=y0[:, 0:n], in0=y0[:, 0:n], in1=y1[:, 0:n], op=ALU.add)
            P.op('act', 'activation', reads=[y0_s], writes=[y1_s], out=y1[:, 0:n], in_=y0[:, 0:n], func=AF.Square)
            P.op('dve', 'tensor_scalar', reads=[y1_s], writes=[y1_s], out=y1[:, 0:n], in0=y1[:, 0:n], scalar1=0.044715, scalar2=1.0, op0=ALU.mult, op1=ALU.add)
            P.op('dve', 'tensor_tensor', reads=[y1_s, y0_s], writes=[y1_s], out=y1[:, 0:n], in0=y1[:, 0:n], in1=y0[:, 0:n], op=ALU.mult)
            P.op('act', 'activation', reads=[y1_s], writes=[y1_s], out=y1[:, 0:n], in_=y1[:, 0:n], func=AF.Sigmoid, scale=1.5957691216057308)
            P.op('dve', 'tensor_tensor', reads=[y1_s, y0_s, YG_s], writes=[YG_s], out=YG[:, c, 0:n], in0=y1[:, 0:n], in1=y0[:, 0:n], op=ALU.mult)
        P.op('act', 'activation', reads=[G_s], writes=[G_s], out=G[:, :, 0:n], in_=G[:, :, 0:n], func=AF.Silu)
        for m in range(4):
            ps, ps_s = ps_r.get()
            for kc in range(4):
                P.op('pe', 'matmul', reads=[YG_s, 's5_wg', ps_s], writes=[ps_s], out=ps[:, 0:n], lhsT=wg[:, kc, m * 128:(m + 1) * 128], rhs=YG[:, kc, 0:n], start=(kc == 0), stop=(kc == 3))
            sg, sg_s = i_r.get()
            P.op('act', 'activation', reads=[ps_s, 'vec'], writes=[sg_s], out=sg[:, 0:n], in_=ps[:, 0:n], func=AF.Sigmoid, bias=vec[:, VO["b_glu"] + m:VO["b_glu"] + m + 1], scale=1.0)
            P.op('dve', 'tensor_tensor', reads=[sg_s, YG_s], writes=[sg_s], out=sg[:, 0:n], in0=sg[:, 0:n], in1=YG[:, m, 0:n], op=ALU.mult)
            ob, ob_s = ob_r.get()
            P.op('dve', 'tensor_tensor', reads=[sg_s, G_s], writes=[ob_s], out=ob[:, 0:n], in0=sg[:, 0:n], in1=G[:, m, 0:n], op=ALU.mult)
            P.dma('sp', ymT[m * 128:(m + 1) * 128, t0:t0 + n], ob[:, 0:n], reads=[ob_s], writes=[('ymT',)])
    P.phase_end()


def phase_pool(P, l, io, vec, zT, T, ymT, t_lo):
    P.phase_begin()
    rows_tot = (T - NCTX) // 64
    wp = P.sbuf("pl_wp", [128, 4, 128])
    P.dma('sp', wp[:], io[f"w_pool{l}"], writes=['pl_wp'])
    icn = P.sbuf("pl_icn", [128, 4, rows_tot + 64 + NCTX])
    P.dma('sp', icn[:], io["icnt"], writes=['pl_icn'])
    ps_r = Rot(P, "pl_ps", [128, 512], 4, psum=True)
    g_r = Rot(P, "pl_g", [128, 512], 3)
    ob_r = Rot(P, "pl_ob", [128, 512], 3, dt=BF16)
    regions = []
    RB = min(64, rows_tot)
    for r0 in range(0, rows_tot, RB):
        regions.append((NCTX, rows_tot, 64, r0, RB, 0, rows_tot))
    if t_lo == 0:
        for r0 in range(0, NCTX, 64):
            regions.append((0, NCTX, 1, r0, 64, rows_tot + 64, None))
    bufA = {64: [P.sbuf(f"pl_A{i}", [128, RB + 16, 80]) for i in range(3)], 1: [P.sbuf(f"pl_B{i}", [128, 80, 1]) for i in range(3)]}
    dif = {64: P.sbuf("pl_d64", [128, RB, 64]), 1: P.sbuf("pl_d1", [128, 64, 1])}
    for gi, w in enumerate((2, 4, 8, 16)):
        zr = slice(1024 + gi * 128, 1024 + (gi + 1) * 128)
        for (tok0, Rtot, C, r0, nr, ico, icc) in regions:
            A = bufA[C]
            An = [f"pl_{'A' if C == 64 else 'B'}{i}" for i in range(3)]
            CP = C + 16 if C == 64 else 1
            c0 = 8 if C == 64 else 0
            U, Us = A[0], An[0]
            P.op('pool', 'memset', reads=[Us], writes=[Us], ap=U[:], constant=0.0)
            ra = max(r0 - 8, 0); rb = min(r0 + nr + 8, Rtot)
            P.dma('sp', U[:, 8 + ra - r0:8 + rb - r0, c0:c0 + C], zT[zr, tok0 + ra * C:tok0 + rb * C].rearrange("p (r c) -> p r c", c=C), reads=[('zT',), Us], writes=[Us])
            NR = nr + 16
            cur, cur_s = U, Us
            idx = 0
            s = 1
            while s < w:
                nidx = 1 if idx != 1 else 2
                nxt, nxt_s = A[nidx], An[nidx]
                P.op('dve', 'tensor_tensor', reads=[cur_s, nxt_s], writes=[nxt_s], out=nxt[:, 0:NR - s, :], in0=cur[:, 0:NR - s, :], in1=cur[:, s:NR, :], op=ALU.add)
                cur, cur_s, idx = nxt, nxt_s, nidx
                s *= 2
            nidx = 1 if idx != 1 else 2
            rm, rm_s = A[nidx], An[nidx]
            P.op('dve', 'tensor_tensor', reads=[cur_s, 'pl_icn', rm_s], writes=[rm_s], out=rm[:, 0:nr, :], in0=cur[:, 8 - w // 2:8 - w // 2 + nr, :],
                 in1=icn[:, gi, ico + r0:ico + r0 + nr].unsqueeze(2).to_broadcast([128, nr, CP]), op=ALU.mult)
            cur, cur_s, idx = rm, rm_s, nidx
            if C == 64:
                s = 1
                while s < w:
                    nidx = [i for i in (1, 2) if i != idx][0] if idx != 0 else 1
                    nxt, nxt_s = A[nidx], An[nidx]
                    P.op('dve', 'tensor_tensor', reads=[cur_s, nxt_s], writes=[nxt_s], out=nxt[:, 0:nr, 0:CP - s], in0=cur[:, 0:nr, 0:CP - s], in1=cur[:, 0:nr, s:CP], op=ALU.add)
                    cur, cur_s, idx = nxt, nxt_s, nidx
                    s *= 2
                nidx = [i for i in (1, 2) if i != idx][0]
                cm, cm_s = A[nidx], An[nidx]
                P.op('dve', 'tensor_tensor', reads=[cur_s, 'pl_icn', cm_s], writes=[cm_s], out=cm[:, 0:nr, 0:64], in0=cur[:, 0:nr, 8 - w // 2:8 - w // 2 + 64],
                     in1=icn[:, gi, icc:icc + 64].unsqueeze(1).to_broadcast([128, nr, 64]), op=ALU.mult)
                cur, cur_s = cm, cm_s
            df, df_s = dif[C], ("pl_d", C)
            P.op('dve', 'tensor_tensor', reads=[cur_s, Us, df_s], writes=[df_s], out=df[:, 0:nr, :], in0=cur[:, 0:nr, 0:C], in1=U[:, 8:8 + nr, c0:c0 + C], op=ALU.subtract)
            ntok = nr * C
            dflat = df[:, 0:nr, :].rearrange("p r c -> p (r c)")
            for q0 in range(0, ntok, 512):
                qn = min(512, ntok - q0)
                ps, ps_s = ps_r.get()
                P.op('pe', 'matmul', reads=[df_s, 'pl_wp'], writes=[ps_s], out=ps[:, 0:qn], lhsT=wp[:, gi, :], rhs=dflat[:, q0:q0 + qn], start=True, stop=True)
                tk = tok0 + r0 * C + q0
                gt, gt_s = g_r.get()
                P.dma('sp', gt[:, 0:qn], zT[1536 + gi * 128:1536 + (gi + 1) * 128, tk:tk + qn], reads=[('zT',)], writes=[gt_s])
                P.op('act', 'activation', reads=[gt_s], writes=[gt_s], out=gt[:, 0:qn], in_=gt[:, 0:qn], func=AF.Silu)
                ob, ob_s = ob_r.get()
                P.op('dve', 'scalar_tensor_tensor', reads=[ps_s, 'vec', gt_s], writes=[ob_s], out=ob[:, 0:qn], in0=ps[:, 0:qn], scalar=vec[:, VO["pool_scale"] + gi:VO["pool_scale"] + gi + 1], in1=gt[:, 0:qn], op0=ALU.mult, op1=ALU.mult)
                P.dma('sp', ymT[512 + gi * 128:512 + (gi + 1) * 128, tk:tk + qn], ob[:, 0:qn], reads=[ob_s], writes=[('ymT',)])
    P.phase_end()


class RowSplit:
    def __init__(self, aps, rows_per):
        self.aps = aps
        self.rp = rows_per

    def __getitem__(self, key):
        if isinstance(key, tuple):
            rs, cs = key
        else:
            rs, cs = key, None
        b = rs.start // self.rp
        assert (rs.stop - 1) // self.rp == b
        ap = self.aps[b][rs.start - b * self.rp:rs.stop - b * self.rp]
        return ap if cs is None else ap[:, cs]

def build(N, nlayers=2, debug=False):
    T = NCTX + N
    nc = bass.Bass("TRN2", target_bir_lowering=False)
    io = {}

    def din(name, shape):
        io[name] = nc.dram_tensor(name, list(shape), F32, kind="ExternalInput").ap()

    def dscr(name, shape):
        return nc.dram_tensor(name, list(shape), F32, kind=("ExternalOutput" if debug else "Internal")).ap()
    rows_tot = N // 64
    din("xT", [D, T]); din("ccT", [128, KC, 2]); din("cst", [128, 1408]); din("icnt", [128, 4, rows_tot + 64 + NCTX])
    for l in range(2):
        din(f"w_ada{l}", [D, 6144]); din(f"w_in{l}", [D, DIN]); din(f"w_out{l}", [D, D]); din(f"vec{l}", [128, VW])
        din(f"s5p{l}", [128, 3, 64]); din(f"s5r{l}", [16, 3, 4096]); din(f"s5bt{l}", [16, 2, 2048])
        din(f"s5ca{l}", [128, 64, 16]); din(f"s5cb{l}", [128, 64, 16]); din(f"w_glu{l}", [512, 512])
        din(f"w_pool{l}", [128, 4, 128]); din(f"rw_w2{l}", [128, 1024]); din(f"rw_a2{l}", [128, 1024])
    yT = nc.dram_tensor("yT", [D, N], F32, kind="ExternalOutput").ap()
    zT = RowSplit([dscr(f"zT{i}", [min(1024, DIN - i * 1024), T]) for i in range(7)], 1024); ymT = nc.dram_tensor("ymT", [D, T], BF16, kind="Internal").ap(); x1T = dscr("x1T", [D, T])
    wbf = {"in": nc.dram_tensor("winb", [D, DIN], BF16, kind="Internal").ap(), "out": nc.dram_tensor("woutb", [D, D], BF16, kind="Internal").ap()}
    S = {"r": dscr("S_r", [1024, T]), "kk": dscr("S_kk", [1024, T]), "v": dscr("S_v", [1024, T]), "bonus": dscr("S_bonus", [1024, T]),
         "lw": [dscr(f"S_lw{d}", [1024, T]) for d in range(2)], "ka": [dscr(f"S_ka{d}", [1024, T]) for d in range(2)],
         "kd": [dscr(f"S_kd{d}", [1024, T]) for d in range(2)], "o": [dscr(f"S_o{d}", [1024, T]) for d in range(2)],
         "ys": [dscr(f"S_ys{d}", [512, T]) for d in range(2)]}
    P = Prog(nc)
    cst = P.sbuf("rwc", [128, 1408])
    P.dma('sp', cst[:], io["cst"], writes=['rwc'])
    onesD = P.sbuf("onesD", [128, 128])
    P.op('pool', 'memset', writes=['onesD'], ap=onesD[:], constant=1.0 / D)
    mod = P.sbuf("mod", [128, 48, 2])
    vec = P.sbuf("vec", [128, VW])
    for l in range(nlayers):
        t_lo = 0 if l == 0 else NCTX
        P.barrier()
        P.dma('sp', vec[:], io[f"vec{l}"], reads=['vec'], writes=['vec'])
        xin = io["xT"] if l == 0 else x1T
        xout = x1T if l == 0 else yT
        SK = []
        phase_mod(P, l, io, mod, vec, wbf)
        if "inproj" not in SK:
            phase_inproj(P, l, io, mod, xin, zT, T, onesD, wbf)
        if "s5" not in SK:
            phase_s5(P, l, io, vec, zT, T, S, ymT, t_lo, cst)
        if "pool" not in SK:
            phase_pool(P, l, io, vec, zT, T, ymT, t_lo)
        if "prep" not in SK:
            phase_rwkv_prep(P, l, io, vec, zT, T, S)
        if "core" not in SK:
            P.phase_begin()
            rwkv_core(P, cst, 16, [(0, NCTX // L), (NCTX, N // L)], S["r"], S["kk"], S["v"], S["lw"], S["ka"], S["kd"], S["o"])
            P.phase_end()
        if "post" not in SK:
            phase_rwkv_post(P, l, vec, zT, T, S, ymT, t_lo)
        phase_outproj(P, l, io, mod, vec, xin, ymT, xout, T, onesD, t_lo, wbf)
    P.finish([('x1T', nlayers - 1)])
    return nc


def _pm(v):
    v = np.asarray(v, np.float32).reshape(-1)
    return np.ascontiguousarray(v.reshape(-1, 128).T)


def host_layout(inp, b, N):
    f32 = np.float32
    A = lambda a: np.ascontiguousarray(np.asarray(a), dtype=f32)
    m = {}
    m["xT"] = A(np.concatenate([np.asarray(inp["ctx"][b]).T, np.asarray(inp["x"][b, :N]).T], axis=1))
    cc = np.stack([np.asarray(inp["c"][b]), np.asarray(inp["c_ctx"])], 0)
    m["ccT"] = A(cc.reshape(2, KC, 128).transpose(2, 1, 0))
    i = np.arange(128)[:, None]; j = np.arange(128)[None, :]
    Jc = np.zeros((128, 128), f32)
    for p in range(64):
        Jc[64 + p, p] = -1.0
        Jc[p, 64 + p] = 1.0
    iota = np.tile(np.arange(1, 513, dtype=f32)[None, :], (128, 1))
    m["cst"] = A(np.concatenate([np.eye(128), (i < j), (i <= j), (i > j), (i >= j), np.ones((128, 128)), Jc, iota], 1))
    rows_tot = N // 64

    def invcnt(n, w):
        idx = np.arange(n)
        lo = np.clip(idx - w // 2, 0, n - 1); hi = np.clip(idx - w // 2 + w - 1, 0, n - 1)
        return (1.0 / (hi - lo + 1)).astype(f32)
    ic = np.stack([np.concatenate([invcnt(rows_tot, w), invcnt(64, w), invcnt(NCTX, w)]) for w in (2, 4, 8, 16)], 0)
    m["icnt"] = A(np.tile(ic[None], (128, 1, 1)))
    for l in range(2):
        g = lambda k: np.asarray(inp[k][l])
        m[f"w_ada{l}"] = A(g("w_ada")); m[f"w_in{l}"] = A(g("w_in")); m[f"w_out{l}"] = A(g("w_out"))
        w0 = g("rwkv_w0").reshape(2, 8, 128).transpose(2, 0, 1).reshape(128, 16)
        a0 = g("rwkv_a0").reshape(2, 8, 128).transpose(2, 0, 1).reshape(128, 16)
        conv = g("conv_rkv").reshape(3, 24, 128).transpose(2, 1, 0).reshape(128, 72)
        parts = [_pm(g("b_ada")), _pm(g("s5_d")), _pm(g("b_glu")), _pm(g("pool_scale")), _pm(g("rwkv_k_k")), _pm(g("rwkv_k_a")),
                 _pm(g("rwkv_r_k")), _pm(g("gn_w")), _pm(g("gn_b")), _pm(g("ln_g")), _pm(g("ln_b")), w0, a0, conv]
        m[f"vec{l}"] = A(np.concatenate(parts, 1))
        lr = g("s5_lam_re").reshape(64, 64).T; li = g("s5_lam_im").reshape(64, 64).T
        ls = np.tile(g("s5_log_step").reshape(1, 64), (64, 1))
        sp = np.stack([lr, li, ls], 1)
        m[f"s5p{l}"] = A(np.concatenate([sp, sp], 0))
        row = np.stack([g("s5_lam_re").reshape(4096), g("s5_lam_im").reshape(4096), np.repeat(g("s5_log_step").reshape(64), 64)], 0)
        m[f"s5r{l}"] = A(np.tile(row[None], (16, 1, 1)))
        m[f"s5bt{l}"] = A(np.stack([g("s5_b_re").transpose(2, 0, 1).reshape(16, 2048), g("s5_b_im").transpose(2, 0, 1).reshape(16, 2048)], 1))
        cr = g("s5_c_re").transpose(3, 0, 1, 2).reshape(64, 64, 16); ci = g("s5_c_im").transpose(3, 0, 1, 2).reshape(64, 64, 16)
        m[f"s5ca{l}"] = A(np.concatenate([cr, ci], 0)); m[f"s5cb{l}"] = A(np.concatenate([ci, cr], 0))
        m[f"w_glu{l}"] = A(g("w_glu")); m[f"w_pool{l}"] = A(g("w_pool").transpose(1, 0, 2))
        m[f"rw_w2{l}"] = A(g("rwkv_w2").reshape(128, 1024)); m[f"rw_a2{l}"] = A(g("rwkv_a2").reshape(128, 1024))
    return m


_NC_CACHE = {}


def run(inputs, N, ncores=8, nlayers=2, debug=False, trace=False):
    key = (N, nlayers, debug)
    if key not in _NC_CACHE:
        _NC_CACHE[key] = build(N, nlayers, debug)
    nc = _NC_CACHE[key]
    maps = [host_layout(inputs, b, N) for b in range(2)]
    in_maps = [maps[c % 2] for c in range(ncores)]
    if trace:
        res = run_bass_kernel_spmd(nc, in_maps, core_ids=list(range(ncores)), trace=True)
        print("EXEC_TIME_NS", res.exec_time_ns)
    else:
        res = run_bass_kernel_spmd(nc, in_maps, core_ids=list(range(ncores)))
    if debug:
        return res
    out = np.stack([np.ascontiguousarray(res.results[b]["yT"].T) for b in range(2)], 0)
    return out.astype(np.float32)


def kernel(**inputs):
    return run(inputs, 16384, ncores=2)
```
